# Optimizing a Trainium2 kernel written in Bass

```python
import jax, jax.numpy as jnp
from jax import lax
import numpy as np

D_MODEL = 2048
BATCH = 4
SEQ = 4096
DEPTH = 4

GRID_W = 64
CTX_LEN = 256
EPS = 1e-6

D_SGU = D_MODEL // 4
SGU_HEADS = 4
SGU_HD = D_SGU // SGU_HEADS
SGU_CHUNK = 128
D_MLSTM = D_MODEL // 2
MLSTM_HEADS = 4
MLSTM_HD = D_MLSTM // MLSTM_HEADS
MLSTM_CHUNK = 128
QK_CONV = 3
D_POOL = D_MODEL // 4
POOL_WINDOWS = (2, 4, 8, 16)
POOL_GC = D_POOL // len(POOL_WINDOWS)
N_GATES = 2 * 2 * MLSTM_HEADS

OFF_U = 0
OFF_V = OFF_U + D_SGU
OFF_P = OFF_V + D_SGU
OFF_Q = OFF_P + D_POOL
OFF_O = OFF_Q + D_MLSTM
OFF_K = OFF_O + D_MLSTM
OFF_VM = OFF_K + D_MLSTM
OFF_G = OFF_VM + D_MLSTM
D_IN = OFF_G + N_GATES

N_KEYS = 128
N_EXPERTS = N_KEYS * N_KEYS
PEER_HEADS = 8
PEER_TOPK = 16
PEER_DK = 256
PEER_BLOCK = 128

kernel_name = 'hybrid_sgu_mlstm_pool_peer_dit'


def rmsnorm(x, g):
    xf = x.astype(jnp.float32)
    y = xf * lax.rsqrt(jnp.mean(xf * xf, axis=-1, keepdims=True) + EPS)
    return (y * g.astype(jnp.float32)).astype(x.dtype)


def ada_modulation(cond, w, b):
    m = jax.nn.silu(cond) @ w + b
    return jnp.split(m[:, None, :], 6, axis=-1)


def modulate(h, shift, scale):
    return h * (1 + scale) + shift


def chunk_sgu(u, v, g, w_s, b_s):
    B, T, _ = u.shape
    v = rmsnorm(v, g).reshape(B, T // SGU_CHUNK, SGU_CHUNK, SGU_HEADS, SGU_HD)
    mixed = jnp.einsum('hts,bcshd->bcthd', w_s, v) + b_s.T[None, None, :, :, None]
    return u * mixed.reshape(B, T, D_SGU)


def centred_dwconv(x, w):
    K, ch = w.shape
    pad = K // 2
    return lax.conv_general_dilated(x, w.astype(x.dtype)[:, None, :], window_strides=(1,),
                                    padding=[(pad, K - 1 - pad)],
                                    dimension_numbers=('NWC', 'WIO', 'NWC'), feature_group_count=ch)


def to_heads(a):
    B, T, _ = a.shape
    return a.reshape(B, T, MLSTM_HEADS, MLSTM_HD).transpose(0, 2, 1, 3)


def mlstm_inputs(q_raw, k_raw, v_raw, g_raw, conv_w, b_gate):
    k = to_heads(jax.nn.silu(centred_dwconv(k_raw, conv_w[:, D_MLSTM:])))
    v = to_heads(v_raw)
    B, T, _ = g_raw.shape
    g = (g_raw + b_gate).astype(jnp.float32).reshape(B, T, 2, 2, MLSTM_HEADS)
    g = g.transpose(2, 3, 0, 4, 1)
    q = None if q_raw is None else to_heads(jax.nn.silu(centred_dwconv(q_raw, conv_w[:, :D_MLSTM])))
    return q, k, v, g


def to_chunks(a):
    B, H, T = a.shape[:3]
    a = a.reshape((B, H, T // MLSTM_CHUNK, MLSTM_CHUNK) + a.shape[3:])
    return jnp.moveaxis(a, 2, 0)


def zero_state(batch):
    return (jnp.zeros((batch, MLSTM_HEADS, MLSTM_HD, MLSTM_HD), jnp.float32),
            jnp.zeros((batch, MLSTM_HEADS, MLSTM_HD), jnp.float32),
            jnp.zeros((batch, MLSTM_HEADS), jnp.float32))


def mlstm_scan(q, k, v, ig, fg, state, with_outputs):
    L = MLSTM_CHUNK
    causal = jnp.tril(jnp.ones((L, L), bool))
    kf = k.astype(jnp.float32) * (MLSTM_HD ** -0.5)
    xs = [to_chunks(a) for a in (kf, v.astype(jnp.float32), ig, fg)]
    if with_outputs:
        xs.append(to_chunks(q.astype(jnp.float32)))

    def step(carry, inp):
        C, n, m = carry
        kc, vc, ic, fc = inp[:4]
        b = jnp.cumsum(jax.nn.log_sigmoid(fc), axis=-1)
        b_last = b[..., -1]
        g = b_last[..., None] - b + ic
        m_new = jnp.maximum(b_last + m, g.max(-1))
        wk = jnp.exp(g - m_new[..., None])
        decay = jnp.exp(b_last + m - m_new)
        C_new = decay[..., None, None] * C + jnp.einsum('bhsv,bhsk->bhvk', vc * wk[..., None], kc)
        n_new = decay[..., None] * n + jnp.einsum('bhs,bhsk->bhk', wk, kc)
        if not with_outputs:
            return (C_new, n_new, m_new), None
        qc = inp[4]
        log_w = jnp.where(causal, b[..., :, None] - b[..., None, :] + ic[..., None, :], -jnp.inf)
        inter = b + m[..., None]
        m_t = jnp.maximum(inter, log_w.max(-1))
        w_prev = jnp.exp(inter - m_t)
        s = jnp.einsum('bhtd,bhsd->bhts', qc, kc) * jnp.exp(log_w - m_t[..., None])
        num = jnp.einsum('bhts,bhsv->bhtv', s, vc) + w_prev[..., None] * jnp.einsum('bhvk,bhtk->bhtv', C, qc)
        den = s.sum(-1) + w_prev * jnp.einsum('bhk,bhtk->bht', n, qc)
        h = num / jnp.maximum(jnp.abs(den), jnp.exp(-m_t))[..., None]
        return (C_new, n_new, m_new), h

    state, hs = lax.scan(step, state, tuple(xs))
    if not with_outputs:
        return None, state
    B, H, T, Dh = v.shape
    return jnp.moveaxis(hs, 0, 2).reshape(B, H, T, Dh), state


def mlstm_bidir(q, k, v, g, states, with_outputs):
    flip = lambda a: None if a is None else jnp.flip(a, axis=2)
    h_f, st_f = mlstm_scan(q, k, v, g[0, 0], g[0, 1], states[0], with_outputs)
    h_b, st_b = mlstm_scan(flip(q), flip(k), flip(v), flip(g[1, 0]), flip(g[1, 1]), states[1], with_outputs)
    h = h_f + jnp.flip(h_b, axis=2) if with_outputs else None
    return h, (st_f, st_b)


def mlstm_out(h, o_raw, g_norm, dtype):
    B, H, T, Dh = h.shape
    hn = h * lax.rsqrt(jnp.mean(h * h, axis=-1, keepdims=True) + EPS)
    hn = hn.transpose(0, 2, 1, 3).reshape(B, T, D_MLSTM) * g_norm.astype(jnp.float32)
    return (jax.nn.sigmoid(o_raw.astype(jnp.float32)) * hn).astype(dtype)


def window_bounds(n, w):
    idx = jnp.arange(n)
    return jnp.clip(idx - w // 2, 0, n), jnp.clip(idx - w // 2 + w, 0, n)


def grid_window_mean(x, rows, cols, w):
    B, T, ch = x.shape
    s = jnp.cumsum(jnp.cumsum(x.astype(jnp.float32).reshape(B, rows, cols, ch), axis=1), axis=2)
    s = jnp.pad(s, ((0, 0), (1, 0), (1, 0), (0, 0)))
    rlo, rhi = window_bounds(rows, w)
    clo, chi = window_bounds(cols, w)
    box = lambda r, c: jnp.take(jnp.take(s, r, axis=1), c, axis=2)
    tot = box(rhi, chi) - box(rlo, chi) - box(rhi, clo) + box(rlo, clo)
    cnt = ((rhi - rlo)[:, None] * (chi - clo)[None, :]).astype(jnp.float32)
    return (tot / cnt[None, :, :, None]).reshape(B, T, ch)


def pool_mixer(x, rows, cols, w_g, scale):
    B, T, _ = x.shape
    xg = x.reshape(B, T, len(POOL_WINDOWS), POOL_GC)
    means = jnp.stack([grid_window_mean(xg[:, :, i], rows, cols, w) for i, w in enumerate(POOL_WINDOWS)], axis=2)
    d = (means - xg.astype(jnp.float32)).astype(x.dtype)
    y = jnp.einsum('btgc,gcd->btgd', d, w_g)
    return y.reshape(B, T, D_POOL) * scale


def stream_mixers(p, h_mlstm, rows, cols, sgu_norm, sgu_w, sgu_b, mlstm_norm, pool_w, pool_scale):
    u = jax.nn.gelu(p[..., OFF_U:OFF_V])
    v = jax.nn.gelu(p[..., OFF_V:OFF_P])
    y_a = chunk_sgu(u, v, sgu_norm, sgu_w, sgu_b)
    y_b = mlstm_out(h_mlstm, p[..., OFF_O:OFF_K], mlstm_norm, p.dtype)
    y_c = pool_mixer(p[..., OFF_P:OFF_Q], rows, cols, pool_w, pool_scale)
    return jnp.concatenate([y_a, y_b, y_c], axis=-1)


def peer_ffn(h, wq, keys, u_tab, v_tab):
    B, T, D = h.shape
    ht = h.reshape(B * T, D)
    n = B * T
    q = (ht @ wq).astype(jnp.float32).reshape(n, PEER_HEADS, 2, PEER_DK // 2)
    s = jnp.einsum('nhpc,pkc->nhpk', q, keys.astype(jnp.float32))
    s1, i1 = lax.top_k(s[:, :, 0], PEER_TOPK)
    s2, i2 = lax.top_k(s[:, :, 1], PEER_TOPK)
    cand = (s1[..., :, None] + s2[..., None, :]).reshape(n, PEER_HEADS, PEER_TOPK * PEER_TOPK)
    sc, ci = lax.top_k(cand, PEER_TOPK)
    e = (jnp.take_along_axis(i1, ci // PEER_TOPK, axis=-1) * N_KEYS
         + jnp.take_along_axis(i2, ci % PEER_TOPK, axis=-1))
    gate = jax.nn.softmax(sc, axis=-1)
    nb = n // PEER_BLOCK

    def block(args):
        xb, eb, gb = args
        a = jnp.einsum('nd,nhkd->nhk', xb, jnp.take(u_tab, eb, axis=0))
        w = (gb * jax.nn.gelu(a.astype(jnp.float32))).astype(xb.dtype)
        return jnp.einsum('nhk,nhkd->nd', w, jnp.take(v_tab, eb, axis=0))

    out = lax.map(block, (ht.reshape(nb, PEER_BLOCK, D),
                          e.reshape(nb, PEER_BLOCK, PEER_HEADS, PEER_TOPK),
                          gate.reshape(nb, PEER_BLOCK, PEER_HEADS, PEER_TOPK)))
    return out.reshape(B, T, D)


def setup_inputs(seed: int = 0) -> dict:
    key = jax.random.key(seed)
    ks = jax.random.split(key, 24)
    L, D = DEPTH, D_MODEL
    nrm = lambda k, shape, s: jax.random.normal(k, shape, jnp.float32) * s
    gate_base = jnp.array([0.0, 3.0], jnp.float32)[None, None, :, None]
    return {
        'x': nrm(ks[0], (BATCH, SEQ, D), 1.0),
        'c': nrm(ks[1], (BATCH, D), 1.0),
        'ctx': nrm(ks[2], (BATCH, CTX_LEN, D), 1.0),
        'c_ctx': nrm(ks[3], (D,), 1.0),
        'ada_w': nrm(ks[4], (L, D, 6 * D), 0.5 * D ** -0.5),
        'ada_b': nrm(ks[5], (L, 6 * D), 0.02),
        'norm_mix': 1.0 + nrm(ks[6], (L, D), 0.05),
        'norm_ffn': 1.0 + nrm(ks[7], (L, D), 0.05),
        'w_in': nrm(ks[8], (L, D, D_IN), D ** -0.5),
        'b_gate': (gate_base + nrm(ks[9], (L, 2, 2, MLSTM_HEADS), 0.1)).reshape(L, N_GATES),
        'sgu_norm': 1.0 + nrm(ks[10], (L, D_SGU), 0.05),
        'sgu_w': nrm(ks[11], (L, SGU_HEADS, SGU_CHUNK, SGU_CHUNK), SGU_CHUNK ** -0.5),
        'sgu_b': 1.0 + nrm(ks[12], (L, SGU_HEADS, SGU_CHUNK), 0.1),
        'qk_conv': nrm(ks[13], (L, QK_CONV, 2 * D_MLSTM), QK_CONV ** -0.5),
        'mlstm_norm': 1.0 + nrm(ks[14], (L, D_MLSTM), 0.05),
        'pool_w': nrm(ks[15], (L, len(POOL_WINDOWS), POOL_GC, POOL_GC), POOL_GC ** -0.5),
        'pool_scale': 1.0 + nrm(ks[16], (L, D_POOL), 0.1),
        'w_out': nrm(ks[17], (L, D, D), D ** -0.5),
        'peer_wq': nrm(ks[18], (L, D, PEER_HEADS * PEER_DK), D ** -0.5),
        'peer_keys': nrm(ks[19], (L, 2, N_KEYS, PEER_DK // 2), (PEER_DK // 2) ** -0.5),
        'peer_u': nrm(ks[20], (L, N_EXPERTS, D), D ** -0.5),
        'peer_v': nrm(ks[21], (L, N_EXPERTS, D), 0.25),
        'norm_final': 1.0 + nrm(ks[22], (D,), 0.05),
    }


def reference(x, c, ctx, c_ctx, ada_w, ada_b, norm_mix, norm_ffn, w_in, b_gate, sgu_norm, sgu_w, sgu_b,
              qk_conv, mlstm_norm, pool_w, pool_scale, w_out, peer_wq, peer_keys, peer_u, peer_v, norm_final):
    rows = x.shape[1] // GRID_W
    ctx_len = ctx.shape[1]
    x_lat, x_ctx = x, ctx
    for l in range(DEPTH):
        last = l == DEPTH - 1
        m_lat = ada_modulation(c, ada_w[l], ada_b[l])
        m_ctx = ada_modulation(c_ctx[None, :], ada_w[l], ada_b[l])
        mix_args = (sgu_norm[l], sgu_w[l], sgu_b[l], mlstm_norm[l], pool_w[l], pool_scale[l])

        hc = modulate(rmsnorm(x_ctx, norm_mix[l]), m_ctx[0], m_ctx[1])
        if last:
            pk = hc @ w_in[l][:, OFF_K:]
            _, kc, vc, gc = mlstm_inputs(None, pk[..., :D_MLSTM], pk[..., D_MLSTM:2 * D_MLSTM],
                                          pk[..., 2 * D_MLSTM:], qk_conv[l], b_gate[l])
            _, ctx_states = mlstm_bidir(None, kc, vc, gc, (zero_state(x.shape[0]), zero_state(x.shape[0])), False)
        else:
            pc = hc @ w_in[l]
            qc, kc, vc, gc = mlstm_inputs(pc[..., OFF_Q:OFF_O], pc[..., OFF_K:OFF_VM], pc[..., OFF_VM:OFF_G],
                                          pc[..., OFF_G:], qk_conv[l], b_gate[l])
            h_c, ctx_states = mlstm_bidir(qc, kc, vc, gc, (zero_state(x.shape[0]), zero_state(x.shape[0])), True)
            y_c = stream_mixers(pc, h_c, 1, ctx_len, *mix_args)
            x_ctx_new = x_ctx + m_ctx[2] * (y_c @ w_out[l])
            hf = modulate(rmsnorm(x_ctx_new, norm_ffn[l]), m_ctx[3], m_ctx[4])
            x_ctx_new = x_ctx_new + m_ctx[5] * peer_ffn(hf, peer_wq[l], peer_keys[l], peer_u[l], peer_v[l])

        hl = modulate(rmsnorm(x_lat, norm_mix[l]), m_lat[0], m_lat[1])
        pl = hl @ w_in[l]
        ql, kl, vl, gl = mlstm_inputs(pl[..., OFF_Q:OFF_O], pl[..., OFF_K:OFF_VM], pl[..., OFF_VM:OFF_G],
                                      pl[..., OFF_G:], qk_conv[l], b_gate[l])
        h_l, _ = mlstm_bidir(ql, kl, vl, gl, ctx_states, True)
        y_l = stream_mixers(pl, h_l, rows, GRID_W, *mix_args)
        x_lat = x_lat + m_lat[2] * (y_l @ w_out[l])
        hf = modulate(rmsnorm(x_lat, norm_ffn[l]), m_lat[3], m_lat[4])
        x_lat = x_lat + m_lat[5] * peer_ffn(hf, peer_wq[l], peer_keys[l], peer_u[l], peer_v[l])

        if not last:
            x_ctx = x_ctx_new
    return rmsnorm(x_lat, norm_final)
```

```python
import numpy as np
from contextlib import ExitStack
import concourse.bass as bass
import concourse.mybir as mybir
from concourse.bass_utils import run_bass_kernel_spmd

F32 = mybir.dt.float32
BF16 = mybir.dt.bfloat16
AF = mybir.ActivationFunctionType
ALU = mybir.AluOpType
AX = mybir.AxisListType

D = 2048
KC = 16
D_IN = 5648
OFF_U, OFF_V, OFF_P, OFF_Q, OFF_O, OFF_K, OFF_VM, OFF_G = 0, 512, 1024, 1536, 2560, 3584, 4608, 5632
EPS = 1e-6
WINS = (2, 4, 8, 16)
NEG = -3.0e38


class SemW:
    def __init__(s, h, dma=False):
        s.h = h; s.total = 0; s.dma = dma


class Trk:
    __slots__ = ('w', 'r')

    def __init__(s):
        s.w = []; s.r = {}


class Buf:
    def __init__(s, t, ds=None):
        s.t = t; s.k = Trk(); s.ds = ds

    def __getitem__(s, idx):
        return s.t[idx]


class KB:
    def __init__(s, nc, es):
        s.nc = nc; s.es = es
        s.E = dict(pe=nc.tensor, dve=nc.vector, act=nc.scalar, pool=nc.gpsimd, sp=nc.sync)
        s.allsems = []
        s.esem = {e: s.newsem('e_' + e) for e in s.E}
        s.pesems = {s.esem['pe']}
        s.seen = {e: {} for e in s.E}
        s.dpool = []; s.dnext = 0; s.dbase = 0
        s.nm = 0

    def newsem(s, name, dma=False):
        h = s.es.enter_context(s.nc.semaphore(name + '_%d' % len(s.allsems)))
        sw = SemW(h, dma); s.allsems.append(sw); return sw

    def dsem(s):
        if s.dnext >= len(s.dpool):
            s.dpool.append(s.newsem('d', True))
        sw = s.dpool[s.dnext]; s.dnext += 1; return sw

    def _wait(s, e, deps):
        best = {}
        for sw, v in deps:
            if sw.dma: v = sw.total
            if v > best.get(sw, 0): best[sw] = v
        for sw, v in best.items():
            if s.seen[e].get(sw, 0) >= v: continue
            if e == 'pe' and sw in s.pesems: continue
            s.E[e].wait_ge(sw.h, v); s.seen[e][sw] = v

    def op(s, e, fn, reads=(), writes=()):
        deps = []
        for r in reads: deps += r.k.w
        for w in writes:
            deps += w.k.w; deps += list(w.k.r.items())
        s._wait(e, deps)
        sw = s.esem[e]
        if sw.total >= 30000:
            sw = s.newsem('e_' + e); s.esem[e] = sw
            if e == 'pe': s.pesems.add(sw)
        inst = fn(); sw.total += 1; inst.then_inc(sw.h, 1)
        for r in reads: r.k.r[sw] = sw.total
        for w in writes:
            w.k.w = [(sw, sw.total)]; w.k.r = {}
        return inst

    def dma(s, out, in_, sem, reads=(), writes=(), q='sp'):
        deps = []
        for r in reads: deps += r.k.w
        for w in writes:
            deps += w.k.w; deps += list(w.k.r.items())
        s._wait(q, deps)
        inst = s.E[q].dma_start(out=out, in_=in_); sem.total += 16; inst.then_inc(sem.h, 16)
        for r in reads: r.k.r[sem] = sem.total
        for w in writes:
            w.k.w = [(sem, sem.total)]; w.k.r = {}

    def load(s, buf, out, in_, q='sp'):
        s.dma(out, in_, buf.ds, writes=[buf], q=q)

    def store(s, buf, out, in_, q='sp'):
        s.dma(out, in_, buf.ds, reads=[buf], q=q)

    def barrier(s):
        deps = [(sw, sw.total) for sw in s.allsems if sw.total > 0]
        for e in s.E: s._wait(e, deps)


class Stage:
    def __init__(s, kb):
        s.kb = kb

    def __enter__(s):
        s.kb.barrier(); s.es = ExitStack(); s.es.__enter__(); s.kb.dnext = s.kb.dbase; return s

    def __exit__(s, *a):
        s.kb.barrier(); return s.es.__exit__(*a)

    def sb(s, shape, dt=F32, dma=False):
        s.kb.nm += 1
        t = s.es.enter_context(s.kb.nc.sbuf_tensor('b%d' % s.kb.nm, list(shape), dt))
        return Buf(t, s.kb.dsem() if dma else None)

    def ps(s, shape=(128, 512), dt=F32):
        s.kb.nm += 1
        t = s.es.enter_context(s.kb.nc.psum_tensor('p%d' % s.kb.nm, list(shape), dt))
        return Buf(t)


def bcast_rows(ap2d_row, n=128):
    a = ap2d_row
    return bass.AP(tensor=a.tensor, offset=a.offset, ap=[[0, n]] + [list(x) for x in list(a.ap)[1:]])


def build(cfg):
    T, TC, L, NK = cfg['T'], cfg['TC'], cfg['L'], cfg['NK']
    NE = NK * NK
    TT = T + TC
    NTC, NTL = TC // 128, T // 128
    NTT = NTC + NTL
    IB = 16 if NK >= 16 else NK
    NIB = NK // IB
    EB = IB * NK
    dbg = cfg.get('dbg', ())
    _i, _t, _pml, _pmc = host_consts(T, TC)
    PML_NZ = [[bool(_pml[g, d].any()) for d in range(9)] for g in range(4)]
    PMC_NZ = [[bool(_pmc[g, d].any()) for d in range(3)] for g in range(4)]
    nc = bass.Bass("TRN2", target_bir_lowering=False)

    def din(name, shape, dt=F32):
        return nc.dram_tensor(name, list(shape), dt, kind="ExternalInput").ap()

    def dsc(name, shape, dt=F32):
        return nc.dram_tensor(name, list(shape), dt, kind="Internal").ap()

    xin = din('xin', [TT, D])
    condT = din('condT', [128, KC, 2])
    ada_w = din('ada_w', [L, D, 6 * D]); ada_b = din('ada_b', [L, 6 * D])
    norm_mix = din('norm_mix', [L, D]); norm_ffn = din('norm_ffn', [L, D]); norm_final = din('norm_final', [1, D])
    w_in = din('w_in', [L, D, D_IN]); b_gate = din('b_gate', [L, 16])
    sgu_norm = din('sgu_norm', [L, 512]); sgu_wT = din('sgu_wT', [L, 4, 128, 128]); sgu_b = din('sgu_b', [L, 4, 128])
    qk_convT = din('qk_convT', [L, 2048, 3]); mlstm_norm = din('mlstm_norm', [L, 1024])
    pool_w = din('pool_w', [L, 4, 128, 128]); pool_scaleT = din('pool_scaleT', [L, 128, 4])
    w_out = din('w_out', [L, D, D]); peer_wq = din('peer_wq', [L, D, D])
    peer_keysT = din('peer_keysT', [L, 2, 128, NK])
    peer_uT = din('peer_uT', [L, D, NE]); peer_v = din('peer_v', [L, NE, D])
    identd = din('ident', [128, 128])
    trid = din('tri', [2, 128, 128])
    pml = din('pml', [4, 9, 128, 128])
    pmc = din('pmc', [4, 3, 128, 128])
    yout = nc.dram_tensor('yout', [T, D], F32, kind="ExternalOutput").ap()
    dbg_out = {}
    for nm, shp in dbg:
        dbg_out[nm] = nc.dram_tensor('dbg_' + nm, list(shp), F32, kind="ExternalOutput").ap()

    XL = dsc('XL', [TT, D])
    MOD = dsc('MOD', [2, 6 * D])
    UT = dsc('UT', [512, TT]); QR = dsc('QR', [1024, TT]); KR = dsc('KR', [1024, TT])
    VS = dsc('VS', [TT, 512]); PS = dsc('PS', [TT, 512]); OS = dsc('OS', [TT, 1024]); VM = dsc('VM', [TT, 1024])
    GS = dsc('GS', [TT, 16])
    YT = dsc('YT', [D, TT], BF16)
    HFT = dsc('HFT', [D, TT], BF16)
    SS = dsc('SS', [TT, 8 * 2 * NK]); BI2 = dsc('BI2', [TT, 8])
    UTB = dsc('UTB', [D, NE], BF16); VTB = dsc('VTB', [NE, D], BF16)

    es = ExitStack()
    with es:
        kb = KB(nc, es)
        V, A_, P_, PE = 'dve', 'act', 'pool', 'pe'

        def kcv(ap2d):
            return ap2d.rearrange("(kc p) c -> p kc c", p=128)

        cst = Stage(kb); cst.__enter__()
        ident = cst.sb([128, 128], F32, True); kb.load(ident, ident[:, :], identd[:, :])
        identb = cst.sb([128, 128], BF16)
        kb.op(V, lambda: nc.vector.tensor_copy(out=identb[:, :], in_=ident[:, :]), [ident], [identb])
        tri = cst.sb([128, 2, 128], F32, True)
        for i in range(2): kb.load(tri, tri[:, i, :], trid[i])
        trib = cst.sb([128, 2, 128], BF16)
        kb.op(V, lambda: nc.vector.tensor_copy(out=trib[:, :, :], in_=tri[:, :, :]), [tri], [trib])
        ones = cst.sb([128, 128], F32)
        kb.op(V, lambda: nc.vector.memset(ones[:, :], 1.0), [], [ones])
        sct = cst.sb([128, KC, 2], F32, True); kb.load(sct, sct[:, :, :], condT[:, :, :])
        kb.op(A_, lambda: nc.scalar.activation(out=sct[:, :, :], in_=sct[:, :, :], func=AF.Silu), [sct], [sct])
        PSB = [cst.ps() for _ in range(8)]
        kb.dbase = kb.dnext
        with Stage(kb) as st0:
            cp = [st0.sb([128, D], F32, True) for _ in range(2)]
            for tt in range(NTT):
                b = cp[tt % 2]
                kb.load(b, b[:, :], xin[tt * 128:(tt + 1) * 128, :])
                kb.store(b, XL[tt * 128:(tt + 1) * 128, :], b[:, :])

        def rms_rstd(st, xt, width, outcol, junk):
            kb.op(V, lambda: nc.vector.scalar_tensor_tensor(out=junk[0], in0=xt[0], scalar=1.0, in1=xt[0],
                                                            op0=ALU.mult, op1=ALU.mult, accum_out=outcol[0]),
                  [xt[1]], [junk[1], outcol[1]])
            kb.op(V, lambda: nc.vector.tensor_scalar(out=outcol[0], in0=outcol[0], scalar1=1.0 / width, scalar2=EPS,
                                                     op0=ALU.mult, op1=ALU.add), [outcol[1]], [outcol[1]])
            kb.op(A_, lambda: nc.scalar.activation(out=outcol[0], in_=outcol[0], func=AF.Sqrt), [outcol[1]], [outcol[1]])
            kb.op(V, lambda: nc.vector.reciprocal(out=outcol[0], in_=outcol[0]), [outcol[1]], [outcol[1]])

        def conv_weights(st, src2d, nk, ncols, dstdram):
            CW = min(ncols, 2048)
            f = [st.sb([128, CW], F32, True) for _ in range(2)]
            g = [st.sb([128, CW], BF16, True) for _ in range(2)]
            i = 0
            for k in range(nk):
                for c0 in range(0, ncols, CW):
                    a, b = f[i % 2], g[i % 2]
                    kb.load(a, a[:, :], src2d[k * 128:(k + 1) * 128, c0:c0 + CW])
                    eng = (V, P_)[i % 2]
                    E = kb.E[eng]
                    kb.op(eng, lambda: E.tensor_copy(out=b[:, :], in_=a[:, :]), [a], [b])
                    kb.store(b, dstdram[k * 128:(k + 1) * 128, c0:c0 + CW], b[:, :])
                    i += 1

        for l in range(L):
            with Stage(kb) as st:
                wb = [st.sb([128, KC, 512], F32, True) for _ in range(2)]
                adab = st.sb([2, 6 * D], F32, True)
                for r in range(2): kb.load(adab, adab[r:r + 1, :], ada_b[l:l + 1, :])
                nrm = st.sb([2, 2, D], F32, True)
                for r in range(2):
                    kb.load(nrm, nrm[r:r + 1, 0, :], norm_mix[l:l + 1, :])
                    kb.load(nrm, nrm[r:r + 1, 1, :], norm_ffn[l:l + 1, :])
                mo = [st.sb([2, 512], F32, True) for _ in range(2)]
                for cb in range(24):
                    w = wb[cb % 2]
                    src = kcv(ada_w[l])
                    for q4 in range(4):
                        kb.load(w, w[:, q4 * 4:(q4 + 1) * 4, :], src[:, q4 * 4:(q4 + 1) * 4, cb * 512:(cb + 1) * 512])
                    pb = PSB[cb % 2]
                    for kc in range(KC):
                        kb.op(PE, lambda: nc.tensor.matmul(pb[0:2, :], lhsT=sct[:, kc, :], rhs=w[:, kc, :],
                                                           start=(kc == 0), stop=(kc == KC - 1)), [sct, w], [pb])
                    m = mo[cb % 2]
                    kb.op(V, lambda: nc.vector.tensor_tensor(out=m[:, :], in0=pb[0:2, :], in1=adab[:, cb * 512:(cb + 1) * 512],
                                                             op=ALU.add), [pb, adab], [m])
                    part = cb // 4
                    if part in (1, 4):
                        j = 0 if part == 1 else 1
                        c0 = (cb % 4) * 512
                        kb.op(V, lambda: nc.vector.scalar_tensor_tensor(out=m[:, :], in0=m[:, :], scalar=1.0,
                                                                        in1=nrm[:, j, c0:c0 + 512], op0=ALU.add, op1=ALU.mult),
                              [m, nrm], [m])
                    kb.store(m, MOD[:, cb * 512:(cb + 1) * 512], m[:, :])

            def modtile(st, buf, cond, part):
                kb.load(buf, buf[:, :], bcast_rows(MOD[cond:cond + 1, part * D:(part + 1) * D]))

            with Stage(kb) as st:
                A1 = st.sb([128, D], F32, True); B1 = st.sb([128, D], F32, True)
                bg = st.sb([128, 16], F32, True)
                kb.load(bg, bg[:, :], bcast_rows(b_gate[l:l + 1, :]))
                xt = [st.sb([128, D], F32, True) for _ in range(2)]
                hh = [st.sb([128, D], F32) for _ in range(2)]
                junk = st.sb([128, D], F32)
                rs = [st.sb([128, 1], F32) for _ in range(2)]
                hT = st.sb([128, KC, 512], BF16)
                wf = [st.sb([128, KC, 512], F32, True) for _ in range(1)]
                wbf = [st.sb([128, KC, 512], BF16) for _ in range(2)]
                ev = [st.sb([128, 512], F32, True) for _ in range(3)]
                evi = [0]
                wi = [0]
                w2 = kcv(w_in[l])
                sts = []
                t0 = 0
                while t0 < NTC: n = min(4, NTC - t0); sts.append((1, t0, n)); t0 += n
                while t0 < NTT: n = min(4, NTT - t0); sts.append((0, t0, n)); t0 += n
                curc = None
                for (cond, tb, n) in sts:
                    if cond != curc:
                        modtile(st, A1, cond, 1); modtile(st, B1, cond, 0); curc = cond
                    NS = n * 128
                    for ti in range(n):
                        tt = tb + ti
                        x = xt[tt % 2]; h = hh[tt % 2]; r = rs[tt % 2]
                        kb.load(x, x[:, :], XL[tt * 128:(tt + 1) * 128, :])
                        rms_rstd(st, (x[:, :], x), D, (r[:, :], r), (junk[:, :], junk))
                        kb.op(V, lambda: nc.vector.scalar_tensor_tensor(out=h[:, :], in0=x[:, :], scalar=r[:, 0:1], in1=A1[:, :],
                                                                        op0=ALU.mult, op1=ALU.mult), [x, r, A1], [h])
                        kb.op(P_, lambda: nc.gpsimd.tensor_tensor(out=h[:, :], in0=h[:, :], in1=B1[:, :], op=ALU.add), [h, B1], [h])
                        for k4 in range(4):
                            pb = PSB[k4 % 2]
                            for j in range(4):
                                kc = k4 * 4 + j
                                kb.op(PE, lambda: nc.tensor.transpose(pb[:, j * 128:(j + 1) * 128], h[:, kc * 128:(kc + 1) * 128], ident[:, :]),
                                      [h, ident], [pb])
                            kb.op(A_, lambda: nc.scalar.copy(out=hT[:, k4 * 4:(k4 + 1) * 4, ti * 128:(ti + 1) * 128],
                                                             in_=pb[:, :].rearrange("p (a b) -> p a b", a=4)), [pb], [hT])

                    def wload(c0, ncols):
                        i = wi[0]; wi[0] += 1
                        a, b = wf[0], wbf[i % 2]
                        for q4 in range(4):
                            kb.load(a, a[:, q4 * 4:(q4 + 1) * 4, 0:ncols], w2[:, q4 * 4:(q4 + 1) * 4, c0:c0 + ncols])
                        eng = (V, P_)[i % 2]; E = kb.E[eng]
                        kb.op(eng, lambda: E.tensor_copy(out=b[:, :, 0:ncols], in_=a[:, :, 0:ncols]), [a], [b])
                        return b

                    def evbuf():
                        e = ev[evi[0] % 3]; evi[0] += 1; return e

                    for (off, nb, dst, fn) in ((OFF_U, 4, UT, AF.Gelu_apprx_tanh), (OFF_Q, 8, QR, None), (OFF_K, 8, KR, None)):
                        for g4 in range(nb // 4):
                            wbb = wload(off + g4 * 512, 512)
                            for j in range(4):
                                blk = g4 * 4 + j
                                pb = PSB[2 + (blk % 2)]
                                for kc in range(KC):
                                    kb.op(PE, lambda: nc.tensor.matmul(pb[:, 0:NS], lhsT=wbb[:, kc, j * 128:(j + 1) * 128], rhs=hT[:, kc, 0:NS],
                                                                       start=(kc == 0), stop=(kc == KC - 1)), [wbb, hT], [pb])
                                e = evbuf()
                                if fn is None:
                                    kb.op(A_, lambda: nc.scalar.copy(out=e[:, 0:NS], in_=pb[:, 0:NS]), [pb], [e])
                                else:
                                    kb.op(A_, lambda: nc.scalar.activation(out=e[:, 0:NS], in_=pb[:, 0:NS], func=fn), [pb], [e])
                                kb.store(e, dst[blk * 128:(blk + 1) * 128, tb * 128:tb * 128 + NS], e[:, 0:NS])
                    for (off, ncols, dst, dc0, fn) in ((OFF_V, 512, VS, 0, AF.Gelu_apprx_tanh), (OFF_P, 512, PS, 0, None),
                                                       (OFF_O, 512, OS, 0, AF.Sigmoid), (OFF_O + 512, 512, OS, 512, AF.Sigmoid),
                                                       (OFF_VM, 512, VM, 0, None), (OFF_VM + 512, 512, VM, 512, None),
                                                       (OFF_G, 16, GS, 0, 'gate')):
                        wbb = wload(off, ncols)
                        for ti in range(n):
                            tt = tb + ti
                            pb = PSB[4 + (ti % 2)]
                            for kc in range(KC):
                                kb.op(PE, lambda: nc.tensor.matmul(pb[:, 0:ncols], lhsT=hT[:, kc, ti * 128:(ti + 1) * 128], rhs=wbb[:, kc, 0:ncols],
                                                                   start=(kc == 0), stop=(kc == KC - 1)), [wbb, hT], [pb])
                            e = evbuf()
                            if fn is None:
                                kb.op(A_, lambda: nc.scalar.copy(out=e[:, 0:ncols], in_=pb[:, 0:ncols]), [pb], [e])
                            elif fn == 'gate':
                                kb.op(V, lambda: nc.vector.tensor_tensor(out=e[:, 0:ncols], in0=pb[:, 0:ncols], in1=bg[:, :], op=ALU.add), [pb, bg], [e])
                            else:
                                kb.op(A_, lambda: nc.scalar.activation(out=e[:, 0:ncols], in_=pb[:, 0:ncols], func=fn), [pb], [e])
                            kb.store(e, dst[tt * 128:(tt + 1) * 128, dc0:dc0 + ncols], e[:, 0:ncols])

            if 'UT' in dbg_out and l == 0:
                with Stage(kb) as st:
                    for (nm, src) in (('UT', UT), ('QR', QR), ('VS', VS), ('GS', GS), ('OS', OS)):
                        if nm in dbg_out:
                            R, C = src.shape
                            for r0 in range(0, R, 128):
                                rr = min(128, R - r0)
                                b = st.sb([128, C], F32, True)
                                kb.load(b, b[0:rr, :], src[r0:r0 + rr, :]); kb.store(b, dbg_out[nm][r0:r0 + rr, :], b[0:rr, :])

            with Stage(kb) as st:
                sgn = st.sb([128, 512], F32, True); kb.load(sgn, sgn[:, :], bcast_rows(sgu_norm[l:l + 1, :]))
                wsf = st.sb([128, 4, 128], F32, True)
                for h in range(4): kb.load(wsf, wsf[:, h, :], sgu_wT[l, h])
                wsb = st.sb([128, 4, 128], BF16)
                kb.op(V, lambda: nc.vector.tensor_copy(out=wsb[:, :, :], in_=wsf[:, :, :]), [wsf], [wsb])
                sbr = st.sb([1, 4, 128], F32, True); kb.load(sbr, sbr[0:1, :, :], sgu_b[l:l + 1, :, :])
                sbb = st.sb([1, 4, 128], BF16)
                kb.op(V, lambda: nc.vector.tensor_copy(out=sbb[:, :, :], in_=sbr[:, :, :]), [sbr], [sbb])
                onesb = st.sb([1, 128], BF16)
                kb.op(V, lambda: nc.vector.memset(onesb[:, :], 1.0), [], [onesb])
                pwf = st.sb([128, 4, 128], F32, True)
                for g in range(4): kb.load(pwf, pwf[:, g, :], pool_w[l, g])
                pwb = st.sb([128, 4, 128], BF16)
                kb.op(V, lambda: nc.vector.tensor_copy(out=pwb[:, :, :], in_=pwf[:, :, :]), [pwf], [pwb])
                psc = st.sb([128, 4], F32, True); kb.load(psc, psc[:, :], pool_scaleT[l])
                pmf = st.sb([128, 9, 128], F32, True)
                pmlb = st.sb([128, 4, 9, 128], BF16); pmcb = st.sb([128, 4, 3, 128], BF16)
                for g in range(4):
                    for dk in range(9): kb.load(pmf, pmf[:, dk, :], pml[g, dk])
                    kb.op(V, lambda: nc.vector.tensor_copy(out=pmlb[:, g, :, :], in_=pmf[:, :, :]), [pmf], [pmlb])
                for g in range(4):
                    for dk in range(3): kb.load(pmf, pmf[:, dk, :], pmc[g, dk])
                    kb.op(V, lambda: nc.vector.tensor_copy(out=pmcb[:, g, :, :], in_=pmf[:, 0:3, :]), [pmf], [pmcb])
                XPf = st.sb([128, NTT, 512], F32, True)
                XP = st.sb([128, NTT, 4, 129], BF16)
                kb.op(P_, lambda: nc.gpsimd.memset(XP[:, :, :, :], 1.0), [], [XP])
                for tt in range(NTT):
                    kb.load(XPf, XPf[:, tt, :], PS[tt * 128:(tt + 1) * 128, :])
                kb.op(V, lambda: nc.vector.tensor_copy(out=XP[:, :, :, 0:128], in_=XPf[:, :, :].rearrange("p t (g c) -> p t g c", g=4)),
                      [XPf], [XP])
                vt = [st.sb([128, 512], F32, True) for _ in range(2)]
                vn = [st.sb([128, 512], BF16) for _ in range(2)]
                ut = [st.sb([128, 4, 128], F32, True) for _ in range(2)]
                ya = [st.sb([128, 4, 128], BF16, True) for _ in range(2)]
                yc = [st.sb([128, 4, 128], BF16, True) for _ in range(2)]
                junk = st.sb([128, 512], F32); rs = [st.sb([128, 1], F32) for _ in range(2)]
                mean = [st.sb([128, 512], F32) for _ in range(2)]
                rc = [st.sb([128, 4], F32) for _ in range(2)]
                dT = [st.sb([128, 4, 128], BF16) for _ in range(2)]
                for tt in range(NTT):
                    i2 = tt % 2
                    isctx = tt < NTC
                    v = vt[i2]; u = ut[i2]; r = rs[i2]; vb = vn[i2]
                    kb.load(v, v[:, :], VS[tt * 128:(tt + 1) * 128, :])
                    kb.load(u, u[:, :, :], UT[:, tt * 128:(tt + 1) * 128].rearrange("(h p) n -> p h n", p=128))
                    rms_rstd(st, (v[:, :], v), 512, (r[:, :], r), (junk[:, :], junk))
                    kb.op(V, lambda: nc.vector.scalar_tensor_tensor(out=vb[:, :], in0=v[:, :], scalar=r[:, 0:1], in1=sgn[:, :],
                                                                    op0=ALU.mult, op1=ALU.mult), [v, r, sgn], [vb])
                    pb = PSB[i2]
                    for h in range(4):
                        kb.op(PE, lambda: nc.tensor.matmul(pb[:, h * 128:(h + 1) * 128], lhsT=vb[:, h * 128:(h + 1) * 128], rhs=wsb[:, h, :],
                                                           start=True, stop=False), [vb, wsb], [pb])
                        kb.op(PE, lambda: nc.tensor.matmul(pb[:, h * 128:(h + 1) * 128], lhsT=onesb[0:1, :], rhs=sbb[0:1, h, :],
                                                           start=False, stop=True), [onesb, sbb], [pb])
                    y = ya[i2]
                    kb.op(V, lambda: nc.vector.tensor_tensor(out=y[:, :, :], in0=pb[:, :].rearrange("p (h n) -> p h n", h=4), in1=u[:, :, :],
                                                             op=ALU.mult), [pb, u], [y])
                    kb.store(y, YT[0:512, tt * 128:(tt + 1) * 128].rearrange("(h p) n -> p h n", p=128), y[:, :, :])
                    if isctx:
                        k, nt, tbase, pmb, dks, dko = tt, NTC, 0, pmcb, (-1, 0, 1), 1
                    else:
                        k, nt, tbase, pmb, dks, dko = tt - NTC, NTL, NTC, pmlb, tuple(range(-4, 5)), 4
                    mn = mean[i2]; rcc = rc[i2]
                    pb2 = PSB[2 + i2]; pb3 = PSB[4 + i2]
                    for g in range(4):
                        w = WINS[g]
                        nzm = PMC_NZ if isctx else PML_NZ
                        use = [dk for dk in dks if 0 <= k + dk < nt and nzm[g][dk + dko]]
                        pbg = pb2 if g < 2 else pb3
                        c0 = (g % 2) * 129
                        for j, dk in enumerate(use):
                            kb.op(PE, lambda: nc.tensor.matmul(pbg[:, c0:c0 + 129], lhsT=pmb[:, g, dk + dko, :], rhs=XP[:, tbase + k + dk, g, :],
                                                               start=(j == 0), stop=(j == len(use) - 1)), [pmb, XP], [pbg])
                    for g in range(4):
                        pbg = pb2 if g < 2 else pb3
                        c0 = (g % 2) * 129
                        kb.op(V, lambda: nc.vector.reciprocal(out=rcc[:, g:g + 1], in_=pbg[:, c0 + 128:c0 + 129]), [pbg], [rcc])
                        kb.op(V, lambda: nc.vector.scalar_tensor_tensor(out=mn[:, g * 128:(g + 1) * 128], in0=pbg[:, c0:c0 + 128], scalar=rcc[:, g:g + 1],
                                                                        in1=XPf[:, tt, g * 128:(g + 1) * 128], op0=ALU.mult, op1=ALU.subtract),
                              [pbg, rcc, XPf], [mn])
                    pb4 = PSB[6 + i2]
                    for g in range(4):
                        kb.op(PE, lambda: nc.tensor.transpose(pb4[:, g * 128:(g + 1) * 128], mn[:, g * 128:(g + 1) * 128], ident[:, :]), [mn, ident], [pb4])
                    d = dT[i2]
                    kb.op(A_, lambda: nc.scalar.copy(out=d[:, :, :], in_=pb4[:, :].rearrange("p (g n) -> p g n", g=4)), [pb4], [d])
                    for g in range(4):
                        kb.op(PE, lambda: nc.tensor.matmul(pb4[:, g * 128:(g + 1) * 128], lhsT=pwb[:, g, :], rhs=d[:, g, :], start=True, stop=True),
                              [pwb, d], [pb4])
                    y2 = yc[i2]
                    for g in range(4):
                        kb.op(V, lambda: nc.vector.tensor_scalar(out=y2[:, g, :], in0=pb4[:, g * 128:(g + 1) * 128], scalar1=psc[:, g:g + 1], scalar2=None,
                                                                 op0=ALU.mult), [pb4, psc], [y2])
                    kb.store(y2, YT[1536:2048, tt * 128:(tt + 1) * 128].rearrange("(g p) n -> p g n", p=128), y2[:, :, :])

            with Stage(kb) as st:
                EA = st.sb([128, NTT, 8], F32); EBc = st.sb([128, NTT, 8], F32); EBL = st.sb([128, NTT, 8], F32)
                gt = [st.sb([128, 16], F32, True) for _ in range(2)]
                lf = [st.sb([128, 8], F32) for _ in range(2)]
                t1 = [st.sb([128, 8], F32) for _ in range(2)]
                gi = [st.sb([128, 8], F32) for _ in range(2)]
                for tt in range(NTT):
                    i2 = tt % 2
                    g = gt[i2]; f = lf[i2]; a = t1[i2]; gii = gi[i2]
                    kb.load(g, g[:, :], GS[tt * 128:(tt + 1) * 128, :])
                    gv = g[:, :].rearrange("p (d g h) -> p d g h", d=2, g=2)
                    fv = f[:, :].rearrange("p (d h) -> p d h", d=2)
                    av = a[:, :].rearrange("p (d h) -> p d h", d=2)
                    kb.op(A_, lambda: nc.scalar.activation(out=av, in_=gv[:, :, 1, :], func=AF.Abs), [g], [a])
                    kb.op(A_, lambda: nc.scalar.activation(out=a[:, :], in_=a[:, :], func=AF.Exp, scale=-1.0), [a], [a])
                    kb.op(A_, lambda: nc.scalar.activation(out=a[:, :], in_=a[:, :], func=AF.Ln, bias=1.0), [a], [a])
                    kb.op(V, lambda: nc.vector.tensor_scalar(out=fv, in0=gv[:, :, 1, :], scalar1=0.0, scalar2=None, op0=ALU.min), [g], [f])
                    kb.op(V, lambda: nc.vector.tensor_tensor(out=f[:, :], in0=f[:, :], in1=a[:, :], op=ALU.subtract), [f, a], [f])
                    kb.op(V, lambda: nc.vector.tensor_copy(out=gii[:, :].rearrange("p (d h) -> p d h", d=2), in_=gv[:, :, 0, :]), [g], [gii])
                    pb = PSB[i2]
                    for dr in range(2):
                        kb.op(PE, lambda: nc.tensor.matmul(pb[:, dr * 4:dr * 4 + 4], lhsT=tri[:, dr, :], rhs=f[:, dr * 4:dr * 4 + 4], start=True, stop=True),
                              [tri, f], [pb])
                    kb.op(PE, lambda: nc.tensor.matmul(pb[:, 8:16], lhsT=ones[:, :], rhs=f[:, :], start=True, stop=True), [ones, f], [pb])
                    kb.op(A_, lambda: nc.scalar.activation(out=EBc[:, tt, :], in_=pb[:, 0:8], func=AF.Exp), [pb], [EBc])
                    kb.op(A_, lambda: nc.scalar.activation(out=EBL[:, tt, :], in_=pb[:, 8:16], func=AF.Exp), [pb], [EBL])
                    kb.op(V, lambda: nc.vector.tensor_tensor(out=gii[:, :], in0=gii[:, :], in1=pb[:, 0:8], op=ALU.subtract), [gii, pb], [gii])
                    kb.op(A_, lambda: nc.scalar.activation(out=EA[:, tt, :], in_=gii[:, :], func=AF.Exp), [gii], [EA])
                gnb = st.sb([128, 1024], F32, True); kb.load(gnb, gnb[:, :], bcast_rows(mlstm_norm[l:l + 1, :]))
                cw = st.sb([128, 16, 3], F32, True)
                kb.load(cw, cw[:, :, :], qk_convT[l].rearrange("(c p) k -> p c k", p=128))
                raw = [st.sb([128, TT + 2], F32, True) for _ in range(1)]
                cv = [st.sb([128, TT], F32) for _ in range(1)]
                qT = st.sb([128, 2, TT], BF16); kT = st.sb([128, 2, TT], BF16)
                kS = st.sb([128, NTT, 256], BF16)
                vf = [st.sb([128, 256], F32, True) for _ in range(2)]
                vw = [st.sb([128, 257], BF16) for _ in range(2)]
                stm = [st.sb([128, 128], BF16) for _ in range(2)]
                C32 = st.sb([128, 2, 257], F32); Cb = st.sb([128, 2, 257], BF16)
                hs = [st.sb([128, 257], F32) for _ in range(2)]
                dn = [st.sb([128, 1], F32) for _ in range(2)]
                HF = st.sb([128, NTT, 256], F32)
                ot = [st.sb([128, 256], F32, True) for _ in range(2)]
                yb = [st.sb([128, 256], F32) for _ in range(2)]
                ybT = [st.sb([128, 2, 128], BF16, True) for _ in range(2)]
                junk = st.sb([128, 256], F32); rs = [st.sb([128, 1], F32) for _ in range(2)]
                for hd in range(4):
                    for which, (src, dstT, scl) in enumerate(((QR, qT, 1.0), (KR, kT, 1.0 / 16.0))):
                        for c in range(2):
                            ch = hd * 2 + c
                            rw = raw[0]; co = cv[0]
                            wcol = which * 8 + ch
                            kb.op(P_, lambda: nc.gpsimd.memset(rw[:, :], 0.0), [], [rw])
                            kb.load(rw, rw[:, 1:TT + 1], src[ch * 128:(ch + 1) * 128, :])
                            for (s0, sl) in ((0, TC), (TC, T)):
                                kb.op(V, lambda: nc.vector.tensor_scalar(out=co[:, s0:s0 + sl], in0=rw[:, 1 + s0:1 + s0 + sl], scalar1=cw[:, wcol, 1:2],
                                                                         scalar2=None, op0=ALU.mult), [rw, cw], [co])
                                kb.op(V, lambda: nc.vector.scalar_tensor_tensor(out=co[:, s0 + 1:s0 + sl], in0=rw[:, 1 + s0:s0 + sl], scalar=cw[:, wcol, 0:1],
                                                                                in1=co[:, s0 + 1:s0 + sl], op0=ALU.mult, op1=ALU.add), [rw, cw, co], [co])
                                kb.op(V, lambda: nc.vector.scalar_tensor_tensor(out=co[:, s0:s0 + sl - 1], in0=rw[:, 2 + s0:1 + s0 + sl], scalar=cw[:, wcol, 2:3],
                                                                                in1=co[:, s0:s0 + sl - 1], op0=ALU.mult, op1=ALU.add), [rw, cw, co], [co])
                            kb.op(A_, lambda: nc.scalar.activation(out=co[:, :], in_=co[:, :], func=AF.Silu), [co], [co])
                            kb.op(V, lambda: nc.vector.tensor_scalar(out=dstT[:, c, :], in0=co[:, :], scalar1=scl, scalar2=None, op0=ALU.mult), [co], [dstT])
                            if which == 1:
                                for t4 in range(0, NTT, 4):
                                    nn = min(4, NTT - t4)
                                    pb = PSB[(t4 // 4) % 2]
                                    for j in range(nn):
                                        kb.op(PE, lambda: nc.tensor.transpose(pb[:, j * 128:(j + 1) * 128], co[:, (t4 + j) * 128:(t4 + j + 1) * 128], ident[:, :]),
                                              [co, ident], [pb])
                                    kb.op(V, lambda: nc.vector.tensor_scalar(out=kS[:, t4:t4 + nn, c * 128:(c + 1) * 128],
                                                                             in0=pb[:, 0:nn * 128].rearrange("p (a b) -> p a b", a=nn),
                                                                             scalar1=scl, scalar2=None, op0=ALU.mult), [pb], [kS])
                    for dr in range(2):
                        col = dr * 4 + hd
                        kb.op(V, lambda: nc.vector.memset(C32[:, :, :], 0.0), [], [C32])
                        kb.op(V, lambda: nc.vector.memset(Cb[:, :, :], 0.0), [], [Cb])
                        order = list(range(NTT)) if dr == 0 else (list(range(NTC - 1, -1, -1)) + list(range(NTT - 1, NTC - 1, -1)))
                        for ci, tt in enumerate(order):
                            i2 = ci % 2
                            v = vf[i2]; vv = vw[i2]; sm = stm[i2]; h_ = hs[i2]; d_ = dn[i2]
                            ts = slice(tt * 128, (tt + 1) * 128)
                            kb.load(v, v[:, :], VM[ts, hd * 256:(hd + 1) * 256])
                            kb.op(P_, lambda: nc.gpsimd.tensor_scalar(out=vv[:, 0:256], in0=v[:, :], scalar1=EA[:, tt, col:col + 1], scalar2=None, op0=ALU.mult),
                                  [v, EA], [vv])
                            kb.op(P_, lambda: nc.gpsimd.tensor_copy(out=vv[:, 256:257], in_=EA[:, tt, col:col + 1]), [EA], [vv])
                            pS = PSB[2 + i2]
                            for c in range(2):
                                kb.op(PE, lambda: nc.tensor.matmul(pS[:, 0:128], lhsT=kT[:, c, ts], rhs=qT[:, c, ts], start=(c == 0), stop=(c == 1)),
                                      [kT, qT], [pS])
                            kb.op(V, lambda: nc.vector.tensor_tensor(out=sm[:, :], in0=pS[:, 0:128], in1=tri[:, dr, :], op=ALU.mult), [pS, tri], [sm])
                            pH = PSB[4 + i2]
                            kb.op(PE, lambda: nc.tensor.matmul(pH[:, 0:257], lhsT=sm[:, :], rhs=vv[:, :], start=True, stop=False), [sm, vv], [pH])
                            for c in range(2):
                                kb.op(PE, lambda: nc.tensor.matmul(pH[:, 0:257], lhsT=qT[:, c, ts], rhs=Cb[:, c, :], start=False, stop=(c == 1)),
                                      [qT, Cb], [pH])
                            kb.op(A_, lambda: nc.scalar.activation(out=h_[:, :], in_=pH[:, 0:257], func=AF.Copy, scale=EBc[:, tt, col:col + 1]), [pH, EBc], [h_])
                            for c in range(2):
                                pC = PSB[6 + c]
                                kb.op(PE, lambda: nc.tensor.matmul(pC[:, 0:257], lhsT=kS[:, tt, c * 128:(c + 1) * 128], rhs=vv[:, :], start=True, stop=True),
                                      [kS, vv], [pC])
                                kb.op(V, lambda: nc.vector.tensor_tensor(out=C32[:, c, :], in0=pC[:, 0:257], in1=C32[:, c, :], op=ALU.add), [pC, C32], [C32])
                            kb.op(V, lambda: nc.vector.tensor_scalar(out=C32[:, :, :], in0=C32[:, :, :], scalar1=EBL[:, tt, col:col + 1], scalar2=None, op0=ALU.mult),
                                  [C32, EBL], [C32])
                            kb.op(P_, lambda: nc.gpsimd.tensor_copy(out=Cb[:, :, :], in_=C32[:, :, :]), [C32], [Cb])
                            kb.op(A_, lambda: nc.scalar.activation(out=d_[:, :], in_=h_[:, 256:257], func=AF.Abs), [h_], [d_])
                            kb.op(V, lambda: nc.vector.tensor_scalar(out=d_[:, :], in0=d_[:, :], scalar1=1.0, scalar2=None, op0=ALU.max), [d_], [d_])
                            kb.op(V, lambda: nc.vector.reciprocal(out=d_[:, :], in_=d_[:, :]), [d_], [d_])
                            if dr == 0:
                                kb.op(V, lambda: nc.vector.tensor_scalar(out=HF[:, tt, :], in0=h_[:, 0:256], scalar1=d_[:, 0:1], scalar2=None, op0=ALU.mult),
                                      [h_, d_], [HF])
                            else:
                                y_ = yb[i2]; o_ = ot[i2]; r = rs[i2]; yt_ = ybT[i2]
                                kb.op(V, lambda: nc.vector.scalar_tensor_tensor(out=y_[:, :], in0=h_[:, 0:256], scalar=d_[:, 0:1], in1=HF[:, tt, :],
                                                                                op0=ALU.mult, op1=ALU.add), [h_, d_, HF], [y_])
                                kb.load(o_, o_[:, :], OS[ts, hd * 256:(hd + 1) * 256])
                                rms_rstd(st, (y_[:, :], y_), 256, (r[:, :], r), (junk[:, :], junk))
                                kb.op(V, lambda: nc.vector.scalar_tensor_tensor(out=y_[:, :], in0=y_[:, :], scalar=r[:, 0:1], in1=gnb[:, hd * 256:(hd + 1) * 256],
                                                                                op0=ALU.mult, op1=ALU.mult), [y_, r, gnb], [y_])
                                kb.op(P_, lambda: nc.gpsimd.tensor_tensor(out=y_[:, :], in0=y_[:, :], in1=o_[:, :], op=ALU.mult), [y_, o_], [y_])
                                pT = PSB[i2]
                                for c in range(2):
                                    kb.op(PE, lambda: nc.tensor.transpose(pT[:, c * 128:(c + 1) * 128], y_[:, c * 128:(c + 1) * 128], ident[:, :]), [y_, ident], [pT])
                                kb.op(A_, lambda: nc.scalar.copy(out=yt_[:, :, :], in_=pT[:, 0:256].rearrange("p (c n) -> p c n", c=2)), [pT], [yt_])
                                kb.store(yt_, YT[512 + hd * 256:512 + (hd + 1) * 256, ts].rearrange("(c p) n -> p c n", p=128), yt_[:, :, :])

            if 'YT' in dbg_out and l == 0:
                with Stage(kb) as st:
                    for r0 in range(0, D, 128):
                        b = st.sb([128, TT], BF16, True); b2 = st.sb([128, TT], F32, True)
                        kb.load(b, b[:, :], YT[r0:r0 + 128, :])
                        kb.op(V, lambda: nc.vector.tensor_copy(out=b2[:, :], in_=b[:, :]), [b], [b2])
                        kb.store(b2, dbg_out['YT'][r0:r0 + 128, :], b2[:, :])

            with Stage(kb) as st:
                conv_weights(st, peer_uT[l], KC, NE, UTB)
            with Stage(kb) as st:
                conv_weights(st, peer_v[l], NE // 128, D, VTB)

            with Stage(kb) as st:
                wo = st.sb([128, KC, D], BF16)
                wf = [st.sb([128, D], F32, True) for _ in range(2)]
                for kc in range(KC):
                    a = wf[kc % 2]
                    kb.load(a, a[:, :], w_out[l, kc * 128:(kc + 1) * 128, :])
                    eng = (V, P_)[kc % 2]; E = kb.E[eng]
                    kb.op(eng, lambda: E.tensor_copy(out=wo[:, kc, :], in_=a[:, :]), [a], [wo])
                G1 = st.sb([128, D], F32, True); A2 = st.sb([128, D], F32, True); B2 = st.sb([128, D], F32, True)
                yt = [st.sb([128, KC, 128], BF16, True) for _ in range(2)]
                xt = [st.sb([128, D], F32, True) for _ in range(2)]
                hh = [st.sb([128, D], F32) for _ in range(2)]
                hT = [st.sb([128, KC, 128], BF16, True) for _ in range(2)]
                junk = st.sb([128, D], F32); rs = [st.sb([128, 1], F32) for _ in range(2)]
                curc = None
                for tt in range(NTT):
                    cond = 1 if tt < NTC else 0
                    if cond != curc:
                        modtile(st, G1, cond, 2); modtile(st, A2, cond, 4); modtile(st, B2, cond, 3); curc = cond
                    i2 = tt % 2
                    y = yt[i2]; x = xt[i2]; h = hh[i2]; r = rs[i2]; ht = hT[i2]
                    ts = slice(tt * 128, (tt + 1) * 128)
                    kb.load(y, y[:, :, :], YT[:, ts].rearrange("(kc p) n -> p kc n", p=128))
                    kb.load(x, x[:, :], XL[ts, :])
                    for nb in range(4):
                        pb = PSB[nb]
                        for kc in range(KC):
                            kb.op(PE, lambda: nc.tensor.matmul(pb[:, :], lhsT=y[:, kc, :], rhs=wo[:, kc, nb * 512:(nb + 1) * 512],
                                                               start=(kc == 0), stop=(kc == KC - 1)), [y, wo], [pb])
                        kb.op(V, lambda: nc.vector.tensor_tensor(out=h[:, nb * 512:(nb + 1) * 512], in0=pb[:, :], in1=G1[:, nb * 512:(nb + 1) * 512], op=ALU.mult),
                              [pb, G1], [h])
                    kb.op(P_, lambda: nc.gpsimd.tensor_tensor(out=x[:, :], in0=x[:, :], in1=h[:, :], op=ALU.add), [x, h], [x])
                    kb.store(x, XL[ts, :], x[:, :])
                    rms_rstd(st, (x[:, :], x), D, (r[:, :], r), (junk[:, :], junk))
                    kb.op(V, lambda: nc.vector.scalar_tensor_tensor(out=h[:, :], in0=x[:, :], scalar=r[:, 0:1], in1=A2[:, :], op0=ALU.mult, op1=ALU.mult),
                          [x, r, A2], [h])
                    kb.op(P_, lambda: nc.gpsimd.tensor_tensor(out=h[:, :], in0=h[:, :], in1=B2[:, :], op=ALU.add), [h, B2], [h])
                    for k4 in range(4):
                        pb = PSB[4 + k4 % 2]
                        for j in range(4):
                            kc = k4 * 4 + j
                            kb.op(PE, lambda: nc.tensor.transpose(pb[:, j * 128:(j + 1) * 128], h[:, kc * 128:(kc + 1) * 128], ident[:, :]), [h, ident], [pb])
                        kb.op(A_, lambda: nc.scalar.copy(out=ht[:, k4 * 4:(k4 + 1) * 4, :], in_=pb[:, :].rearrange("p (a b) -> p a b", a=4)), [pb], [ht])
                    kb.store(ht, HFT[:, ts].rearrange("(kc p) n -> p kc n", p=128), ht[:, :, :])

            with Stage(kb) as st:
                wq = st.sb([128, KC, D], BF16)
                wf = [st.sb([128, D], F32, True) for _ in range(2)]
                for kc in range(KC):
                    a = wf[kc % 2]
                    kb.load(a, a[:, :], peer_wq[l, kc * 128:(kc + 1) * 128, :])
                    eng = (V, P_)[kc % 2]; E = kb.E[eng]
                    kb.op(eng, lambda: E.tensor_copy(out=wq[:, kc, :], in_=a[:, :]), [a], [wq])
                kf = st.sb([128, 2, NK], F32, True)
                for p in range(2): kb.load(kf, kf[:, p, :], peer_keysT[l, p])
                kbb = st.sb([128, 2, NK], BF16)
                kb.op(V, lambda: nc.vector.tensor_copy(out=kbb[:, :, :], in_=kf[:, :, :]), [kf], [kbb])
                hfT = [st.sb([128, KC, 128], BF16, True) for _ in range(2)]
                qTt = [st.sb([128, 16, 128], BF16) for _ in range(2)]
                S = [st.sb([128, 8, 2, NK], F32, True) for _ in range(2)]
                S2 = [st.sb([128, 8, 2, NK], F32) for _ in range(2)]
                top = [st.sb([128, 8, 2, 16], F32) for _ in range(2)]
                cand = [st.sb([128, 8, 16, 16], F32) for _ in range(2)]
                cand2 = [st.sb([128, 8, 256], F32) for _ in range(2)]
                ctop = [st.sb([128, 8, 16], F32) for _ in range(2)]
                ex = [st.sb([128, 8, 16], F32) for _ in range(2)]
                zz = [st.sb([128, 8], F32) for _ in range(2)]
                b2 = [st.sb([128, 8], F32, True) for _ in range(2)]
                for tt in range(NTT):
                    i2 = tt % 2
                    ts = slice(tt * 128, (tt + 1) * 128)
                    hf = hfT[i2]; q = qTt[i2]; s_ = S[i2]; s2_ = S2[i2]; tp = top[i2]; cd = cand[i2]; cd2 = cand2[i2]
                    ct = ctop[i2]; e_ = ex[i2]; z_ = zz[i2]; bb = b2[i2]
                    kb.load(hf, hf[:, :, :], HFT[:, ts].rearrange("(kc p) n -> p kc n", p=128))
                    for c4 in range(4):
                        pb = PSB[c4 % 2]
                        for j in range(4):
                            oc = c4 * 4 + j
                            for kc in range(KC):
                                kb.op(PE, lambda: nc.tensor.matmul(pb[:, j * 128:(j + 1) * 128], lhsT=wq[:, kc, oc * 128:(oc + 1) * 128], rhs=hf[:, kc, :],
                                                                   start=(kc == 0), stop=(kc == KC - 1)), [wq, hf], [pb])
                        kb.op(A_, lambda: nc.scalar.copy(out=q[:, c4 * 4:(c4 + 1) * 4, :], in_=pb[:, :].rearrange("p (a b) -> p a b", a=4)), [pb], [q])
                    for hp in range(16):
                        pb = PSB[2 + (hp * NK) // 512 % 2] if NK == 128 else PSB[2]
                        c0 = (hp * NK) % 512
                        kb.op(PE, lambda: nc.tensor.matmul(pb[:, c0:c0 + NK], lhsT=q[:, hp, :], rhs=kbb[:, hp % 2, :], start=True, stop=True), [q, kbb], [pb])
                        if (c0 + NK == 512) or hp == 15:
                            n_in = (c0 + NK) // NK
                            hp0 = hp + 1 - n_in
                            sv = s_[:, :, :, :].rearrange("p h t k -> p (h t) k")
                            kb.op(A_, lambda: nc.scalar.copy(out=sv[:, hp0:hp + 1, :], in_=pb[:, 0:n_in * NK].rearrange("p (a k) -> p a k", k=NK)), [pb], [s_])
                    for h in range(8):
                        for p in range(2):
                            kb.op(V, lambda: nc.vector.max(out=tp[:, h, p, 0:8], in_=s_[:, h, p, :]), [s_], [tp])
                            kb.op(V, lambda: nc.vector.match_replace(out=s2_[:, h, p, :], in_to_replace=tp[:, h, p, 0:8], in_values=s_[:, h, p, :], imm_value=NEG),
                                  [tp, s_], [s2_])
                            kb.op(V, lambda: nc.vector.max(out=tp[:, h, p, 8:16], in_=s2_[:, h, p, :]), [s2_], [tp])
                    for h in range(8):
                        kb.op(P_, lambda: nc.gpsimd.tensor_tensor(out=cd[:, h, :, :], in0=tp[:, h, 0, :].unsqueeze(2).to_broadcast([128, 16, 16]),
                                                                  in1=tp[:, h, 1, :].unsqueeze(1).to_broadcast([128, 16, 16]), op=ALU.add), [tp], [cd])
                    for h in range(8):
                        cf = cd[:, h, :, :].rearrange("p a b -> p (a b)")
                        kb.op(V, lambda: nc.vector.max(out=ct[:, h, 0:8], in_=cf), [cd], [ct])
                        kb.op(V, lambda: nc.vector.match_replace(out=cd2[:, h, :], in_to_replace=ct[:, h, 0:8], in_values=cf, imm_value=NEG), [ct, cd], [cd2])
                        kb.op(V, lambda: nc.vector.max(out=ct[:, h, 8:16], in_=cd2[:, h, :]), [cd2], [ct])
                    kb.op(V, lambda: nc.vector.tensor_tensor(out=e_[:, :, :], in0=ct[:, :, :], in1=ct[:, :, 0:1].to_broadcast([128, 8, 16]), op=ALU.subtract),
                          [ct], [e_])
                    kb.op(A_, lambda: nc.scalar.activation(out=e_[:, :, :], in_=e_[:, :, :], func=AF.Exp), [e_], [e_])
                    kb.op(V, lambda: nc.vector.tensor_reduce(out=z_[:, :], in_=e_[:, :, :], axis=AX.X, op=ALU.add), [e_], [z_])
                    kb.op(A_, lambda: nc.scalar.activation(out=z_[:, :], in_=z_[:, :], func=AF.Ln), [z_], [z_])
                    kb.op(V, lambda: nc.vector.tensor_tensor(out=bb[:, :], in0=ct[:, :, 15], in1=ct[:, :, 0], op=ALU.subtract), [ct], [bb])
                    kb.op(V, lambda: nc.vector.tensor_tensor(out=bb[:, :], in0=bb[:, :], in1=z_[:, :], op=ALU.subtract), [bb, z_], [bb])
                    kb.op(V, lambda: nc.vector.tensor_tensor(out=s_[:, :, 0, :], in0=s_[:, :, 0, :], in1=ct[:, :, 15:16].to_broadcast([128, 8, NK]), op=ALU.subtract),
                          [s_, ct], [s_])
                    kb.store(s_, SS[ts, :], s_[:, :, :, :].rearrange("p h t k -> p (h t k)"))
                    kb.store(bb, BI2[ts, :], bb[:, :])

            with Stage(kb) as st:
                G2 = st.sb([128, D], F32, True)
                hfT = [st.sb([128, KC, 128], BF16, True) for _ in range(2)]
                S = [st.sb([128, 8, 2, NK], F32, True) for _ in range(2)]
                b2 = [st.sb([128, 8], F32, True) for _ in range(2)]
                ub = [st.sb([128, KC, 512], BF16, True) for _ in range(2)]
                vb = [st.sb([128, D], BF16, True) for _ in range(3)]
                gA = [st.sb([128, EB], BF16) for _ in range(2)]
                zt = [st.sb([128, IB, NK], F32) for _ in range(2)]
                et = [st.sb([128, IB, NK], F32) for _ in range(2)]
                Wa = [st.sb([128, EB], F32) for _ in range(2)]
                WgT = [st.sb([128, IB, 128], BF16) for _ in range(2)]
                xt = [st.sb([128, D], F32, True) for _ in range(2)]
                fo = st.sb([128, D], F32)
                curc = None
                ui = [0]; vi = [0]; bi = [0]
                NEB = EB // 512 if EB >= 512 else 1
                EW = min(512, EB)
                for tt in range(NTT):
                    if l == L - 1 and tt < NTC: continue
                    cond = 1 if tt < NTC else 0
                    if cond != curc:
                        modtile(st, G2, cond, 5); curc = cond
                    i2 = tt % 2
                    ts = slice(tt * 128, (tt + 1) * 128)
                    hf = hfT[i2]; s_ = S[i2]; bb = b2[i2]; x = xt[i2]
                    kb.load(hf, hf[:, :, :], HFT[:, ts].rearrange("(kc p) n -> p kc n", p=128))
                    kb.load(s_, s_[:, :, :, :].rearrange("p h t k -> p (h t k)"), SS[ts, :])
                    kb.load(bb, bb[:, :], BI2[ts, :])
                    kb.load(x, x[:, :], XL[ts, :])
                    PO = PSB[4:8]
                    for ib in range(NIB):
                        j2 = bi[0] % 2; bi[0] += 1
                        ga = gA[j2]; wa = Wa[j2]; wg = wa; wgt = WgT[j2]
                        for eb in range(NEB):
                            u = ub[ui[0] % 2]; ui[0] += 1
                            e0 = ib * EB + eb * EW
                            for q4 in range(4):
                                kb.load(u, u[:, q4 * 4:(q4 + 1) * 4, 0:EW], kcv(UTB)[:, q4 * 4:(q4 + 1) * 4, e0:e0 + EW])
                            pb = PSB[eb % 2]
                            for kc in range(KC):
                                kb.op(PE, lambda: nc.tensor.matmul(pb[:, 0:EW], lhsT=hf[:, kc, :], rhs=u[:, kc, 0:EW], start=(kc == 0), stop=(kc == KC - 1)),
                                      [hf, u], [pb])
                            kb.op(A_, lambda: nc.scalar.activation(out=ga[:, eb * EW:(eb + 1) * EW], in_=pb[:, 0:EW], func=AF.Gelu_apprx_tanh), [pb], [ga])
                        for h in range(8):
                            z = zt[h % 2]; e = et[h % 2]
                            kb.op(P_, lambda: nc.gpsimd.tensor_tensor(out=z[:, :, :], in0=s_[:, h, 0, ib * IB:(ib + 1) * IB].unsqueeze(2).to_broadcast([128, IB, NK]),
                                                                      in1=s_[:, h, 1, :].unsqueeze(1).to_broadcast([128, IB, NK]), op=ALU.add), [s_], [z])
                            kb.op(A_, lambda: nc.scalar.activation(out=e[:, :, :], in_=z[:, :, :], func=AF.Exp, bias=bb[:, h:h + 1]), [z, bb], [e])
                            zf = z[:, :, :].rearrange("p a b -> p (a b)"); ef = e[:, :, :].rearrange("p a b -> p (a b)")
                            if h == 0:
                                kb.op(V, lambda: nc.vector.scalar_tensor_tensor(out=wa[:, :], in0=zf, scalar=0.0, in1=ef, op0=ALU.is_ge, op1=ALU.mult), [z, e], [wa])
                            else:
                                kb.op(V, lambda: nc.vector.scalar_tensor_tensor(out=e[:, :, :].rearrange("p a b -> p (a b)"), in0=zf, scalar=0.0, in1=ef,
                                                                                op0=ALU.is_ge, op1=ALU.mult), [z, e], [e])
                                kb.op(P_, lambda: nc.gpsimd.tensor_tensor(out=wa[:, :], in0=wa[:, :], in1=ef, op=ALU.add), [wa, e], [wa])
                        kb.op(V, lambda: nc.vector.tensor_tensor(out=wa[:, :], in0=wa[:, :], in1=ga[:, :], op=ALU.mult), [wa, ga], [wa])
                        for i4 in range(0, IB, 4):
                            pb = PSB[2 + (i4 // 4) % 2]
                            nn = min(4, IB - i4)
                            for j in range(nn):
                                kb.op(PE, lambda: nc.tensor.transpose(pb[0:NK, j * 128:(j + 1) * 128], wg[:, (i4 + j) * NK:(i4 + j + 1) * NK], ident[:, :]), [wg, ident], [pb])
                            kb.op(A_, lambda: nc.scalar.copy(out=wgt[0:NK, i4:i4 + nn, :], in_=pb[0:NK, 0:nn * 128].rearrange("p (a b) -> p a b", a=nn)), [pb], [wgt])
                        for i in range(IB):
                            vv_ = vb[vi[0] % 3]; vi[0] += 1
                            e0 = (ib * IB + i) * NK
                            kb.load(vv_, vv_[0:NK, :], VTB[e0:e0 + NK, :])
                            first = (ib == 0 and i == 0); last = (ib == NIB - 1 and i == IB - 1)
                            for nb in range(4):
                                kb.op(PE, lambda: nc.tensor.matmul(PO[nb][:, :], lhsT=wgt[0:NK, i, :], rhs=vv_[0:NK, nb * 512:(nb + 1) * 512], start=first, stop=last),
                                      [wgt, vv_], [PO[nb]])
                    for nb in range(4):
                        kb.op(V, lambda: nc.vector.tensor_tensor(out=fo[:, nb * 512:(nb + 1) * 512], in0=PO[nb][:, :], in1=G2[:, nb * 512:(nb + 1) * 512], op=ALU.mult),
                              [PO[nb], G2], [fo])
                    kb.op(P_, lambda: nc.gpsimd.tensor_tensor(out=x[:, :], in0=x[:, :], in1=fo[:, :], op=ALU.add), [x, fo], [x])
                    kb.store(x, XL[ts, :], x[:, :])

        with Stage(kb) as st:
            nf = st.sb([128, D], F32, True); kb.load(nf, nf[:, :], bcast_rows(norm_final[0:1, :]))
            xt = [st.sb([128, D], F32, True) for _ in range(2)]
            ot = [st.sb([128, D], F32, True) for _ in range(2)]
            junk = st.sb([128, D], F32); rs = [st.sb([128, 1], F32) for _ in range(2)]
            for k in range(NTL):
                tt = NTC + k; i2 = k % 2
                x = xt[i2]; o = ot[i2]; r = rs[i2]
                kb.load(x, x[:, :], XL[tt * 128:(tt + 1) * 128, :])
                rms_rstd(st, (x[:, :], x), D, (r[:, :], r), (junk[:, :], junk))
                kb.op(V, lambda: nc.vector.scalar_tensor_tensor(out=o[:, :], in0=x[:, :], scalar=r[:, 0:1], in1=nf[:, :], op0=ALU.mult, op1=ALU.mult),
                      [x, r, nf], [o])
                kb.store(o, yout[k * 128:(k + 1) * 128, :], o[:, :])
        kb.barrier()
        cst.__exit__(None, None, None)
    return nc


def host_consts(T, TC, grid_w=64):
    ident = np.eye(128, dtype=np.float32)
    s = np.arange(128)
    tri = np.stack([(s[:, None] <= s[None, :]), (s[:, None] >= s[None, :])]).astype(np.float32)
    pml = np.zeros((4, 9, 128, 128), np.float32)
    pmc = np.zeros((4, 3, 128, 128), np.float32)
    rl = s // grid_w; c = s % grid_w
    for g, w in enumerate(WINS):
        for dk in range(-4, 5):
            rin = 2 * dk + rl[:, None]
            rout = rl[None, :]
            okr = (rin >= rout - w // 2) & (rin < rout - w // 2 + w)
            okc = (c[:, None] >= c[None, :] - w // 2) & (c[:, None] < c[None, :] - w // 2 + w)
            pml[g, dk + 4] = (okr & okc)
        for dk in range(-1, 2):
            cin = 128 * dk + s[:, None]; cout = s[None, :]
            pmc[g, dk + 1] = (cin >= cout - w // 2) & (cin < cout - w // 2 + w)
    return ident, tri, pml, pmc


def make_in_maps(cfg, inp, nb):
    T, TC, L, NK = cfg['T'], cfg['TC'], cfg['L'], cfg['NK']
    f = lambda a: np.ascontiguousarray(np.asarray(a, dtype=np.float32))
    ident, tri, pml, pmc = host_consts(T, TC)
    shared = dict(
        ada_w=f(inp['ada_w']), ada_b=f(inp['ada_b']), norm_mix=f(inp['norm_mix']), norm_ffn=f(inp['norm_ffn']),
        norm_final=f(inp['norm_final']).reshape(1, D), w_in=f(inp['w_in']), b_gate=f(inp['b_gate']),
        sgu_norm=f(inp['sgu_norm']), sgu_wT=f(np.swapaxes(np.asarray(inp['sgu_w']), -1, -2)), sgu_b=f(inp['sgu_b']),
        qk_convT=f(np.swapaxes(np.asarray(inp['qk_conv']), -1, -2)), mlstm_norm=f(inp['mlstm_norm']),
        pool_w=f(inp['pool_w']), pool_scaleT=f(np.swapaxes(np.asarray(inp['pool_scale']).reshape(L, 4, 128), -1, -2)),
        w_out=f(inp['w_out']), peer_wq=f(inp['peer_wq']),
        peer_keysT=f(np.swapaxes(np.asarray(inp['peer_keys']), -1, -2)),
        peer_uT=f(np.swapaxes(np.asarray(inp['peer_u']), -1, -2)), peer_v=f(inp['peer_v']),
        ident=ident, tri=tri, pml=pml, pmc=pmc)
    maps = []
    for b in range(nb):
        m = dict(shared)
        m['xin'] = f(np.concatenate([np.asarray(inp['ctx'])[b], np.asarray(inp['x'])[b]], axis=0))
        cond = np.stack([np.asarray(inp['c'])[b], np.asarray(inp['c_ctx'])], axis=1)
        m['condT'] = f(cond.reshape(KC, 128, 2).transpose(1, 0, 2))
        maps.append(m)
    return maps


def kernel(**inputs):
    cfg = dict(T=4096, TC=256, L=4, NK=128)
    nb = 4
    nc = build(cfg)
    maps = make_in_maps(cfg, inputs, nb)
    res = run_bass_kernel_spmd(nc, maps, core_ids=list(range(nb)))
    out = np.stack([np.asarray(res.results[b]['yout'], dtype=np.float32) for b in range(nb)], axis=0)
    return out
```

```python
import numpy as np
from contextlib import ExitStack
import concourse.bass as bass
import concourse.mybir as mybir
from concourse.bass_utils import run_bass_kernel_spmd

F32 = mybir.dt.float32
BF16 = mybir.dt.bfloat16
AF = mybir.ActivationFunctionType
ALU = mybir.AluOpType
AX = mybir.AxisListType

D = 2048
KC = 16
D_IN = 5648
OFF_U, OFF_V, OFF_P, OFF_Q, OFF_O, OFF_K, OFF_VM, OFF_G = 0, 512, 1024, 1536, 2560, 3584, 4608, 5632
EPS = 1e-6
WINS = (2, 4, 8, 16)
NEG = -3.0e38


class SemW:
    def __init__(s, h, dma=False):
        s.h = h; s.total = 0; s.dma = dma


class Trk:
    __slots__ = ('w', 'r')

    def __init__(s):
        s.w = []; s.r = {}


class Buf:
    def __init__(s, t, ds=None):
        s.t = t; s.k = Trk(); s.ds = ds

    def __getitem__(s, idx):
        return s.t[idx]


class KB:
    def __init__(s, nc, es):
        s.nc = nc; s.es = es
        s.E = dict(pe=nc.tensor, dve=nc.vector, act=nc.scalar, pool=nc.gpsimd, sp=nc.sync)
        s.allsems = []
        s.esem = {e: s.newsem('e_' + e) for e in s.E}
        s.pesems = {s.esem['pe']}
        s.seen = {e: {} for e in s.E}
        s.dpool = []; s.dnext = 0; s.dbase = 0
        s.nm = 0

    def newsem(s, name, dma=False):
        h = s.es.enter_context(s.nc.semaphore(name + '_%d' % len(s.allsems)))
        sw = SemW(h, dma); s.allsems.append(sw); return sw

    def dsem(s):
        if s.dnext >= len(s.dpool):
            s.dpool.append(s.newsem('d', True))
        sw = s.dpool[s.dnext]; s.dnext += 1; return sw

    def _wait(s, e, deps):
        best = {}
        for sw, v in deps:
            if sw.dma: v = sw.total
            if v > best.get(sw, 0): best[sw] = v
        for sw, v in best.items():
            if s.seen[e].get(sw, 0) >= v: continue
            if e == 'pe' and sw in s.pesems: continue
            s.E[e].wait_ge(sw.h, v); s.seen[e][sw] = v

    def op(s, e, fn, reads=(), writes=()):
        deps = []
        for r in reads: deps += r.k.w
        for w in writes:
            deps += w.k.w; deps += list(w.k.r.items())
        s._wait(e, deps)
        sw = s.esem[e]
        if sw.total >= 30000:
            sw = s.newsem('e_' + e); s.esem[e] = sw
            if e == 'pe': s.pesems.add(sw)
        inst = fn(); sw.total += 1; inst.then_inc(sw.h, 1)
        for r in reads: r.k.r[sw] = sw.total
        for w in writes:
            w.k.w = [(sw, sw.total)]; w.k.r = {}
        return inst

    def dma(s, out, in_, sem, reads=(), writes=(), q='sp'):
        deps = []
        for r in reads: deps += r.k.w
        for w in writes:
            deps += w.k.w; deps += list(w.k.r.items())
        s._wait(q, deps)
        inst = s.E[q].dma_start(out=out, in_=in_); sem.total += 16; inst.then_inc(sem.h, 16)
        for r in reads: r.k.r[sem] = sem.total
        for w in writes:
            w.k.w = [(sem, sem.total)]; w.k.r = {}

    def load(s, buf, out, in_, q='sp'):
        s.dma(out, in_, buf.ds, writes=[buf], q=q)

    def store(s, buf, out, in_, q='sp'):
        s.dma(out, in_, buf.ds, reads=[buf], q=q)

    def barrier(s):
        deps = [(sw, sw.total) for sw in s.allsems if sw.total > 0]
        for e in s.E: s._wait(e, deps)


class Stage:
    def __init__(s, kb):
        s.kb = kb

    def __enter__(s):
        s.kb.barrier(); s.es = ExitStack(); s.es.__enter__(); s.kb.dnext = s.kb.dbase; return s

    def __exit__(s, *a):
        s.kb.barrier(); return s.es.__exit__(*a)

    def sb(s, shape, dt=F32, dma=False):
        s.kb.nm += 1
        t = s.es.enter_context(s.kb.nc.sbuf_tensor('b%d' % s.kb.nm, list(shape), dt))
        return Buf(t, s.kb.dsem() if dma else None)

    def ps(s, shape=(128, 512), dt=F32):
        s.kb.nm += 1
        t = s.es.enter_context(s.kb.nc.psum_tensor('p%d' % s.kb.nm, list(shape), dt))
        return Buf(t)


def bcast_rows(ap2d_row, n=128):
    a = ap2d_row
    return bass.AP(tensor=a.tensor, offset=a.offset, ap=[[0, n]] + [list(x) for x in list(a.ap)[1:]])


def build(cfg):
    T, TC, L, NK = cfg['T'], cfg['TC'], cfg['L'], cfg['NK']
    NE = NK * NK
    TT = T + TC
    NTC, NTL = TC // 128, T // 128
    NTT = NTC + NTL
    IB = 16 if NK >= 16 else NK
    NIB = NK // IB
    EB = IB * NK
    dbg = cfg.get('dbg', ())
    _i, _t, _pml, _pmc = host_consts(T, TC)
    PML_NZ = [[bool(_pml[g, d].any()) for d in range(9)] for g in range(4)]
    PMC_NZ = [[bool(_pmc[g, d].any()) for d in range(3)] for g in range(4)]
    nc = bass.Bass("TRN2", target_bir_lowering=False)

    def din(name, shape, dt=F32):
        return nc.dram_tensor(name, list(shape), dt, kind="ExternalInput").ap()

    def dsc(name, shape, dt=F32):
        return nc.dram_tensor(name, list(shape), dt, kind="Internal").ap()

    xin = din('xin', [TT, D])
    condT = din('condT', [128, KC, 2])
    ada_w = din('ada_w', [L, D, 6 * D]); ada_b = din('ada_b', [L, 6 * D])
    norm_mix = din('norm_mix', [L, D]); norm_ffn = din('norm_ffn', [L, D]); norm_final = din('norm_final', [1, D])
    w_in = din('w_in', [L, D, D_IN]); b_gate = din('b_gate', [L, 16])
    sgu_norm = din('sgu_norm', [L, 512]); sgu_wT = din('sgu_wT', [L, 4, 128, 128]); sgu_b = din('sgu_b', [L, 4, 128])
    qk_convT = din('qk_convT', [L, 2048, 3]); mlstm_norm = din('mlstm_norm', [L, 1024])
    pool_w = din('pool_w', [L, 4, 128, 128]); pool_scaleT = din('pool_scaleT', [L, 128, 4])
    w_out = din('w_out', [L, D, D]); peer_wq = din('peer_wq', [L, D, D])
    peer_keysT = din('peer_keysT', [L, 2, 128, NK])
    peer_uT = din('peer_uT', [L, D, NE]); peer_v = din('peer_v', [L, NE, D])
    identd = din('ident', [128, 128])
    trid = din('tri', [2, 128, 128])
    pml = din('pml', [4, 9, 128, 128])
    pmc = din('pmc', [4, 3, 128, 128])
    yout = nc.dram_tensor('yout', [T, D], F32, kind="ExternalOutput").ap()
    dbg_out = {}
    for nm, shp in dbg:
        dbg_out[nm] = nc.dram_tensor('dbg_' + nm, list(shp), F32, kind="ExternalOutput").ap()

    XL = dsc('XL', [TT, D])
    MOD = dsc('MOD', [2, 6 * D])
    UT = dsc('UT', [512, TT]); QR = dsc('QR', [1024, TT]); KR = dsc('KR', [1024, TT])
    VS = dsc('VS', [TT, 512]); PS = dsc('PS', [TT, 512]); OS = dsc('OS', [TT, 1024]); VM = dsc('VM', [TT, 1024])
    GS = dsc('GS', [TT, 16])
    YT = dsc('YT', [D, TT], BF16)
    HFT = dsc('HFT', [D, TT], BF16)
    SS = dsc('SS', [TT, 8 * 2 * NK]); BI2 = dsc('BI2', [TT, 8])
    UTB = dsc('UTB', [D, NE], BF16); VTB = dsc('VTB', [NE, D], BF16)

    es = ExitStack()
    with es:
        kb = KB(nc, es)
        V, A_, P_, PE = 'dve', 'act', 'pool', 'pe'

        def kcv(ap2d):
            return ap2d.rearrange("(kc p) c -> p kc c", p=128)

        cst = Stage(kb); cst.__enter__()
        ident = cst.sb([128, 128], F32, True); kb.load(ident, ident[:, :], identd[:, :])
        identb = cst.sb([128, 128], BF16)
        kb.op(V, lambda: nc.vector.tensor_copy(out=identb[:, :], in_=ident[:, :]), [ident], [identb])
        tri = cst.sb([128, 2, 128], F32, True)
        for i in range(2): kb.load(tri, tri[:, i, :], trid[i])
        trib = cst.sb([128, 2, 128], BF16)
        kb.op(V, lambda: nc.vector.tensor_copy(out=trib[:, :, :], in_=tri[:, :, :]), [tri], [trib])
        ones = cst.sb([128, 128], F32)
        kb.op(V, lambda: nc.vector.memset(ones[:, :], 1.0), [], [ones])
        sct = cst.sb([128, KC, 2], F32, True); kb.load(sct, sct[:, :, :], condT[:, :, :])
        kb.op(A_, lambda: nc.scalar.activation(out=sct[:, :, :], in_=sct[:, :, :], func=AF.Silu), [sct], [sct])
        PSB = [cst.ps() for _ in range(8)]
        kb.dbase = kb.dnext
        with Stage(kb) as st0:
            cp = [st0.sb([128, D], F32, True) for _ in range(2)]
            for tt in range(NTT):
                b = cp[tt % 2]
                kb.load(b, b[:, :], xin[tt * 128:(tt + 1) * 128, :])
                kb.store(b, XL[tt * 128:(tt + 1) * 128, :], b[:, :])

        def rms_rstd(st, xt, width, outcol, junk):
            kb.op(V, lambda: nc.vector.scalar_tensor_tensor(out=junk[0], in0=xt[0], scalar=1.0, in1=xt[0],
                                                            op0=ALU.mult, op1=ALU.mult, accum_out=outcol[0]),
                  [xt[1]], [junk[1], outcol[1]])
            kb.op(V, lambda: nc.vector.tensor_scalar(out=outcol[0], in0=outcol[0], scalar1=1.0 / width, scalar2=EPS,
                                                     op0=ALU.mult, op1=ALU.add), [outcol[1]], [outcol[1]])
            kb.op(A_, lambda: nc.scalar.activation(out=outcol[0], in_=outcol[0], func=AF.Sqrt), [outcol[1]], [outcol[1]])
            kb.op(V, lambda: nc.vector.reciprocal(out=outcol[0], in_=outcol[0]), [outcol[1]], [outcol[1]])

        def conv_weights(st, src2d, nk, ncols, dstdram):
            CW = min(ncols, 2048)
            f = [st.sb([128, CW], F32, True) for _ in range(2)]
            g = [st.sb([128, CW], BF16, True) for _ in range(2)]
            i = 0
            for k in range(nk):
                for c0 in range(0, ncols, CW):
                    a, b = f[i % 2], g[i % 2]
                    kb.load(a, a[:, :], src2d[k * 128:(k + 1) * 128, c0:c0 + CW])
                    eng = (V, P_)[i % 2]
                    E = kb.E[eng]
                    kb.op(eng, lambda: E.tensor_copy(out=b[:, :], in_=a[:, :]), [a], [b])
                    kb.store(b, dstdram[k * 128:(k + 1) * 128, c0:c0 + CW], b[:, :])
                    i += 1

        for l in range(L):
            with Stage(kb) as st:
                wb = [st.sb([128, KC, 512], F32, True) for _ in range(2)]
                adab = st.sb([2, 6 * D], F32, True)
                for r in range(2): kb.load(adab, adab[r:r + 1, :], ada_b[l:l + 1, :])
                nrm = st.sb([2, 2, D], F32, True)
                for r in range(2):
                    kb.load(nrm, nrm[r:r + 1, 0, :], norm_mix[l:l + 1, :])
                    kb.load(nrm, nrm[r:r + 1, 1, :], norm_ffn[l:l + 1, :])
                mo = [st.sb([2, 512], F32, True) for _ in range(2)]
                for cb in range(24):
                    w = wb[cb % 2]
                    src = kcv(ada_w[l])
                    for q4 in range(4):
                        kb.load(w, w[:, q4 * 4:(q4 + 1) * 4, :], src[:, q4 * 4:(q4 + 1) * 4, cb * 512:(cb + 1) * 512])
                    pb = PSB[cb % 2]
                    for kc in range(KC):
                        kb.op(PE, lambda: nc.tensor.matmul(pb[0:2, :], lhsT=sct[:, kc, :], rhs=w[:, kc, :],
                                                           start=(kc == 0), stop=(kc == KC - 1)), [sct, w], [pb])
                    m = mo[cb % 2]
                    kb.op(V, lambda: nc.vector.tensor_tensor(out=m[:, :], in0=pb[0:2, :], in1=adab[:, cb * 512:(cb + 1) * 512],
                                                             op=ALU.add), [pb, adab], [m])
                    part = cb // 4
                    if part in (1, 4):
                        j = 0 if part == 1 else 1
                        c0 = (cb % 4) * 512
                        kb.op(V, lambda: nc.vector.scalar_tensor_tensor(out=m[:, :], in0=m[:, :], scalar=1.0,
                                                                        in1=nrm[:, j, c0:c0 + 512], op0=ALU.add, op1=ALU.mult),
                              [m, nrm], [m])
                    kb.store(m, MOD[:, cb * 512:(cb + 1) * 512], m[:, :])

            def modtile(st, buf, cond, part):
                kb.load(buf, buf[:, :], bcast_rows(MOD[cond:cond + 1, part * D:(part + 1) * D]))

            with Stage(kb) as st:
                A1 = st.sb([128, D], F32, True); B1 = st.sb([128, D], F32, True)
                bg = st.sb([128, 16], F32, True)
                kb.load(bg, bg[:, :], bcast_rows(b_gate[l:l + 1, :]))
                xt = [st.sb([128, D], F32, True) for _ in range(2)]
                hh = [st.sb([128, D], F32) for _ in range(2)]
                junk = st.sb([128, D], F32)
                rs = [st.sb([128, 1], F32) for _ in range(2)]
                hT = st.sb([128, KC, 512], BF16)
                wf = [st.sb([128, KC, 512], F32, True) for _ in range(1)]
                wbf = [st.sb([128, KC, 512], BF16) for _ in range(2)]
                ev = [st.sb([128, 512], F32, True) for _ in range(3)]
                evi = [0]
                wi = [0]
                w2 = kcv(w_in[l])
                sts = []
                t0 = 0
                while t0 < NTC: n = min(4, NTC - t0); sts.append((1, t0, n)); t0 += n
                while t0 < NTT: n = min(4, NTT - t0); sts.append((0, t0, n)); t0 += n
                curc = None
                for (cond, tb, n) in sts:
                    if cond != curc:
                        modtile(st, A1, cond, 1); modtile(st, B1, cond, 0); curc = cond
                    NS = n * 128
                    for ti in range(n):
                        tt = tb + ti
                        x = xt[tt % 2]; h = hh[tt % 2]; r = rs[tt % 2]
                        kb.load(x, x[:, :], XL[tt * 128:(tt + 1) * 128, :])
                        rms_rstd(st, (x[:, :], x), D, (r[:, :], r), (junk[:, :], junk))
                        kb.op(V, lambda: nc.vector.scalar_tensor_tensor(out=h[:, :], in0=x[:, :], scalar=r[:, 0:1], in1=A1[:, :],
                                                                        op0=ALU.mult, op1=ALU.mult), [x, r, A1], [h])
                        kb.op(P_, lambda: nc.gpsimd.tensor_tensor(out=h[:, :], in0=h[:, :], in1=B1[:, :], op=ALU.add), [h, B1], [h])
                        for k4 in range(4):
                            pb = PSB[k4 % 2]
                            for j in range(4):
                                kc = k4 * 4 + j
                                kb.op(PE, lambda: nc.tensor.transpose(pb[:, j * 128:(j + 1) * 128], h[:, kc * 128:(kc + 1) * 128], ident[:, :]),
                                      [h, ident], [pb])
                            kb.op(A_, lambda: nc.scalar.copy(out=hT[:, k4 * 4:(k4 + 1) * 4, ti * 128:(ti + 1) * 128],
                                                             in_=pb[:, :].rearrange("p (a b) -> p a b", a=4)), [pb], [hT])

                    def wload(c0, ncols):
                        i = wi[0]; wi[0] += 1
                        a, b = wf[0], wbf[i % 2]
                        for q4 in range(4):
                            kb.load(a, a[:, q4 * 4:(q4 + 1) * 4, 0:ncols], w2[:, q4 * 4:(q4 + 1) * 4, c0:c0 + ncols])
                        eng = (V, P_)[i % 2]; E = kb.E[eng]
                        kb.op(eng, lambda: E.tensor_copy(out=b[:, :, 0:ncols], in_=a[:, :, 0:ncols]), [a], [b])
                        return b

                    def evbuf():
                        e = ev[evi[0] % 3]; evi[0] += 1; return e

                    for (off, nb, dst, fn) in ((OFF_U, 4, UT, AF.Gelu_apprx_tanh), (OFF_Q, 8, QR, None), (OFF_K, 8, KR, None)):
                        for g4 in range(nb // 4):
                            wbb = wload(off + g4 * 512, 512)
                            for j in range(4):
                                blk = g4 * 4 + j
                                pb = PSB[2 + (blk % 2)]
                                for kc in range(KC):
                                    kb.op(PE, lambda: nc.tensor.matmul(pb[:, 0:NS], lhsT=wbb[:, kc, j * 128:(j + 1) * 128], rhs=hT[:, kc, 0:NS],
                                                                       start=(kc == 0), stop=(kc == KC - 1)), [wbb, hT], [pb])
                                e = evbuf()
                                if fn is None:
                                    kb.op(A_, lambda: nc.scalar.copy(out=e[:, 0:NS], in_=pb[:, 0:NS]), [pb], [e])
                                else:
                                    kb.op(A_, lambda: nc.scalar.activation(out=e[:, 0:NS], in_=pb[:, 0:NS], func=fn), [pb], [e])
                                kb.store(e, dst[blk * 128:(blk + 1) * 128, tb * 128:tb * 128 + NS], e[:, 0:NS])
                    for (off, ncols, dst, dc0, fn) in ((OFF_V, 512, VS, 0, AF.Gelu_apprx_tanh), (OFF_P, 512, PS, 0, None),
                                                       (OFF_O, 512, OS, 0, AF.Sigmoid), (OFF_O + 512, 512, OS, 512, AF.Sigmoid),
                                                       (OFF_VM, 512, VM, 0, None), (OFF_VM + 512, 512, VM, 512, None),
                                                       (OFF_G, 16, GS, 0, 'gate')):
                        wbb = wload(off, ncols)
                        for ti in range(n):
                            tt = tb + ti
                            pb = PSB[4 + (ti % 2)]
                            for kc in range(KC):
                                kb.op(PE, lambda: nc.tensor.matmul(pb[:, 0:ncols], lhsT=hT[:, kc, ti * 128:(ti + 1) * 128], rhs=wbb[:, kc, 0:ncols],
                                                                   start=(kc == 0), stop=(kc == KC - 1)), [wbb, hT], [pb])
                            e = evbuf()
                            if fn is None:
                                kb.op(A_, lambda: nc.scalar.copy(out=e[:, 0:ncols], in_=pb[:, 0:ncols]), [pb], [e])
                            elif fn == 'gate':
                                kb.op(V, lambda: nc.vector.tensor_tensor(out=e[:, 0:ncols], in0=pb[:, 0:ncols], in1=bg[:, :], op=ALU.add), [pb, bg], [e])
                            else:
                                kb.op(A_, lambda: nc.scalar.activation(out=e[:, 0:ncols], in_=pb[:, 0:ncols], func=fn), [pb], [e])
                            kb.store(e, dst[tt * 128:(tt + 1) * 128, dc0:dc0 + ncols], e[:, 0:ncols])

            if 'UT' in dbg_out and l == 0:
                with Stage(kb) as st:
                    for (nm, src) in (('UT', UT), ('QR', QR), ('VS', VS), ('GS', GS), ('OS', OS)):
                        if nm in dbg_out:
                            R, C = src.shape
                            for r0 in range(0, R, 128):
                                rr = min(128, R - r0)
                                b = st.sb([128, C], F32, True)
                                kb.load(b, b[0:rr, :], src[r0:r0 + rr, :]); kb.store(b, dbg_out[nm][r0:r0 + rr, :], b[0:rr, :])

            with Stage(kb) as st:
                sgn = st.sb([128, 512], F32, True); kb.load(sgn, sgn[:, :], bcast_rows(sgu_norm[l:l + 1, :]))
                wsf = st.sb([128, 4, 128], F32, True)
                for h in range(4): kb.load(wsf, wsf[:, h, :], sgu_wT[l, h])
                wsb = st.sb([128, 4, 128], BF16)
                kb.op(V, lambda: nc.vector.tensor_copy(out=wsb[:, :, :], in_=wsf[:, :, :]), [wsf], [wsb])
                sbr = st.sb([1, 4, 128], F32, True); kb.load(sbr, sbr[0:1, :, :], sgu_b[l:l + 1, :, :])
                sbb = st.sb([1, 4, 128], BF16)
                kb.op(V, lambda: nc.vector.tensor_copy(out=sbb[:, :, :], in_=sbr[:, :, :]), [sbr], [sbb])
                onesb = st.sb([1, 128], BF16)
                kb.op(V, lambda: nc.vector.memset(onesb[:, :], 1.0), [], [onesb])
                pwf = st.sb([128, 4, 128], F32, True)
                for g in range(4): kb.load(pwf, pwf[:, g, :], pool_w[l, g])
                pwb = st.sb([128, 4, 128], BF16)
                kb.op(V, lambda: nc.vector.tensor_copy(out=pwb[:, :, :], in_=pwf[:, :, :]), [pwf], [pwb])
                psc = st.sb([128, 4], F32, True); kb.load(psc, psc[:, :], pool_scaleT[l])
                pmf = st.sb([128, 9, 128], F32, True)
                pmlb = st.sb([128, 4, 9, 128], BF16); pmcb = st.sb([128, 4, 3, 128], BF16)
                for g in range(4):
                    for dk in range(9): kb.load(pmf, pmf[:, dk, :], pml[g, dk])
                    kb.op(V, lambda: nc.vector.tensor_copy(out=pmlb[:, g, :, :], in_=pmf[:, :, :]), [pmf], [pmlb])
                for g in range(4):
                    for dk in range(3): kb.load(pmf, pmf[:, dk, :], pmc[g, dk])
                    kb.op(V, lambda: nc.vector.tensor_copy(out=pmcb[:, g, :, :], in_=pmf[:, 0:3, :]), [pmf], [pmcb])
                XPf = st.sb([128, NTT, 512], F32, True)
                XP = st.sb([128, NTT, 4, 129], BF16)
                kb.op(P_, lambda: nc.gpsimd.memset(XP[:, :, :, :], 1.0), [], [XP])
                for tt in range(NTT):
                    kb.load(XPf, XPf[:, tt, :], PS[tt * 128:(tt + 1) * 128, :])
                kb.op(V, lambda: nc.vector.tensor_copy(out=XP[:, :, :, 0:128], in_=XPf[:, :, :].rearrange("p t (g c) -> p t g c", g=4)),
                      [XPf], [XP])
                vt = [st.sb([128, 512], F32, True) for _ in range(2)]
                vn = [st.sb([128, 512], BF16) for _ in range(2)]
                ut = [st.sb([128, 4, 128], F32, True) for _ in range(2)]
                ya = [st.sb([128, 4, 128], BF16, True) for _ in range(2)]
                yc = [st.sb([128, 4, 128], BF16, True) for _ in range(2)]
                junk = st.sb([128, 512], F32); rs = [st.sb([128, 1], F32) for _ in range(2)]
                mean = [st.sb([128, 512], F32) for _ in range(2)]
                rc = [st.sb([128, 4], F32) for _ in range(2)]
                dT = [st.sb([128, 4, 128], BF16) for _ in range(2)]
                for tt in range(NTT):
                    i2 = tt % 2
                    isctx = tt < NTC
                    v = vt[i2]; u = ut[i2]; r = rs[i2]; vb = vn[i2]
                    kb.load(v, v[:, :], VS[tt * 128:(tt + 1) * 128, :])
                    kb.load(u, u[:, :, :], UT[:, tt * 128:(tt + 1) * 128].rearrange("(h p) n -> p h n", p=128))
                    rms_rstd(st, (v[:, :], v), 512, (r[:, :], r), (junk[:, :], junk))
                    kb.op(V, lambda: nc.vector.scalar_tensor_tensor(out=vb[:, :], in0=v[:, :], scalar=r[:, 0:1], in1=sgn[:, :],
                                                                    op0=ALU.mult, op1=ALU.mult), [v, r, sgn], [vb])
                    pb = PSB[i2]
                    for h in range(4):
                        kb.op(PE, lambda: nc.tensor.matmul(pb[:, h * 128:(h + 1) * 128], lhsT=vb[:, h * 128:(h + 1) * 128], rhs=wsb[:, h, :],
                                                           start=True, stop=False), [vb, wsb], [pb])
                        kb.op(PE, lambda: nc.tensor.matmul(pb[:, h * 128:(h + 1) * 128], lhsT=onesb[0:1, :], rhs=sbb[0:1, h, :],
                                                           start=False, stop=True), [onesb, sbb], [pb])
                    y = ya[i2]
                    kb.op(V, lambda: nc.vector.tensor_tensor(out=y[:, :, :], in0=pb[:, :].rearrange("p (h n) -> p h n", h=4), in1=u[:, :, :],
                                                             op=ALU.mult), [pb, u], [y])
                    kb.store(y, YT[0:512, tt * 128:(tt + 1) * 128].rearrange("(h p) n -> p h n", p=128), y[:, :, :])
                    if isctx:
                        k, nt, tbase, pmb, dks, dko = tt, NTC, 0, pmcb, (-1, 0, 1), 1
                    else:
                        k, nt, tbase, pmb, dks, dko = tt - NTC, NTL, NTC, pmlb, tuple(range(-4, 5)), 4
                    mn = mean[i2]; rcc = rc[i2]
                    pb2 = PSB[2 + i2]; pb3 = PSB[4 + i2]
                    for g in range(4):
                        w = WINS[g]
                        nzm = PMC_NZ if isctx else PML_NZ
                        use = [dk for dk in dks if 0 <= k + dk < nt and nzm[g][dk + dko]]
                        pbg = pb2 if g < 2 else pb3
                        c0 = (g % 2) * 129
                        for j, dk in enumerate(use):
                            kb.op(PE, lambda: nc.tensor.matmul(pbg[:, c0:c0 + 129], lhsT=pmb[:, g, dk + dko, :], rhs=XP[:, tbase + k + dk, g, :],
                                                               start=(j == 0), stop=(j == len(use) - 1)), [pmb, XP], [pbg])
                    for g in range(4):
                        pbg = pb2 if g < 2 else pb3
                        c0 = (g % 2) * 129
                        kb.op(V, lambda: nc.vector.reciprocal(out=rcc[:, g:g + 1], in_=pbg[:, c0 + 128:c0 + 129]), [pbg], [rcc])
                        kb.op(V, lambda: nc.vector.scalar_tensor_tensor(out=mn[:, g * 128:(g + 1) * 128], in0=pbg[:, c0:c0 + 128], scalar=rcc[:, g:g + 1],
                                                                        in1=XPf[:, tt, g * 128:(g + 1) * 128], op0=ALU.mult, op1=ALU.subtract),
                              [pbg, rcc, XPf], [mn])
                    pb4 = PSB[6 + i2]
                    for g in range(4):
                        kb.op(PE, lambda: nc.tensor.transpose(pb4[:, g * 128:(g + 1) * 128], mn[:, g * 128:(g + 1) * 128], ident[:, :]), [mn, ident], [pb4])
                    d = dT[i2]
                    kb.op(A_, lambda: nc.scalar.copy(out=d[:, :, :], in_=pb4[:, :].rearrange("p (g n) -> p g n", g=4)), [pb4], [d])
                    for g in range(4):
                        kb.op(PE, lambda: nc.tensor.matmul(pb4[:, g * 128:(g + 1) * 128], lhsT=pwb[:, g, :], rhs=d[:, g, :], start=True, stop=True),
                              [pwb, d], [pb4])
                    y2 = yc[i2]
                    for g in range(4):
                        kb.op(V, lambda: nc.vector.tensor_scalar(out=y2[:, g, :], in0=pb4[:, g * 128:(g + 1) * 128], scalar1=psc[:, g:g + 1], scalar2=None,
                                                                 op0=ALU.mult), [pb4, psc], [y2])
                    kb.store(y2, YT[1536:2048, tt * 128:(tt + 1) * 128].rearrange("(g p) n -> p g n", p=128), y2[:, :, :])

            with Stage(kb) as st:
                EA = st.sb([128, NTT, 8], F32); EBc = st.sb([128, NTT, 8], F32); EBL = st.sb([128, NTT, 8], F32)
                gt = [st.sb([128, 16], F32, True) for _ in range(2)]
                lf = [st.sb([128, 8], F32) for _ in range(2)]
                t1 = [st.sb([128, 8], F32) for _ in range(2)]
                gi = [st.sb([128, 8], F32) for _ in range(2)]
                for tt in range(NTT):
                    i2 = tt % 2
                    g = gt[i2]; f = lf[i2]; a = t1[i2]; gii = gi[i2]
                    kb.load(g, g[:, :], GS[tt * 128:(tt + 1) * 128, :])
                    gv = g[:, :].rearrange("p (d g h) -> p d g h", d=2, g=2)
                    fv = f[:, :].rearrange("p (d h) -> p d h", d=2)
                    av = a[:, :].rearrange("p (d h) -> p d h", d=2)
                    kb.op(A_, lambda: nc.scalar.activation(out=av, in_=gv[:, :, 1, :], func=AF.Abs), [g], [a])
                    kb.op(A_, lambda: nc.scalar.activation(out=a[:, :], in_=a[:, :], func=AF.Exp, scale=-1.0), [a], [a])
                    kb.op(A_, lambda: nc.scalar.activation(out=a[:, :], in_=a[:, :], func=AF.Ln, bias=1.0), [a], [a])
                    kb.op(V, lambda: nc.vector.tensor_scalar(out=fv, in0=gv[:, :, 1, :], scalar1=0.0, scalar2=None, op0=ALU.min), [g], [f])
                    kb.op(V, lambda: nc.vector.tensor_tensor(out=f[:, :], in0=f[:, :], in1=a[:, :], op=ALU.subtract), [f, a], [f])
                    kb.op(V, lambda: nc.vector.tensor_copy(out=gii[:, :].rearrange("p (d h) -> p d h", d=2), in_=gv[:, :, 0, :]), [g], [gii])
                    pb = PSB[i2]
                    for dr in range(2):
                        kb.op(PE, lambda: nc.tensor.matmul(pb[:, dr * 4:dr * 4 + 4], lhsT=tri[:, dr, :], rhs=f[:, dr * 4:dr * 4 + 4], start=True, stop=True),
                              [tri, f], [pb])
                    kb.op(PE, lambda: nc.tensor.matmul(pb[:, 8:16], lhsT=ones[:, :], rhs=f[:, :], start=True, stop=True), [ones, f], [pb])
                    kb.op(A_, lambda: nc.scalar.activation(out=EBc[:, tt, :], in_=pb[:, 0:8], func=AF.Exp), [pb], [EBc])
                    kb.op(A_, lambda: nc.scalar.activation(out=EBL[:, tt, :], in_=pb[:, 8:16], func=AF.Exp), [pb], [EBL])
                    kb.op(V, lambda: nc.vector.tensor_tensor(out=gii[:, :], in0=gii[:, :], in1=pb[:, 0:8], op=ALU.subtract), [gii, pb], [gii])
                    kb.op(A_, lambda: nc.scalar.activation(out=EA[:, tt, :], in_=gii[:, :], func=AF.Exp), [gii], [EA])
                gnb = st.sb([128, 1024], F32, True); kb.load(gnb, gnb[:, :], bcast_rows(mlstm_norm[l:l + 1, :]))
                cw = st.sb([128, 16, 3], F32, True)
                kb.load(cw, cw[:, :, :], qk_convT[l].rearrange("(c p) k -> p c k", p=128))
                raw = [st.sb([128, TT + 2], F32, True) for _ in range(1)]
                cv = [st.sb([128, TT], F32) for _ in range(1)]
                qT = st.sb([128, 2, TT], BF16); kT = st.sb([128, 2, TT], BF16)
                kS = st.sb([128, NTT, 256], BF16)
                vf = [st.sb([128, 256], F32, True) for _ in range(2)]
                vw = [st.sb([128, 257], BF16) for _ in range(2)]
                stm = [st.sb([128, 128], BF16) for _ in range(2)]
                C32 = st.sb([128, 2, 257], F32); Cb = st.sb([128, 2, 257], BF16)
                hs = [st.sb([128, 257], F32) for _ in range(2)]
                dn = [st.sb([128, 1], F32) for _ in range(2)]
                HF = st.sb([128, NTT, 256], F32)
                ot = [st.sb([128, 256], F32, True) for _ in range(2)]
                yb = [st.sb([128, 256], F32) for _ in range(2)]
                ybT = [st.sb([128, 2, 128], BF16, True) for _ in range(2)]
                junk = st.sb([128, 256], F32); rs = [st.sb([128, 1], F32) for _ in range(2)]
                for hd in range(4):
                    for which, (src, dstT, scl) in enumerate(((QR, qT, 1.0), (KR, kT, 1.0 / 16.0))):
                        for c in range(2):
                            ch = hd * 2 + c
                            rw = raw[0]; co = cv[0]
                            wcol = which * 8 + ch
                            kb.op(P_, lambda: nc.gpsimd.memset(rw[:, :], 0.0), [], [rw])
                            kb.load(rw, rw[:, 1:TT + 1], src[ch * 128:(ch + 1) * 128, :])
                            for (s0, sl) in ((0, TC), (TC, T)):
                                kb.op(V, lambda: nc.vector.tensor_scalar(out=co[:, s0:s0 + sl], in0=rw[:, 1 + s0:1 + s0 + sl], scalar1=cw[:, wcol, 1:2],
                                                                         scalar2=None, op0=ALU.mult), [rw, cw], [co])
                                kb.op(V, lambda: nc.vector.scalar_tensor_tensor(out=co[:, s0 + 1:s0 + sl], in0=rw[:, 1 + s0:s0 + sl], scalar=cw[:, wcol, 0:1],
                                                                                in1=co[:, s0 + 1:s0 + sl], op0=ALU.mult, op1=ALU.add), [rw, cw, co], [co])
                                kb.op(V, lambda: nc.vector.scalar_tensor_tensor(out=co[:, s0:s0 + sl - 1], in0=rw[:, 2 + s0:1 + s0 + sl], scalar=cw[:, wcol, 2:3],
                                                                                in1=co[:, s0:s0 + sl - 1], op0=ALU.mult, op1=ALU.add), [rw, cw, co], [co])
                            kb.op(A_, lambda: nc.scalar.activation(out=co[:, :], in_=co[:, :], func=AF.Silu), [co], [co])
                            kb.op(V, lambda: nc.vector.tensor_scalar(out=dstT[:, c, :], in0=co[:, :], scalar1=scl, scalar2=None, op0=ALU.mult), [co], [dstT])
                            if which == 1:
                                for t4 in range(0, NTT, 4):
                                    nn = min(4, NTT - t4)
                                    pb = PSB[(t4 // 4) % 2]
                                    for j in range(nn):
                                        kb.op(PE, lambda: nc.tensor.transpose(pb[:, j * 128:(j + 1) * 128], co[:, (t4 + j) * 128:(t4 + j + 1) * 128], ident[:, :]),
                                              [co, ident], [pb])
                                    kb.op(V, lambda: nc.vector.tensor_scalar(out=kS[:, t4:t4 + nn, c * 128:(c + 1) * 128],
                                                                             in0=pb[:, 0:nn * 128].rearrange("p (a b) -> p a b", a=nn),
                                                                             scalar1=scl, scalar2=None, op0=ALU.mult), [pb], [kS])
                    for dr in range(2):
                        col = dr * 4 + hd
                        kb.op(V, lambda: nc.vector.memset(C32[:, :, :], 0.0), [], [C32])
                        kb.op(V, lambda: nc.vector.memset(Cb[:, :, :], 0.0), [], [Cb])
                        order = list(range(NTT)) if dr == 0 else (list(range(NTC - 1, -1, -1)) + list(range(NTT - 1, NTC - 1, -1)))
                        for ci, tt in enumerate(order):
                            i2 = ci % 2
                            v = vf[i2]; vv = vw[i2]; sm = stm[i2]; h_ = hs[i2]; d_ = dn[i2]
                            ts = slice(tt * 128, (tt + 1) * 128)
                            kb.load(v, v[:, :], VM[ts, hd * 256:(hd + 1) * 256])
                            kb.op(P_, lambda: nc.gpsimd.tensor_scalar(out=vv[:, 0:256], in0=v[:, :], scalar1=EA[:, tt, col:col + 1], scalar2=None, op0=ALU.mult),
                                  [v, EA], [vv])
                            kb.op(P_, lambda: nc.gpsimd.tensor_copy(out=vv[:, 256:257], in_=EA[:, tt, col:col + 1]), [EA], [vv])
                            pS = PSB[2 + i2]
                            for c in range(2):
                                kb.op(PE, lambda: nc.tensor.matmul(pS[:, 0:128], lhsT=kT[:, c, ts], rhs=qT[:, c, ts], start=(c == 0), stop=(c == 1)),
                                      [kT, qT], [pS])
                            kb.op(V, lambda: nc.vector.tensor_tensor(out=sm[:, :], in0=pS[:, 0:128], in1=tri[:, dr, :], op=ALU.mult), [pS, tri], [sm])
                            pH = PSB[4 + i2]
                            kb.op(PE, lambda: nc.tensor.matmul(pH[:, 0:257], lhsT=sm[:, :], rhs=vv[:, :], start=True, stop=False), [sm, vv], [pH])
                            for c in range(2):
                                kb.op(PE, lambda: nc.tensor.matmul(pH[:, 0:257], lhsT=qT[:, c, ts], rhs=Cb[:, c, :], start=False, stop=(c == 1)),
                                      [qT, Cb], [pH])
                            kb.op(A_, lambda: nc.scalar.activation(out=h_[:, :], in_=pH[:, 0:257], func=AF.Copy, scale=EBc[:, tt, col:col + 1]), [pH, EBc], [h_])
                            for c in range(2):
                                pC = PSB[6 + c]
                                kb.op(PE, lambda: nc.tensor.matmul(pC[:, 0:257], lhsT=kS[:, tt, c * 128:(c + 1) * 128], rhs=vv[:, :], start=True, stop=True),
                                      [kS, vv], [pC])
                                kb.op(V, lambda: nc.vector.tensor_tensor(out=C32[:, c, :], in0=pC[:, 0:257], in1=C32[:, c, :], op=ALU.add), [pC, C32], [C32])
                            kb.op(V, lambda: nc.vector.tensor_scalar(out=C32[:, :, :], in0=C32[:, :, :], scalar1=EBL[:, tt, col:col + 1], scalar2=None, op0=ALU.mult),
                                  [C32, EBL], [C32])
                            kb.op(P_, lambda: nc.gpsimd.tensor_copy(out=Cb[:, :, :], in_=C32[:, :, :]), [C32], [Cb])
                            kb.op(A_, lambda: nc.scalar.activation(out=d_[:, :], in_=h_[:, 256:257], func=AF.Abs), [h_], [d_])
                            kb.op(V, lambda: nc.vector.tensor_scalar(out=d_[:, :], in0=d_[:, :], scalar1=1.0, scalar2=None, op0=ALU.max), [d_], [d_])
                            kb.op(V, lambda: nc.vector.reciprocal(out=d_[:, :], in_=d_[:, :]), [d_], [d_])
                            if dr == 0:
                                kb.op(V, lambda: nc.vector.tensor_scalar(out=HF[:, tt, :], in0=h_[:, 0:256], scalar1=d_[:, 0:1], scalar2=None, op0=ALU.mult),
                                      [h_, d_], [HF])
                            else:
                                y_ = yb[i2]; o_ = ot[i2]; r = rs[i2]; yt_ = ybT[i2]
                                kb.op(V, lambda: nc.vector.scalar_tensor_tensor(out=y_[:, :], in0=h_[:, 0:256], scalar=d_[:, 0:1], in1=HF[:, tt, :],
                                                                                op0=ALU.mult, op1=ALU.add), [h_, d_, HF], [y_])
                                kb.load(o_, o_[:, :], OS[ts, hd * 256:(hd + 1) * 256])
                                rms_rstd(st, (y_[:, :], y_), 256, (r[:, :], r), (junk[:, :], junk))
                                kb.op(V, lambda: nc.vector.scalar_tensor_tensor(out=y_[:, :], in0=y_[:, :], scalar=r[:, 0:1], in1=gnb[:, hd * 256:(hd + 1) * 256],
                                                                                op0=ALU.mult, op1=ALU.mult), [y_, r, gnb], [y_])
                                kb.op(P_, lambda: nc.gpsimd.tensor_tensor(out=y_[:, :], in0=y_[:, :], in1=o_[:, :], op=ALU.mult), [y_, o_], [y_])
                                pT = PSB[i2]
                                for c in range(2):
                                    kb.op(PE, lambda: nc.tensor.transpose(pT[:, c * 128:(c + 1) * 128], y_[:, c * 128:(c + 1) * 128], ident[:, :]), [y_, ident], [pT])
                                kb.op(A_, lambda: nc.scalar.copy(out=yt_[:, :, :], in_=pT[:, 0:256].rearrange("p (c n) -> p c n", c=2)), [pT], [yt_])
                                kb.store(yt_, YT[512 + hd * 256:512 + (hd + 1) * 256, ts].rearrange("(c p) n -> p c n", p=128), yt_[:, :, :])

            if 'YT' in dbg_out and l == 0:
                with Stage(kb) as st:
                    for r0 in range(0, D, 128):
                        b = st.sb([128, TT], BF16, True); b2 = st.sb([128, TT], F32, True)
                        kb.load(b, b[:, :], YT[r0:r0 + 128, :])
                        kb.op(V, lambda: nc.vector.tensor_copy(out=b2[:, :], in_=b[:, :]), [b], [b2])
                        kb.store(b2, dbg_out['YT'][r0:r0 + 128, :], b2[:, :])

            with Stage(kb) as st:
                conv_weights(st, peer_uT[l], KC, NE, UTB)
            with Stage(kb) as st:
                conv_weights(st, peer_v[l], NE // 128, D, VTB)

            with Stage(kb) as st:
                wo = st.sb([128, KC, D], BF16)
                wf = [st.sb([128, D], F32, True) for _ in range(2)]
                for kc in range(KC):
                    a = wf[kc % 2]
                    kb.load(a, a[:, :], w_out[l, kc * 128:(kc + 1) * 128, :])
                    eng = (V, P_)[kc % 2]; E = kb.E[eng]
                    kb.op(eng, lambda: E.tensor_copy(out=wo[:, kc, :], in_=a[:, :]), [a], [wo])
                G1 = st.sb([128, D], F32, True); A2 = st.sb([128, D], F32, True); B2 = st.sb([128, D], F32, True)
                yt = [st.sb([128, KC, 128], BF16, True) for _ in range(2)]
                xt = [st.sb([128, D], F32, True) for _ in range(2)]
                hh = [st.sb([128, D], F32) for _ in range(2)]
                hT = [st.sb([128, KC, 128], BF16, True) for _ in range(2)]
                junk = st.sb([128, D], F32); rs = [st.sb([128, 1], F32) for _ in range(2)]
                curc = None
                for tt in range(NTT):
                    cond = 1 if tt < NTC else 0
                    if cond != curc:
                        modtile(st, G1, cond, 2); modtile(st, A2, cond, 4); modtile(st, B2, cond, 3); curc = cond
                    i2 = tt % 2
                    y = yt[i2]; x = xt[i2]; h = hh[i2]; r = rs[i2]; ht = hT[i2]
                    ts = slice(tt * 128, (tt + 1) * 128)
                    kb.load(y, y[:, :, :], YT[:, ts].rearrange("(kc p) n -> p kc n", p=128))
                    kb.load(x, x[:, :], XL[ts, :])
                    for nb in range(4):
                        pb = PSB[nb]
                        for kc in range(KC):
                            kb.op(PE, lambda: nc.tensor.matmul(pb[:, :], lhsT=y[:, kc, :], rhs=wo[:, kc, nb * 512:(nb + 1) * 512],
                                                               start=(kc == 0), stop=(kc == KC - 1)), [y, wo], [pb])
                        kb.op(V, lambda: nc.vector.tensor_tensor(out=h[:, nb * 512:(nb + 1) * 512], in0=pb[:, :], in1=G1[:, nb * 512:(nb + 1) * 512], op=ALU.mult),
                              [pb, G1], [h])
                    kb.op(P_, lambda: nc.gpsimd.tensor_tensor(out=x[:, :], in0=x[:, :], in1=h[:, :], op=ALU.add), [x, h], [x])
                    kb.store(x, XL[ts, :], x[:, :])
                    rms_rstd(st, (x[:, :], x), D, (r[:, :], r), (junk[:, :], junk))
                    kb.op(V, lambda: nc.vector.scalar_tensor_tensor(out=h[:, :], in0=x[:, :], scalar=r[:, 0:1], in1=A2[:, :], op0=ALU.mult, op1=ALU.mult),
                          [x, r, A2], [h])
                    kb.op(P_, lambda: nc.gpsimd.tensor_tensor(out=h[:, :], in0=h[:, :], in1=B2[:, :], op=ALU.add), [h, B2], [h])
                    for k4 in range(4):
                        pb = PSB[4 + k4 % 2]
                        for j in range(4):
                            kc = k4 * 4 + j
                            kb.op(PE, lambda: nc.tensor.transpose(pb[:, j * 128:(j + 1) * 128], h[:, kc * 128:(kc + 1) * 128], ident[:, :]), [h, ident], [pb])
                        kb.op(A_, lambda: nc.scalar.copy(out=ht[:, k4 * 4:(k4 + 1) * 4, :], in_=pb[:, :].rearrange("p (a b) -> p a b", a=4)), [pb], [ht])
                    kb.store(ht, HFT[:, ts].rearrange("(kc p) n -> p kc n", p=128), ht[:, :, :])

            with Stage(kb) as st:
                wq = st.sb([128, KC, D], BF16)
                wf = [st.sb([128, D], F32, True) for _ in range(2)]
                for kc in range(KC):
                    a = wf[kc % 2]
                    kb.load(a, a[:, :], peer_wq[l, kc * 128:(kc + 1) * 128, :])
                    eng = (V, P_)[kc % 2]; E = kb.E[eng]
                    kb.op(eng, lambda: E.tensor_copy(out=wq[:, kc, :], in_=a[:, :]), [a], [wq])
                kf = st.sb([128, 2, NK], F32, True)
                for p in range(2): kb.load(kf, kf[:, p, :], peer_keysT[l, p])
                kbb = st.sb([128, 2, NK], BF16)
                kb.op(V, lambda: nc.vector.tensor_copy(out=kbb[:, :, :], in_=kf[:, :, :]), [kf], [kbb])
                hfT = [st.sb([128, KC, 128], BF16, True) for _ in range(2)]
                qTt = [st.sb([128, 16, 128], BF16) for _ in range(2)]
                S = [st.sb([128, 8, 2, NK], F32, True) for _ in range(2)]
                S2 = [st.sb([128, 8, 2, NK], F32) for _ in range(2)]
                top = [st.sb([128, 8, 2, 16], F32) for _ in range(2)]
                cand = [st.sb([128, 8, 16, 16], F32) for _ in range(2)]
                cand2 = [st.sb([128, 8, 256], F32) for _ in range(2)]
                ctop = [st.sb([128, 8, 16], F32) for _ in range(2)]
                ex = [st.sb([128, 8, 16], F32) for _ in range(2)]
                zz = [st.sb([128, 8], F32) for _ in range(2)]
                b2 = [st.sb([128, 8], F32, True) for _ in range(2)]
                for tt in range(NTT):
                    i2 = tt % 2
                    ts = slice(tt * 128, (tt + 1) * 128)
                    hf = hfT[i2]; q = qTt[i2]; s_ = S[i2]; s2_ = S2[i2]; tp = top[i2]; cd = cand[i2]; cd2 = cand2[i2]
                    ct = ctop[i2]; e_ = ex[i2]; z_ = zz[i2]; bb = b2[i2]
                    kb.load(hf, hf[:, :, :], HFT[:, ts].rearrange("(kc p) n -> p kc n", p=128))
                    for c4 in range(4):
                        pb = PSB[c4 % 2]
                        for j in range(4):
                            oc = c4 * 4 + j
                            for kc in range(KC):
                                kb.op(PE, lambda: nc.tensor.matmul(pb[:, j * 128:(j + 1) * 128], lhsT=wq[:, kc, oc * 128:(oc + 1) * 128], rhs=hf[:, kc, :],
                                                                   start=(kc == 0), stop=(kc == KC - 1)), [wq, hf], [pb])
                        kb.op(A_, lambda: nc.scalar.copy(out=q[:, c4 * 4:(c4 + 1) * 4, :], in_=pb[:, :].rearrange("p (a b) -> p a b", a=4)), [pb], [q])
                    for hp in range(16):
                        pb = PSB[2 + (hp * NK) // 512 % 2] if NK == 128 else PSB[2]
                        c0 = (hp * NK) % 512
                        kb.op(PE, lambda: nc.tensor.matmul(pb[:, c0:c0 + NK], lhsT=q[:, hp, :], rhs=kbb[:, hp % 2, :], start=True, stop=True), [q, kbb], [pb])
                        if (c0 + NK == 512) or hp == 15:
                            n_in = (c0 + NK) // NK
                            hp0 = hp + 1 - n_in
                            sv = s_[:, :, :, :].rearrange("p h t k -> p (h t) k")
                            kb.op(A_, lambda: nc.scalar.copy(out=sv[:, hp0:hp + 1, :], in_=pb[:, 0:n_in * NK].rearrange("p (a k) -> p a k", k=NK)), [pb], [s_])
                    for h in range(8):
                        for p in range(2):
                            kb.op(V, lambda: nc.vector.max(out=tp[:, h, p, 0:8], in_=s_[:, h, p, :]), [s_], [tp])
                            kb.op(V, lambda: nc.vector.match_replace(out=s2_[:, h, p, :], in_to_replace=tp[:, h, p, 0:8], in_values=s_[:, h, p, :], imm_value=NEG),
                                  [tp, s_], [s2_])
                            kb.op(V, lambda: nc.vector.max(out=tp[:, h, p, 8:16], in_=s2_[:, h, p, :]), [s2_], [tp])
                    for h in range(8):
                        kb.op(P_, lambda: nc.gpsimd.tensor_tensor(out=cd[:, h, :, :], in0=tp[:, h, 0, :].unsqueeze(2).to_broadcast([128, 16, 16]),
                                                                  in1=tp[:, h, 1, :].unsqueeze(1).to_broadcast([128, 16, 16]), op=ALU.add), [tp], [cd])
                    for h in range(8):
                        cf = cd[:, h, :, :].rearrange("p a b -> p (a b)")
                        kb.op(V, lambda: nc.vector.max(out=ct[:, h, 0:8], in_=cf), [cd], [ct])
                        kb.op(V, lambda: nc.vector.match_replace(out=cd2[:, h, :], in_to_replace=ct[:, h, 0:8], in_values=cf, imm_value=NEG), [ct, cd], [cd2])
                        kb.op(V, lambda: nc.vector.max(out=ct[:, h, 8:16], in_=cd2[:, h, :]), [cd2], [ct])
                    kb.op(V, lambda: nc.vector.tensor_tensor(out=e_[:, :, :], in0=ct[:, :, :], in1=ct[:, :, 0:1].to_broadcast([128, 8, 16]), op=ALU.subtract),
                          [ct], [e_])
                    kb.op(A_, lambda: nc.scalar.activation(out=e_[:, :, :], in_=e_[:, :, :], func=AF.Exp), [e_], [e_])
                    kb.op(V, lambda: nc.vector.tensor_reduce(out=z_[:, :], in_=e_[:, :, :], axis=AX.X, op=ALU.add), [e_], [z_])
                    kb.op(A_, lambda: nc.scalar.activation(out=z_[:, :], in_=z_[:, :], func=AF.Ln), [z_], [z_])
                    kb.op(V, lambda: nc.vector.tensor_tensor(out=bb[:, :], in0=ct[:, :, 15], in1=ct[:, :, 0], op=ALU.subtract), [ct], [bb])
                    kb.op(V, lambda: nc.vector.tensor_tensor(out=bb[:, :], in0=bb[:, :], in1=z_[:, :], op=ALU.subtract), [bb, z_], [bb])
                    kb.op(V, lambda: nc.vector.tensor_tensor(out=s_[:, :, 0, :], in0=s_[:, :, 0, :], in1=ct[:, :, 15:16].to_broadcast([128, 8, NK]), op=ALU.subtract),
                          [s_, ct], [s_])
                    kb.store(s_, SS[ts, :], s_[:, :, :, :].rearrange("p h t k -> p (h t k)"))
                    kb.store(bb, BI2[ts, :], bb[:, :])

            with Stage(kb) as st:
                G = 4
                BI = 4
                EBK = BI * NK
                NBLK = NK // BI
                G2 = st.sb([128, D], F32, True)
                ub = [st.sb([128, KC, EBK], BF16, True) for _ in range(2)]
                vb = [st.sb([128, BI, D], BF16, True) for _ in range(2)]
                acc = st.sb([128, G, D], F32)
                hfg = st.sb([128, G, KC, 128], BF16, True)
                Sg = st.sb([128, G, 8, 2, NK], F32, True)
                b2g = st.sb([128, G, 8], F32, True)
                RD = 4
                zt = [st.sb([128, BI, NK], F32) for _ in range(RD)]
                et = [st.sb([128, BI, NK], F32) for _ in range(RD)]
                Gb = [st.sb([128, EBK], BF16) for _ in range(RD)]
                gA = [st.sb([128, EBK], BF16) for _ in range(G)]
                Wg = [st.sb([128, EBK], F32) for _ in range(2)]
                WgT = [st.sb([128, BI, 128], BF16) for _ in range(2)]
                xt = st.sb([128, D], F32, True)
                pA = PSB[0]; pW = [PSB[1], PSB[2]]; pT = PSB[3]; PO = PSB[4:8]
                tiles = [tt for tt in range(NTT) if not (l == L - 1 and tt < NTC)]
                groups = []
                cur = []
                for tt in tiles:
                    if cur and ((tt < NTC) != (cur[0] < NTC) or len(cur) == G):
                        groups.append(cur); cur = []
                    cur.append(tt)
                if cur: groups.append(cur)
                curc = None
                cnt = [0]
                uTk = kcv(UTB)
                for grp in groups:
                    cond = 1 if grp[0] < NTC else 0
                    if cond != curc:
                        modtile(st, G2, cond, 5); curc = cond
                    for gi_, tt in enumerate(grp):
                        ts = slice(tt * 128, (tt + 1) * 128)
                        kb.load(hfg, hfg[:, gi_, :, :], HFT[:, ts].rearrange("(kc p) n -> p kc n", p=128))
                        kb.load(Sg, Sg[:, gi_, :, :, :].rearrange("p h t k -> p (h t k)"), SS[ts, :])
                        kb.load(b2g, b2g[:, gi_, :], BI2[ts, :])
                    for blk in range(NBLK):
                        u = ub[blk % 2]; v = vb[blk % 2]
                        e0 = blk * EBK
                        for q4 in range(4):
                            kb.load(u, u[:, q4 * 4:(q4 + 1) * 4, :], uTk[:, q4 * 4:(q4 + 1) * 4, e0:e0 + EBK])
                        for c in range(BI):
                            kb.load(v, v[0:NK, c, :], VTB[e0 + c * NK:e0 + (c + 1) * NK, :])
                        for gi_, tt in enumerate(grp):
                            ga = gA[gi_]
                            for kc in range(KC):
                                kb.op(PE, lambda: nc.tensor.matmul(pA[:, 0:EBK], lhsT=hfg[:, gi_, kc, :], rhs=u[:, kc, :], start=(kc == 0), stop=(kc == KC - 1)),
                                      [hfg, u], [pA])
                            kb.op(A_, lambda: nc.scalar.activation(out=ga[:, :], in_=pA[:, 0:EBK], func=AF.Gelu_apprx_tanh), [pA], [ga])

                        def heads(gi_):
                            k2 = gi_ % 2
                            pw = pW[k2]; wg = Wg[k2]; ga = gA[gi_]
                            for h in range(8):
                                r4 = cnt[0] % RD; cnt[0] += 1
                                z = zt[r4]; e = et[r4]; gb = Gb[r4]
                                kb.op(P_, lambda: nc.gpsimd.tensor_tensor(out=z[:, :, :],
                                                                          in0=Sg[:, gi_, h, 0, blk * BI:(blk + 1) * BI].unsqueeze(2).to_broadcast([128, BI, NK]),
                                                                          in1=Sg[:, gi_, h, 1, :].unsqueeze(1).to_broadcast([128, BI, NK]), op=ALU.add), [Sg], [z])
                                kb.op(A_, lambda: nc.scalar.activation(out=e[:, :, :], in_=z[:, :, :], func=AF.Exp, bias=b2g[:, gi_, h:h + 1]), [z, b2g], [e])
                                kb.op(V, lambda: nc.vector.scalar_tensor_tensor(out=gb[:, :], in0=z[:, :, :].rearrange("p a b -> p (a b)"), scalar=0.0,
                                                                                in1=e[:, :, :].rearrange("p a b -> p (a b)"), op0=ALU.is_ge, op1=ALU.mult), [z, e], [gb])
                                kb.op(PE, lambda: nc.tensor.matmul(pw[:, 0:EBK], lhsT=identb[:, :], rhs=gb[:, :], start=(h == 0), stop=(h == 7)), [identb, gb], [pw])
                            kb.op(V, lambda: nc.vector.tensor_tensor(out=wg[:, :], in0=pw[:, 0:EBK], in1=ga[:, :], op=ALU.mult), [pw, ga], [wg])

                        def tail(gi_):
                            k2 = gi_ % 2
                            wg = Wg[k2]; wgt = WgT[k2]
                            for c in range(BI):
                                kb.op(PE, lambda: nc.tensor.transpose(pT[0:NK, c * 128:(c + 1) * 128], wg[:, c * NK:(c + 1) * NK], ident[:, :]), [wg, ident], [pT])
                            kb.op(A_, lambda: nc.scalar.copy(out=wgt[0:NK, :, :], in_=pT[0:NK, 0:BI * 128].rearrange("p (a b) -> p a b", a=BI)), [pT], [wgt])
                            for nb in range(4):
                                for c in range(BI):
                                    kb.op(PE, lambda: nc.tensor.matmul(PO[nb][:, :], lhsT=wgt[0:NK, c, :], rhs=v[0:NK, c, nb * 512:(nb + 1) * 512],
                                                                       start=(c == 0), stop=(c == BI - 1)), [wgt, v], [PO[nb]])
                                if blk == 0:
                                    kb.op(V, lambda: nc.vector.tensor_copy(out=acc[:, gi_, nb * 512:(nb + 1) * 512], in_=PO[nb][:, :]), [PO[nb]], [acc])
                                else:
                                    kb.op(V, lambda: nc.vector.tensor_tensor(out=acc[:, gi_, nb * 512:(nb + 1) * 512], in0=PO[nb][:, :],
                                                                             in1=acc[:, gi_, nb * 512:(nb + 1) * 512], op=ALU.add), [PO[nb], acc], [acc])

                        prev = None
                        for gi_, tt in enumerate(grp):
                            heads(gi_)
                            if prev is not None: tail(prev)
                            prev = gi_
                        tail(prev)
                    for gi_, tt in enumerate(grp):
                        ts = slice(tt * 128, (tt + 1) * 128)
                        kb.load(xt, xt[:, :], XL[ts, :])
                        kb.op(P_, lambda: nc.gpsimd.tensor_tensor(out=acc[:, gi_, :], in0=acc[:, gi_, :], in1=G2[:, :], op=ALU.mult), [acc, G2], [acc])
                        kb.op(V, lambda: nc.vector.tensor_tensor(out=xt[:, :], in0=xt[:, :], in1=acc[:, gi_, :], op=ALU.add), [xt, acc], [xt])
                        kb.store(xt, XL[ts, :], xt[:, :])

        with Stage(kb) as st:
            nf = st.sb([128, D], F32, True); kb.load(nf, nf[:, :], bcast_rows(norm_final[0:1, :]))
            xt = [st.sb([128, D], F32, True) for _ in range(2)]
            ot = [st.sb([128, D], F32, True) for _ in range(2)]
            junk = st.sb([128, D], F32); rs = [st.sb([128, 1], F32) for _ in range(2)]
            for k in range(NTL):
                tt = NTC + k; i2 = k % 2
                x = xt[i2]; o = ot[i2]; r = rs[i2]
                kb.load(x, x[:, :], XL[tt * 128:(tt + 1) * 128, :])
                rms_rstd(st, (x[:, :], x), D, (r[:, :], r), (junk[:, :], junk))
                kb.op(V, lambda: nc.vector.scalar_tensor_tensor(out=o[:, :], in0=x[:, :], scalar=r[:, 0:1], in1=nf[:, :], op0=ALU.mult, op1=ALU.mult),
                      [x, r, nf], [o])
                kb.store(o, yout[k * 128:(k + 1) * 128, :], o[:, :])
        kb.barrier()
        cst.__exit__(None, None, None)
    return nc


def host_consts(T, TC, grid_w=64):
    ident = np.eye(128, dtype=np.float32)
    s = np.arange(128)
    tri = np.stack([(s[:, None] <= s[None, :]), (s[:, None] >= s[None, :])]).astype(np.float32)
    pml = np.zeros((4, 9, 128, 128), np.float32)
    pmc = np.zeros((4, 3, 128, 128), np.float32)
    rl = s // grid_w; c = s % grid_w
    for g, w in enumerate(WINS):
        for dk in range(-4, 5):
            rin = 2 * dk + rl[:, None]
            rout = rl[None, :]
            okr = (rin >= rout - w // 2) & (rin < rout - w // 2 + w)
            okc = (c[:, None] >= c[None, :] - w // 2) & (c[:, None] < c[None, :] - w // 2 + w)
            pml[g, dk + 4] = (okr & okc)
        for dk in range(-1, 2):
            cin = 128 * dk + s[:, None]; cout = s[None, :]
            pmc[g, dk + 1] = (cin >= cout - w // 2) & (cin < cout - w // 2 + w)
    return ident, tri, pml, pmc


def make_in_maps(cfg, inp, nb):
    T, TC, L, NK = cfg['T'], cfg['TC'], cfg['L'], cfg['NK']
    f = lambda a: np.ascontiguousarray(np.asarray(a, dtype=np.float32))
    ident, tri, pml, pmc = host_consts(T, TC)
    shared = dict(
        ada_w=f(inp['ada_w']), ada_b=f(inp['ada_b']), norm_mix=f(inp['norm_mix']), norm_ffn=f(inp['norm_ffn']),
        norm_final=f(inp['norm_final']).reshape(1, D), w_in=f(inp['w_in']), b_gate=f(inp['b_gate']),
        sgu_norm=f(inp['sgu_norm']), sgu_wT=f(np.swapaxes(np.asarray(inp['sgu_w']), -1, -2)), sgu_b=f(inp['sgu_b']),
        qk_convT=f(np.swapaxes(np.asarray(inp['qk_conv']), -1, -2)), mlstm_norm=f(inp['mlstm_norm']),
        pool_w=f(inp['pool_w']), pool_scaleT=f(np.swapaxes(np.asarray(inp['pool_scale']).reshape(L, 4, 128), -1, -2)),
        w_out=f(inp['w_out']), peer_wq=f(inp['peer_wq']),
        peer_keysT=f(np.swapaxes(np.asarray(inp['peer_keys']), -1, -2)),
        peer_uT=f(np.swapaxes(np.asarray(inp['peer_u']), -1, -2)), peer_v=f(inp['peer_v']),
        ident=ident, tri=tri, pml=pml, pmc=pmc)
    maps = []
    for b in range(nb):
        m = dict(shared)
        m['xin'] = f(np.concatenate([np.asarray(inp['ctx'])[b], np.asarray(inp['x'])[b]], axis=0))
        cond = np.stack([np.asarray(inp['c'])[b], np.asarray(inp['c_ctx'])], axis=1)
        m['condT'] = f(cond.reshape(KC, 128, 2).transpose(1, 0, 2))
        maps.append(m)
    return maps


def kernel(**inputs):
    cfg = dict(T=4096, TC=256, L=4, NK=128)
    nb = 4
    nc = build(cfg)
    maps = make_in_maps(cfg, inputs, nb)
    res = run_bass_kernel_spmd(nc, maps, core_ids=list(range(nb)))
    out = np.stack([np.asarray(res.results[b]['yout'], dtype=np.float32) for b in range(nb)], axis=0)
    return out
```

```python
import numpy as np
from contextlib import ExitStack
import concourse.bass as bass
import concourse.mybir as mybir
from concourse.bass_utils import run_bass_kernel_spmd

F32 = mybir.dt.float32
BF16 = mybir.dt.bfloat16
AF = mybir.ActivationFunctionType
ALU = mybir.AluOpType
AX = mybir.AxisListType

D = 2048
KC = 16
D_IN = 5648
OFF_U, OFF_V, OFF_P, OFF_Q, OFF_O, OFF_K, OFF_VM, OFF_G = 0, 512, 1024, 1536, 2560, 3584, 4608, 5632
EPS = 1e-6
WINS = (2, 4, 8, 16)
NEG = -3.0e38


class SemW:
    def __init__(s, h, dma=False):
        s.h = h; s.total = 0; s.dma = dma


class Trk:
    __slots__ = ('w', 'r')

    def __init__(s):
        s.w = []; s.r = {}


class Buf:
    def __init__(s, t, ds=None):
        s.t = t; s.k = Trk(); s.ds = ds

    def __getitem__(s, idx):
        return s.t[idx]


class KB:
    def __init__(s, nc, es):
        s.nc = nc; s.es = es
        s.E = dict(pe=nc.tensor, dve=nc.vector, act=nc.scalar, pool=nc.gpsimd, sp=nc.sync)
        s.allsems = []
        s.esem = {e: s.newsem('e_' + e) for e in s.E}
        s.pesems = {s.esem['pe']}
        s.seen = {e: {} for e in s.E}
        s.dpool = []; s.dnext = 0; s.dbase = 0
        s.nm = 0

    def newsem(s, name, dma=False):
        h = s.es.enter_context(s.nc.semaphore(name + '_%d' % len(s.allsems)))
        sw = SemW(h, dma); s.allsems.append(sw); return sw

    def dsem(s):
        if s.dnext >= len(s.dpool):
            s.dpool.append(s.newsem('d', True))
        sw = s.dpool[s.dnext]; s.dnext += 1; return sw

    def _wait(s, e, deps):
        best = {}
        for sw, v in deps:
            if sw.dma: v = sw.total
            if v > best.get(sw, 0): best[sw] = v
        for sw, v in best.items():
            if s.seen[e].get(sw, 0) >= v: continue
            if e == 'pe' and sw in s.pesems: continue
            s.E[e].wait_ge(sw.h, v); s.seen[e][sw] = v

    def op(s, e, fn, reads=(), writes=()):
        deps = []
        for r in reads: deps += r.k.w
        for w in writes:
            deps += w.k.w; deps += list(w.k.r.items())
        s._wait(e, deps)
        sw = s.esem[e]
        if sw.total >= 30000:
            sw = s.newsem('e_' + e); s.esem[e] = sw
            if e == 'pe': s.pesems.add(sw)
        inst = fn(); sw.total += 1; inst.then_inc(sw.h, 1)
        for r in reads: r.k.r[sw] = sw.total
        for w in writes:
            w.k.w = [(sw, sw.total)]; w.k.r = {}
        return inst

    def dma(s, out, in_, sem, reads=(), writes=(), q='sp'):
        deps = []
        for r in reads: deps += r.k.w
        for w in writes:
            deps += w.k.w; deps += list(w.k.r.items())
        s._wait(q, deps)
        inst = s.E[q].dma_start(out=out, in_=in_); sem.total += 16; inst.then_inc(sem.h, 16)
        for r in reads: r.k.r[sem] = sem.total
        for w in writes:
            w.k.w = [(sem, sem.total)]; w.k.r = {}

    def load(s, buf, out, in_, q='sp'):
        s.dma(out, in_, buf.ds, writes=[buf], q=q)

    def store(s, buf, out, in_, q='sp'):
        s.dma(out, in_, buf.ds, reads=[buf], q=q)

    def barrier(s):
        deps = [(sw, sw.total) for sw in s.allsems if sw.total > 0]
        for e in s.E: s._wait(e, deps)


class Stage:
    def __init__(s, kb):
        s.kb = kb

    def __enter__(s):
        s.kb.barrier(); s.es = ExitStack(); s.es.__enter__(); s.kb.dnext = s.kb.dbase; return s

    def __exit__(s, *a):
        s.kb.barrier(); return s.es.__exit__(*a)

    def sb(s, shape, dt=F32, dma=False):
        s.kb.nm += 1
        t = s.es.enter_context(s.kb.nc.sbuf_tensor('b%d' % s.kb.nm, list(shape), dt))
        return Buf(t, s.kb.dsem() if dma else None)

    def ps(s, shape=(128, 512), dt=F32):
        s.kb.nm += 1
        t = s.es.enter_context(s.kb.nc.psum_tensor('p%d' % s.kb.nm, list(shape), dt))
        return Buf(t)


def bcast_rows(ap2d_row, n=128):
    a = ap2d_row
    return bass.AP(tensor=a.tensor, offset=a.offset, ap=[[0, n]] + [list(x) for x in list(a.ap)[1:]])


def build(cfg):
    T, TC, L, NK = cfg['T'], cfg['TC'], cfg['L'], cfg['NK']
    NE = NK * NK
    TT = T + TC
    NTC, NTL = TC // 128, T // 128
    NTT = NTC + NTL
    IB = 16 if NK >= 16 else NK
    NIB = NK // IB
    EB = IB * NK
    dbg = cfg.get('dbg', ())
    _i, _t, _pml, _pmc = host_consts(T, TC)
    PML_NZ = [[bool(_pml[g, d].any()) for d in range(9)] for g in range(4)]
    PMC_NZ = [[bool(_pmc[g, d].any()) for d in range(3)] for g in range(4)]
    nc = bass.Bass("TRN2", target_bir_lowering=False)

    def din(name, shape, dt=F32):
        return nc.dram_tensor(name, list(shape), dt, kind="ExternalInput").ap()

    def dsc(name, shape, dt=F32):
        return nc.dram_tensor(name, list(shape), dt, kind="Internal").ap()

    xin = din('xin', [TT, D])
    condT = din('condT', [128, KC, 2])
    ada_w = din('ada_w', [L, D, 6 * D]); ada_b = din('ada_b', [L, 6 * D])
    norm_mix = din('norm_mix', [L, D]); norm_ffn = din('norm_ffn', [L, D]); norm_final = din('norm_final', [1, D])
    w_in = din('w_in', [L, D, D_IN]); b_gate = din('b_gate', [L, 16])
    sgu_norm = din('sgu_norm', [L, 512]); sgu_wT = din('sgu_wT', [L, 4, 128, 128]); sgu_b = din('sgu_b', [L, 4, 128])
    qk_convT = din('qk_convT', [L, 2048, 3]); mlstm_norm = din('mlstm_norm', [L, 1024])
    pool_w = din('pool_w', [L, 4, 128, 128]); pool_scaleT = din('pool_scaleT', [L, 128, 4])
    w_out = din('w_out', [L, D, D]); peer_wq = din('peer_wq', [L, D, D])
    peer_keysT = din('peer_keysT', [L, 2, 128, NK])
    peer_uT = din('peer_uT', [L, D, NE]); peer_v = din('peer_v', [L, NE, D])
    identd = din('ident', [128, 128])
    trid = din('tri', [2, 128, 128])
    pml = din('pml', [4, 9, 128, 128])
    pmc = din('pmc', [4, 3, 128, 128])
    yout = nc.dram_tensor('yout', [T, D], F32, kind="ExternalOutput").ap()
    dbg_out = {}
    for nm, shp in dbg:
        dbg_out[nm] = nc.dram_tensor('dbg_' + nm, list(shp), F32, kind="ExternalOutput").ap()

    XL = dsc('XL', [TT, D])
    MOD = dsc('MOD', [2, 6 * D])
    UT = dsc('UT', [512, TT]); QR = dsc('QR', [1024, TT]); KR = dsc('KR', [1024, TT])
    VS = dsc('VS', [TT, 512]); PS = dsc('PS', [TT, 512]); OS = dsc('OS', [TT, 1024]); VM = dsc('VM', [TT, 1024])
    GS = dsc('GS', [TT, 16])
    YT = dsc('YT', [D, TT], BF16)
    HFT = dsc('HFT', [D, TT], BF16)
    SS = dsc('SS', [TT, 8 * 2 * NK]); BI2 = dsc('BI2', [TT, 8])
    UTB = dsc('UTB', [D, NE], BF16); VTB = dsc('VTB', [NE, D], BF16)

    es = ExitStack()
    with es:
        kb = KB(nc, es)
        V, A_, P_, PE = 'dve', 'act', 'pool', 'pe'

        def kcv(ap2d):
            return ap2d.rearrange("(kc p) c -> p kc c", p=128)

        cst = Stage(kb); cst.__enter__()
        ident = cst.sb([128, 128], F32, True); kb.load(ident, ident[:, :], identd[:, :])
        identb = cst.sb([128, 128], BF16)
        kb.op(V, lambda: nc.vector.tensor_copy(out=identb[:, :], in_=ident[:, :]), [ident], [identb])
        tri = cst.sb([128, 2, 128], F32, True)
        for i in range(2): kb.load(tri, tri[:, i, :], trid[i])
        trib = cst.sb([128, 2, 128], BF16)
        kb.op(V, lambda: nc.vector.tensor_copy(out=trib[:, :, :], in_=tri[:, :, :]), [tri], [trib])
        ones = cst.sb([128, 128], F32)
        kb.op(V, lambda: nc.vector.memset(ones[:, :], 1.0), [], [ones])
        sct = cst.sb([128, KC, 2], F32, True); kb.load(sct, sct[:, :, :], condT[:, :, :])
        kb.op(A_, lambda: nc.scalar.activation(out=sct[:, :, :], in_=sct[:, :, :], func=AF.Silu), [sct], [sct])
        PSB = [cst.ps() for _ in range(8)]
        kb.dbase = kb.dnext
        with Stage(kb) as st0:
            cp = [st0.sb([128, D], F32, True) for _ in range(2)]
            for tt in range(NTT):
                b = cp[tt % 2]
                kb.load(b, b[:, :], xin[tt * 128:(tt + 1) * 128, :])
                kb.store(b, XL[tt * 128:(tt + 1) * 128, :], b[:, :])

        def rms_rstd(st, xt, width, outcol, junk):
            kb.op(V, lambda: nc.vector.scalar_tensor_tensor(out=junk[0], in0=xt[0], scalar=1.0, in1=xt[0],
                                                            op0=ALU.mult, op1=ALU.mult, accum_out=outcol[0]),
                  [xt[1]], [junk[1], outcol[1]])
            kb.op(V, lambda: nc.vector.tensor_scalar(out=outcol[0], in0=outcol[0], scalar1=1.0 / width, scalar2=EPS,
                                                     op0=ALU.mult, op1=ALU.add), [outcol[1]], [outcol[1]])
            kb.op(A_, lambda: nc.scalar.activation(out=outcol[0], in_=outcol[0], func=AF.Sqrt), [outcol[1]], [outcol[1]])
            kb.op(V, lambda: nc.vector.reciprocal(out=outcol[0], in_=outcol[0]), [outcol[1]], [outcol[1]])

        def conv_weights(st, src2d, nk, ncols, dstdram):
            CW = min(ncols, 2048)
            f = [st.sb([128, CW], F32, True) for _ in range(4)]
            g = [st.sb([128, CW], BF16, True) for _ in range(4)]
            i = 0
            for k in range(nk):
                for c0 in range(0, ncols, CW):
                    a, b = f[i % 4], g[i % 4]
                    kb.load(a, a[:, :], src2d[k * 128:(k + 1) * 128, c0:c0 + CW])
                    eng = (V, P_)[i % 2]
                    E = kb.E[eng]
                    kb.op(eng, lambda: E.tensor_copy(out=b[:, :], in_=a[:, :]), [a], [b])
                    kb.store(b, dstdram[k * 128:(k + 1) * 128, c0:c0 + CW], b[:, :])
                    i += 1

        for l in range(L):
            with Stage(kb) as st:
                wb = [st.sb([128, KC, 512], F32, True) for _ in range(2)]
                adab = st.sb([2, 6 * D], F32, True)
                for r in range(2): kb.load(adab, adab[r:r + 1, :], ada_b[l:l + 1, :])
                nrm = st.sb([2, 2, D], F32, True)
                for r in range(2):
                    kb.load(nrm, nrm[r:r + 1, 0, :], norm_mix[l:l + 1, :])
                    kb.load(nrm, nrm[r:r + 1, 1, :], norm_ffn[l:l + 1, :])
                mo = [st.sb([2, 512], F32, True) for _ in range(2)]
                for cb in range(24):
                    w = wb[cb % 2]
                    src = kcv(ada_w[l])
                    for q4 in range(4):
                        kb.load(w, w[:, q4 * 4:(q4 + 1) * 4, :], src[:, q4 * 4:(q4 + 1) * 4, cb * 512:(cb + 1) * 512])
                    pb = PSB[cb % 2]
                    for kc in range(KC):
                        kb.op(PE, lambda: nc.tensor.matmul(pb[0:2, :], lhsT=sct[:, kc, :], rhs=w[:, kc, :],
                                                           start=(kc == 0), stop=(kc == KC - 1)), [sct, w], [pb])
                    m = mo[cb % 2]
                    kb.op(V, lambda: nc.vector.tensor_tensor(out=m[:, :], in0=pb[0:2, :], in1=adab[:, cb * 512:(cb + 1) * 512],
                                                             op=ALU.add), [pb, adab], [m])
                    part = cb // 4
                    if part in (1, 4):
                        j = 0 if part == 1 else 1
                        c0 = (cb % 4) * 512
                        kb.op(V, lambda: nc.vector.scalar_tensor_tensor(out=m[:, :], in0=m[:, :], scalar=1.0,
                                                                        in1=nrm[:, j, c0:c0 + 512], op0=ALU.add, op1=ALU.mult),
                              [m, nrm], [m])
                    kb.store(m, MOD[:, cb * 512:(cb + 1) * 512], m[:, :])

            def modtile(st, buf, cond, part):
                kb.load(buf, buf[:, :], bcast_rows(MOD[cond:cond + 1, part * D:(part + 1) * D]))

            with Stage(kb) as st:
                A1 = st.sb([128, D], F32, True); B1 = st.sb([128, D], F32, True)
                bg = st.sb([128, 16], F32, True)
                kb.load(bg, bg[:, :], bcast_rows(b_gate[l:l + 1, :]))
                xt = [st.sb([128, D], F32, True) for _ in range(2)]
                hh = [st.sb([128, D], F32) for _ in range(2)]
                junk = st.sb([128, D], F32)
                rs = [st.sb([128, 1], F32) for _ in range(2)]
                hT = st.sb([128, KC, 1024], BF16)
                wf = [st.sb([128, KC, 512], F32, True) for _ in range(1)]
                wbf = [st.sb([128, KC, 512], BF16) for _ in range(2)]
                ev = [st.sb([128, 512], F32, True) for _ in range(3)]
                evi = [0]
                wi = [0]
                pbi = [0]
                w2 = kcv(w_in[l])
                sts = []
                t0 = 0
                while t0 < NTC: n = min(8, NTC - t0); sts.append((1, t0, n)); t0 += n
                while t0 < NTT: n = min(8, NTT - t0); sts.append((0, t0, n)); t0 += n
                curc = None
                for (cond, tb, n) in sts:
                    if cond != curc:
                        modtile(st, A1, cond, 1); modtile(st, B1, cond, 0); curc = cond
                    NS = n * 128
                    for ti in range(n):
                        tt = tb + ti
                        x = xt[tt % 2]; h = hh[tt % 2]; r = rs[tt % 2]
                        kb.load(x, x[:, :], XL[tt * 128:(tt + 1) * 128, :])
                        rms_rstd(st, (x[:, :], x), D, (r[:, :], r), (junk[:, :], junk))
                        kb.op(V, lambda: nc.vector.scalar_tensor_tensor(out=h[:, :], in0=x[:, :], scalar=r[:, 0:1], in1=A1[:, :],
                                                                        op0=ALU.mult, op1=ALU.mult), [x, r, A1], [h])
                        kb.op(P_, lambda: nc.gpsimd.tensor_tensor(out=h[:, :], in0=h[:, :], in1=B1[:, :], op=ALU.add), [h, B1], [h])
                        for k4 in range(4):
                            pb = PSB[k4 % 2]
                            for j in range(4):
                                kc = k4 * 4 + j
                                kb.op(PE, lambda: nc.tensor.transpose(pb[:, j * 128:(j + 1) * 128], h[:, kc * 128:(kc + 1) * 128], ident[:, :]),
                                      [h, ident], [pb])
                            kb.op(A_, lambda: nc.scalar.copy(out=hT[:, k4 * 4:(k4 + 1) * 4, ti * 128:(ti + 1) * 128],
                                                             in_=pb[:, :].rearrange("p (a b) -> p a b", a=4)), [pb], [hT])

                    def wload(c0, ncols):
                        i = wi[0]; wi[0] += 1
                        a, b = wf[0], wbf[i % 2]
                        for q4 in range(4):
                            kb.load(a, a[:, q4 * 4:(q4 + 1) * 4, 0:ncols], w2[:, q4 * 4:(q4 + 1) * 4, c0:c0 + ncols])
                        eng = (V, P_)[i % 2]; E = kb.E[eng]
                        kb.op(eng, lambda: E.tensor_copy(out=b[:, :, 0:ncols], in_=a[:, :, 0:ncols]), [a], [b])
                        return b

                    def evbuf():
                        e = ev[evi[0] % 3]; evi[0] += 1; return e

                    for (off, nb, dst, fn) in ((OFF_U, 4, UT, AF.Gelu_apprx_tanh), (OFF_Q, 8, QR, None), (OFF_K, 8, KR, None)):
                        for g4 in range(nb // 4):
                            wbb = wload(off + g4 * 512, 512)
                            for j in range(4):
                              for n0 in range(0, NS, 512):
                                nw = min(512, NS - n0)
                                blk = g4 * 4 + j
                                pbi[0] += 1
                                pb = PSB[2 + (pbi[0] % 2)]
                                for kc in range(KC):
                                    kb.op(PE, lambda: nc.tensor.matmul(pb[:, 0:nw], lhsT=wbb[:, kc, j * 128:(j + 1) * 128], rhs=hT[:, kc, n0:n0 + nw],
                                                                       start=(kc == 0), stop=(kc == KC - 1)), [wbb, hT], [pb])
                                e = evbuf()
                                if fn is None:
                                    kb.op(A_, lambda: nc.scalar.copy(out=e[:, 0:nw], in_=pb[:, 0:nw]), [pb], [e])
                                else:
                                    kb.op(A_, lambda: nc.scalar.activation(out=e[:, 0:nw], in_=pb[:, 0:nw], func=fn), [pb], [e])
                                kb.store(e, dst[blk * 128:(blk + 1) * 128, tb * 128 + n0:tb * 128 + n0 + nw], e[:, 0:nw])
                    for (off, ncols, dst, dc0, fn) in ((OFF_V, 512, VS, 0, AF.Gelu_apprx_tanh), (OFF_P, 512, PS, 0, None),
                                                       (OFF_O, 512, OS, 0, AF.Sigmoid), (OFF_O + 512, 512, OS, 512, AF.Sigmoid),
                                                       (OFF_VM, 512, VM, 0, None), (OFF_VM + 512, 512, VM, 512, None),
                                                       (OFF_G, 16, GS, 0, 'gate')):
                        wbb = wload(off, ncols)
                        for ti in range(n):
                            tt = tb + ti
                            pb = PSB[4 + (ti % 2)]
                            for kc in range(KC):
                                kb.op(PE, lambda: nc.tensor.matmul(pb[:, 0:ncols], lhsT=hT[:, kc, ti * 128:(ti + 1) * 128], rhs=wbb[:, kc, 0:ncols],
                                                                   start=(kc == 0), stop=(kc == KC - 1)), [wbb, hT], [pb])
                            e = evbuf()
                            if fn is None:
                                kb.op(A_, lambda: nc.scalar.copy(out=e[:, 0:ncols], in_=pb[:, 0:ncols]), [pb], [e])
                            elif fn == 'gate':
                                kb.op(V, lambda: nc.vector.tensor_tensor(out=e[:, 0:ncols], in0=pb[:, 0:ncols], in1=bg[:, :], op=ALU.add), [pb, bg], [e])
                            else:
                                kb.op(A_, lambda: nc.scalar.activation(out=e[:, 0:ncols], in_=pb[:, 0:ncols], func=fn), [pb], [e])
                            kb.store(e, dst[tt * 128:(tt + 1) * 128, dc0:dc0 + ncols], e[:, 0:ncols])

            if 'UT' in dbg_out and l == 0:
                with Stage(kb) as st:
                    for (nm, src) in (('UT', UT), ('QR', QR), ('VS', VS), ('GS', GS), ('OS', OS)):
                        if nm in dbg_out:
                            R, C = src.shape
                            for r0 in range(0, R, 128):
                                rr = min(128, R - r0)
                                b = st.sb([128, C], F32, True)
                                kb.load(b, b[0:rr, :], src[r0:r0 + rr, :]); kb.store(b, dbg_out[nm][r0:r0 + rr, :], b[0:rr, :])

            with Stage(kb) as st:
                sgn = st.sb([128, 512], F32, True); kb.load(sgn, sgn[:, :], bcast_rows(sgu_norm[l:l + 1, :]))
                wsf = st.sb([128, 4, 128], F32, True)
                for h in range(4): kb.load(wsf, wsf[:, h, :], sgu_wT[l, h])
                wsb = st.sb([128, 4, 128], BF16)
                kb.op(V, lambda: nc.vector.tensor_copy(out=wsb[:, :, :], in_=wsf[:, :, :]), [wsf], [wsb])
                sbr = st.sb([1, 4, 128], F32, True); kb.load(sbr, sbr[0:1, :, :], sgu_b[l:l + 1, :, :])
                sbb = st.sb([1, 4, 128], BF16)
                kb.op(V, lambda: nc.vector.tensor_copy(out=sbb[:, :, :], in_=sbr[:, :, :]), [sbr], [sbb])
                onesb = st.sb([1, 128], BF16)
                kb.op(V, lambda: nc.vector.memset(onesb[:, :], 1.0), [], [onesb])
                pwf = st.sb([128, 4, 128], F32, True)
                for g in range(4): kb.load(pwf, pwf[:, g, :], pool_w[l, g])
                pwb = st.sb([128, 4, 128], BF16)
                kb.op(V, lambda: nc.vector.tensor_copy(out=pwb[:, :, :], in_=pwf[:, :, :]), [pwf], [pwb])
                psc = st.sb([128, 4], F32, True); kb.load(psc, psc[:, :], pool_scaleT[l])
                pmf = st.sb([128, 9, 128], F32, True)
                pmlb = st.sb([128, 4, 9, 128], BF16); pmcb = st.sb([128, 4, 3, 128], BF16)
                for g in range(4):
                    for dk in range(9): kb.load(pmf, pmf[:, dk, :], pml[g, dk])
                    kb.op(V, lambda: nc.vector.tensor_copy(out=pmlb[:, g, :, :], in_=pmf[:, :, :]), [pmf], [pmlb])
                for g in range(4):
                    for dk in range(3): kb.load(pmf, pmf[:, dk, :], pmc[g, dk])
                    kb.op(V, lambda: nc.vector.tensor_copy(out=pmcb[:, g, :, :], in_=pmf[:, 0:3, :]), [pmf], [pmcb])
                XPf = st.sb([128, NTT, 512], F32, True)
                XP = st.sb([128, NTT, 4, 129], BF16)
                kb.op(P_, lambda: nc.gpsimd.memset(XP[:, :, :, :], 1.0), [], [XP])
                for tt in range(NTT):
                    kb.load(XPf, XPf[:, tt, :], PS[tt * 128:(tt + 1) * 128, :])
                kb.op(V, lambda: nc.vector.tensor_copy(out=XP[:, :, :, 0:128], in_=XPf[:, :, :].rearrange("p t (g c) -> p t g c", g=4)),
                      [XPf], [XP])
                vt = [st.sb([128, 512], F32, True) for _ in range(2)]
                vn = [st.sb([128, 512], BF16) for _ in range(2)]
                ut = [st.sb([128, 4, 128], F32, True) for _ in range(2)]
                ya = [st.sb([128, 4, 128], BF16, True) for _ in range(2)]
                yc = [st.sb([128, 4, 128], BF16, True) for _ in range(2)]
                junk = st.sb([128, 512], F32); rs = [st.sb([128, 1], F32) for _ in range(2)]
                mean = [st.sb([128, 512], F32) for _ in range(2)]
                rc = [st.sb([128, 4], F32) for _ in range(2)]
                dT = [st.sb([128, 4, 128], BF16) for _ in range(2)]
                for tt in range(NTT):
                    i2 = tt % 2
                    isctx = tt < NTC
                    v = vt[i2]; u = ut[i2]; r = rs[i2]; vb = vn[i2]
                    kb.load(v, v[:, :], VS[tt * 128:(tt + 1) * 128, :])
                    kb.load(u, u[:, :, :], UT[:, tt * 128:(tt + 1) * 128].rearrange("(h p) n -> p h n", p=128))
                    rms_rstd(st, (v[:, :], v), 512, (r[:, :], r), (junk[:, :], junk))
                    kb.op(V, lambda: nc.vector.scalar_tensor_tensor(out=vb[:, :], in0=v[:, :], scalar=r[:, 0:1], in1=sgn[:, :],
                                                                    op0=ALU.mult, op1=ALU.mult), [v, r, sgn], [vb])
                    pb = PSB[i2]
                    for h in range(4):
                        kb.op(PE, lambda: nc.tensor.matmul(pb[:, h * 128:(h + 1) * 128], lhsT=vb[:, h * 128:(h + 1) * 128], rhs=wsb[:, h, :],
                                                           start=True, stop=False), [vb, wsb], [pb])
                        kb.op(PE, lambda: nc.tensor.matmul(pb[:, h * 128:(h + 1) * 128], lhsT=onesb[0:1, :], rhs=sbb[0:1, h, :],
                                                           start=False, stop=True), [onesb, sbb], [pb])
                    y = ya[i2]
                    kb.op(V, lambda: nc.vector.tensor_tensor(out=y[:, :, :], in0=pb[:, :].rearrange("p (h n) -> p h n", h=4), in1=u[:, :, :],
                                                             op=ALU.mult), [pb, u], [y])
                    kb.store(y, YT[0:512, tt * 128:(tt + 1) * 128].rearrange("(h p) n -> p h n", p=128), y[:, :, :])
                    if isctx:
                        k, nt, tbase, pmb, dks, dko = tt, NTC, 0, pmcb, (-1, 0, 1), 1
                    else:
                        k, nt, tbase, pmb, dks, dko = tt - NTC, NTL, NTC, pmlb, tuple(range(-4, 5)), 4
                    mn = mean[i2]; rcc = rc[i2]
                    pb2 = PSB[2 + i2]; pb3 = PSB[4 + i2]
                    for g in range(4):
                        w = WINS[g]
                        nzm = PMC_NZ if isctx else PML_NZ
                        use = [dk for dk in dks if 0 <= k + dk < nt and nzm[g][dk + dko]]
                        pbg = pb2 if g < 2 else pb3
                        c0 = (g % 2) * 129
                        for j, dk in enumerate(use):
                            kb.op(PE, lambda: nc.tensor.matmul(pbg[:, c0:c0 + 129], lhsT=pmb[:, g, dk + dko, :], rhs=XP[:, tbase + k + dk, g, :],
                                                               start=(j == 0), stop=(j == len(use) - 1)), [pmb, XP], [pbg])
                    for g in range(4):
                        pbg = pb2 if g < 2 else pb3
                        c0 = (g % 2) * 129
                        kb.op(V, lambda: nc.vector.reciprocal(out=rcc[:, g:g + 1], in_=pbg[:, c0 + 128:c0 + 129]), [pbg], [rcc])
                        kb.op(V, lambda: nc.vector.scalar_tensor_tensor(out=mn[:, g * 128:(g + 1) * 128], in0=pbg[:, c0:c0 + 128], scalar=rcc[:, g:g + 1],
                                                                        in1=XPf[:, tt, g * 128:(g + 1) * 128], op0=ALU.mult, op1=ALU.subtract),
                              [pbg, rcc, XPf], [mn])
                    pb4 = PSB[6 + i2]
                    for g in range(4):
                        kb.op(PE, lambda: nc.tensor.transpose(pb4[:, g * 128:(g + 1) * 128], mn[:, g * 128:(g + 1) * 128], ident[:, :]), [mn, ident], [pb4])
                    d = dT[i2]
                    kb.op(A_, lambda: nc.scalar.copy(out=d[:, :, :], in_=pb4[:, :].rearrange("p (g n) -> p g n", g=4)), [pb4], [d])
                    for g in range(4):
                        kb.op(PE, lambda: nc.tensor.matmul(pb4[:, g * 128:(g + 1) * 128], lhsT=pwb[:, g, :], rhs=d[:, g, :], start=True, stop=True),
                              [pwb, d], [pb4])
                    y2 = yc[i2]
                    for g in range(4):
                        kb.op(V, lambda: nc.vector.tensor_scalar(out=y2[:, g, :], in0=pb4[:, g * 128:(g + 1) * 128], scalar1=psc[:, g:g + 1], scalar2=None,
                                                                 op0=ALU.mult), [pb4, psc], [y2])
                    kb.store(y2, YT[1536:2048, tt * 128:(tt + 1) * 128].rearrange("(g p) n -> p g n", p=128), y2[:, :, :])

            with Stage(kb) as st:
                EA = st.sb([128, NTT, 8], F32); EBc = st.sb([128, NTT, 8], F32); EBL = st.sb([128, NTT, 8], F32)
                gt = [st.sb([128, 16], F32, True) for _ in range(2)]
                lf = [st.sb([128, 8], F32) for _ in range(2)]
                t1 = [st.sb([128, 8], F32) for _ in range(2)]
                gi = [st.sb([128, 8], F32) for _ in range(2)]
                for tt in range(NTT):
                    i2 = tt % 2
                    g = gt[i2]; f = lf[i2]; a = t1[i2]; gii = gi[i2]
                    kb.load(g, g[:, :], GS[tt * 128:(tt + 1) * 128, :])
                    gv = g[:, :].rearrange("p (d g h) -> p d g h", d=2, g=2)
                    fv = f[:, :].rearrange("p (d h) -> p d h", d=2)
                    av = a[:, :].rearrange("p (d h) -> p d h", d=2)
                    kb.op(A_, lambda: nc.scalar.activation(out=av, in_=gv[:, :, 1, :], func=AF.Abs), [g], [a])
                    kb.op(A_, lambda: nc.scalar.activation(out=a[:, :], in_=a[:, :], func=AF.Exp, scale=-1.0), [a], [a])
                    kb.op(A_, lambda: nc.scalar.activation(out=a[:, :], in_=a[:, :], func=AF.Ln, bias=1.0), [a], [a])
                    kb.op(V, lambda: nc.vector.tensor_scalar(out=fv, in0=gv[:, :, 1, :], scalar1=0.0, scalar2=None, op0=ALU.min), [g], [f])
                    kb.op(V, lambda: nc.vector.tensor_tensor(out=f[:, :], in0=f[:, :], in1=a[:, :], op=ALU.subtract), [f, a], [f])
                    kb.op(V, lambda: nc.vector.tensor_copy(out=gii[:, :].rearrange("p (d h) -> p d h", d=2), in_=gv[:, :, 0, :]), [g], [gii])
                    pb = PSB[i2]
                    for dr in range(2):
                        kb.op(PE, lambda: nc.tensor.matmul(pb[:, dr * 4:dr * 4 + 4], lhsT=tri[:, dr, :], rhs=f[:, dr * 4:dr * 4 + 4], start=True, stop=True),
                              [tri, f], [pb])
                    kb.op(PE, lambda: nc.tensor.matmul(pb[:, 8:16], lhsT=ones[:, :], rhs=f[:, :], start=True, stop=True), [ones, f], [pb])
                    kb.op(A_, lambda: nc.scalar.activation(out=EBc[:, tt, :], in_=pb[:, 0:8], func=AF.Exp), [pb], [EBc])
                    kb.op(A_, lambda: nc.scalar.activation(out=EBL[:, tt, :], in_=pb[:, 8:16], func=AF.Exp), [pb], [EBL])
                    kb.op(V, lambda: nc.vector.tensor_tensor(out=gii[:, :], in0=gii[:, :], in1=pb[:, 0:8], op=ALU.subtract), [gii, pb], [gii])
                    kb.op(A_, lambda: nc.scalar.activation(out=EA[:, tt, :], in_=gii[:, :], func=AF.Exp), [gii], [EA])
                gnb = st.sb([128, 1024], F32, True); kb.load(gnb, gnb[:, :], bcast_rows(mlstm_norm[l:l + 1, :]))
                cw = st.sb([128, 16, 3], F32, True)
                kb.load(cw, cw[:, :, :], qk_convT[l].rearrange("(c p) k -> p c k", p=128))
                raw = [st.sb([128, TT + 2], F32, True) for _ in range(1)]
                cv = [st.sb([128, TT], F32) for _ in range(1)]
                qT = st.sb([128, 2, TT], BF16); kT = st.sb([128, 2, TT], BF16)
                kS = st.sb([128, NTT, 256], BF16)
                vf = [st.sb([128, 256], F32, True) for _ in range(2)]
                vw = [st.sb([128, 257], BF16) for _ in range(2)]
                stm = [st.sb([128, 128], BF16) for _ in range(2)]
                C32 = st.sb([128, 2, 257], F32); Cb = st.sb([128, 2, 257], BF16)
                hs = [st.sb([128, 257], F32) for _ in range(2)]
                dn = [st.sb([128, 1], F32) for _ in range(2)]
                HF = st.sb([128, NTT, 256], F32)
                ot = [st.sb([128, 256], F32, True) for _ in range(2)]
                yb = [st.sb([128, 256], F32) for _ in range(2)]
                ybT = [st.sb([128, 2, 128], BF16, True) for _ in range(2)]
                junk = st.sb([128, 256], F32); rs = [st.sb([128, 1], F32) for _ in range(2)]
                for hd in range(4):
                    for which, (src, dstT, scl) in enumerate(((QR, qT, 1.0), (KR, kT, 1.0 / 16.0))):
                        for c in range(2):
                            ch = hd * 2 + c
                            rw = raw[0]; co = cv[0]
                            wcol = which * 8 + ch
                            kb.op(P_, lambda: nc.gpsimd.memset(rw[:, :], 0.0), [], [rw])
                            kb.load(rw, rw[:, 1:TT + 1], src[ch * 128:(ch + 1) * 128, :])
                            for (s0, sl) in ((0, TC), (TC, T)):
                                kb.op(V, lambda: nc.vector.tensor_scalar(out=co[:, s0:s0 + sl], in0=rw[:, 1 + s0:1 + s0 + sl], scalar1=cw[:, wcol, 1:2],
                                                                         scalar2=None, op0=ALU.mult), [rw, cw], [co])
                                kb.op(V, lambda: nc.vector.scalar_tensor_tensor(out=co[:, s0 + 1:s0 + sl], in0=rw[:, 1 + s0:s0 + sl], scalar=cw[:, wcol, 0:1],
                                                                                in1=co[:, s0 + 1:s0 + sl], op0=ALU.mult, op1=ALU.add), [rw, cw, co], [co])
                                kb.op(V, lambda: nc.vector.scalar_tensor_tensor(out=co[:, s0:s0 + sl - 1], in0=rw[:, 2 + s0:1 + s0 + sl], scalar=cw[:, wcol, 2:3],
                                                                                in1=co[:, s0:s0 + sl - 1], op0=ALU.mult, op1=ALU.add), [rw, cw, co], [co])
                            kb.op(A_, lambda: nc.scalar.activation(out=co[:, :], in_=co[:, :], func=AF.Silu), [co], [co])
                            kb.op(V, lambda: nc.vector.tensor_scalar(out=dstT[:, c, :], in0=co[:, :], scalar1=scl, scalar2=None, op0=ALU.mult), [co], [dstT])
                            if which == 1:
                                for t4 in range(0, NTT, 4):
                                    nn = min(4, NTT - t4)
                                    pb = PSB[(t4 // 4) % 2]
                                    for j in range(nn):
                                        kb.op(PE, lambda: nc.tensor.transpose(pb[:, j * 128:(j + 1) * 128], co[:, (t4 + j) * 128:(t4 + j + 1) * 128], ident[:, :]),
                                              [co, ident], [pb])
                                    kb.op(V, lambda: nc.vector.tensor_scalar(out=kS[:, t4:t4 + nn, c * 128:(c + 1) * 128],
                                                                             in0=pb[:, 0:nn * 128].rearrange("p (a b) -> p a b", a=nn),
                                                                             scalar1=scl, scalar2=None, op0=ALU.mult), [pb], [kS])
                    for dr in range(2):
                        col = dr * 4 + hd
                        kb.op(V, lambda: nc.vector.memset(C32[:, :, :], 0.0), [], [C32])
                        kb.op(V, lambda: nc.vector.memset(Cb[:, :, :], 0.0), [], [Cb])
                        order = list(range(NTT)) if dr == 0 else (list(range(NTC - 1, -1, -1)) + list(range(NTT - 1, NTC - 1, -1)))
                        for ci, tt in enumerate(order):
                            i2 = ci % 2
                            v = vf[i2]; vv = vw[i2]; sm = stm[i2]; h_ = hs[i2]; d_ = dn[i2]
                            ts = slice(tt * 128, (tt + 1) * 128)
                            kb.load(v, v[:, :], VM[ts, hd * 256:(hd + 1) * 256])
                            kb.op(P_, lambda: nc.gpsimd.tensor_scalar(out=vv[:, 0:256], in0=v[:, :], scalar1=EA[:, tt, col:col + 1], scalar2=None, op0=ALU.mult),
                                  [v, EA], [vv])
                            kb.op(P_, lambda: nc.gpsimd.tensor_copy(out=vv[:, 256:257], in_=EA[:, tt, col:col + 1]), [EA], [vv])
                            pS = PSB[2 + i2]
                            for c in range(2):
                                kb.op(PE, lambda: nc.tensor.matmul(pS[:, 0:128], lhsT=kT[:, c, ts], rhs=qT[:, c, ts], start=(c == 0), stop=(c == 1)),
                                      [kT, qT], [pS])
                            kb.op(V, lambda: nc.vector.tensor_tensor(out=sm[:, :], in0=pS[:, 0:128], in1=tri[:, dr, :], op=ALU.mult), [pS, tri], [sm])
                            pH = PSB[4 + i2]
                            kb.op(PE, lambda: nc.tensor.matmul(pH[:, 0:257], lhsT=sm[:, :], rhs=vv[:, :], start=True, stop=False), [sm, vv], [pH])
                            for c in range(2):
                                kb.op(PE, lambda: nc.tensor.matmul(pH[:, 0:257], lhsT=qT[:, c, ts], rhs=Cb[:, c, :], start=False, stop=(c == 1)),
                                      [qT, Cb], [pH])
                            kb.op(A_, lambda: nc.scalar.activation(out=h_[:, :], in_=pH[:, 0:257], func=AF.Copy, scale=EBc[:, tt, col:col + 1]), [pH, EBc], [h_])
                            for c in range(2):
                                pC = PSB[6 + c]
                                kb.op(PE, lambda: nc.tensor.matmul(pC[:, 0:257], lhsT=kS[:, tt, c * 128:(c + 1) * 128], rhs=vv[:, :], start=True, stop=True),
                                      [kS, vv], [pC])
                                kb.op(V, lambda: nc.vector.tensor_tensor(out=C32[:, c, :], in0=pC[:, 0:257], in1=C32[:, c, :], op=ALU.add), [pC, C32], [C32])
                            kb.op(V, lambda: nc.vector.tensor_scalar(out=C32[:, :, :], in0=C32[:, :, :], scalar1=EBL[:, tt, col:col + 1], scalar2=None, op0=ALU.mult),
                                  [C32, EBL], [C32])
                            kb.op(P_, lambda: nc.gpsimd.tensor_copy(out=Cb[:, :, :], in_=C32[:, :, :]), [C32], [Cb])
                            kb.op(A_, lambda: nc.scalar.activation(out=d_[:, :], in_=h_[:, 256:257], func=AF.Abs), [h_], [d_])
                            kb.op(V, lambda: nc.vector.tensor_scalar(out=d_[:, :], in0=d_[:, :], scalar1=1.0, scalar2=None, op0=ALU.max), [d_], [d_])
                            kb.op(V, lambda: nc.vector.reciprocal(out=d_[:, :], in_=d_[:, :]), [d_], [d_])
                            if dr == 0:
                                kb.op(V, lambda: nc.vector.tensor_scalar(out=HF[:, tt, :], in0=h_[:, 0:256], scalar1=d_[:, 0:1], scalar2=None, op0=ALU.mult),
                                      [h_, d_], [HF])
                            else:
                                y_ = yb[i2]; o_ = ot[i2]; r = rs[i2]; yt_ = ybT[i2]
                                kb.op(V, lambda: nc.vector.scalar_tensor_tensor(out=y_[:, :], in0=h_[:, 0:256], scalar=d_[:, 0:1], in1=HF[:, tt, :],
                                                                                op0=ALU.mult, op1=ALU.add), [h_, d_, HF], [y_])
                                kb.load(o_, o_[:, :], OS[ts, hd * 256:(hd + 1) * 256])
                                rms_rstd(st, (y_[:, :], y_), 256, (r[:, :], r), (junk[:, :], junk))
                                kb.op(V, lambda: nc.vector.scalar_tensor_tensor(out=y_[:, :], in0=y_[:, :], scalar=r[:, 0:1], in1=gnb[:, hd * 256:(hd + 1) * 256],
                                                                                op0=ALU.mult, op1=ALU.mult), [y_, r, gnb], [y_])
                                kb.op(P_, lambda: nc.gpsimd.tensor_tensor(out=y_[:, :], in0=y_[:, :], in1=o_[:, :], op=ALU.mult), [y_, o_], [y_])
                                pT = PSB[i2]
                                for c in range(2):
                                    kb.op(PE, lambda: nc.tensor.transpose(pT[:, c * 128:(c + 1) * 128], y_[:, c * 128:(c + 1) * 128], ident[:, :]), [y_, ident], [pT])
                                kb.op(A_, lambda: nc.scalar.copy(out=yt_[:, :, :], in_=pT[:, 0:256].rearrange("p (c n) -> p c n", c=2)), [pT], [yt_])
                                kb.store(yt_, YT[512 + hd * 256:512 + (hd + 1) * 256, ts].rearrange("(c p) n -> p c n", p=128), yt_[:, :, :])

            if 'YT' in dbg_out and l == 0:
                with Stage(kb) as st:
                    for r0 in range(0, D, 128):
                        b = st.sb([128, TT], BF16, True); b2 = st.sb([128, TT], F32, True)
                        kb.load(b, b[:, :], YT[r0:r0 + 128, :])
                        kb.op(V, lambda: nc.vector.tensor_copy(out=b2[:, :], in_=b[:, :]), [b], [b2])
                        kb.store(b2, dbg_out['YT'][r0:r0 + 128, :], b2[:, :])

            with Stage(kb) as st:
                conv_weights(st, peer_uT[l], KC, NE, UTB)
            with Stage(kb) as st:
                conv_weights(st, peer_v[l], NE // 128, D, VTB)

            with Stage(kb) as st:
                wo = st.sb([128, KC, D], BF16)
                wf = [st.sb([128, D], F32, True) for _ in range(2)]
                for kc in range(KC):
                    a = wf[kc % 2]
                    kb.load(a, a[:, :], w_out[l, kc * 128:(kc + 1) * 128, :])
                    eng = (V, P_)[kc % 2]; E = kb.E[eng]
                    kb.op(eng, lambda: E.tensor_copy(out=wo[:, kc, :], in_=a[:, :]), [a], [wo])
                G1 = st.sb([128, D], F32, True); A2 = st.sb([128, D], F32, True); B2 = st.sb([128, D], F32, True)
                yt = [st.sb([128, KC, 128], BF16, True) for _ in range(2)]
                xt = [st.sb([128, D], F32, True) for _ in range(2)]
                hh = [st.sb([128, D], F32) for _ in range(2)]
                hT = [st.sb([128, KC, 128], BF16, True) for _ in range(2)]
                junk = st.sb([128, D], F32); rs = [st.sb([128, 1], F32) for _ in range(2)]
                curc = None
                for tt in range(NTT):
                    cond = 1 if tt < NTC else 0
                    if cond != curc:
                        modtile(st, G1, cond, 2); modtile(st, A2, cond, 4); modtile(st, B2, cond, 3); curc = cond
                    i2 = tt % 2
                    y = yt[i2]; x = xt[i2]; h = hh[i2]; r = rs[i2]; ht = hT[i2]
                    ts = slice(tt * 128, (tt + 1) * 128)
                    kb.load(y, y[:, :, :], YT[:, ts].rearrange("(kc p) n -> p kc n", p=128))
                    kb.load(x, x[:, :], XL[ts, :])
                    for nb in range(4):
                        pb = PSB[nb]
                        for kc in range(KC):
                            kb.op(PE, lambda: nc.tensor.matmul(pb[:, :], lhsT=y[:, kc, :], rhs=wo[:, kc, nb * 512:(nb + 1) * 512],
                                                               start=(kc == 0), stop=(kc == KC - 1)), [y, wo], [pb])
                        kb.op(V, lambda: nc.vector.tensor_tensor(out=h[:, nb * 512:(nb + 1) * 512], in0=pb[:, :], in1=G1[:, nb * 512:(nb + 1) * 512], op=ALU.mult),
                              [pb, G1], [h])
                    kb.op(P_, lambda: nc.gpsimd.tensor_tensor(out=x[:, :], in0=x[:, :], in1=h[:, :], op=ALU.add), [x, h], [x])
                    kb.store(x, XL[ts, :], x[:, :])
                    rms_rstd(st, (x[:, :], x), D, (r[:, :], r), (junk[:, :], junk))
                    kb.op(V, lambda: nc.vector.scalar_tensor_tensor(out=h[:, :], in0=x[:, :], scalar=r[:, 0:1], in1=A2[:, :], op0=ALU.mult, op1=ALU.mult),
                          [x, r, A2], [h])
                    kb.op(P_, lambda: nc.gpsimd.tensor_tensor(out=h[:, :], in0=h[:, :], in1=B2[:, :], op=ALU.add), [h, B2], [h])
                    for k4 in range(4):
                        pb = PSB[4 + k4 % 2]
                        for j in range(4):
                            kc = k4 * 4 + j
                            kb.op(PE, lambda: nc.tensor.transpose(pb[:, j * 128:(j + 1) * 128], h[:, kc * 128:(kc + 1) * 128], ident[:, :]), [h, ident], [pb])
                        kb.op(A_, lambda: nc.scalar.copy(out=ht[:, k4 * 4:(k4 + 1) * 4, :], in_=pb[:, :].rearrange("p (a b) -> p a b", a=4)), [pb], [ht])
                    kb.store(ht, HFT[:, ts].rearrange("(kc p) n -> p kc n", p=128), ht[:, :, :])

            with Stage(kb) as st:
                wq = st.sb([128, KC, D], BF16)
                wf = [st.sb([128, D], F32, True) for _ in range(2)]
                for kc in range(KC):
                    a = wf[kc % 2]
                    kb.load(a, a[:, :], peer_wq[l, kc * 128:(kc + 1) * 128, :])
                    eng = (V, P_)[kc % 2]; E = kb.E[eng]
                    kb.op(eng, lambda: E.tensor_copy(out=wq[:, kc, :], in_=a[:, :]), [a], [wq])
                kf = st.sb([128, 2, NK], F32, True)
                for p in range(2): kb.load(kf, kf[:, p, :], peer_keysT[l, p])
                kbb = st.sb([128, 2, NK], BF16)
                kb.op(V, lambda: nc.vector.tensor_copy(out=kbb[:, :, :], in_=kf[:, :, :]), [kf], [kbb])
                hfT = [st.sb([128, KC, 128], BF16, True) for _ in range(2)]
                qTt = [st.sb([128, 16, 128], BF16) for _ in range(2)]
                S = [st.sb([128, 8, 2, NK], F32, True) for _ in range(2)]
                S2 = [st.sb([128, 8, 2, NK], F32) for _ in range(2)]
                top = [st.sb([128, 8, 2, 16], F32) for _ in range(2)]
                cand = [st.sb([128, 8, 16, 16], F32) for _ in range(2)]
                cand2 = [st.sb([128, 8, 256], F32) for _ in range(2)]
                ctop = [st.sb([128, 8, 16], F32) for _ in range(2)]
                ex = [st.sb([128, 8, 16], F32) for _ in range(2)]
                zz = [st.sb([128, 8], F32) for _ in range(2)]
                b2 = [st.sb([128, 8], F32, True) for _ in range(2)]
                for tt in range(NTT):
                    i2 = tt % 2
                    ts = slice(tt * 128, (tt + 1) * 128)
                    hf = hfT[i2]; q = qTt[i2]; s_ = S[i2]; s2_ = S2[i2]; tp = top[i2]; cd = cand[i2]; cd2 = cand2[i2]
                    ct = ctop[i2]; e_ = ex[i2]; z_ = zz[i2]; bb = b2[i2]
                    kb.load(hf, hf[:, :, :], HFT[:, ts].rearrange("(kc p) n -> p kc n", p=128))
                    for c4 in range(4):
                        pb = PSB[c4 % 2]
                        for j in range(4):
                            oc = c4 * 4 + j
                            for kc in range(KC):
                                kb.op(PE, lambda: nc.tensor.matmul(pb[:, j * 128:(j + 1) * 128], lhsT=wq[:, kc, oc * 128:(oc + 1) * 128], rhs=hf[:, kc, :],
                                                                   start=(kc == 0), stop=(kc == KC - 1)), [wq, hf], [pb])
                        kb.op(A_, lambda: nc.scalar.copy(out=q[:, c4 * 4:(c4 + 1) * 4, :], in_=pb[:, :].rearrange("p (a b) -> p a b", a=4)), [pb], [q])
                    for hp in range(16):
                        pb = PSB[2 + (hp * NK) // 512 % 2] if NK == 128 else PSB[2]
                        c0 = (hp * NK) % 512
                        kb.op(PE, lambda: nc.tensor.matmul(pb[:, c0:c0 + NK], lhsT=q[:, hp, :], rhs=kbb[:, hp % 2, :], start=True, stop=True), [q, kbb], [pb])
                        if (c0 + NK == 512) or hp == 15:
                            n_in = (c0 + NK) // NK
                            hp0 = hp + 1 - n_in
                            sv = s_[:, :, :, :].rearrange("p h t k -> p (h t) k")
                            kb.op(A_, lambda: nc.scalar.copy(out=sv[:, hp0:hp + 1, :], in_=pb[:, 0:n_in * NK].rearrange("p (a k) -> p a k", k=NK)), [pb], [s_])
                    for h in range(8):
                        for p in range(2):
                            kb.op(V, lambda: nc.vector.max(out=tp[:, h, p, 0:8], in_=s_[:, h, p, :]), [s_], [tp])
                            kb.op(V, lambda: nc.vector.match_replace(out=s2_[:, h, p, :], in_to_replace=tp[:, h, p, 0:8], in_values=s_[:, h, p, :], imm_value=NEG),
                                  [tp, s_], [s2_])
                            kb.op(V, lambda: nc.vector.max(out=tp[:, h, p, 8:16], in_=s2_[:, h, p, :]), [s2_], [tp])
                    for h in range(8):
                        kb.op(P_, lambda: nc.gpsimd.tensor_tensor(out=cd[:, h, :, :], in0=tp[:, h, 0, :].unsqueeze(2).to_broadcast([128, 16, 16]),
                                                                  in1=tp[:, h, 1, :].unsqueeze(1).to_broadcast([128, 16, 16]), op=ALU.add), [tp], [cd])
                    for h in range(8):
                        cf = cd[:, h, :, :].rearrange("p a b -> p (a b)")
                        kb.op(V, lambda: nc.vector.max(out=ct[:, h, 0:8], in_=cf), [cd], [ct])
                        kb.op(V, lambda: nc.vector.match_replace(out=cd2[:, h, :], in_to_replace=ct[:, h, 0:8], in_values=cf, imm_value=NEG), [ct, cd], [cd2])
                        kb.op(V, lambda: nc.vector.max(out=ct[:, h, 8:16], in_=cd2[:, h, :]), [cd2], [ct])
                    kb.op(V, lambda: nc.vector.tensor_tensor(out=e_[:, :, :], in0=ct[:, :, :], in1=ct[:, :, 0:1].to_broadcast([128, 8, 16]), op=ALU.subtract),
                          [ct], [e_])
                    kb.op(A_, lambda: nc.scalar.activation(out=e_[:, :, :], in_=e_[:, :, :], func=AF.Exp), [e_], [e_])
                    kb.op(V, lambda: nc.vector.tensor_reduce(out=z_[:, :], in_=e_[:, :, :], axis=AX.X, op=ALU.add), [e_], [z_])
                    kb.op(A_, lambda: nc.scalar.activation(out=z_[:, :], in_=z_[:, :], func=AF.Ln), [z_], [z_])
                    kb.op(V, lambda: nc.vector.tensor_tensor(out=bb[:, :], in0=ct[:, :, 15], in1=ct[:, :, 0], op=ALU.subtract), [ct], [bb])
                    kb.op(V, lambda: nc.vector.tensor_tensor(out=bb[:, :], in0=bb[:, :], in1=z_[:, :], op=ALU.subtract), [bb, z_], [bb])
                    kb.op(V, lambda: nc.vector.tensor_tensor(out=s_[:, :, 0, :], in0=s_[:, :, 0, :], in1=ct[:, :, 15:16].to_broadcast([128, 8, NK]), op=ALU.subtract),
                          [s_, ct], [s_])
                    kb.store(s_, SS[ts, :], s_[:, :, :, :].rearrange("p h t k -> p (h t k)"))
                    kb.store(bb, BI2[ts, :], bb[:, :])

            with Stage(kb) as st:
                G = 4
                BI = 4
                EBK = BI * NK
                NBLK = NK // BI
                G2 = st.sb([128, D], F32, True)
                ub = [st.sb([128, KC, EBK], BF16, True) for _ in range(2)]
                vb = [st.sb([128, BI, D], BF16, True) for _ in range(2)]
                acc = st.sb([128, G, D], F32)
                hfg = st.sb([128, G, KC, 128], BF16, True)
                Sg = st.sb([128, G, 8, 2, NK], F32, True)
                b2g = st.sb([128, G, 8], F32, True)
                RD = 4
                zt = [st.sb([128, BI, NK], F32) for _ in range(RD)]
                et = [st.sb([128, BI, NK], F32) for _ in range(RD)]
                Gb = [st.sb([128, EBK], BF16) for _ in range(RD)]
                gA = [st.sb([128, EBK], BF16) for _ in range(G)]
                Wg = [st.sb([128, EBK], F32) for _ in range(2)]
                WgT = [st.sb([128, BI, 128], BF16) for _ in range(2)]
                xt = st.sb([128, D], F32, True)
                pA = PSB[0]; pW = [PSB[1], PSB[2]]; pT = PSB[3]; PO = PSB[4:8]
                tiles = [tt for tt in range(NTT) if not (l == L - 1 and tt < NTC)]
                groups = []
                cur = []
                for tt in tiles:
                    if cur and ((tt < NTC) != (cur[0] < NTC) or len(cur) == G):
                        groups.append(cur); cur = []
                    cur.append(tt)
                if cur: groups.append(cur)
                curc = None
                cnt = [0]
                uTk = kcv(UTB)
                for grp in groups:
                    cond = 1 if grp[0] < NTC else 0
                    if cond != curc:
                        modtile(st, G2, cond, 5); curc = cond
                    for gi_, tt in enumerate(grp):
                        ts = slice(tt * 128, (tt + 1) * 128)
                        kb.load(hfg, hfg[:, gi_, :, :], HFT[:, ts].rearrange("(kc p) n -> p kc n", p=128))
                        kb.load(Sg, Sg[:, gi_, :, :, :].rearrange("p h t k -> p (h t k)"), SS[ts, :])
                        kb.load(b2g, b2g[:, gi_, :], BI2[ts, :])
                    for blk in range(NBLK):
                        u = ub[blk % 2]; v = vb[blk % 2]
                        e0 = blk * EBK
                        for q4 in range(4):
                            kb.load(u, u[:, q4 * 4:(q4 + 1) * 4, :], uTk[:, q4 * 4:(q4 + 1) * 4, e0:e0 + EBK])
                        for c in range(BI):
                            kb.load(v, v[0:NK, c, :], VTB[e0 + c * NK:e0 + (c + 1) * NK, :])
                        for gi_, tt in enumerate(grp):
                            ga = gA[gi_]
                            for kc in range(KC):
                                kb.op(PE, lambda: nc.tensor.matmul(pA[:, 0:EBK], lhsT=hfg[:, gi_, kc, :], rhs=u[:, kc, :], start=(kc == 0), stop=(kc == KC - 1)),
                                      [hfg, u], [pA])
                            kb.op(A_, lambda: nc.scalar.activation(out=ga[:, :], in_=pA[:, 0:EBK], func=AF.Gelu_apprx_tanh), [pA], [ga])

                        def tail_T(gi_):
                            wg = Wg[gi_ % 2]
                            for c in range(BI):
                                kb.op(PE, lambda: nc.tensor.transpose(pT[0:NK, c * 128:(c + 1) * 128], wg[:, c * NK:(c + 1) * NK], ident[:, :]), [wg, ident], [pT])

                        def tail_cast(gi_):
                            wgt = WgT[gi_ % 2]
                            kb.op(A_, lambda: nc.scalar.copy(out=wgt[0:NK, :, :], in_=pT[0:NK, 0:BI * 128].rearrange("p (a b) -> p a b", a=BI)), [pT], [wgt])

                        def tail_mm(gi_, nb):
                            wgt = WgT[gi_ % 2]
                            for c in range(BI):
                                kb.op(PE, lambda: nc.tensor.matmul(PO[nb][:, :], lhsT=wgt[0:NK, c, :], rhs=v[0:NK, c, nb * 512:(nb + 1) * 512],
                                                                   start=(c == 0), stop=(c == BI - 1)), [wgt, v], [PO[nb]])

                        def tail_add(gi_, nb):
                            if blk == 0:
                                kb.op(V, lambda: nc.vector.tensor_copy(out=acc[:, gi_, nb * 512:(nb + 1) * 512], in_=PO[nb][:, :]), [PO[nb]], [acc])
                            else:
                                kb.op(V, lambda: nc.vector.tensor_tensor(out=acc[:, gi_, nb * 512:(nb + 1) * 512], in0=PO[nb][:, :],
                                                                         in1=acc[:, gi_, nb * 512:(nb + 1) * 512], op=ALU.add), [PO[nb], acc], [acc])

                        def heads(gi_, prev):
                            k2 = gi_ % 2
                            pw = pW[k2]; wg = Wg[k2]; ga = gA[gi_]
                            for h in range(8):
                                r4 = cnt[0] % RD; cnt[0] += 1
                                z = zt[r4]; e = et[r4]; gb = Gb[r4]
                                kb.op(P_, lambda: nc.gpsimd.tensor_tensor(out=z[:, :, :],
                                                                          in0=Sg[:, gi_, h, 0, blk * BI:(blk + 1) * BI].unsqueeze(2).to_broadcast([128, BI, NK]),
                                                                          in1=Sg[:, gi_, h, 1, :].unsqueeze(1).to_broadcast([128, BI, NK]), op=ALU.add), [Sg], [z])
                                kb.op(A_, lambda: nc.scalar.activation(out=e[:, :, :], in_=z[:, :, :], func=AF.Exp, bias=b2g[:, gi_, h:h + 1]), [z, b2g], [e])
                                kb.op(V, lambda: nc.vector.scalar_tensor_tensor(out=gb[:, :], in0=z[:, :, :].rearrange("p a b -> p (a b)"), scalar=0.0,
                                                                                in1=e[:, :, :].rearrange("p a b -> p (a b)"), op0=ALU.is_ge, op1=ALU.mult), [z, e], [gb])
                                kb.op(PE, lambda: nc.tensor.matmul(pw[:, 0:EBK], lhsT=identb[:, :], rhs=gb[:, :], start=(h == 0), stop=(h == 7)), [identb, gb], [pw])
                                if prev is not None:
                                    if h == 1: tail_cast(prev)
                                    if 2 <= h <= 5: tail_mm(prev, h - 2)
                                    if 3 <= h <= 6: tail_add(prev, h - 3)
                            kb.op(V, lambda: nc.vector.tensor_tensor(out=wg[:, :], in0=pw[:, 0:EBK], in1=ga[:, :], op=ALU.mult), [pw, ga], [wg])
                            tail_T(gi_)

                        prev = None
                        for gi_, tt in enumerate(grp):
                            heads(gi_, prev)
                            prev = gi_
                        tail_cast(prev)
                        for nb in range(4): tail_mm(prev, nb)
                        for nb in range(4): tail_add(prev, nb)
                    for gi_, tt in enumerate(grp):
                        ts = slice(tt * 128, (tt + 1) * 128)
                        kb.load(xt, xt[:, :], XL[ts, :])
                        kb.op(P_, lambda: nc.gpsimd.tensor_tensor(out=acc[:, gi_, :], in0=acc[:, gi_, :], in1=G2[:, :], op=ALU.mult), [acc, G2], [acc])
                        kb.op(V, lambda: nc.vector.tensor_tensor(out=xt[:, :], in0=xt[:, :], in1=acc[:, gi_, :], op=ALU.add), [xt, acc], [xt])
                        kb.store(xt, XL[ts, :], xt[:, :])

        with Stage(kb) as st:
            nf = st.sb([128, D], F32, True); kb.load(nf, nf[:, :], bcast_rows(norm_final[0:1, :]))
            xt = [st.sb([128, D], F32, True) for _ in range(2)]
            ot = [st.sb([128, D], F32, True) for _ in range(2)]
            junk = st.sb([128, D], F32); rs = [st.sb([128, 1], F32) for _ in range(2)]
            for k in range(NTL):
                tt = NTC + k; i2 = k % 2
                x = xt[i2]; o = ot[i2]; r = rs[i2]
                kb.load(x, x[:, :], XL[tt * 128:(tt + 1) * 128, :])
                rms_rstd(st, (x[:, :], x), D, (r[:, :], r), (junk[:, :], junk))
                kb.op(V, lambda: nc.vector.scalar_tensor_tensor(out=o[:, :], in0=x[:, :], scalar=r[:, 0:1], in1=nf[:, :], op0=ALU.mult, op1=ALU.mult),
                      [x, r, nf], [o])
                kb.store(o, yout[k * 128:(k + 1) * 128, :], o[:, :])
        kb.barrier()
        cst.__exit__(None, None, None)
    return nc


def host_consts(T, TC, grid_w=64):
    ident = np.eye(128, dtype=np.float32)
    s = np.arange(128)
    tri = np.stack([(s[:, None] <= s[None, :]), (s[:, None] >= s[None, :])]).astype(np.float32)
    pml = np.zeros((4, 9, 128, 128), np.float32)
    pmc = np.zeros((4, 3, 128, 128), np.float32)
    rl = s // grid_w; c = s % grid_w
    for g, w in enumerate(WINS):
        for dk in range(-4, 5):
            rin = 2 * dk + rl[:, None]
            rout = rl[None, :]
            okr = (rin >= rout - w // 2) & (rin < rout - w // 2 + w)
            okc = (c[:, None] >= c[None, :] - w // 2) & (c[:, None] < c[None, :] - w // 2 + w)
            pml[g, dk + 4] = (okr & okc)
        for dk in range(-1, 2):
            cin = 128 * dk + s[:, None]; cout = s[None, :]
            pmc[g, dk + 1] = (cin >= cout - w // 2) & (cin < cout - w // 2 + w)
    return ident, tri, pml, pmc


def make_in_maps(cfg, inp, nb):
    T, TC, L, NK = cfg['T'], cfg['TC'], cfg['L'], cfg['NK']
    f = lambda a: np.ascontiguousarray(np.asarray(a, dtype=np.float32))
    ident, tri, pml, pmc = host_consts(T, TC)
    shared = dict(
        ada_w=f(inp['ada_w']), ada_b=f(inp['ada_b']), norm_mix=f(inp['norm_mix']), norm_ffn=f(inp['norm_ffn']),
        norm_final=f(inp['norm_final']).reshape(1, D), w_in=f(inp['w_in']), b_gate=f(inp['b_gate']),
        sgu_norm=f(inp['sgu_norm']), sgu_wT=f(np.swapaxes(np.asarray(inp['sgu_w']), -1, -2)), sgu_b=f(inp['sgu_b']),
        qk_convT=f(np.swapaxes(np.asarray(inp['qk_conv']), -1, -2)), mlstm_norm=f(inp['mlstm_norm']),
        pool_w=f(inp['pool_w']), pool_scaleT=f(np.swapaxes(np.asarray(inp['pool_scale']).reshape(L, 4, 128), -1, -2)),
        w_out=f(inp['w_out']), peer_wq=f(inp['peer_wq']),
        peer_keysT=f(np.swapaxes(np.asarray(inp['peer_keys']), -1, -2)),
        peer_uT=f(np.swapaxes(np.asarray(inp['peer_u']), -1, -2)), peer_v=f(inp['peer_v']),
        ident=ident, tri=tri, pml=pml, pmc=pmc)
    maps = []
    for b in range(nb):
        m = dict(shared)
        m['xin'] = f(np.concatenate([np.asarray(inp['ctx'])[b], np.asarray(inp['x'])[b]], axis=0))
        cond = np.stack([np.asarray(inp['c'])[b], np.asarray(inp['c_ctx'])], axis=1)
        m['condT'] = f(cond.reshape(KC, 128, 2).transpose(1, 0, 2))
        maps.append(m)
    return maps


def kernel(**inputs):
    cfg = dict(T=4096, TC=256, L=4, NK=128)
    nb = 4
    nc = build(cfg)
    maps = make_in_maps(cfg, inputs, nb)
    res = run_bass_kernel_spmd(nc, maps, core_ids=list(range(nb)))
    out = np.stack([np.asarray(res.results[b]['yout'], dtype=np.float32) for b in range(nb)], axis=0)
    return out
```

```python
import numpy as np
from contextlib import ExitStack
import concourse.bass as bass
import concourse.mybir as mybir
from concourse.bass_utils import run_bass_kernel_spmd

F32 = mybir.dt.float32
BF16 = mybir.dt.bfloat16
AF = mybir.ActivationFunctionType
ALU = mybir.AluOpType
AX = mybir.AxisListType

D = 2048
KC = 16
D_IN = 5648
OFF_U, OFF_V, OFF_P, OFF_Q, OFF_O, OFF_K, OFF_VM, OFF_G = 0, 512, 1024, 1536, 2560, 3584, 4608, 5632
EPS = 1e-6
WINS = (2, 4, 8, 16)
NEG = -3.0e38


class SemW:
    def __init__(s, h, dma=False):
        s.h = h; s.total = 0; s.dma = dma


class Trk:
    __slots__ = ('w', 'r')

    def __init__(s):
        s.w = []; s.r = {}


class Buf:
    def __init__(s, t, ds=None):
        s.t = t; s.k = Trk(); s.ds = ds

    def __getitem__(s, idx):
        return s.t[idx]


class KB:
    def __init__(s, nc, es):
        s.nc = nc; s.es = es
        s.E = dict(pe=nc.tensor, dve=nc.vector, act=nc.scalar, pool=nc.gpsimd, sp=nc.sync)
        s.allsems = []
        s.esem = {e: s.newsem('e_' + e) for e in s.E}
        s.pesems = {s.esem['pe']}
        s.seen = {e: {} for e in s.E}
        s.dpool = []; s.dnext = 0; s.dbase = 0
        s.nm = 0

    def newsem(s, name, dma=False):
        h = s.es.enter_context(s.nc.semaphore(name + '_%d' % len(s.allsems)))
        sw = SemW(h, dma); s.allsems.append(sw); return sw

    def dsem(s):
        if s.dnext >= len(s.dpool):
            s.dpool.append(s.newsem('d', True))
        sw = s.dpool[s.dnext]; s.dnext += 1; return sw

    def _wait(s, e, deps):
        best = {}
        for sw, v in deps:
            if sw.dma: v = sw.total
            if v > best.get(sw, 0): best[sw] = v
        for sw, v in best.items():
            if s.seen[e].get(sw, 0) >= v: continue
            if e == 'pe' and sw in s.pesems: continue
            s.E[e].wait_ge(sw.h, v); s.seen[e][sw] = v

    def op(s, e, fn, reads=(), writes=()):
        deps = []
        for r in reads: deps += r.k.w
        for w in writes:
            deps += w.k.w; deps += list(w.k.r.items())
        s._wait(e, deps)
        sw = s.esem[e]
        if sw.total >= 30000:
            sw = s.newsem('e_' + e); s.esem[e] = sw
            if e == 'pe': s.pesems.add(sw)
        inst = fn(); sw.total += 1; inst.then_inc(sw.h, 1)
        for r in reads: r.k.r[sw] = sw.total
        for w in writes:
            w.k.w = [(sw, sw.total)]; w.k.r = {}
        return inst

    def dma(s, out, in_, sem, reads=(), writes=(), q='sp'):
        deps = []
        for r in reads: deps += r.k.w
        for w in writes:
            deps += w.k.w; deps += list(w.k.r.items())
        s._wait(q, deps)
        inst = s.E[q].dma_start(out=out, in_=in_); sem.total += 16; inst.then_inc(sem.h, 16)
        for r in reads: r.k.r[sem] = sem.total
        for w in writes:
            w.k.w = [(sem, sem.total)]; w.k.r = {}

    def load(s, buf, out, in_, q='sp'):
        s.dma(out, in_, buf.ds, writes=[buf], q=q)

    def store(s, buf, out, in_, q='sp'):
        s.dma(out, in_, buf.ds, reads=[buf], q=q)

    def barrier(s):
        deps = [(sw, sw.total) for sw in s.allsems if sw.total > 0]
        for e in s.E: s._wait(e, deps)


class Stage:
    def __init__(s, kb):
        s.kb = kb

    def __enter__(s):
        s.kb.barrier(); s.es = ExitStack(); s.es.__enter__(); s.kb.dnext = s.kb.dbase; return s

    def __exit__(s, *a):
        s.kb.barrier(); return s.es.__exit__(*a)

    def sb(s, shape, dt=F32, dma=False):
        s.kb.nm += 1
        t = s.es.enter_context(s.kb.nc.sbuf_tensor('b%d' % s.kb.nm, list(shape), dt))
        return Buf(t, s.kb.dsem() if dma else None)

    def ps(s, shape=(128, 512), dt=F32):
        s.kb.nm += 1
        t = s.es.enter_context(s.kb.nc.psum_tensor('p%d' % s.kb.nm, list(shape), dt))
        return Buf(t)


def bcast_rows(ap2d_row, n=128):
    a = ap2d_row
    return bass.AP(tensor=a.tensor, offset=a.offset, ap=[[0, n]] + [list(x) for x in list(a.ap)[1:]])


def build(cfg):
    T, TC, L, NK = cfg['T'], cfg['TC'], cfg['L'], cfg['NK']
    NE = NK * NK
    TT = T + TC
    NTC, NTL = TC // 128, T // 128
    NTT = NTC + NTL
    IB = 16 if NK >= 16 else NK
    NIB = NK // IB
    EB = IB * NK
    dbg = cfg.get('dbg', ())
    _i, _t, _pml, _pmc = host_consts(T, TC)
    PML_NZ = [[bool(_pml[g, d].any()) for d in range(9)] for g in range(4)]
    PMC_NZ = [[bool(_pmc[g, d].any()) for d in range(3)] for g in range(4)]
    nc = bass.Bass("TRN2", target_bir_lowering=False)

    def din(name, shape, dt=F32):
        return nc.dram_tensor(name, list(shape), dt, kind="ExternalInput").ap()

    def dsc(name, shape, dt=F32):
        return nc.dram_tensor(name, list(shape), dt, kind="Internal").ap()

    xin = din('xin', [TT, D])
    condT = din('condT', [128, KC, 2])
    ada_w = din('ada_w', [L, D, 6 * D]); ada_b = din('ada_b', [L, 6 * D])
    norm_mix = din('norm_mix', [L, D]); norm_ffn = din('norm_ffn', [L, D]); norm_final = din('norm_final', [1, D])
    w_in = din('w_in', [L, D, D_IN]); b_gate = din('b_gate', [L, 16])
    sgu_norm = din('sgu_norm', [L, 512]); sgu_wT = din('sgu_wT', [L, 4, 128, 128]); sgu_b = din('sgu_b', [L, 4, 128])
    qk_convT = din('qk_convT', [L, 2048, 3]); mlstm_norm = din('mlstm_norm', [L, 1024])
    pool_w = din('pool_w', [L, 4, 128, 128]); pool_scaleT = din('pool_scaleT', [L, 128, 4])
    w_out = din('w_out', [L, D, D]); peer_wq = din('peer_wq', [L, D, D])
    peer_keysT = din('peer_keysT', [L, 2, 128, NK])
    peer_uT = din('peer_uT', [L, D, NE]); peer_v = din('peer_v', [L, NE, D])
    identd = din('ident', [128, 128])
    trid = din('tri', [2, 128, 128])
    pml = din('pml', [4, 9, 128, 128])
    pmc = din('pmc', [4, 3, 128, 128])
    yout = nc.dram_tensor('yout', [T, D], F32, kind="ExternalOutput").ap()
    dbg_out = {}
    for nm, shp in dbg:
        dbg_out[nm] = nc.dram_tensor('dbg_' + nm, list(shp), F32, kind="ExternalOutput").ap()

    XL = dsc('XL', [TT, D])
    MOD = dsc('MOD', [2, 6 * D])
    UT = dsc('UT', [512, TT]); QR = dsc('QR', [1024, TT]); KR = dsc('KR', [1024, TT])
    VS = dsc('VS', [TT, 512]); PS = dsc('PS', [TT, 512]); OS = dsc('OS', [TT, 1024]); VM = dsc('VM', [TT, 1024])
    GS = dsc('GS', [TT, 16])
    YT = dsc('YT', [D, TT], BF16)
    HFT = dsc('HFT', [D, TT], BF16)
    SS = dsc('SS', [TT, 8 * 2 * NK]); BI2 = dsc('BI2', [TT, 8])
    UTB = dsc('UTB', [D, NE], BF16); VTB = dsc('VTB', [NE, D], BF16)

    es = ExitStack()
    with es:
        kb = KB(nc, es)
        V, A_, P_, PE = 'dve', 'act', 'pool', 'pe'

        def kcv(ap2d):
            return ap2d.rearrange("(kc p) c -> p kc c", p=128)

        cst = Stage(kb); cst.__enter__()
        ident = cst.sb([128, 128], F32, True); kb.load(ident, ident[:, :], identd[:, :])
        identb = cst.sb([128, 128], BF16)
        kb.op(V, lambda: nc.vector.tensor_copy(out=identb[:, :], in_=ident[:, :]), [ident], [identb])
        tri = cst.sb([128, 2, 128], F32, True)
        for i in range(2): kb.load(tri, tri[:, i, :], trid[i])
        trib = cst.sb([128, 2, 128], BF16)
        kb.op(V, lambda: nc.vector.tensor_copy(out=trib[:, :, :], in_=tri[:, :, :]), [tri], [trib])
        ones = cst.sb([128, 128], F32)
        kb.op(V, lambda: nc.vector.memset(ones[:, :], 1.0), [], [ones])
        sct = cst.sb([128, KC, 2], F32, True); kb.load(sct, sct[:, :, :], condT[:, :, :])
        kb.op(A_, lambda: nc.scalar.activation(out=sct[:, :, :], in_=sct[:, :, :], func=AF.Silu), [sct], [sct])
        PSB = [cst.ps() for _ in range(8)]
        kb.dbase = kb.dnext
        with Stage(kb) as st0:
            cp = [st0.sb([128, D], F32, True) for _ in range(2)]
            for tt in range(NTT):
                b = cp[tt % 2]
                kb.load(b, b[:, :], xin[tt * 128:(tt + 1) * 128, :])
                kb.store(b, XL[tt * 128:(tt + 1) * 128, :], b[:, :])

        def rms_rstd(st, xt, width, outcol, junk):
            kb.op(V, lambda: nc.vector.scalar_tensor_tensor(out=junk[0], in0=xt[0], scalar=1.0, in1=xt[0],
                                                            op0=ALU.mult, op1=ALU.mult, accum_out=outcol[0]),
                  [xt[1]], [junk[1], outcol[1]])
            kb.op(V, lambda: nc.vector.tensor_scalar(out=outcol[0], in0=outcol[0], scalar1=1.0 / width, scalar2=EPS,
                                                     op0=ALU.mult, op1=ALU.add), [outcol[1]], [outcol[1]])
            kb.op(A_, lambda: nc.scalar.activation(out=outcol[0], in_=outcol[0], func=AF.Sqrt), [outcol[1]], [outcol[1]])
            kb.op(V, lambda: nc.vector.reciprocal(out=outcol[0], in_=outcol[0]), [outcol[1]], [outcol[1]])

        def conv_weights(st, src2d, nk, ncols, dstdram):
            CW = min(ncols, 2048)
            f = [st.sb([128, CW], F32, True) for _ in range(4)]
            g = [st.sb([128, CW], BF16, True) for _ in range(4)]
            i = 0
            for k in range(nk):
                for c0 in range(0, ncols, CW):
                    a, b = f[i % 4], g[i % 4]
                    kb.load(a, a[:, :], src2d[k * 128:(k + 1) * 128, c0:c0 + CW])
                    eng = (V, P_)[i % 2]
                    E = kb.E[eng]
                    kb.op(eng, lambda: E.tensor_copy(out=b[:, :], in_=a[:, :]), [a], [b])
                    kb.store(b, dstdram[k * 128:(k + 1) * 128, c0:c0 + CW], b[:, :])
                    i += 1

        for l in range(L):
            with Stage(kb) as st:
                wb = [st.sb([128, KC, 512], F32, True) for _ in range(2)]
                adab = st.sb([2, 6 * D], F32, True)
                for r in range(2): kb.load(adab, adab[r:r + 1, :], ada_b[l:l + 1, :])
                nrm = st.sb([2, 2, D], F32, True)
                for r in range(2):
                    kb.load(nrm, nrm[r:r + 1, 0, :], norm_mix[l:l + 1, :])
                    kb.load(nrm, nrm[r:r + 1, 1, :], norm_ffn[l:l + 1, :])
                mo = [st.sb([2, 512], F32, True) for _ in range(2)]
                for cb in range(24):
                    w = wb[cb % 2]
                    src = kcv(ada_w[l])
                    for q4 in range(4):
                        kb.load(w, w[:, q4 * 4:(q4 + 1) * 4, :], src[:, q4 * 4:(q4 + 1) * 4, cb * 512:(cb + 1) * 512])
                    pb = PSB[cb % 2]
                    for kc in range(KC):
                        kb.op(PE, lambda: nc.tensor.matmul(pb[0:2, :], lhsT=sct[:, kc, :], rhs=w[:, kc, :],
                                                           start=(kc == 0), stop=(kc == KC - 1)), [sct, w], [pb])
                    m = mo[cb % 2]
                    kb.op(V, lambda: nc.vector.tensor_tensor(out=m[:, :], in0=pb[0:2, :], in1=adab[:, cb * 512:(cb + 1) * 512],
                                                             op=ALU.add), [pb, adab], [m])
                    part = cb // 4
                    if part in (1, 4):
                        j = 0 if part == 1 else 1
                        c0 = (cb % 4) * 512
                        kb.op(V, lambda: nc.vector.scalar_tensor_tensor(out=m[:, :], in0=m[:, :], scalar=1.0,
                                                                        in1=nrm[:, j, c0:c0 + 512], op0=ALU.add, op1=ALU.mult),
                              [m, nrm], [m])
                    kb.store(m, MOD[:, cb * 512:(cb + 1) * 512], m[:, :])

            def modtile(st, buf, cond, part):
                kb.load(buf, buf[:, :], bcast_rows(MOD[cond:cond + 1, part * D:(part + 1) * D]))

            with Stage(kb) as st:
                A1 = st.sb([128, D], F32, True); B1 = st.sb([128, D], F32, True)
                bg = st.sb([128, 16], F32, True)
                kb.load(bg, bg[:, :], bcast_rows(b_gate[l:l + 1, :]))
                xt = [st.sb([128, D], F32, True) for _ in range(2)]
                hh = [st.sb([128, D], F32) for _ in range(2)]
                junk = st.sb([128, D], F32)
                rs = [st.sb([128, 1], F32) for _ in range(2)]
                hT = st.sb([128, KC, 1024], BF16)
                wf = [st.sb([128, KC, 512], F32, True) for _ in range(1)]
                wbf = [st.sb([128, KC, 512], BF16) for _ in range(2)]
                ev = [st.sb([128, 512], F32, True) for _ in range(3)]
                evi = [0]
                wi = [0]
                pbi = [0]
                w2 = kcv(w_in[l])
                sts = []
                t0 = 0
                while t0 < NTC: n = min(8, NTC - t0); sts.append((1, t0, n)); t0 += n
                while t0 < NTT: n = min(8, NTT - t0); sts.append((0, t0, n)); t0 += n
                curc = None
                for (cond, tb, n) in sts:
                    if cond != curc:
                        modtile(st, A1, cond, 1); modtile(st, B1, cond, 0); curc = cond
                    NS = n * 128
                    for ti in range(n):
                        tt = tb + ti
                        x = xt[tt % 2]; h = hh[tt % 2]; r = rs[tt % 2]
                        kb.load(x, x[:, :], XL[tt * 128:(tt + 1) * 128, :])
                        rms_rstd(st, (x[:, :], x), D, (r[:, :], r), (junk[:, :], junk))
                        kb.op(V, lambda: nc.vector.scalar_tensor_tensor(out=h[:, :], in0=x[:, :], scalar=r[:, 0:1], in1=A1[:, :],
                                                                        op0=ALU.mult, op1=ALU.mult), [x, r, A1], [h])
                        kb.op(P_, lambda: nc.gpsimd.tensor_tensor(out=h[:, :], in0=h[:, :], in1=B1[:, :], op=ALU.add), [h, B1], [h])
                        for k4 in range(4):
                            pb = PSB[k4 % 2]
                            for j in range(4):
                                kc = k4 * 4 + j
                                kb.op(PE, lambda: nc.tensor.transpose(pb[:, j * 128:(j + 1) * 128], h[:, kc * 128:(kc + 1) * 128], ident[:, :]),
                                      [h, ident], [pb])
                            kb.op(A_, lambda: nc.scalar.copy(out=hT[:, k4 * 4:(k4 + 1) * 4, ti * 128:(ti + 1) * 128],
                                                             in_=pb[:, :].rearrange("p (a b) -> p a b", a=4)), [pb], [hT])

                    def wload(c0, ncols):
                        i = wi[0]; wi[0] += 1
                        a, b = wf[0], wbf[i % 2]
                        for q4 in range(4):
                            kb.load(a, a[:, q4 * 4:(q4 + 1) * 4, 0:ncols], w2[:, q4 * 4:(q4 + 1) * 4, c0:c0 + ncols])
                        eng = (V, P_)[i % 2]; E = kb.E[eng]
                        kb.op(eng, lambda: E.tensor_copy(out=b[:, :, 0:ncols], in_=a[:, :, 0:ncols]), [a], [b])
                        return b

                    def evbuf():
                        e = ev[evi[0] % 3]; evi[0] += 1; return e

                    for (off, nb, dst, fn) in ((OFF_U, 4, UT, AF.Gelu_apprx_tanh), (OFF_Q, 8, QR, None), (OFF_K, 8, KR, None)):
                        for g4 in range(nb // 4):
                            wbb = wload(off + g4 * 512, 512)
                            for j in range(4):
                              for n0 in range(0, NS, 512):
                                nw = min(512, NS - n0)
                                blk = g4 * 4 + j
                                pbi[0] += 1
                                pb = PSB[2 + (pbi[0] % 2)]
                                for kc in range(KC):
                                    kb.op(PE, lambda: nc.tensor.matmul(pb[:, 0:nw], lhsT=wbb[:, kc, j * 128:(j + 1) * 128], rhs=hT[:, kc, n0:n0 + nw],
                                                                       start=(kc == 0), stop=(kc == KC - 1)), [wbb, hT], [pb])
                                e = evbuf()
                                if fn is None:
                                    kb.op(A_, lambda: nc.scalar.copy(out=e[:, 0:nw], in_=pb[:, 0:nw]), [pb], [e])
                                else:
                                    kb.op(A_, lambda: nc.scalar.activation(out=e[:, 0:nw], in_=pb[:, 0:nw], func=fn), [pb], [e])
                                kb.store(e, dst[blk * 128:(blk + 1) * 128, tb * 128 + n0:tb * 128 + n0 + nw], e[:, 0:nw])
                    for (off, ncols, dst, dc0, fn) in ((OFF_V, 512, VS, 0, AF.Gelu_apprx_tanh), (OFF_P, 512, PS, 0, None),
                                                       (OFF_O, 512, OS, 0, AF.Sigmoid), (OFF_O + 512, 512, OS, 512, AF.Sigmoid),
                                                       (OFF_VM, 512, VM, 0, None), (OFF_VM + 512, 512, VM, 512, None),
                                                       (OFF_G, 16, GS, 0, 'gate')):
                        wbb = wload(off, ncols)
                        for ti in range(n):
                            tt = tb + ti
                            pb = PSB[4 + (ti % 2)]
                            for kc in range(KC):
                                kb.op(PE, lambda: nc.tensor.matmul(pb[:, 0:ncols], lhsT=hT[:, kc, ti * 128:(ti + 1) * 128], rhs=wbb[:, kc, 0:ncols],
                                                                   start=(kc == 0), stop=(kc == KC - 1)), [wbb, hT], [pb])
                            e = evbuf()
                            if fn is None:
                                kb.op(A_, lambda: nc.scalar.copy(out=e[:, 0:ncols], in_=pb[:, 0:ncols]), [pb], [e])
                            elif fn == 'gate':
                                kb.op(V, lambda: nc.vector.tensor_tensor(out=e[:, 0:ncols], in0=pb[:, 0:ncols], in1=bg[:, :], op=ALU.add), [pb, bg], [e])
                            else:
                                kb.op(A_, lambda: nc.scalar.activation(out=e[:, 0:ncols], in_=pb[:, 0:ncols], func=fn), [pb], [e])
                            kb.store(e, dst[tt * 128:(tt + 1) * 128, dc0:dc0 + ncols], e[:, 0:ncols])

            if 'UT' in dbg_out and l == 0:
                with Stage(kb) as st:
                    for (nm, src) in (('UT', UT), ('QR', QR), ('VS', VS), ('GS', GS), ('OS', OS)):
                        if nm in dbg_out:
                            R, C = src.shape
                            for r0 in range(0, R, 128):
                                rr = min(128, R - r0)
                                b = st.sb([128, C], F32, True)
                                kb.load(b, b[0:rr, :], src[r0:r0 + rr, :]); kb.store(b, dbg_out[nm][r0:r0 + rr, :], b[0:rr, :])

            with Stage(kb) as st:
                sgn = st.sb([128, 512], F32, True); kb.load(sgn, sgn[:, :], bcast_rows(sgu_norm[l:l + 1, :]))
                wsf = st.sb([128, 4, 128], F32, True)
                for h in range(4): kb.load(wsf, wsf[:, h, :], sgu_wT[l, h])
                wsb = st.sb([128, 4, 128], BF16)
                kb.op(V, lambda: nc.vector.tensor_copy(out=wsb[:, :, :], in_=wsf[:, :, :]), [wsf], [wsb])
                sbr = st.sb([1, 4, 128], F32, True); kb.load(sbr, sbr[0:1, :, :], sgu_b[l:l + 1, :, :])
                sbb = st.sb([1, 4, 128], BF16)
                kb.op(V, lambda: nc.vector.tensor_copy(out=sbb[:, :, :], in_=sbr[:, :, :]), [sbr], [sbb])
                onesb = st.sb([1, 128], BF16)
                kb.op(V, lambda: nc.vector.memset(onesb[:, :], 1.0), [], [onesb])
                pwf = st.sb([128, 4, 128], F32, True)
                for g in range(4): kb.load(pwf, pwf[:, g, :], pool_w[l, g])
                pwb = st.sb([128, 4, 128], BF16)
                kb.op(V, lambda: nc.vector.tensor_copy(out=pwb[:, :, :], in_=pwf[:, :, :]), [pwf], [pwb])
                psc = st.sb([128, 4], F32, True); kb.load(psc, psc[:, :], pool_scaleT[l])
                pmf = st.sb([128, 9, 128], F32, True)
                pmlb = st.sb([128, 4, 9, 128], BF16); pmcb = st.sb([128, 4, 3, 128], BF16)
                for g in range(4):
                    for dk in range(9): kb.load(pmf, pmf[:, dk, :], pml[g, dk])
                    kb.op(V, lambda: nc.vector.tensor_copy(out=pmlb[:, g, :, :], in_=pmf[:, :, :]), [pmf], [pmlb])
                for g in range(4):
                    for dk in range(3): kb.load(pmf, pmf[:, dk, :], pmc[g, dk])
                    kb.op(V, lambda: nc.vector.tensor_copy(out=pmcb[:, g, :, :], in_=pmf[:, 0:3, :]), [pmf], [pmcb])
                XPf = st.sb([128, NTT, 512], F32, True)
                XP = st.sb([128, NTT, 4, 129], BF16)
                kb.op(P_, lambda: nc.gpsimd.memset(XP[:, :, :, :], 1.0), [], [XP])
                for tt in range(NTT):
                    kb.load(XPf, XPf[:, tt, :], PS[tt * 128:(tt + 1) * 128, :])
                kb.op(V, lambda: nc.vector.tensor_copy(out=XP[:, :, :, 0:128], in_=XPf[:, :, :].rearrange("p t (g c) -> p t g c", g=4)),
                      [XPf], [XP])
                vt = [st.sb([128, 512], F32, True) for _ in range(2)]
                vn = [st.sb([128, 512], BF16) for _ in range(2)]
                ut = [st.sb([128, 4, 128], F32, True) for _ in range(2)]
                ya = [st.sb([128, 4, 128], BF16, True) for _ in range(2)]
                yc = [st.sb([128, 4, 128], BF16, True) for _ in range(2)]
                junk = st.sb([128, 512], F32); rs = [st.sb([128, 1], F32) for _ in range(2)]
                mean = [st.sb([128, 512], F32) for _ in range(2)]
                rc = [st.sb([128, 4], F32) for _ in range(2)]
                dT = [st.sb([128, 4, 128], BF16) for _ in range(2)]
                for tt in range(NTT):
                    i2 = tt % 2
                    isctx = tt < NTC
                    v = vt[i2]; u = ut[i2]; r = rs[i2]; vb = vn[i2]
                    kb.load(v, v[:, :], VS[tt * 128:(tt + 1) * 128, :])
                    kb.load(u, u[:, :, :], UT[:, tt * 128:(tt + 1) * 128].rearrange("(h p) n -> p h n", p=128))
                    rms_rstd(st, (v[:, :], v), 512, (r[:, :], r), (junk[:, :], junk))
                    kb.op(V, lambda: nc.vector.scalar_tensor_tensor(out=vb[:, :], in0=v[:, :], scalar=r[:, 0:1], in1=sgn[:, :],
                                                                    op0=ALU.mult, op1=ALU.mult), [v, r, sgn], [vb])
                    pb = PSB[i2]
                    for h in range(4):
                        kb.op(PE, lambda: nc.tensor.matmul(pb[:, h * 128:(h + 1) * 128], lhsT=vb[:, h * 128:(h + 1) * 128], rhs=wsb[:, h, :],
                                                           start=True, stop=False), [vb, wsb], [pb])
                        kb.op(PE, lambda: nc.tensor.matmul(pb[:, h * 128:(h + 1) * 128], lhsT=onesb[0:1, :], rhs=sbb[0:1, h, :],
                                                           start=False, stop=True), [onesb, sbb], [pb])
                    y = ya[i2]
                    kb.op(V, lambda: nc.vector.tensor_tensor(out=y[:, :, :], in0=pb[:, :].rearrange("p (h n) -> p h n", h=4), in1=u[:, :, :],
                                                             op=ALU.mult), [pb, u], [y])
                    kb.store(y, YT[0:512, tt * 128:(tt + 1) * 128].rearrange("(h p) n -> p h n", p=128), y[:, :, :])
                    if isctx:
                        k, nt, tbase, pmb, dks, dko = tt, NTC, 0, pmcb, (-1, 0, 1), 1
                    else:
                        k, nt, tbase, pmb, dks, dko = tt - NTC, NTL, NTC, pmlb, tuple(range(-4, 5)), 4
                    mn = mean[i2]; rcc = rc[i2]
                    pb2 = PSB[2 + i2]; pb3 = PSB[4 + i2]
                    for g in range(4):
                        w = WINS[g]
                        nzm = PMC_NZ if isctx else PML_NZ
                        use = [dk for dk in dks if 0 <= k + dk < nt and nzm[g][dk + dko]]
                        pbg = pb2 if g < 2 else pb3
                        c0 = (g % 2) * 129
                        for j, dk in enumerate(use):
                            kb.op(PE, lambda: nc.tensor.matmul(pbg[:, c0:c0 + 129], lhsT=pmb[:, g, dk + dko, :], rhs=XP[:, tbase + k + dk, g, :],
                                                               start=(j == 0), stop=(j == len(use) - 1)), [pmb, XP], [pbg])
                    for g in range(4):
                        pbg = pb2 if g < 2 else pb3
                        c0 = (g % 2) * 129
                        kb.op(V, lambda: nc.vector.reciprocal(out=rcc[:, g:g + 1], in_=pbg[:, c0 + 128:c0 + 129]), [pbg], [rcc])
                        kb.op(V, lambda: nc.vector.scalar_tensor_tensor(out=mn[:, g * 128:(g + 1) * 128], in0=pbg[:, c0:c0 + 128], scalar=rcc[:, g:g + 1],
                                                                        in1=XPf[:, tt, g * 128:(g + 1) * 128], op0=ALU.mult, op1=ALU.subtract),
                              [pbg, rcc, XPf], [mn])
                    pb4 = PSB[6 + i2]
                    for g in range(4):
                        kb.op(PE, lambda: nc.tensor.transpose(pb4[:, g * 128:(g + 1) * 128], mn[:, g * 128:(g + 1) * 128], ident[:, :]), [mn, ident], [pb4])
                    d = dT[i2]
                    kb.op(A_, lambda: nc.scalar.copy(out=d[:, :, :], in_=pb4[:, :].rearrange("p (g n) -> p g n", g=4)), [pb4], [d])
                    for g in range(4):
                        kb.op(PE, lambda: nc.tensor.matmul(pb4[:, g * 128:(g + 1) * 128], lhsT=pwb[:, g, :], rhs=d[:, g, :], start=True, stop=True),
                              [pwb, d], [pb4])
                    y2 = yc[i2]
                    for g in range(4):
                        kb.op(V, lambda: nc.vector.tensor_scalar(out=y2[:, g, :], in0=pb4[:, g * 128:(g + 1) * 128], scalar1=psc[:, g:g + 1], scalar2=None,
                                                                 op0=ALU.mult), [pb4, psc], [y2])
                    kb.store(y2, YT[1536:2048, tt * 128:(tt + 1) * 128].rearrange("(g p) n -> p g n", p=128), y2[:, :, :])

            with Stage(kb) as st:
                EA = st.sb([128, NTT, 8], F32); EBc = st.sb([128, NTT, 8], F32); EBL = st.sb([128, NTT, 8], F32)
                gt = [st.sb([128, 16], F32, True) for _ in range(2)]
                lf = [st.sb([128, 8], F32) for _ in range(2)]
                t1 = [st.sb([128, 8], F32) for _ in range(2)]
                gi = [st.sb([128, 8], F32) for _ in range(2)]
                for tt in range(NTT):
                    i2 = tt % 2
                    g = gt[i2]; f = lf[i2]; a = t1[i2]; gii = gi[i2]
                    kb.load(g, g[:, :], GS[tt * 128:(tt + 1) * 128, :])
                    gv = g[:, :].rearrange("p (d g h) -> p d g h", d=2, g=2)
                    fv = f[:, :].rearrange("p (d h) -> p d h", d=2)
                    av = a[:, :].rearrange("p (d h) -> p d h", d=2)
                    kb.op(A_, lambda: nc.scalar.activation(out=av, in_=gv[:, :, 1, :], func=AF.Abs), [g], [a])
                    kb.op(A_, lambda: nc.scalar.activation(out=a[:, :], in_=a[:, :], func=AF.Exp, scale=-1.0), [a], [a])
                    kb.op(A_, lambda: nc.scalar.activation(out=a[:, :], in_=a[:, :], func=AF.Ln, bias=1.0), [a], [a])
                    kb.op(V, lambda: nc.vector.tensor_scalar(out=fv, in0=gv[:, :, 1, :], scalar1=0.0, scalar2=None, op0=ALU.min), [g], [f])
                    kb.op(V, lambda: nc.vector.tensor_tensor(out=f[:, :], in0=f[:, :], in1=a[:, :], op=ALU.subtract), [f, a], [f])
                    kb.op(V, lambda: nc.vector.tensor_copy(out=gii[:, :].rearrange("p (d h) -> p d h", d=2), in_=gv[:, :, 0, :]), [g], [gii])
                    pb = PSB[i2]
                    for dr in range(2):
                        kb.op(PE, lambda: nc.tensor.matmul(pb[:, dr * 4:dr * 4 + 4], lhsT=tri[:, dr, :], rhs=f[:, dr * 4:dr * 4 + 4], start=True, stop=True),
                              [tri, f], [pb])
                    kb.op(PE, lambda: nc.tensor.matmul(pb[:, 8:16], lhsT=ones[:, :], rhs=f[:, :], start=True, stop=True), [ones, f], [pb])
                    kb.op(A_, lambda: nc.scalar.activation(out=EBc[:, tt, :], in_=pb[:, 0:8], func=AF.Exp), [pb], [EBc])
                    kb.op(A_, lambda: nc.scalar.activation(out=EBL[:, tt, :], in_=pb[:, 8:16], func=AF.Exp), [pb], [EBL])
                    kb.op(V, lambda: nc.vector.tensor_tensor(out=gii[:, :], in0=gii[:, :], in1=pb[:, 0:8], op=ALU.subtract), [gii, pb], [gii])
                    kb.op(A_, lambda: nc.scalar.activation(out=EA[:, tt, :], in_=gii[:, :], func=AF.Exp), [gii], [EA])
                gnb = st.sb([128, 1024], F32, True); kb.load(gnb, gnb[:, :], bcast_rows(mlstm_norm[l:l + 1, :]))
                cw = st.sb([128, 16, 3], F32, True)
                kb.load(cw, cw[:, :, :], qk_convT[l].rearrange("(c p) k -> p c k", p=128))
                raw = [st.sb([128, TT + 2], F32, True) for _ in range(1)]
                cv = [st.sb([128, TT], F32) for _ in range(1)]
                qT = st.sb([128, 2, TT], BF16); kT = st.sb([128, 2, TT], BF16)
                kS = st.sb([128, NTT, 256], BF16)
                vf = [st.sb([128, 256], F32, True) for _ in range(2)]
                vw = [st.sb([128, 257], BF16) for _ in range(2)]
                stm = [st.sb([128, 128], BF16) for _ in range(2)]
                C32 = st.sb([128, 2, 257], F32); Cb = st.sb([128, 2, 257], BF16)
                hs = [st.sb([128, 257], F32) for _ in range(2)]
                dn = [st.sb([128, 1], F32) for _ in range(2)]
                HF = st.sb([128, NTT, 256], F32)
                ot = [st.sb([128, 256], F32, True) for _ in range(2)]
                yb = [st.sb([128, 256], F32) for _ in range(2)]
                ybT = [st.sb([128, 2, 128], BF16, True) for _ in range(2)]
                junk = st.sb([128, 256], F32); rs = [st.sb([128, 1], F32) for _ in range(2)]
                for hd in range(4):
                    for which, (src, dstT, scl) in enumerate(((QR, qT, 1.0), (KR, kT, 1.0 / 16.0))):
                        for c in range(2):
                            ch = hd * 2 + c
                            rw = raw[0]; co = cv[0]
                            wcol = which * 8 + ch
                            kb.op(P_, lambda: nc.gpsimd.memset(rw[:, :], 0.0), [], [rw])
                            kb.load(rw, rw[:, 1:TT + 1], src[ch * 128:(ch + 1) * 128, :])
                            for (s0, sl) in ((0, TC), (TC, T)):
                                kb.op(V, lambda: nc.vector.tensor_scalar(out=co[:, s0:s0 + sl], in0=rw[:, 1 + s0:1 + s0 + sl], scalar1=cw[:, wcol, 1:2],
                                                                         scalar2=None, op0=ALU.mult), [rw, cw], [co])
                                kb.op(V, lambda: nc.vector.scalar_tensor_tensor(out=co[:, s0 + 1:s0 + sl], in0=rw[:, 1 + s0:s0 + sl], scalar=cw[:, wcol, 0:1],
                                                                                in1=co[:, s0 + 1:s0 + sl], op0=ALU.mult, op1=ALU.add), [rw, cw, co], [co])
                                kb.op(V, lambda: nc.vector.scalar_tensor_tensor(out=co[:, s0:s0 + sl - 1], in0=rw[:, 2 + s0:1 + s0 + sl], scalar=cw[:, wcol, 2:3],
                                                                                in1=co[:, s0:s0 + sl - 1], op0=ALU.mult, op1=ALU.add), [rw, cw, co], [co])
                            kb.op(A_, lambda: nc.scalar.activation(out=co[:, :], in_=co[:, :], func=AF.Silu), [co], [co])
                            kb.op(V, lambda: nc.vector.tensor_scalar(out=dstT[:, c, :], in0=co[:, :], scalar1=scl, scalar2=None, op0=ALU.mult), [co], [dstT])
                            if which == 1:
                                for t4 in range(0, NTT, 4):
                                    nn = min(4, NTT - t4)
                                    pb = PSB[(t4 // 4) % 2]
                                    for j in range(nn):
                                        kb.op(PE, lambda: nc.tensor.transpose(pb[:, j * 128:(j + 1) * 128], co[:, (t4 + j) * 128:(t4 + j + 1) * 128], ident[:, :]),
                                              [co, ident], [pb])
                                    kb.op(V, lambda: nc.vector.tensor_scalar(out=kS[:, t4:t4 + nn, c * 128:(c + 1) * 128],
                                                                             in0=pb[:, 0:nn * 128].rearrange("p (a b) -> p a b", a=nn),
                                                                             scalar1=scl, scalar2=None, op0=ALU.mult), [pb], [kS])
                    for dr in range(2):
                        col = dr * 4 + hd
                        kb.op(V, lambda: nc.vector.memset(C32[:, :, :], 0.0), [], [C32])
                        kb.op(V, lambda: nc.vector.memset(Cb[:, :, :], 0.0), [], [Cb])
                        order = list(range(NTT)) if dr == 0 else (list(range(NTC - 1, -1, -1)) + list(range(NTT - 1, NTC - 1, -1)))
                        for ci, tt in enumerate(order):
                            i2 = ci % 2
                            v = vf[i2]; vv = vw[i2]; sm = stm[i2]; h_ = hs[i2]; d_ = dn[i2]
                            ts = slice(tt * 128, (tt + 1) * 128)
                            kb.load(v, v[:, :], VM[ts, hd * 256:(hd + 1) * 256])
                            kb.op(P_, lambda: nc.gpsimd.tensor_scalar(out=vv[:, 0:256], in0=v[:, :], scalar1=EA[:, tt, col:col + 1], scalar2=None, op0=ALU.mult),
                                  [v, EA], [vv])
                            kb.op(P_, lambda: nc.gpsimd.tensor_copy(out=vv[:, 256:257], in_=EA[:, tt, col:col + 1]), [EA], [vv])
                            pS = PSB[2 + i2]
                            for c in range(2):
                                kb.op(PE, lambda: nc.tensor.matmul(pS[:, 0:128], lhsT=kT[:, c, ts], rhs=qT[:, c, ts], start=(c == 0), stop=(c == 1)),
                                      [kT, qT], [pS])
                            kb.op(V, lambda: nc.vector.tensor_tensor(out=sm[:, :], in0=pS[:, 0:128], in1=tri[:, dr, :], op=ALU.mult), [pS, tri], [sm])
                            pH = PSB[4 + i2]
                            kb.op(PE, lambda: nc.tensor.matmul(pH[:, 0:257], lhsT=sm[:, :], rhs=vv[:, :], start=True, stop=False), [sm, vv], [pH])
                            for c in range(2):
                                kb.op(PE, lambda: nc.tensor.matmul(pH[:, 0:257], lhsT=qT[:, c, ts], rhs=Cb[:, c, :], start=False, stop=(c == 1)),
                                      [qT, Cb], [pH])
                            kb.op(A_, lambda: nc.scalar.activation(out=h_[:, :], in_=pH[:, 0:257], func=AF.Copy, scale=EBc[:, tt, col:col + 1]), [pH, EBc], [h_])
                            for c in range(2):
                                pC = PSB[6 + c]
                                kb.op(PE, lambda: nc.tensor.matmul(pC[:, 0:257], lhsT=kS[:, tt, c * 128:(c + 1) * 128], rhs=vv[:, :], start=True, stop=True),
                                      [kS, vv], [pC])
                                kb.op(V, lambda: nc.vector.tensor_tensor(out=C32[:, c, :], in0=pC[:, 0:257], in1=C32[:, c, :], op=ALU.add), [pC, C32], [C32])
                            kb.op(V, lambda: nc.vector.tensor_scalar(out=C32[:, :, :], in0=C32[:, :, :], scalar1=EBL[:, tt, col:col + 1], scalar2=None, op0=ALU.mult),
                                  [C32, EBL], [C32])
                            kb.op(P_, lambda: nc.gpsimd.tensor_copy(out=Cb[:, :, :], in_=C32[:, :, :]), [C32], [Cb])
                            kb.op(A_, lambda: nc.scalar.activation(out=d_[:, :], in_=h_[:, 256:257], func=AF.Abs), [h_], [d_])
                            kb.op(V, lambda: nc.vector.tensor_scalar(out=d_[:, :], in0=d_[:, :], scalar1=1.0, scalar2=None, op0=ALU.max), [d_], [d_])
                            kb.op(V, lambda: nc.vector.reciprocal(out=d_[:, :], in_=d_[:, :]), [d_], [d_])
                            if dr == 0:
                                kb.op(V, lambda: nc.vector.tensor_scalar(out=HF[:, tt, :], in0=h_[:, 0:256], scalar1=d_[:, 0:1], scalar2=None, op0=ALU.mult),
                                      [h_, d_], [HF])
                            else:
                                y_ = yb[i2]; o_ = ot[i2]; r = rs[i2]; yt_ = ybT[i2]
                                kb.op(V, lambda: nc.vector.scalar_tensor_tensor(out=y_[:, :], in0=h_[:, 0:256], scalar=d_[:, 0:1], in1=HF[:, tt, :],
                                                                                op0=ALU.mult, op1=ALU.add), [h_, d_, HF], [y_])
                                kb.load(o_, o_[:, :], OS[ts, hd * 256:(hd + 1) * 256])
                                rms_rstd(st, (y_[:, :], y_), 256, (r[:, :], r), (junk[:, :], junk))
                                kb.op(V, lambda: nc.vector.scalar_tensor_tensor(out=y_[:, :], in0=y_[:, :], scalar=r[:, 0:1], in1=gnb[:, hd * 256:(hd + 1) * 256],
                                                                                op0=ALU.mult, op1=ALU.mult), [y_, r, gnb], [y_])
                                kb.op(P_, lambda: nc.gpsimd.tensor_tensor(out=y_[:, :], in0=y_[:, :], in1=o_[:, :], op=ALU.mult), [y_, o_], [y_])
                                pT = PSB[i2]
                                for c in range(2):
                                    kb.op(PE, lambda: nc.tensor.transpose(pT[:, c * 128:(c + 1) * 128], y_[:, c * 128:(c + 1) * 128], ident[:, :]), [y_, ident], [pT])
                                kb.op(A_, lambda: nc.scalar.copy(out=yt_[:, :, :], in_=pT[:, 0:256].rearrange("p (c n) -> p c n", c=2)), [pT], [yt_])
                                kb.store(yt_, YT[512 + hd * 256:512 + (hd + 1) * 256, ts].rearrange("(c p) n -> p c n", p=128), yt_[:, :, :])

            if 'YT' in dbg_out and l == 0:
                with Stage(kb) as st:
                    for r0 in range(0, D, 128):
                        b = st.sb([128, TT], BF16, True); b2 = st.sb([128, TT], F32, True)
                        kb.load(b, b[:, :], YT[r0:r0 + 128, :])
                        kb.op(V, lambda: nc.vector.tensor_copy(out=b2[:, :], in_=b[:, :]), [b], [b2])
                        kb.store(b2, dbg_out['YT'][r0:r0 + 128, :], b2[:, :])

            with Stage(kb) as st:
                conv_weights(st, peer_uT[l], KC, NE, UTB)
            with Stage(kb) as st:
                conv_weights(st, peer_v[l], NE // 128, D, VTB)

            with Stage(kb) as st:
                wo = st.sb([128, KC, D], BF16)
                wf = [st.sb([128, D], F32, True) for _ in range(2)]
                for kc in range(KC):
                    a = wf[kc % 2]
                    kb.load(a, a[:, :], w_out[l, kc * 128:(kc + 1) * 128, :])
                    eng = (V, P_)[kc % 2]; E = kb.E[eng]
                    kb.op(eng, lambda: E.tensor_copy(out=wo[:, kc, :], in_=a[:, :]), [a], [wo])
                G1 = st.sb([128, D], F32, True); A2 = st.sb([128, D], F32, True); B2 = st.sb([128, D], F32, True)
                yt = [st.sb([128, KC, 128], BF16, True) for _ in range(2)]
                xt = [st.sb([128, D], F32, True) for _ in range(2)]
                hh = [st.sb([128, D], F32) for _ in range(2)]
                hT = [st.sb([128, KC, 128], BF16, True) for _ in range(2)]
                junk = st.sb([128, D], F32); rs = [st.sb([128, 1], F32) for _ in range(2)]
                curc = None
                for tt in range(NTT):
                    cond = 1 if tt < NTC else 0
                    if cond != curc:
                        modtile(st, G1, cond, 2); modtile(st, A2, cond, 4); modtile(st, B2, cond, 3); curc = cond
                    i2 = tt % 2
                    y = yt[i2]; x = xt[i2]; h = hh[i2]; r = rs[i2]; ht = hT[i2]
                    ts = slice(tt * 128, (tt + 1) * 128)
                    kb.load(y, y[:, :, :], YT[:, ts].rearrange("(kc p) n -> p kc n", p=128))
                    kb.load(x, x[:, :], XL[ts, :])
                    for nb in range(4):
                        pb = PSB[nb]
                        for kc in range(KC):
                            kb.op(PE, lambda: nc.tensor.matmul(pb[:, :], lhsT=y[:, kc, :], rhs=wo[:, kc, nb * 512:(nb + 1) * 512],
                                                               start=(kc == 0), stop=(kc == KC - 1)), [y, wo], [pb])
                        kb.op(V, lambda: nc.vector.tensor_tensor(out=h[:, nb * 512:(nb + 1) * 512], in0=pb[:, :], in1=G1[:, nb * 512:(nb + 1) * 512], op=ALU.mult),
                              [pb, G1], [h])
                    kb.op(P_, lambda: nc.gpsimd.tensor_tensor(out=x[:, :], in0=x[:, :], in1=h[:, :], op=ALU.add), [x, h], [x])
                    kb.store(x, XL[ts, :], x[:, :])
                    rms_rstd(st, (x[:, :], x), D, (r[:, :], r), (junk[:, :], junk))
                    kb.op(V, lambda: nc.vector.scalar_tensor_tensor(out=h[:, :], in0=x[:, :], scalar=r[:, 0:1], in1=A2[:, :], op0=ALU.mult, op1=ALU.mult),
                          [x, r, A2], [h])
                    kb.op(P_, lambda: nc.gpsimd.tensor_tensor(out=h[:, :], in0=h[:, :], in1=B2[:, :], op=ALU.add), [h, B2], [h])
                    for k4 in range(4):
                        pb = PSB[4 + k4 % 2]
                        for j in range(4):
                            kc = k4 * 4 + j
                            kb.op(PE, lambda: nc.tensor.transpose(pb[:, j * 128:(j + 1) * 128], h[:, kc * 128:(kc + 1) * 128], ident[:, :]), [h, ident], [pb])
                        kb.op(A_, lambda: nc.scalar.copy(out=ht[:, k4 * 4:(k4 + 1) * 4, :], in_=pb[:, :].rearrange("p (a b) -> p a b", a=4)), [pb], [ht])
                    kb.store(ht, HFT[:, ts].rearrange("(kc p) n -> p kc n", p=128), ht[:, :, :])

            with Stage(kb) as st:
                wq = st.sb([128, KC, D], BF16)
                wf = [st.sb([128, D], F32, True) for _ in range(2)]
                for kc in range(KC):
                    a = wf[kc % 2]
                    kb.load(a, a[:, :], peer_wq[l, kc * 128:(kc + 1) * 128, :])
                    eng = (V, P_)[kc % 2]; E = kb.E[eng]
                    kb.op(eng, lambda: E.tensor_copy(out=wq[:, kc, :], in_=a[:, :]), [a], [wq])
                kf = st.sb([128, 2, NK], F32, True)
                for p in range(2): kb.load(kf, kf[:, p, :], peer_keysT[l, p])
                kbb = st.sb([128, 2, NK], BF16)
                kb.op(V, lambda: nc.vector.tensor_copy(out=kbb[:, :, :], in_=kf[:, :, :]), [kf], [kbb])
                hfT = [st.sb([128, KC, 128], BF16, True) for _ in range(2)]
                qTt = [st.sb([128, 16, 128], BF16) for _ in range(2)]
                S = [st.sb([128, 8, 2, NK], F32, True) for _ in range(2)]
                S2 = [st.sb([128, 8, 2, NK], F32) for _ in range(2)]
                top = [st.sb([128, 8, 2, 16], F32) for _ in range(2)]
                cand = [st.sb([128, 8, 16, 16], F32) for _ in range(2)]
                cand2 = [st.sb([128, 8, 256], F32) for _ in range(2)]
                ctop = [st.sb([128, 8, 16], F32) for _ in range(2)]
                ex = [st.sb([128, 8, 16], F32) for _ in range(2)]
                zz = [st.sb([128, 8], F32) for _ in range(2)]
                b2 = [st.sb([128, 8], F32, True) for _ in range(2)]
                for tt in range(NTT):
                    i2 = tt % 2
                    ts = slice(tt * 128, (tt + 1) * 128)
                    hf = hfT[i2]; q = qTt[i2]; s_ = S[i2]; s2_ = S2[i2]; tp = top[i2]; cd = cand[i2]; cd2 = cand2[i2]
                    ct = ctop[i2]; e_ = ex[i2]; z_ = zz[i2]; bb = b2[i2]
                    kb.load(hf, hf[:, :, :], HFT[:, ts].rearrange("(kc p) n -> p kc n", p=128))
                    for c4 in range(4):
                        pb = PSB[c4 % 2]
                        for j in range(4):
                            oc = c4 * 4 + j
                            for kc in range(KC):
                                kb.op(PE, lambda: nc.tensor.matmul(pb[:, j * 128:(j + 1) * 128], lhsT=wq[:, kc, oc * 128:(oc + 1) * 128], rhs=hf[:, kc, :],
                                                                   start=(kc == 0), stop=(kc == KC - 1)), [wq, hf], [pb])
                        kb.op(A_, lambda: nc.scalar.copy(out=q[:, c4 * 4:(c4 + 1) * 4, :], in_=pb[:, :].rearrange("p (a b) -> p a b", a=4)), [pb], [q])
                    for hp in range(16):
                        pb = PSB[2 + (hp * NK) // 512 % 2] if NK == 128 else PSB[2]
                        c0 = (hp * NK) % 512
                        kb.op(PE, lambda: nc.tensor.matmul(pb[:, c0:c0 + NK], lhsT=q[:, hp, :], rhs=kbb[:, hp % 2, :], start=True, stop=True), [q, kbb], [pb])
                        if (c0 + NK == 512) or hp == 15:
                            n_in = (c0 + NK) // NK
                            hp0 = hp + 1 - n_in
                            sv = s_[:, :, :, :].rearrange("p h t k -> p (h t) k")
                            kb.op(A_, lambda: nc.scalar.copy(out=sv[:, hp0:hp + 1, :], in_=pb[:, 0:n_in * NK].rearrange("p (a k) -> p a k", k=NK)), [pb], [s_])
                    for h in range(8):
                        for p in range(2):
                            kb.op(V, lambda: nc.vector.max(out=tp[:, h, p, 0:8], in_=s_[:, h, p, :]), [s_], [tp])
                            kb.op(V, lambda: nc.vector.match_replace(out=s2_[:, h, p, :], in_to_replace=tp[:, h, p, 0:8], in_values=s_[:, h, p, :], imm_value=NEG),
                                  [tp, s_], [s2_])
                            kb.op(V, lambda: nc.vector.max(out=tp[:, h, p, 8:16], in_=s2_[:, h, p, :]), [s2_], [tp])
                    for h in range(8):
                        kb.op(P_, lambda: nc.gpsimd.tensor_tensor(out=cd[:, h, :, :], in0=tp[:, h, 0, :].unsqueeze(2).to_broadcast([128, 16, 16]),
                                                                  in1=tp[:, h, 1, :].unsqueeze(1).to_broadcast([128, 16, 16]), op=ALU.add), [tp], [cd])
                    for h in range(8):
                        cf = cd[:, h, :, :].rearrange("p a b -> p (a b)")
                        kb.op(V, lambda: nc.vector.max(out=ct[:, h, 0:8], in_=cf), [cd], [ct])
                        kb.op(V, lambda: nc.vector.match_replace(out=cd2[:, h, :], in_to_replace=ct[:, h, 0:8], in_values=cf, imm_value=NEG), [ct, cd], [cd2])
                        kb.op(V, lambda: nc.vector.max(out=ct[:, h, 8:16], in_=cd2[:, h, :]), [cd2], [ct])
                    kb.op(V, lambda: nc.vector.tensor_tensor(out=e_[:, :, :], in0=ct[:, :, :], in1=ct[:, :, 0:1].to_broadcast([128, 8, 16]), op=ALU.subtract),
                          [ct], [e_])
                    kb.op(A_, lambda: nc.scalar.activation(out=e_[:, :, :], in_=e_[:, :, :], func=AF.Exp), [e_], [e_])
                    kb.op(V, lambda: nc.vector.tensor_reduce(out=z_[:, :], in_=e_[:, :, :], axis=AX.X, op=ALU.add), [e_], [z_])
                    kb.op(A_, lambda: nc.scalar.activation(out=z_[:, :], in_=z_[:, :], func=AF.Ln), [z_], [z_])
                    kb.op(V, lambda: nc.vector.tensor_tensor(out=bb[:, :], in0=ct[:, :, 15], in1=ct[:, :, 0], op=ALU.subtract), [ct], [bb])
                    kb.op(V, lambda: nc.vector.tensor_tensor(out=bb[:, :], in0=bb[:, :], in1=z_[:, :], op=ALU.subtract), [bb, z_], [bb])
                    kb.op(V, lambda: nc.vector.tensor_tensor(out=s_[:, :, 0, :], in0=s_[:, :, 0, :], in1=ct[:, :, 15:16].to_broadcast([128, 8, NK]), op=ALU.subtract),
                          [s_, ct], [s_])
                    kb.store(s_, SS[ts, :], s_[:, :, :, :].rearrange("p h t k -> p (h t k)"))
                    kb.store(bb, BI2[ts, :], bb[:, :])

            with Stage(kb) as st:
                G = 4
                BI = 4
                EBK = BI * NK
                NBLK = NK // BI
                G2 = st.sb([128, D], F32, True)
                ub = [st.sb([128, KC, EBK], BF16, True) for _ in range(2)]
                vb = [st.sb([128, BI, D], BF16, True) for _ in range(2)]
                acc = st.sb([128, G, D], F32)
                hfg = st.sb([128, G, KC, 128], BF16, True)
                Sg = st.sb([128, G, 8, 2, NK], F32, True)
                b2g = st.sb([128, G, 8], F32, True)
                RD = 4
                zt = [st.sb([128, BI, NK], F32) for _ in range(RD)]
                et = [st.sb([128, BI, NK], F32) for _ in range(RD)]
                Gb = [st.sb([128, EBK], BF16) for _ in range(RD)]
                gA = [st.sb([128, EBK], BF16) for _ in range(G)]
                rawA = [st.sb([128, EBK], BF16) for _ in range(G)]
                Wg = [st.sb([128, EBK], F32) for _ in range(2)]
                WgT = [st.sb([128, BI, 128], BF16) for _ in range(2)]
                xt = st.sb([128, D], F32, True)
                pA = PSB[0]; pW = [PSB[1], PSB[2]]; pT = PSB[3]; PO = PSB[4:8]
                tiles = [tt for tt in range(NTT) if not (l == L - 1 and tt < NTC)]
                groups = []
                cur = []
                for tt in tiles:
                    if cur and ((tt < NTC) != (cur[0] < NTC) or len(cur) == G):
                        groups.append(cur); cur = []
                    cur.append(tt)
                if cur: groups.append(cur)
                curc = None
                cnt = [0]
                uTk = kcv(UTB)
                for grp in groups:
                    cond = 1 if grp[0] < NTC else 0
                    if cond != curc:
                        modtile(st, G2, cond, 5); curc = cond
                    for gi_, tt in enumerate(grp):
                        ts = slice(tt * 128, (tt + 1) * 128)
                        kb.load(hfg, hfg[:, gi_, :, :], HFT[:, ts].rearrange("(kc p) n -> p kc n", p=128))
                        kb.load(Sg, Sg[:, gi_, :, :, :].rearrange("p h t k -> p (h t k)"), SS[ts, :])
                        kb.load(b2g, b2g[:, gi_, :], BI2[ts, :])
                    def wload_u(blk):
                        u = ub[blk % 2]
                        e0 = blk * EBK
                        for q4 in range(4):
                            kb.load(u, u[:, q4 * 4:(q4 + 1) * 4, :], uTk[:, q4 * 4:(q4 + 1) * 4, e0:e0 + EBK])

                    def wload_v(blk):
                        v = vb[blk % 2]
                        e0 = blk * EBK
                        for c in range(BI):
                            kb.load(v, v[0:NK, c, :], VTB[e0 + c * NK:e0 + (c + 1) * NK, :])

                    def a_mm(blk, gi_):
                        u = ub[blk % 2]
                        for kc in range(KC):
                            kb.op(PE, lambda: nc.tensor.matmul(pA[:, 0:EBK], lhsT=hfg[:, gi_, kc, :], rhs=u[:, kc, :], start=(kc == 0), stop=(kc == KC - 1)),
                                  [hfg, u], [pA])

                    def tail_T(gi_):
                        wg = Wg[gi_ % 2]
                        for c in range(BI):
                            kb.op(PE, lambda: nc.tensor.transpose(pT[0:NK, c * 128:(c + 1) * 128], wg[:, c * NK:(c + 1) * NK], ident[:, :]), [wg, ident], [pT])

                    def tail_cast(pv):
                        wgt = WgT[pv[0] % 2]
                        kb.op(A_, lambda: nc.scalar.copy(out=wgt[0:NK, :, :], in_=pT[0:NK, 0:BI * 128].rearrange("p (a b) -> p a b", a=BI)), [pT], [wgt])

                    def tail_mm(pv, nb):
                        gi_, pblk = pv
                        wgt = WgT[gi_ % 2]; v = vb[pblk % 2]
                        for c in range(BI):
                            kb.op(PE, lambda: nc.tensor.matmul(PO[nb][:, :], lhsT=wgt[0:NK, c, :], rhs=v[0:NK, c, nb * 512:(nb + 1) * 512],
                                                               start=(c == 0), stop=(c == BI - 1)), [wgt, v], [PO[nb]])

                    def tail_add(pv, nb):
                        gi_, pblk = pv
                        if pblk == 0:
                            kb.op(V, lambda: nc.vector.tensor_copy(out=acc[:, gi_, nb * 512:(nb + 1) * 512], in_=PO[nb][:, :]), [PO[nb]], [acc])
                        else:
                            kb.op(V, lambda: nc.vector.tensor_tensor(out=acc[:, gi_, nb * 512:(nb + 1) * 512], in0=PO[nb][:, :],
                                                                     in1=acc[:, gi_, nb * 512:(nb + 1) * 512], op=ALU.add), [PO[nb], acc], [acc])

                    def heads(blk, gi_, pv):
                        k2 = gi_ % 2
                        pw = pW[k2]; wg = Wg[k2]; ga = gA[gi_]
                        nxt = blk + 1 < NBLK
                        for h in range(8):
                            r4 = cnt[0] % RD; cnt[0] += 1
                            z = zt[r4]; e = et[r4]; gb = Gb[r4]
                            kb.op(P_, lambda: nc.gpsimd.tensor_tensor(out=z[:, :, :],
                                                                      in0=Sg[:, gi_, h, 0, blk * BI:(blk + 1) * BI].unsqueeze(2).to_broadcast([128, BI, NK]),
                                                                      in1=Sg[:, gi_, h, 1, :].unsqueeze(1).to_broadcast([128, BI, NK]), op=ALU.add), [Sg], [z])
                            kb.op(A_, lambda: nc.scalar.activation(out=e[:, :, :], in_=z[:, :, :], func=AF.Exp, bias=b2g[:, gi_, h:h + 1]), [z, b2g], [e])
                            kb.op(V, lambda: nc.vector.scalar_tensor_tensor(out=gb[:, :], in0=z[:, :, :].rearrange("p a b -> p (a b)"), scalar=0.0,
                                                                            in1=e[:, :, :].rearrange("p a b -> p (a b)"), op0=ALU.is_ge, op1=ALU.mult), [z, e], [gb])
                            kb.op(PE, lambda: nc.tensor.matmul(pw[:, 0:EBK], lhsT=identb[:, :], rhs=gb[:, :], start=(h == 0), stop=(h == 7)), [identb, gb], [pw])
                            if nxt and h == 0: a_mm(blk + 1, gi_)
                            if pv is not None and h == 1: tail_cast(pv)
                            if nxt and h == 2:
                                kb.op(A_, lambda: nc.scalar.copy(out=rawA[gi_][:, :], in_=pA[:, 0:EBK]), [pA], [rawA[gi_]])
                            if pv is not None:
                                if 2 <= h <= 5: tail_mm(pv, h - 2)
                                if 3 <= h <= 6: tail_add(pv, h - 3)
                        kb.op(V, lambda: nc.vector.tensor_tensor(out=wg[:, :], in0=pw[:, 0:EBK], in1=ga[:, :], op=ALU.mult), [pw, ga], [wg])
                        tail_T(gi_)

                    wload_u(0); wload_v(0)
                    for gi_, tt in enumerate(grp):
                        a_mm(0, gi_)
                        kb.op(A_, lambda: nc.scalar.activation(out=gA[gi_][:, :], in_=pA[:, 0:EBK], func=AF.Gelu_apprx_tanh), [pA], [gA[gi_]])
                    pv = None
                    for blk in range(NBLK):
                        if blk + 1 < NBLK: wload_u(blk + 1)
                        for gi_, tt in enumerate(grp):
                            heads(blk, gi_, pv)
                            pv = (gi_, blk)
                            if gi_ == 0 and blk + 1 < NBLK: wload_v(blk + 1)
                        if blk + 1 < NBLK:
                            for gi_, tt in enumerate(grp):
                                kb.op(A_, lambda: nc.scalar.activation(out=gA[gi_][:, :], in_=rawA[gi_][:, :], func=AF.Gelu_apprx_tanh), [rawA[gi_]], [gA[gi_]])
                    tail_cast(pv)
                    for nb in range(4): tail_mm(pv, nb)
                    for nb in range(4): tail_add(pv, nb)
                    for gi_, tt in enumerate(grp):
                        ts = slice(tt * 128, (tt + 1) * 128)
                        kb.load(xt, xt[:, :], XL[ts, :])
                        kb.op(P_, lambda: nc.gpsimd.tensor_tensor(out=acc[:, gi_, :], in0=acc[:, gi_, :], in1=G2[:, :], op=ALU.mult), [acc, G2], [acc])
                        kb.op(V, lambda: nc.vector.tensor_tensor(out=xt[:, :], in0=xt[:, :], in1=acc[:, gi_, :], op=ALU.add), [xt, acc], [xt])
                        kb.store(xt, XL[ts, :], xt[:, :])

        with Stage(kb) as st:
            nf = st.sb([128, D], F32, True); kb.load(nf, nf[:, :], bcast_rows(norm_final[0:1, :]))
            xt = [st.sb([128, D], F32, True) for _ in range(2)]
            ot = [st.sb([128, D], F32, True) for _ in range(2)]
            junk = st.sb([128, D], F32); rs = [st.sb([128, 1], F32) for _ in range(2)]
            for k in range(NTL):
                tt = NTC + k; i2 = k % 2
                x = xt[i2]; o = ot[i2]; r = rs[i2]
                kb.load(x, x[:, :], XL[tt * 128:(tt + 1) * 128, :])
                rms_rstd(st, (x[:, :], x), D, (r[:, :], r), (junk[:, :], junk))
                kb.op(V, lambda: nc.vector.scalar_tensor_tensor(out=o[:, :], in0=x[:, :], scalar=r[:, 0:1], in1=nf[:, :], op0=ALU.mult, op1=ALU.mult),
                      [x, r, nf], [o])
                kb.store(o, yout[k * 128:(k + 1) * 128, :], o[:, :])
        kb.barrier()
        cst.__exit__(None, None, None)
    return nc


def host_consts(T, TC, grid_w=64):
    ident = np.eye(128, dtype=np.float32)
    s = np.arange(128)
    tri = np.stack([(s[:, None] <= s[None, :]), (s[:, None] >= s[None, :])]).astype(np.float32)
    pml = np.zeros((4, 9, 128, 128), np.float32)
    pmc = np.zeros((4, 3, 128, 128), np.float32)
    rl = s // grid_w; c = s % grid_w
    for g, w in enumerate(WINS):
        for dk in range(-4, 5):
            rin = 2 * dk + rl[:, None]
            rout = rl[None, :]
            okr = (rin >= rout - w // 2) & (rin < rout - w // 2 + w)
            okc = (c[:, None] >= c[None, :] - w // 2) & (c[:, None] < c[None, :] - w // 2 + w)
            pml[g, dk + 4] = (okr & okc)
        for dk in range(-1, 2):
            cin = 128 * dk + s[:, None]; cout = s[None, :]
            pmc[g, dk + 1] = (cin >= cout - w // 2) & (cin < cout - w // 2 + w)
    return ident, tri, pml, pmc


def make_in_maps(cfg, inp, nb):
    T, TC, L, NK = cfg['T'], cfg['TC'], cfg['L'], cfg['NK']
    f = lambda a: np.ascontiguousarray(np.asarray(a, dtype=np.float32))
    ident, tri, pml, pmc = host_consts(T, TC)
    shared = dict(
        ada_w=f(inp['ada_w']), ada_b=f(inp['ada_b']), norm_mix=f(inp['norm_mix']), norm_ffn=f(inp['norm_ffn']),
        norm_final=f(inp['norm_final']).reshape(1, D), w_in=f(inp['w_in']), b_gate=f(inp['b_gate']),
        sgu_norm=f(inp['sgu_norm']), sgu_wT=f(np.swapaxes(np.asarray(inp['sgu_w']), -1, -2)), sgu_b=f(inp['sgu_b']),
        qk_convT=f(np.swapaxes(np.asarray(inp['qk_conv']), -1, -2)), mlstm_norm=f(inp['mlstm_norm']),
        pool_w=f(inp['pool_w']), pool_scaleT=f(np.swapaxes(np.asarray(inp['pool_scale']).reshape(L, 4, 128), -1, -2)),
        w_out=f(inp['w_out']), peer_wq=f(inp['peer_wq']),
        peer_keysT=f(np.swapaxes(np.asarray(inp['peer_keys']), -1, -2)),
        peer_uT=f(np.swapaxes(np.asarray(inp['peer_u']), -1, -2)), peer_v=f(inp['peer_v']),
        ident=ident, tri=tri, pml=pml, pmc=pmc)
    maps = []
    for b in range(nb):
        m = dict(shared)
        m['xin'] = f(np.concatenate([np.asarray(inp['ctx'])[b], np.asarray(inp['x'])[b]], axis=0))
        cond = np.stack([np.asarray(inp['c'])[b], np.asarray(inp['c_ctx'])], axis=1)
        m['condT'] = f(cond.reshape(KC, 128, 2).transpose(1, 0, 2))
        maps.append(m)
    return maps


def kernel(**inputs):
    cfg = dict(T=4096, TC=256, L=4, NK=128)
    nb = 4
    nc = build(cfg)
    maps = make_in_maps(cfg, inputs, nb)
    res = run_bass_kernel_spmd(nc, maps, core_ids=list(range(nb)))
    out = np.stack([np.asarray(res.results[b]['yout'], dtype=np.float32) for b in range(nb)], axis=0)
    return out
```

```python
import numpy as np
from contextlib import ExitStack
import concourse.bass as bass
import concourse.mybir as mybir
from concourse.bass_utils import run_bass_kernel_spmd

F32 = mybir.dt.float32
BF16 = mybir.dt.bfloat16
AF = mybir.ActivationFunctionType
ALU = mybir.AluOpType
AX = mybir.AxisListType

D = 2048
KC = 16
D_IN = 5648
OFF_U, OFF_V, OFF_P, OFF_Q, OFF_O, OFF_K, OFF_VM, OFF_G = 0, 512, 1024, 1536, 2560, 3584, 4608, 5632
EPS = 1e-6
WINS = (2, 4, 8, 16)
NEG = -3.0e38


class SemW:
    def __init__(s, h, dma=False):
        s.h = h; s.total = 0; s.dma = dma


class Trk:
    __slots__ = ('w', 'r')

    def __init__(s):
        s.w = []; s.r = {}


class Buf:
    def __init__(s, t, ds=None):
        s.t = t; s.k = Trk(); s.ds = ds

    def __getitem__(s, idx):
        return s.t[idx]


class KB:
    def __init__(s, nc, es):
        s.nc = nc; s.es = es
        s.E = dict(pe=nc.tensor, dve=nc.vector, act=nc.scalar, pool=nc.gpsimd, sp=nc.sync)
        s.allsems = []
        s.esem = {e: s.newsem('e_' + e) for e in s.E}
        s.pesems = {s.esem['pe']}
        s.seen = {e: {} for e in s.E}
        s.dpool = []; s.dnext = 0; s.dbase = 0
        s.nm = 0

    def newsem(s, name, dma=False):
        h = s.es.enter_context(s.nc.semaphore(name + '_%d' % len(s.allsems)))
        sw = SemW(h, dma); s.allsems.append(sw); return sw

    def dsem(s):
        if s.dnext >= len(s.dpool):
            s.dpool.append(s.newsem('d', True))
        sw = s.dpool[s.dnext]; s.dnext += 1; return sw

    def _wait(s, e, deps):
        best = {}
        for sw, v in deps:
            if sw.dma: v = sw.total
            if v > best.get(sw, 0): best[sw] = v
        for sw, v in best.items():
            if s.seen[e].get(sw, 0) >= v: continue
            if e == 'pe' and sw in s.pesems: continue
            s.E[e].wait_ge(sw.h, v); s.seen[e][sw] = v

    def op(s, e, fn, reads=(), writes=()):
        deps = []
        for r in reads: deps += r.k.w
        for w in writes:
            deps += w.k.w; deps += list(w.k.r.items())
        s._wait(e, deps)
        sw = s.esem[e]
        if sw.total >= 30000:
            sw = s.newsem('e_' + e); s.esem[e] = sw
            if e == 'pe': s.pesems.add(sw)
        inst = fn(); sw.total += 1; inst.then_inc(sw.h, 1)
        for r in reads: r.k.r[sw] = sw.total
        for w in writes:
            w.k.w = [(sw, sw.total)]; w.k.r = {}
        return inst

    def dma(s, out, in_, sem, reads=(), writes=(), q='sp'):
        deps = []
        for r in reads: deps += r.k.w
        for w in writes:
            deps += w.k.w; deps += list(w.k.r.items())
        s._wait(q, deps)
        inst = s.E[q].dma_start(out=out, in_=in_); sem.total += 16; inst.then_inc(sem.h, 16)
        for r in reads: r.k.r[sem] = sem.total
        for w in writes:
            w.k.w = [(sem, sem.total)]; w.k.r = {}

    def load(s, buf, out, in_, q='sp'):
        s.dma(out, in_, buf.ds, writes=[buf], q=q)

    def store(s, buf, out, in_, q='sp'):
        s.dma(out, in_, buf.ds, reads=[buf], q=q)

    def barrier(s):
        deps = [(sw, sw.total) for sw in s.allsems if sw.total > 0]
        for e in s.E: s._wait(e, deps)


class Stage:
    def __init__(s, kb):
        s.kb = kb

    def __enter__(s):
        s.kb.barrier(); s.es = ExitStack(); s.es.__enter__(); s.kb.dnext = s.kb.dbase; return s

    def __exit__(s, *a):
        s.kb.barrier(); return s.es.__exit__(*a)

    def sb(s, shape, dt=F32, dma=False):
        s.kb.nm += 1
        t = s.es.enter_context(s.kb.nc.sbuf_tensor('b%d' % s.kb.nm, list(shape), dt))
        return Buf(t, s.kb.dsem() if dma else None)

    def ps(s, shape=(128, 512), dt=F32):
        s.kb.nm += 1
        t = s.es.enter_context(s.kb.nc.psum_tensor('p%d' % s.kb.nm, list(shape), dt))
        return Buf(t)


def bcast_rows(ap2d_row, n=128):
    a = ap2d_row
    return bass.AP(tensor=a.tensor, offset=a.offset, ap=[[0, n]] + [list(x) for x in list(a.ap)[1:]])


def build(cfg):
    T, TC, L, NK = cfg['T'], cfg['TC'], cfg['L'], cfg['NK']
    NE = NK * NK
    TT = T + TC
    NTC, NTL = TC // 128, T // 128
    NTT = NTC + NTL
    IB = 16 if NK >= 16 else NK
    NIB = NK // IB
    EB = IB * NK
    dbg = cfg.get('dbg', ())
    _i, _t, _pml, _pmc = host_consts(T, TC)
    PML_NZ = [[bool(_pml[g, d].any()) for d in range(9)] for g in range(4)]
    PMC_NZ = [[bool(_pmc[g, d].any()) for d in range(3)] for g in range(4)]
    nc = bass.Bass("TRN2", target_bir_lowering=False)

    def din(name, shape, dt=F32):
        return nc.dram_tensor(name, list(shape), dt, kind="ExternalInput").ap()

    def dsc(name, shape, dt=F32):
        return nc.dram_tensor(name, list(shape), dt, kind="Internal").ap()

    xin = din('xin', [TT, D])
    condT = din('condT', [128, KC, 2])
    ada_w = din('ada_w', [L, D, 6 * D]); ada_b = din('ada_b', [L, 6 * D])
    norm_mix = din('norm_mix', [L, D]); norm_ffn = din('norm_ffn', [L, D]); norm_final = din('norm_final', [1, D])
    w_in = din('w_in', [L, D, D_IN]); b_gate = din('b_gate', [L, 16])
    sgu_norm = din('sgu_norm', [L, 512]); sgu_wT = din('sgu_wT', [L, 4, 128, 128]); sgu_b = din('sgu_b', [L, 4, 128])
    qk_convT = din('qk_convT', [L, 2048, 3]); mlstm_norm = din('mlstm_norm', [L, 1024])
    pool_w = din('pool_w', [L, 4, 128, 128]); pool_scaleT = din('pool_scaleT', [L, 128, 4])
    w_out = din('w_out', [L, D, D]); peer_wq = din('peer_wq', [L, D, D])
    peer_keysT = din('peer_keysT', [L, 2, 128, NK])
    peer_uT = din('peer_uT', [L, D, NE]); peer_v = din('peer_v', [L, NE, D])
    identd = din('ident', [128, 128])
    trid = din('tri', [2, 128, 128])
    pml = din('pml', [4, 9, 128, 128])
    pmc = din('pmc', [4, 3, 128, 128])
    yout = nc.dram_tensor('yout', [T, D], F32, kind="ExternalOutput").ap()
    dbg_out = {}
    for nm, shp in dbg:
        dbg_out[nm] = nc.dram_tensor('dbg_' + nm, list(shp), F32, kind="ExternalOutput").ap()

    XL = dsc('XL', [TT, D])
    MOD = dsc('MOD', [2, 6 * D])
    UT = dsc('UT', [512, TT]); QR = dsc('QR', [1024, TT]); KR = dsc('KR', [1024, TT])
    VS = dsc('VS', [TT, 512]); PS = dsc('PS', [TT, 512]); OS = dsc('OS', [TT, 1024]); VM = dsc('VM', [TT, 1024])
    GS = dsc('GS', [TT, 16])
    YT = dsc('YT', [D, TT], BF16)
    HFT = dsc('HFT', [D, TT], BF16)
    SS = dsc('SS', [TT, 8 * 2 * NK]); BI2 = dsc('BI2', [TT, 8])
    UTB = dsc('UTB', [D, NE], BF16); VTB = dsc('VTB', [NE, D], BF16)

    es = ExitStack()
    with es:
        kb = KB(nc, es)
        V, A_, P_, PE = 'dve', 'act', 'pool', 'pe'

        def kcv(ap2d):
            return ap2d.rearrange("(kc p) c -> p kc c", p=128)

        cst = Stage(kb); cst.__enter__()
        ident = cst.sb([128, 128], F32, True); kb.load(ident, ident[:, :], identd[:, :])
        identb = cst.sb([128, 128], BF16)
        kb.op(V, lambda: nc.vector.tensor_copy(out=identb[:, :], in_=ident[:, :]), [ident], [identb])
        tri = cst.sb([128, 2, 128], F32, True)
        for i in range(2): kb.load(tri, tri[:, i, :], trid[i])
        trib = cst.sb([128, 2, 128], BF16)
        kb.op(V, lambda: nc.vector.tensor_copy(out=trib[:, :, :], in_=tri[:, :, :]), [tri], [trib])
        ones = cst.sb([128, 128], F32)
        kb.op(V, lambda: nc.vector.memset(ones[:, :], 1.0), [], [ones])
        sct = cst.sb([128, KC, 2], F32, True); kb.load(sct, sct[:, :, :], condT[:, :, :])
        kb.op(A_, lambda: nc.scalar.activation(out=sct[:, :, :], in_=sct[:, :, :], func=AF.Silu), [sct], [sct])
        PSB = [cst.ps() for _ in range(8)]
        kb.dbase = kb.dnext
        with Stage(kb) as st0:
            cp = [st0.sb([128, D], F32, True) for _ in range(2)]
            for tt in range(NTT):
                b = cp[tt % 2]
                kb.load(b, b[:, :], xin[tt * 128:(tt + 1) * 128, :])
                kb.store(b, XL[tt * 128:(tt + 1) * 128, :], b[:, :])

        def rms_rstd(st, xt, width, outcol, junk):
            kb.op(V, lambda: nc.vector.scalar_tensor_tensor(out=junk[0], in0=xt[0], scalar=1.0, in1=xt[0],
                                                            op0=ALU.mult, op1=ALU.mult, accum_out=outcol[0]),
                  [xt[1]], [junk[1], outcol[1]])
            kb.op(V, lambda: nc.vector.tensor_scalar(out=outcol[0], in0=outcol[0], scalar1=1.0 / width, scalar2=EPS,
                                                     op0=ALU.mult, op1=ALU.add), [outcol[1]], [outcol[1]])
            kb.op(A_, lambda: nc.scalar.activation(out=outcol[0], in_=outcol[0], func=AF.Sqrt), [outcol[1]], [outcol[1]])
            kb.op(V, lambda: nc.vector.reciprocal(out=outcol[0], in_=outcol[0]), [outcol[1]], [outcol[1]])

        def conv_weights(st, src2d, nk, ncols, dstdram):
            CW = min(ncols, 2048)
            f = [st.sb([128, CW], F32, True) for _ in range(4)]
            g = [st.sb([128, CW], BF16, True) for _ in range(4)]
            i = 0
            for k in range(nk):
                for c0 in range(0, ncols, CW):
                    a, b = f[i % 4], g[i % 4]
                    kb.load(a, a[:, :], src2d[k * 128:(k + 1) * 128, c0:c0 + CW])
                    eng = (V, P_)[i % 2]
                    E = kb.E[eng]
                    kb.op(eng, lambda: E.tensor_copy(out=b[:, :], in_=a[:, :]), [a], [b])
                    kb.store(b, dstdram[k * 128:(k + 1) * 128, c0:c0 + CW], b[:, :], q='act')
                    i += 1

        for l in range(L):
            with Stage(kb) as st:
                wb = [st.sb([128, KC, 512], F32, True) for _ in range(2)]
                adab = st.sb([2, 6 * D], F32, True)
                for r in range(2): kb.load(adab, adab[r:r + 1, :], ada_b[l:l + 1, :])
                nrm = st.sb([2, 2, D], F32, True)
                for r in range(2):
                    kb.load(nrm, nrm[r:r + 1, 0, :], norm_mix[l:l + 1, :])
                    kb.load(nrm, nrm[r:r + 1, 1, :], norm_ffn[l:l + 1, :])
                mo = [st.sb([2, 512], F32, True) for _ in range(2)]
                for cb in range(24):
                    w = wb[cb % 2]
                    src = kcv(ada_w[l])
                    for q4 in range(4):
                        kb.load(w, w[:, q4 * 4:(q4 + 1) * 4, :], src[:, q4 * 4:(q4 + 1) * 4, cb * 512:(cb + 1) * 512])
                    pb = PSB[cb % 2]
                    for kc in range(KC):
                        kb.op(PE, lambda: nc.tensor.matmul(pb[0:2, :], lhsT=sct[:, kc, :], rhs=w[:, kc, :],
                                                           start=(kc == 0), stop=(kc == KC - 1)), [sct, w], [pb])
                    m = mo[cb % 2]
                    kb.op(V, lambda: nc.vector.tensor_tensor(out=m[:, :], in0=pb[0:2, :], in1=adab[:, cb * 512:(cb + 1) * 512],
                                                             op=ALU.add), [pb, adab], [m])
                    part = cb // 4
                    if part in (1, 4):
                        j = 0 if part == 1 else 1
                        c0 = (cb % 4) * 512
                        kb.op(V, lambda: nc.vector.scalar_tensor_tensor(out=m[:, :], in0=m[:, :], scalar=1.0,
                                                                        in1=nrm[:, j, c0:c0 + 512], op0=ALU.add, op1=ALU.mult),
                              [m, nrm], [m])
                    kb.store(m, MOD[:, cb * 512:(cb + 1) * 512], m[:, :])

            def modtile(st, buf, cond, part):
                kb.load(buf, buf[:, :], bcast_rows(MOD[cond:cond + 1, part * D:(part + 1) * D]))

            with Stage(kb) as st:
                A1 = st.sb([128, D], F32, True); B1 = st.sb([128, D], F32, True)
                bg = st.sb([128, 16], F32, True)
                kb.load(bg, bg[:, :], bcast_rows(b_gate[l:l + 1, :]))
                xt = [st.sb([128, D], F32, True) for _ in range(2)]
                hh = [st.sb([128, D], F32) for _ in range(2)]
                junk = st.sb([128, D], F32)
                rs = [st.sb([128, 1], F32) for _ in range(2)]
                hT = st.sb([128, KC, 1024], BF16)
                wf = [st.sb([128, KC, 512], F32, True) for _ in range(1)]
                wbf = [st.sb([128, KC, 512], BF16) for _ in range(2)]
                ev = [st.sb([128, 512], F32, True) for _ in range(3)]
                evi = [0]
                wi = [0]
                pbi = [0]
                w2 = kcv(w_in[l])
                sts = []
                t0 = 0
                while t0 < NTC: n = min(8, NTC - t0); sts.append((1, t0, n)); t0 += n
                while t0 < NTT: n = min(8, NTT - t0); sts.append((0, t0, n)); t0 += n
                curc = None
                for (cond, tb, n) in sts:
                    if cond != curc:
                        modtile(st, A1, cond, 1); modtile(st, B1, cond, 0); curc = cond
                    NS = n * 128
                    for ti in range(n):
                        tt = tb + ti
                        x = xt[tt % 2]; h = hh[tt % 2]; r = rs[tt % 2]
                        kb.load(x, x[:, :], XL[tt * 128:(tt + 1) * 128, :])
                        rms_rstd(st, (x[:, :], x), D, (r[:, :], r), (junk[:, :], junk))
                        kb.op(V, lambda: nc.vector.scalar_tensor_tensor(out=h[:, :], in0=x[:, :], scalar=r[:, 0:1], in1=A1[:, :],
                                                                        op0=ALU.mult, op1=ALU.mult), [x, r, A1], [h])
                        kb.op(P_, lambda: nc.gpsimd.tensor_tensor(out=h[:, :], in0=h[:, :], in1=B1[:, :], op=ALU.add), [h, B1], [h])
                        for k4 in range(4):
                            pb = PSB[k4 % 2]
                            for j in range(4):
                                kc = k4 * 4 + j
                                kb.op(PE, lambda: nc.tensor.transpose(pb[:, j * 128:(j + 1) * 128], h[:, kc * 128:(kc + 1) * 128], ident[:, :]),
                                      [h, ident], [pb])
                            kb.op(A_, lambda: nc.scalar.copy(out=hT[:, k4 * 4:(k4 + 1) * 4, ti * 128:(ti + 1) * 128],
                                                             in_=pb[:, :].rearrange("p (a b) -> p a b", a=4)), [pb], [hT])

                    def wload(c0, ncols):
                        i = wi[0]; wi[0] += 1
                        a, b = wf[0], wbf[i % 2]
                        for q4 in range(4):
                            kb.load(a, a[:, q4 * 4:(q4 + 1) * 4, 0:ncols], w2[:, q4 * 4:(q4 + 1) * 4, c0:c0 + ncols])
                        eng = (V, P_)[i % 2]; E = kb.E[eng]
                        kb.op(eng, lambda: E.tensor_copy(out=b[:, :, 0:ncols], in_=a[:, :, 0:ncols]), [a], [b])
                        return b

                    def evbuf():
                        e = ev[evi[0] % 3]; evi[0] += 1; return e

                    for (off, nb, dst, fn) in ((OFF_U, 4, UT, AF.Gelu_apprx_tanh), (OFF_Q, 8, QR, None), (OFF_K, 8, KR, None)):
                        for g4 in range(nb // 4):
                            wbb = wload(off + g4 * 512, 512)
                            for j in range(4):
                              for n0 in range(0, NS, 512):
                                nw = min(512, NS - n0)
                                blk = g4 * 4 + j
                                pbi[0] += 1
                                pb = PSB[2 + (pbi[0] % 2)]
                                for kc in range(KC):
                                    kb.op(PE, lambda: nc.tensor.matmul(pb[:, 0:nw], lhsT=wbb[:, kc, j * 128:(j + 1) * 128], rhs=hT[:, kc, n0:n0 + nw],
                                                                       start=(kc == 0), stop=(kc == KC - 1)), [wbb, hT], [pb])
                                e = evbuf()
                                if fn is None:
                                    kb.op(A_, lambda: nc.scalar.copy(out=e[:, 0:nw], in_=pb[:, 0:nw]), [pb], [e])
                                else:
                                    kb.op(A_, lambda: nc.scalar.activation(out=e[:, 0:nw], in_=pb[:, 0:nw], func=fn), [pb], [e])
                                kb.store(e, dst[blk * 128:(blk + 1) * 128, tb * 128 + n0:tb * 128 + n0 + nw], e[:, 0:nw])
                    for (off, ncols, dst, dc0, fn) in ((OFF_V, 512, VS, 0, AF.Gelu_apprx_tanh), (OFF_P, 512, PS, 0, None),
                                                       (OFF_O, 512, OS, 0, AF.Sigmoid), (OFF_O + 512, 512, OS, 512, AF.Sigmoid),
                                                       (OFF_VM, 512, VM, 0, None), (OFF_VM + 512, 512, VM, 512, None),
                                                       (OFF_G, 16, GS, 0, 'gate')):
                        wbb = wload(off, ncols)
                        for ti in range(n):
                            tt = tb + ti
                            pb = PSB[4 + (ti % 2)]
                            for kc in range(KC):
                                kb.op(PE, lambda: nc.tensor.matmul(pb[:, 0:ncols], lhsT=hT[:, kc, ti * 128:(ti + 1) * 128], rhs=wbb[:, kc, 0:ncols],
                                                                   start=(kc == 0), stop=(kc == KC - 1)), [wbb, hT], [pb])
                            e = evbuf()
                            if fn is None:
                                kb.op(A_, lambda: nc.scalar.copy(out=e[:, 0:ncols], in_=pb[:, 0:ncols]), [pb], [e])
                            elif fn == 'gate':
                                kb.op(V, lambda: nc.vector.tensor_tensor(out=e[:, 0:ncols], in0=pb[:, 0:ncols], in1=bg[:, :], op=ALU.add), [pb, bg], [e])
                            else:
                                kb.op(A_, lambda: nc.scalar.activation(out=e[:, 0:ncols], in_=pb[:, 0:ncols], func=fn), [pb], [e])
                            kb.store(e, dst[tt * 128:(tt + 1) * 128, dc0:dc0 + ncols], e[:, 0:ncols])

            if 'UT' in dbg_out and l == 0:
                with Stage(kb) as st:
                    for (nm, src) in (('UT', UT), ('QR', QR), ('VS', VS), ('GS', GS), ('OS', OS)):
                        if nm in dbg_out:
                            R, C = src.shape
                            for r0 in range(0, R, 128):
                                rr = min(128, R - r0)
                                b = st.sb([128, C], F32, True)
                                kb.load(b, b[0:rr, :], src[r0:r0 + rr, :]); kb.store(b, dbg_out[nm][r0:r0 + rr, :], b[0:rr, :])

            with Stage(kb) as st:
                sgn = st.sb([128, 512], F32, True); kb.load(sgn, sgn[:, :], bcast_rows(sgu_norm[l:l + 1, :]))
                wsf = st.sb([128, 4, 128], F32, True)
                for h in range(4): kb.load(wsf, wsf[:, h, :], sgu_wT[l, h])
                wsb = st.sb([128, 4, 128], BF16)
                kb.op(V, lambda: nc.vector.tensor_copy(out=wsb[:, :, :], in_=wsf[:, :, :]), [wsf], [wsb])
                sbr = st.sb([1, 4, 128], F32, True); kb.load(sbr, sbr[0:1, :, :], sgu_b[l:l + 1, :, :])
                sbb = st.sb([1, 4, 128], BF16)
                kb.op(V, lambda: nc.vector.tensor_copy(out=sbb[:, :, :], in_=sbr[:, :, :]), [sbr], [sbb])
                onesb = st.sb([1, 128], BF16)
                kb.op(V, lambda: nc.vector.memset(onesb[:, :], 1.0), [], [onesb])
                pwf = st.sb([128, 4, 128], F32, True)
                for g in range(4): kb.load(pwf, pwf[:, g, :], pool_w[l, g])
                pwb = st.sb([128, 4, 128], BF16)
                kb.op(V, lambda: nc.vector.tensor_copy(out=pwb[:, :, :], in_=pwf[:, :, :]), [pwf], [pwb])
                psc = st.sb([128, 4], F32, True); kb.load(psc, psc[:, :], pool_scaleT[l])
                pmf = st.sb([128, 9, 128], F32, True)
                pmlb = st.sb([128, 4, 9, 128], BF16); pmcb = st.sb([128, 4, 3, 128], BF16)
                for g in range(4):
                    for dk in range(9): kb.load(pmf, pmf[:, dk, :], pml[g, dk])
                    kb.op(V, lambda: nc.vector.tensor_copy(out=pmlb[:, g, :, :], in_=pmf[:, :, :]), [pmf], [pmlb])
                for g in range(4):
                    for dk in range(3): kb.load(pmf, pmf[:, dk, :], pmc[g, dk])
                    kb.op(V, lambda: nc.vector.tensor_copy(out=pmcb[:, g, :, :], in_=pmf[:, 0:3, :]), [pmf], [pmcb])
                XPf = st.sb([128, NTT, 512], F32, True)
                XP = st.sb([128, NTT, 4, 129], BF16)
                kb.op(P_, lambda: nc.gpsimd.memset(XP[:, :, :, :], 1.0), [], [XP])
                for tt in range(NTT):
                    kb.load(XPf, XPf[:, tt, :], PS[tt * 128:(tt + 1) * 128, :])
                kb.op(V, lambda: nc.vector.tensor_copy(out=XP[:, :, :, 0:128], in_=XPf[:, :, :].rearrange("p t (g c) -> p t g c", g=4)),
                      [XPf], [XP])
                vt = [st.sb([128, 512], F32, True) for _ in range(2)]
                vn = [st.sb([128, 512], BF16) for _ in range(2)]
                ut = [st.sb([128, 4, 128], F32, True) for _ in range(2)]
                ya = [st.sb([128, 4, 128], BF16, True) for _ in range(2)]
                yc = [st.sb([128, 4, 128], BF16, True) for _ in range(2)]
                junk = st.sb([128, 512], F32); rs = [st.sb([128, 1], F32) for _ in range(2)]
                mean = [st.sb([128, 512], F32) for _ in range(2)]
                rc = [st.sb([128, 4], F32) for _ in range(2)]
                dT = [st.sb([128, 4, 128], BF16) for _ in range(2)]
                for tt in range(NTT):
                    i2 = tt % 2
                    isctx = tt < NTC
                    v = vt[i2]; u = ut[i2]; r = rs[i2]; vb = vn[i2]
                    kb.load(v, v[:, :], VS[tt * 128:(tt + 1) * 128, :])
                    kb.load(u, u[:, :, :], UT[:, tt * 128:(tt + 1) * 128].rearrange("(h p) n -> p h n", p=128))
                    rms_rstd(st, (v[:, :], v), 512, (r[:, :], r), (junk[:, :], junk))
                    kb.op(V, lambda: nc.vector.scalar_tensor_tensor(out=vb[:, :], in0=v[:, :], scalar=r[:, 0:1], in1=sgn[:, :],
                                                                    op0=ALU.mult, op1=ALU.mult), [v, r, sgn], [vb])
                    pb = PSB[i2]
                    for h in range(4):
                        kb.op(PE, lambda: nc.tensor.matmul(pb[:, h * 128:(h + 1) * 128], lhsT=vb[:, h * 128:(h + 1) * 128], rhs=wsb[:, h, :],
                                                           start=True, stop=False), [vb, wsb], [pb])
                        kb.op(PE, lambda: nc.tensor.matmul(pb[:, h * 128:(h + 1) * 128], lhsT=onesb[0:1, :], rhs=sbb[0:1, h, :],
                                                           start=False, stop=True), [onesb, sbb], [pb])
                    y = ya[i2]
                    kb.op(V, lambda: nc.vector.tensor_tensor(out=y[:, :, :], in0=pb[:, :].rearrange("p (h n) -> p h n", h=4), in1=u[:, :, :],
                                                             op=ALU.mult), [pb, u], [y])
                    kb.store(y, YT[0:512, tt * 128:(tt + 1) * 128].rearrange("(h p) n -> p h n", p=128), y[:, :, :])
                    if isctx:
                        k, nt, tbase, pmb, dks, dko = tt, NTC, 0, pmcb, (-1, 0, 1), 1
                    else:
                        k, nt, tbase, pmb, dks, dko = tt - NTC, NTL, NTC, pmlb, tuple(range(-4, 5)), 4
                    mn = mean[i2]; rcc = rc[i2]
                    pb2 = PSB[2 + i2]; pb3 = PSB[4 + i2]
                    for g in range(4):
                        w = WINS[g]
                        nzm = PMC_NZ if isctx else PML_NZ
                        use = [dk for dk in dks if 0 <= k + dk < nt and nzm[g][dk + dko]]
                        pbg = pb2 if g < 2 else pb3
                        c0 = (g % 2) * 129
                        for j, dk in enumerate(use):
                            kb.op(PE, lambda: nc.tensor.matmul(pbg[:, c0:c0 + 129], lhsT=pmb[:, g, dk + dko, :], rhs=XP[:, tbase + k + dk, g, :],
                                                               start=(j == 0), stop=(j == len(use) - 1)), [pmb, XP], [pbg])
                    for g in range(4):
                        pbg = pb2 if g < 2 else pb3
                        c0 = (g % 2) * 129
                        kb.op(V, lambda: nc.vector.reciprocal(out=rcc[:, g:g + 1], in_=pbg[:, c0 + 128:c0 + 129]), [pbg], [rcc])
                        kb.op(V, lambda: nc.vector.scalar_tensor_tensor(out=mn[:, g * 128:(g + 1) * 128], in0=pbg[:, c0:c0 + 128], scalar=rcc[:, g:g + 1],
                                                                        in1=XPf[:, tt, g * 128:(g + 1) * 128], op0=ALU.mult, op1=ALU.subtract),
                              [pbg, rcc, XPf], [mn])
                    pb4 = PSB[6 + i2]
                    for g in range(4):
                        kb.op(PE, lambda: nc.tensor.transpose(pb4[:, g * 128:(g + 1) * 128], mn[:, g * 128:(g + 1) * 128], ident[:, :]), [mn, ident], [pb4])
                    d = dT[i2]
                    kb.op(A_, lambda: nc.scalar.copy(out=d[:, :, :], in_=pb4[:, :].rearrange("p (g n) -> p g n", g=4)), [pb4], [d])
                    for g in range(4):
                        kb.op(PE, lambda: nc.tensor.matmul(pb4[:, g * 128:(g + 1) * 128], lhsT=pwb[:, g, :], rhs=d[:, g, :], start=True, stop=True),
                              [pwb, d], [pb4])
                    y2 = yc[i2]
                    for g in range(4):
                        kb.op(V, lambda: nc.vector.tensor_scalar(out=y2[:, g, :], in0=pb4[:, g * 128:(g + 1) * 128], scalar1=psc[:, g:g + 1], scalar2=None,
                                                                 op0=ALU.mult), [pb4, psc], [y2])
                    kb.store(y2, YT[1536:2048, tt * 128:(tt + 1) * 128].rearrange("(g p) n -> p g n", p=128), y2[:, :, :])

            with Stage(kb) as st:
                EA = st.sb([128, NTT, 8], F32); EBc = st.sb([128, NTT, 8], F32); EBL = st.sb([128, NTT, 8], F32)
                gt = [st.sb([128, 16], F32, True) for _ in range(2)]
                lf = [st.sb([128, 8], F32) for _ in range(2)]
                t1 = [st.sb([128, 8], F32) for _ in range(2)]
                gi = [st.sb([128, 8], F32) for _ in range(2)]
                for tt in range(NTT):
                    i2 = tt % 2
                    g = gt[i2]; f = lf[i2]; a = t1[i2]; gii = gi[i2]
                    kb.load(g, g[:, :], GS[tt * 128:(tt + 1) * 128, :])
                    gv = g[:, :].rearrange("p (d g h) -> p d g h", d=2, g=2)
                    fv = f[:, :].rearrange("p (d h) -> p d h", d=2)
                    av = a[:, :].rearrange("p (d h) -> p d h", d=2)
                    kb.op(A_, lambda: nc.scalar.activation(out=av, in_=gv[:, :, 1, :], func=AF.Abs), [g], [a])
                    kb.op(A_, lambda: nc.scalar.activation(out=a[:, :], in_=a[:, :], func=AF.Exp, scale=-1.0), [a], [a])
                    kb.op(A_, lambda: nc.scalar.activation(out=a[:, :], in_=a[:, :], func=AF.Ln, bias=1.0), [a], [a])
                    kb.op(V, lambda: nc.vector.tensor_scalar(out=fv, in0=gv[:, :, 1, :], scalar1=0.0, scalar2=None, op0=ALU.min), [g], [f])
                    kb.op(V, lambda: nc.vector.tensor_tensor(out=f[:, :], in0=f[:, :], in1=a[:, :], op=ALU.subtract), [f, a], [f])
                    kb.op(V, lambda: nc.vector.tensor_copy(out=gii[:, :].rearrange("p (d h) -> p d h", d=2), in_=gv[:, :, 0, :]), [g], [gii])
                    pb = PSB[i2]
                    for dr in range(2):
                        kb.op(PE, lambda: nc.tensor.matmul(pb[:, dr * 4:dr * 4 + 4], lhsT=tri[:, dr, :], rhs=f[:, dr * 4:dr * 4 + 4], start=True, stop=True),
                              [tri, f], [pb])
                    kb.op(PE, lambda: nc.tensor.matmul(pb[:, 8:16], lhsT=ones[:, :], rhs=f[:, :], start=True, stop=True), [ones, f], [pb])
                    kb.op(A_, lambda: nc.scalar.activation(out=EBc[:, tt, :], in_=pb[:, 0:8], func=AF.Exp), [pb], [EBc])
                    kb.op(A_, lambda: nc.scalar.activation(out=EBL[:, tt, :], in_=pb[:, 8:16], func=AF.Exp), [pb], [EBL])
                    kb.op(V, lambda: nc.vector.tensor_tensor(out=gii[:, :], in0=gii[:, :], in1=pb[:, 0:8], op=ALU.subtract), [gii, pb], [gii])
                    kb.op(A_, lambda: nc.scalar.activation(out=EA[:, tt, :], in_=gii[:, :], func=AF.Exp), [gii], [EA])
                gnb = st.sb([128, 1024], F32, True); kb.load(gnb, gnb[:, :], bcast_rows(mlstm_norm[l:l + 1, :]))
                cw = st.sb([128, 16, 3], F32, True)
                kb.load(cw, cw[:, :, :], qk_convT[l].rearrange("(c p) k -> p c k", p=128))
                raw = [st.sb([128, TT + 2], F32, True) for _ in range(1)]
                cv = [st.sb([128, TT], F32) for _ in range(1)]
                qT = st.sb([128, 2, TT], BF16); kT = st.sb([128, 2, TT], BF16)
                kS = st.sb([128, NTT, 256], BF16)
                vf = [st.sb([128, 256], F32, True) for _ in range(2)]
                vw = [st.sb([128, 257], BF16) for _ in range(2)]
                stm = [st.sb([128, 128], BF16) for _ in range(2)]
                C32 = st.sb([128, 2, 257], F32); Cb = st.sb([128, 2, 257], BF16)
                hs = [st.sb([128, 257], F32) for _ in range(2)]
                dn = [st.sb([128, 1], F32) for _ in range(2)]
                HF = st.sb([128, NTT, 256], F32)
                ot = [st.sb([128, 256], F32, True) for _ in range(2)]
                yb = [st.sb([128, 256], F32) for _ in range(2)]
                ybT = [st.sb([128, 2, 128], BF16, True) for _ in range(2)]
                junk = st.sb([128, 256], F32); rs = [st.sb([128, 1], F32) for _ in range(2)]
                for hd in range(4):
                    for which, (src, dstT, scl) in enumerate(((QR, qT, 1.0), (KR, kT, 1.0 / 16.0))):
                        for c in range(2):
                            ch = hd * 2 + c
                            rw = raw[0]; co = cv[0]
                            wcol = which * 8 + ch
                            kb.op(P_, lambda: nc.gpsimd.memset(rw[:, :], 0.0), [], [rw])
                            kb.load(rw, rw[:, 1:TT + 1], src[ch * 128:(ch + 1) * 128, :])
                            for (s0, sl) in ((0, TC), (TC, T)):
                                kb.op(V, lambda: nc.vector.tensor_scalar(out=co[:, s0:s0 + sl], in0=rw[:, 1 + s0:1 + s0 + sl], scalar1=cw[:, wcol, 1:2],
                                                                         scalar2=None, op0=ALU.mult), [rw, cw], [co])
                                kb.op(V, lambda: nc.vector.scalar_tensor_tensor(out=co[:, s0 + 1:s0 + sl], in0=rw[:, 1 + s0:s0 + sl], scalar=cw[:, wcol, 0:1],
                                                                                in1=co[:, s0 + 1:s0 + sl], op0=ALU.mult, op1=ALU.add), [rw, cw, co], [co])
                                kb.op(V, lambda: nc.vector.scalar_tensor_tensor(out=co[:, s0:s0 + sl - 1], in0=rw[:, 2 + s0:1 + s0 + sl], scalar=cw[:, wcol, 2:3],
                                                                                in1=co[:, s0:s0 + sl - 1], op0=ALU.mult, op1=ALU.add), [rw, cw, co], [co])
                            kb.op(A_, lambda: nc.scalar.activation(out=co[:, :], in_=co[:, :], func=AF.Silu), [co], [co])
                            kb.op(V, lambda: nc.vector.tensor_scalar(out=dstT[:, c, :], in0=co[:, :], scalar1=scl, scalar2=None, op0=ALU.mult), [co], [dstT])
                            if which == 1:
                                for t4 in range(0, NTT, 4):
                                    nn = min(4, NTT - t4)
                                    pb = PSB[(t4 // 4) % 2]
                                    for j in range(nn):
                                        kb.op(PE, lambda: nc.tensor.transpose(pb[:, j * 128:(j + 1) * 128], co[:, (t4 + j) * 128:(t4 + j + 1) * 128], ident[:, :]),
                                              [co, ident], [pb])
                                    kb.op(V, lambda: nc.vector.tensor_scalar(out=kS[:, t4:t4 + nn, c * 128:(c + 1) * 128],
                                                                             in0=pb[:, 0:nn * 128].rearrange("p (a b) -> p a b", a=nn),
                                                                             scalar1=scl, scalar2=None, op0=ALU.mult), [pb], [kS])
                    for dr in range(2):
                        col = dr * 4 + hd
                        kb.op(V, lambda: nc.vector.memset(C32[:, :, :], 0.0), [], [C32])
                        kb.op(V, lambda: nc.vector.memset(Cb[:, :, :], 0.0), [], [Cb])
                        order = list(range(NTT)) if dr == 0 else (list(range(NTC - 1, -1, -1)) + list(range(NTT - 1, NTC - 1, -1)))
                        for ci, tt in enumerate(order):
                            i2 = ci % 2
                            v = vf[i2]; vv = vw[i2]; sm = stm[i2]; h_ = hs[i2]; d_ = dn[i2]
                            ts = slice(tt * 128, (tt + 1) * 128)
                            kb.load(v, v[:, :], VM[ts, hd * 256:(hd + 1) * 256])
                            kb.op(P_, lambda: nc.gpsimd.tensor_scalar(out=vv[:, 0:256], in0=v[:, :], scalar1=EA[:, tt, col:col + 1], scalar2=None, op0=ALU.mult),
                                  [v, EA], [vv])
                            kb.op(P_, lambda: nc.gpsimd.tensor_copy(out=vv[:, 256:257], in_=EA[:, tt, col:col + 1]), [EA], [vv])
                            pS = PSB[2 + i2]
                            for c in range(2):
                                kb.op(PE, lambda: nc.tensor.matmul(pS[:, 0:128], lhsT=kT[:, c, ts], rhs=qT[:, c, ts], start=(c == 0), stop=(c == 1)),
                                      [kT, qT], [pS])
                            kb.op(V, lambda: nc.vector.tensor_tensor(out=sm[:, :], in0=pS[:, 0:128], in1=tri[:, dr, :], op=ALU.mult), [pS, tri], [sm])
                            pH = PSB[4 + i2]
                            kb.op(PE, lambda: nc.tensor.matmul(pH[:, 0:257], lhsT=sm[:, :], rhs=vv[:, :], start=True, stop=False), [sm, vv], [pH])
                            for c in range(2):
                                kb.op(PE, lambda: nc.tensor.matmul(pH[:, 0:257], lhsT=qT[:, c, ts], rhs=Cb[:, c, :], start=False, stop=(c == 1)),
                                      [qT, Cb], [pH])
                            kb.op(A_, lambda: nc.scalar.activation(out=h_[:, :], in_=pH[:, 0:257], func=AF.Copy, scale=EBc[:, tt, col:col + 1]), [pH, EBc], [h_])
                            for c in range(2):
                                pC = PSB[6 + c]
                                kb.op(PE, lambda: nc.tensor.matmul(pC[:, 0:257], lhsT=kS[:, tt, c * 128:(c + 1) * 128], rhs=vv[:, :], start=True, stop=True),
                                      [kS, vv], [pC])
                                kb.op(V, lambda: nc.vector.tensor_tensor(out=C32[:, c, :], in0=pC[:, 0:257], in1=C32[:, c, :], op=ALU.add), [pC, C32], [C32])
                            kb.op(V, lambda: nc.vector.tensor_scalar(out=C32[:, :, :], in0=C32[:, :, :], scalar1=EBL[:, tt, col:col + 1], scalar2=None, op0=ALU.mult),
                                  [C32, EBL], [C32])
                            kb.op(P_, lambda: nc.gpsimd.tensor_copy(out=Cb[:, :, :], in_=C32[:, :, :]), [C32], [Cb])
                            kb.op(A_, lambda: nc.scalar.activation(out=d_[:, :], in_=h_[:, 256:257], func=AF.Abs), [h_], [d_])
                            kb.op(V, lambda: nc.vector.tensor_scalar(out=d_[:, :], in0=d_[:, :], scalar1=1.0, scalar2=None, op0=ALU.max), [d_], [d_])
                            kb.op(V, lambda: nc.vector.reciprocal(out=d_[:, :], in_=d_[:, :]), [d_], [d_])
                            if dr == 0:
                                kb.op(V, lambda: nc.vector.tensor_scalar(out=HF[:, tt, :], in0=h_[:, 0:256], scalar1=d_[:, 0:1], scalar2=None, op0=ALU.mult),
                                      [h_, d_], [HF])
                            else:
                                y_ = yb[i2]; o_ = ot[i2]; r = rs[i2]; yt_ = ybT[i2]
                                kb.op(V, lambda: nc.vector.scalar_tensor_tensor(out=y_[:, :], in0=h_[:, 0:256], scalar=d_[:, 0:1], in1=HF[:, tt, :],
                                                                                op0=ALU.mult, op1=ALU.add), [h_, d_, HF], [y_])
                                kb.load(o_, o_[:, :], OS[ts, hd * 256:(hd + 1) * 256])
                                rms_rstd(st, (y_[:, :], y_), 256, (r[:, :], r), (junk[:, :], junk))
                                kb.op(V, lambda: nc.vector.scalar_tensor_tensor(out=y_[:, :], in0=y_[:, :], scalar=r[:, 0:1], in1=gnb[:, hd * 256:(hd + 1) * 256],
                                                                                op0=ALU.mult, op1=ALU.mult), [y_, r, gnb], [y_])
                                kb.op(P_, lambda: nc.gpsimd.tensor_tensor(out=y_[:, :], in0=y_[:, :], in1=o_[:, :], op=ALU.mult), [y_, o_], [y_])
                                pT = PSB[i2]
                                for c in range(2):
                                    kb.op(PE, lambda: nc.tensor.transpose(pT[:, c * 128:(c + 1) * 128], y_[:, c * 128:(c + 1) * 128], ident[:, :]), [y_, ident], [pT])
                                kb.op(A_, lambda: nc.scalar.copy(out=yt_[:, :, :], in_=pT[:, 0:256].rearrange("p (c n) -> p c n", c=2)), [pT], [yt_])
                                kb.store(yt_, YT[512 + hd * 256:512 + (hd + 1) * 256, ts].rearrange("(c p) n -> p c n", p=128), yt_[:, :, :])

            if 'YT' in dbg_out and l == 0:
                with Stage(kb) as st:
                    for r0 in range(0, D, 128):
                        b = st.sb([128, TT], BF16, True); b2 = st.sb([128, TT], F32, True)
                        kb.load(b, b[:, :], YT[r0:r0 + 128, :])
                        kb.op(V, lambda: nc.vector.tensor_copy(out=b2[:, :], in_=b[:, :]), [b], [b2])
                        kb.store(b2, dbg_out['YT'][r0:r0 + 128, :], b2[:, :])

            with Stage(kb) as st:
                conv_weights(st, peer_uT[l], KC, NE, UTB)
            with Stage(kb) as st:
                conv_weights(st, peer_v[l], NE // 128, D, VTB)

            with Stage(kb) as st:
                wo = st.sb([128, KC, D], BF16)
                wf = [st.sb([128, D], F32, True) for _ in range(2)]
                for kc in range(KC):
                    a = wf[kc % 2]
                    kb.load(a, a[:, :], w_out[l, kc * 128:(kc + 1) * 128, :])
                    eng = (V, P_)[kc % 2]; E = kb.E[eng]
                    kb.op(eng, lambda: E.tensor_copy(out=wo[:, kc, :], in_=a[:, :]), [a], [wo])
                G1 = st.sb([128, D], F32, True); A2 = st.sb([128, D], F32, True); B2 = st.sb([128, D], F32, True)
                yt = [st.sb([128, KC, 128], BF16, True) for _ in range(2)]
                xt = [st.sb([128, D], F32, True) for _ in range(2)]
                hh = [st.sb([128, D], F32) for _ in range(2)]
                hT = [st.sb([128, KC, 128], BF16, True) for _ in range(2)]
                junk = st.sb([128, D], F32); rs = [st.sb([128, 1], F32) for _ in range(2)]
                curc = None
                for tt in range(NTT):
                    cond = 1 if tt < NTC else 0
                    if cond != curc:
                        modtile(st, G1, cond, 2); modtile(st, A2, cond, 4); modtile(st, B2, cond, 3); curc = cond
                    i2 = tt % 2
                    y = yt[i2]; x = xt[i2]; h = hh[i2]; r = rs[i2]; ht = hT[i2]
                    ts = slice(tt * 128, (tt + 1) * 128)
                    kb.load(y, y[:, :, :], YT[:, ts].rearrange("(kc p) n -> p kc n", p=128))
                    kb.load(x, x[:, :], XL[ts, :])
                    for nb in range(4):
                        pb = PSB[nb]
                        for kc in range(KC):
                            kb.op(PE, lambda: nc.tensor.matmul(pb[:, :], lhsT=y[:, kc, :], rhs=wo[:, kc, nb * 512:(nb + 1) * 512],
                                                               start=(kc == 0), stop=(kc == KC - 1)), [y, wo], [pb])
                        kb.op(V, lambda: nc.vector.tensor_tensor(out=h[:, nb * 512:(nb + 1) * 512], in0=pb[:, :], in1=G1[:, nb * 512:(nb + 1) * 512], op=ALU.mult),
                              [pb, G1], [h])
                    kb.op(P_, lambda: nc.gpsimd.tensor_tensor(out=x[:, :], in0=x[:, :], in1=h[:, :], op=ALU.add), [x, h], [x])
                    kb.store(x, XL[ts, :], x[:, :])
                    rms_rstd(st, (x[:, :], x), D, (r[:, :], r), (junk[:, :], junk))
                    kb.op(V, lambda: nc.vector.scalar_tensor_tensor(out=h[:, :], in0=x[:, :], scalar=r[:, 0:1], in1=A2[:, :], op0=ALU.mult, op1=ALU.mult),
                          [x, r, A2], [h])
                    kb.op(P_, lambda: nc.gpsimd.tensor_tensor(out=h[:, :], in0=h[:, :], in1=B2[:, :], op=ALU.add), [h, B2], [h])
                    for k4 in range(4):
                        pb = PSB[4 + k4 % 2]
                        for j in range(4):
                            kc = k4 * 4 + j
                            kb.op(PE, lambda: nc.tensor.transpose(pb[:, j * 128:(j + 1) * 128], h[:, kc * 128:(kc + 1) * 128], ident[:, :]), [h, ident], [pb])
                        kb.op(A_, lambda: nc.scalar.copy(out=ht[:, k4 * 4:(k4 + 1) * 4, :], in_=pb[:, :].rearrange("p (a b) -> p a b", a=4)), [pb], [ht])
                    kb.store(ht, HFT[:, ts].rearrange("(kc p) n -> p kc n", p=128), ht[:, :, :])

            with Stage(kb) as st:
                wq = st.sb([128, KC, D], BF16)
                wf = [st.sb([128, D], F32, True) for _ in range(2)]
                for kc in range(KC):
                    a = wf[kc % 2]
                    kb.load(a, a[:, :], peer_wq[l, kc * 128:(kc + 1) * 128, :])
                    eng = (V, P_)[kc % 2]; E = kb.E[eng]
                    kb.op(eng, lambda: E.tensor_copy(out=wq[:, kc, :], in_=a[:, :]), [a], [wq])
                kf = st.sb([128, 2, NK], F32, True)
                for p in range(2): kb.load(kf, kf[:, p, :], peer_keysT[l, p])
                kbb = st.sb([128, 2, NK], BF16)
                kb.op(V, lambda: nc.vector.tensor_copy(out=kbb[:, :, :], in_=kf[:, :, :]), [kf], [kbb])
                hfT = [st.sb([128, KC, 128], BF16, True) for _ in range(2)]
                qTt = [st.sb([128, 16, 128], BF16) for _ in range(2)]
                S = [st.sb([128, 8, 2, NK], F32, True) for _ in range(2)]
                S2 = [st.sb([128, 8, 2, NK], F32) for _ in range(2)]
                top = [st.sb([128, 8, 2, 16], F32) for _ in range(2)]
                cand = [st.sb([128, 8, 16, 16], F32) for _ in range(2)]
                cand2 = [st.sb([128, 8, 256], F32) for _ in range(2)]
                ctop = [st.sb([128, 8, 16], F32) for _ in range(2)]
                ex = [st.sb([128, 8, 16], F32) for _ in range(2)]
                zz = [st.sb([128, 8], F32) for _ in range(2)]
                b2 = [st.sb([128, 8], F32, True) for _ in range(2)]
                for tt in range(NTT):
                    i2 = tt % 2
                    ts = slice(tt * 128, (tt + 1) * 128)
                    hf = hfT[i2]; q = qTt[i2]; s_ = S[i2]; s2_ = S2[i2]; tp = top[i2]; cd = cand[i2]; cd2 = cand2[i2]
                    ct = ctop[i2]; e_ = ex[i2]; z_ = zz[i2]; bb = b2[i2]
                    kb.load(hf, hf[:, :, :], HFT[:, ts].rearrange("(kc p) n -> p kc n", p=128))
                    for c4 in range(4):
                        pb = PSB[c4 % 2]
                        for j in range(4):
                            oc = c4 * 4 + j
                            for kc in range(KC):
                                kb.op(PE, lambda: nc.tensor.matmul(pb[:, j * 128:(j + 1) * 128], lhsT=wq[:, kc, oc * 128:(oc + 1) * 128], rhs=hf[:, kc, :],
                                                                   start=(kc == 0), stop=(kc == KC - 1)), [wq, hf], [pb])
                        kb.op(A_, lambda: nc.scalar.copy(out=q[:, c4 * 4:(c4 + 1) * 4, :], in_=pb[:, :].rearrange("p (a b) -> p a b", a=4)), [pb], [q])
                    for hp in range(16):
                        pb = PSB[2 + (hp * NK) // 512 % 2] if NK == 128 else PSB[2]
                        c0 = (hp * NK) % 512
                        kb.op(PE, lambda: nc.tensor.matmul(pb[:, c0:c0 + NK], lhsT=q[:, hp, :], rhs=kbb[:, hp % 2, :], start=True, stop=True), [q, kbb], [pb])
                        if (c0 + NK == 512) or hp == 15:
                            n_in = (c0 + NK) // NK
                            hp0 = hp + 1 - n_in
                            sv = s_[:, :, :, :].rearrange("p h t k -> p (h t) k")
                            kb.op(A_, lambda: nc.scalar.copy(out=sv[:, hp0:hp + 1, :], in_=pb[:, 0:n_in * NK].rearrange("p (a k) -> p a k", k=NK)), [pb], [s_])
                    for h in range(8):
                        for p in range(2):
                            kb.op(V, lambda: nc.vector.max(out=tp[:, h, p, 0:8], in_=s_[:, h, p, :]), [s_], [tp])
                            kb.op(V, lambda: nc.vector.match_replace(out=s2_[:, h, p, :], in_to_replace=tp[:, h, p, 0:8], in_values=s_[:, h, p, :], imm_value=NEG),
                                  [tp, s_], [s2_])
                            kb.op(V, lambda: nc.vector.max(out=tp[:, h, p, 8:16], in_=s2_[:, h, p, :]), [s2_], [tp])
                    for h in range(8):
                        kb.op(P_, lambda: nc.gpsimd.tensor_tensor(out=cd[:, h, :, :], in0=tp[:, h, 0, :].unsqueeze(2).to_broadcast([128, 16, 16]),
                                                                  in1=tp[:, h, 1, :].unsqueeze(1).to_broadcast([128, 16, 16]), op=ALU.add), [tp], [cd])
                    for h in range(8):
                        cf = cd[:, h, :, :].rearrange("p a b -> p (a b)")
                        kb.op(V, lambda: nc.vector.max(out=ct[:, h, 0:8], in_=cf), [cd], [ct])
                        kb.op(V, lambda: nc.vector.match_replace(out=cd2[:, h, :], in_to_replace=ct[:, h, 0:8], in_values=cf, imm_value=NEG), [ct, cd], [cd2])
                        kb.op(V, lambda: nc.vector.max(out=ct[:, h, 8:16], in_=cd2[:, h, :]), [cd2], [ct])
                    kb.op(V, lambda: nc.vector.tensor_tensor(out=e_[:, :, :], in0=ct[:, :, :], in1=ct[:, :, 0:1].to_broadcast([128, 8, 16]), op=ALU.subtract),
                          [ct], [e_])
                    kb.op(A_, lambda: nc.scalar.activation(out=e_[:, :, :], in_=e_[:, :, :], func=AF.Exp), [e_], [e_])
                    kb.op(V, lambda: nc.vector.tensor_reduce(out=z_[:, :], in_=e_[:, :, :], axis=AX.X, op=ALU.add), [e_], [z_])
                    kb.op(A_, lambda: nc.scalar.activation(out=z_[:, :], in_=z_[:, :], func=AF.Ln), [z_], [z_])
                    kb.op(V, lambda: nc.vector.tensor_tensor(out=bb[:, :], in0=ct[:, :, 15], in1=ct[:, :, 0], op=ALU.subtract), [ct], [bb])
                    kb.op(V, lambda: nc.vector.tensor_tensor(out=bb[:, :], in0=bb[:, :], in1=z_[:, :], op=ALU.subtract), [bb, z_], [bb])
                    kb.op(V, lambda: nc.vector.tensor_tensor(out=s_[:, :, 0, :], in0=s_[:, :, 0, :], in1=ct[:, :, 15:16].to_broadcast([128, 8, NK]), op=ALU.subtract),
                          [s_, ct], [s_])
                    kb.store(s_, SS[ts, :], s_[:, :, :, :].rearrange("p h t k -> p (h t k)"))
                    kb.store(bb, BI2[ts, :], bb[:, :])

            with Stage(kb) as st:
                G = 4
                BI = 4
                EBK = BI * NK
                NBLK = NK // BI
                G2 = st.sb([128, D], F32, True)
                ub = [st.sb([128, KC, EBK], BF16, True) for _ in range(2)]
                vb = [st.sb([128, BI, D], BF16, True) for _ in range(2)]
                acc = st.sb([128, G, D], F32)
                hfg = st.sb([128, G, KC, 128], BF16, True)
                Sg = st.sb([128, G, 8, 2, NK], F32, True)
                b2g = st.sb([128, G, 8], F32, True)
                RD = 4
                zt = [st.sb([128, BI, NK], F32) for _ in range(RD)]
                et = [st.sb([128, BI, NK], F32) for _ in range(RD)]
                Gb = [st.sb([128, EBK], BF16) for _ in range(RD)]
                gA = [st.sb([128, EBK], BF16) for _ in range(G)]
                rawA = [st.sb([128, EBK], BF16) for _ in range(G)]
                Wg = [st.sb([128, EBK], F32) for _ in range(2)]
                WgT = [st.sb([128, BI, 128], BF16) for _ in range(2)]
                xt = st.sb([128, D], F32, True)
                pA = PSB[0]; pW = [PSB[1], PSB[2]]; pT = PSB[3]; PO = PSB[4:8]
                tiles = [tt for tt in range(NTT) if not (l == L - 1 and tt < NTC)]
                groups = []
                cur = []
                for tt in tiles:
                    if cur and ((tt < NTC) != (cur[0] < NTC) or len(cur) == G):
                        groups.append(cur); cur = []
                    cur.append(tt)
                if cur: groups.append(cur)
                curc = None
                cnt = [0]
                uTk = kcv(UTB)
                for grp in groups:
                    cond = 1 if grp[0] < NTC else 0
                    if cond != curc:
                        modtile(st, G2, cond, 5); curc = cond
                    for gi_, tt in enumerate(grp):
                        ts = slice(tt * 128, (tt + 1) * 128)
                        kb.load(hfg, hfg[:, gi_, :, :], HFT[:, ts].rearrange("(kc p) n -> p kc n", p=128))
                        kb.load(Sg, Sg[:, gi_, :, :, :].rearrange("p h t k -> p (h t k)"), SS[ts, :])
                        kb.load(b2g, b2g[:, gi_, :], BI2[ts, :])
                    def wload_u(blk):
                        u = ub[blk % 2]
                        e0 = blk * EBK
                        for q4 in range(4):
                            kb.load(u, u[:, q4 * 4:(q4 + 1) * 4, :], uTk[:, q4 * 4:(q4 + 1) * 4, e0:e0 + EBK])

                    def wload_v(blk):
                        v = vb[blk % 2]
                        e0 = blk * EBK
                        for c in range(BI):
                            kb.load(v, v[0:NK, c, :], VTB[e0 + c * NK:e0 + (c + 1) * NK, :])

                    def a_mm(blk, gi_):
                        u = ub[blk % 2]
                        for kc in range(KC):
                            kb.op(PE, lambda: nc.tensor.matmul(pA[:, 0:EBK], lhsT=hfg[:, gi_, kc, :], rhs=u[:, kc, :], start=(kc == 0), stop=(kc == KC - 1)),
                                  [hfg, u], [pA])

                    def tail_T(gi_):
                        wg = Wg[gi_ % 2]
                        for c in range(BI):
                            kb.op(PE, lambda: nc.tensor.transpose(pT[0:NK, c * 128:(c + 1) * 128], wg[:, c * NK:(c + 1) * NK], ident[:, :]), [wg, ident], [pT])

                    def tail_cast(pv):
                        wgt = WgT[pv[0] % 2]
                        kb.op(A_, lambda: nc.scalar.copy(out=wgt[0:NK, :, :], in_=pT[0:NK, 0:BI * 128].rearrange("p (a b) -> p a b", a=BI)), [pT], [wgt])

                    def tail_mm(pv, nb):
                        gi_, pblk = pv
                        wgt = WgT[gi_ % 2]; v = vb[pblk % 2]
                        for c in range(BI):
                            kb.op(PE, lambda: nc.tensor.matmul(PO[nb][:, :], lhsT=wgt[0:NK, c, :], rhs=v[0:NK, c, nb * 512:(nb + 1) * 512],
                                                               start=(c == 0), stop=(c == BI - 1)), [wgt, v], [PO[nb]])

                    def tail_add(pv, nb):
                        gi_, pblk = pv
                        if pblk == 0:
                            kb.op(V, lambda: nc.vector.tensor_copy(out=acc[:, gi_, nb * 512:(nb + 1) * 512], in_=PO[nb][:, :]), [PO[nb]], [acc])
                        else:
                            kb.op(V, lambda: nc.vector.tensor_tensor(out=acc[:, gi_, nb * 512:(nb + 1) * 512], in0=PO[nb][:, :],
                                                                     in1=acc[:, gi_, nb * 512:(nb + 1) * 512], op=ALU.add), [PO[nb], acc], [acc])

                    def heads(blk, gi_, pv):
                        k2 = gi_ % 2
                        pw = pW[k2]; wg = Wg[k2]; ga = gA[gi_]
                        nxt = blk + 1 < NBLK
                        for h in range(8):
                            r4 = cnt[0] % RD; cnt[0] += 1
                            z = zt[r4]; e = et[r4]; gb = Gb[r4]
                            kb.op(P_, lambda: nc.gpsimd.tensor_tensor(out=z[:, :, :],
                                                                      in0=Sg[:, gi_, h, 0, blk * BI:(blk + 1) * BI].unsqueeze(2).to_broadcast([128, BI, NK]),
                                                                      in1=Sg[:, gi_, h, 1, :].unsqueeze(1).to_broadcast([128, BI, NK]), op=ALU.add), [Sg], [z])
                            kb.op(A_, lambda: nc.scalar.activation(out=e[:, :, :], in_=z[:, :, :], func=AF.Exp, bias=b2g[:, gi_, h:h + 1]), [z, b2g], [e])
                            kb.op(V, lambda: nc.vector.scalar_tensor_tensor(out=gb[:, :], in0=z[:, :, :].rearrange("p a b -> p (a b)"), scalar=0.0,
                                                                            in1=e[:, :, :].rearrange("p a b -> p (a b)"), op0=ALU.is_ge, op1=ALU.mult), [z, e], [gb])
                            kb.op(PE, lambda: nc.tensor.matmul(pw[:, 0:EBK], lhsT=identb[:, :], rhs=gb[:, :], start=(h == 0), stop=(h == 7)), [identb, gb], [pw])
                            if nxt and h == 0: a_mm(blk + 1, gi_)
                            if pv is not None and h == 1: tail_cast(pv)
                            if nxt and h == 2:
                                kb.op(A_, lambda: nc.scalar.copy(out=rawA[gi_][:, :], in_=pA[:, 0:EBK]), [pA], [rawA[gi_]])
                            if pv is not None:
                                if 2 <= h <= 5: tail_mm(pv, h - 2)
                                if 3 <= h <= 6: tail_add(pv, h - 3)
                        kb.op(V, lambda: nc.vector.tensor_tensor(out=wg[:, :], in0=pw[:, 0:EBK], in1=ga[:, :], op=ALU.mult), [pw, ga], [wg])
                        tail_T(gi_)

                    wload_u(0); wload_v(0)
                    for gi_, tt in enumerate(grp):
                        a_mm(0, gi_)
                        kb.op(A_, lambda: nc.scalar.activation(out=gA[gi_][:, :], in_=pA[:, 0:EBK], func=AF.Gelu_apprx_tanh), [pA], [gA[gi_]])
                    pv = None
                    for blk in range(NBLK):
                        if blk + 1 < NBLK: wload_u(blk + 1)
                        for gi_, tt in enumerate(grp):
                            heads(blk, gi_, pv)
                            pv = (gi_, blk)
                            if gi_ == 0 and blk + 1 < NBLK: wload_v(blk + 1)
                        if blk + 1 < NBLK:
                            for gi_, tt in enumerate(grp):
                                kb.op(A_, lambda: nc.scalar.activation(out=gA[gi_][:, :], in_=rawA[gi_][:, :], func=AF.Gelu_apprx_tanh), [rawA[gi_]], [gA[gi_]])
                    tail_cast(pv)
                    for nb in range(4): tail_mm(pv, nb)
                    for nb in range(4): tail_add(pv, nb)
                    for gi_, tt in enumerate(grp):
                        ts = slice(tt * 128, (tt + 1) * 128)
                        kb.load(xt, xt[:, :], XL[ts, :])
                        kb.op(P_, lambda: nc.gpsimd.tensor_tensor(out=acc[:, gi_, :], in0=acc[:, gi_, :], in1=G2[:, :], op=ALU.mult), [acc, G2], [acc])
                        kb.op(V, lambda: nc.vector.tensor_tensor(out=xt[:, :], in0=xt[:, :], in1=acc[:, gi_, :], op=ALU.add), [xt, acc], [xt])
                        kb.store(xt, XL[ts, :], xt[:, :])

        with Stage(kb) as st:
            nf = st.sb([128, D], F32, True); kb.load(nf, nf[:, :], bcast_rows(norm_final[0:1, :]))
            xt = [st.sb([128, D], F32, True) for _ in range(2)]
            ot = [st.sb([128, D], F32, True) for _ in range(2)]
            junk = st.sb([128, D], F32); rs = [st.sb([128, 1], F32) for _ in range(2)]
            for k in range(NTL):
                tt = NTC + k; i2 = k % 2
                x = xt[i2]; o = ot[i2]; r = rs[i2]
                kb.load(x, x[:, :], XL[tt * 128:(tt + 1) * 128, :])
                rms_rstd(st, (x[:, :], x), D, (r[:, :], r), (junk[:, :], junk))
                kb.op(V, lambda: nc.vector.scalar_tensor_tensor(out=o[:, :], in0=x[:, :], scalar=r[:, 0:1], in1=nf[:, :], op0=ALU.mult, op1=ALU.mult),
                      [x, r, nf], [o])
                kb.store(o, yout[k * 128:(k + 1) * 128, :], o[:, :])
        kb.barrier()
        cst.__exit__(None, None, None)
    return nc


def host_consts(T, TC, grid_w=64):
    ident = np.eye(128, dtype=np.float32)
    s = np.arange(128)
    tri = np.stack([(s[:, None] <= s[None, :]), (s[:, None] >= s[None, :])]).astype(np.float32)
    pml = np.zeros((4, 9, 128, 128), np.float32)
    pmc = np.zeros((4, 3, 128, 128), np.float32)
    rl = s // grid_w; c = s % grid_w
    for g, w in enumerate(WINS):
        for dk in range(-4, 5):
            rin = 2 * dk + rl[:, None]
            rout = rl[None, :]
            okr = (rin >= rout - w // 2) & (rin < rout - w // 2 + w)
            okc = (c[:, None] >= c[None, :] - w // 2) & (c[:, None] < c[None, :] - w // 2 + w)
            pml[g, dk + 4] = (okr & okc)
        for dk in range(-1, 2):
            cin = 128 * dk + s[:, None]; cout = s[None, :]
            pmc[g, dk + 1] = (cin >= cout - w // 2) & (cin < cout - w // 2 + w)
    return ident, tri, pml, pmc


def make_in_maps(cfg, inp, nb):
    T, TC, L, NK = cfg['T'], cfg['TC'], cfg['L'], cfg['NK']
    f = lambda a: np.ascontiguousarray(np.asarray(a, dtype=np.float32))
    ident, tri, pml, pmc = host_consts(T, TC)
    shared = dict(
        ada_w=f(inp['ada_w']), ada_b=f(inp['ada_b']), norm_mix=f(inp['norm_mix']), norm_ffn=f(inp['norm_ffn']),
        norm_final=f(inp['norm_final']).reshape(1, D), w_in=f(inp['w_in']), b_gate=f(inp['b_gate']),
        sgu_norm=f(inp['sgu_norm']), sgu_wT=f(np.swapaxes(np.asarray(inp['sgu_w']), -1, -2)), sgu_b=f(inp['sgu_b']),
        qk_convT=f(np.swapaxes(np.asarray(inp['qk_conv']), -1, -2)), mlstm_norm=f(inp['mlstm_norm']),
        pool_w=f(inp['pool_w']), pool_scaleT=f(np.swapaxes(np.asarray(inp['pool_scale']).reshape(L, 4, 128), -1, -2)),
        w_out=f(inp['w_out']), peer_wq=f(inp['peer_wq']),
        peer_keysT=f(np.swapaxes(np.asarray(inp['peer_keys']), -1, -2)),
        peer_uT=f(np.swapaxes(np.asarray(inp['peer_u']), -1, -2)), peer_v=f(inp['peer_v']),
        ident=ident, tri=tri, pml=pml, pmc=pmc)
    maps = []
    for b in range(nb):
        m = dict(shared)
        m['xin'] = f(np.concatenate([np.asarray(inp['ctx'])[b], np.asarray(inp['x'])[b]], axis=0))
        cond = np.stack([np.asarray(inp['c'])[b], np.asarray(inp['c_ctx'])], axis=1)
        m['condT'] = f(cond.reshape(KC, 128, 2).transpose(1, 0, 2))
        maps.append(m)
    return maps


def kernel(**inputs):
    cfg = dict(T=4096, TC=256, L=4, NK=128)
    nb = 4
    nc = build(cfg)
    maps = make_in_maps(cfg, inputs, nb)
    res = run_bass_kernel_spmd(nc, maps, core_ids=list(range(nb)))
    out = np.stack([np.asarray(res.results[b]['yout'], dtype=np.float32) for b in range(nb)], axis=0)
    return out
```

```python
import numpy as np
from contextlib import ExitStack
import concourse.bass as bass
import concourse.mybir as mybir
from concourse.bass_utils import run_bass_kernel_spmd

F32 = mybir.dt.float32
BF16 = mybir.dt.bfloat16
AF = mybir.ActivationFunctionType
ALU = mybir.AluOpType
AX = mybir.AxisListType

D = 2048
KC = 16
D_IN = 5648
OFF_U, OFF_V, OFF_P, OFF_Q, OFF_O, OFF_K, OFF_VM, OFF_G = 0, 512, 1024, 1536, 2560, 3584, 4608, 5632
EPS = 1e-6
WINS = (2, 4, 8, 16)
NEG = -3.0e38


class SemW:
    def __init__(s, h, dma=False):
        s.h = h; s.total = 0; s.dma = dma


class Trk:
    __slots__ = ('w', 'r')

    def __init__(s):
        s.w = []; s.r = {}


class Buf:
    def __init__(s, t, ds=None):
        s.t = t; s.k = Trk(); s.ds = ds

    def __getitem__(s, idx):
        return s.t[idx]


class KB:
    def __init__(s, nc, es):
        s.nc = nc; s.es = es
        s.E = dict(pe=nc.tensor, dve=nc.vector, act=nc.scalar, pool=nc.gpsimd, sp=nc.sync)
        s.allsems = []
        s.esem = {e: s.newsem('e_' + e) for e in s.E}
        s.pesems = {s.esem['pe']}
        s.seen = {e: {} for e in s.E}
        s.dpool = []; s.dnext = 0; s.dbase = 0
        s.nm = 0

    def newsem(s, name, dma=False):
        h = s.es.enter_context(s.nc.semaphore(name + '_%d' % len(s.allsems)))
        sw = SemW(h, dma); s.allsems.append(sw); return sw

    def dsem(s):
        if s.dnext >= len(s.dpool):
            s.dpool.append(s.newsem('d', True))
        sw = s.dpool[s.dnext]; s.dnext += 1; return sw

    def _wait(s, e, deps):
        best = {}
        for sw, v in deps:
            if sw.dma: v = sw.total
            if v > best.get(sw, 0): best[sw] = v
        for sw, v in best.items():
            if s.seen[e].get(sw, 0) >= v: continue
            if e == 'pe' and sw in s.pesems: continue
            s.E[e].wait_ge(sw.h, v); s.seen[e][sw] = v

    def op(s, e, fn, reads=(), writes=()):
        deps = []
        for r in reads: deps += r.k.w
        for w in writes:
            deps += w.k.w; deps += list(w.k.r.items())
        s._wait(e, deps)
        sw = s.esem[e]
        if sw.total >= 30000:
            sw = s.newsem('e_' + e); s.esem[e] = sw
            if e == 'pe': s.pesems.add(sw)
        inst = fn(); sw.total += 1; inst.then_inc(sw.h, 1)
        for r in reads: r.k.r[sw] = sw.total
        for w in writes:
            w.k.w = [(sw, sw.total)]; w.k.r = {}
        return inst

    def dma(s, out, in_, sem, reads=(), writes=(), q='sp'):
        deps = []
        for r in reads: deps += r.k.w
        for w in writes:
            deps += w.k.w; deps += list(w.k.r.items())
        s._wait(q, deps)
        inst = s.E[q].dma_start(out=out, in_=in_); sem.total += 16; inst.then_inc(sem.h, 16)
        for r in reads: r.k.r[sem] = sem.total
        for w in writes:
            w.k.w = [(sem, sem.total)]; w.k.r = {}

    def load(s, buf, out, in_, q='sp'):
        s.dma(out, in_, buf.ds, writes=[buf], q=q)

    def store(s, buf, out, in_, q='sp'):
        s.dma(out, in_, buf.ds, reads=[buf], q=q)

    def barrier(s):
        deps = [(sw, sw.total) for sw in s.allsems if sw.total > 0]
        for e in s.E: s._wait(e, deps)


class Stage:
    def __init__(s, kb):
        s.kb = kb

    def __enter__(s):
        s.kb.barrier(); s.es = ExitStack(); s.es.__enter__(); s.kb.dnext = s.kb.dbase; return s

    def __exit__(s, *a):
        s.kb.barrier(); return s.es.__exit__(*a)

    def sb(s, shape, dt=F32, dma=False):
        s.kb.nm += 1
        t = s.es.enter_context(s.kb.nc.sbuf_tensor('b%d' % s.kb.nm, list(shape), dt))
        return Buf(t, s.kb.dsem() if dma else None)

    def ps(s, shape=(128, 512), dt=F32):
        s.kb.nm += 1
        t = s.es.enter_context(s.kb.nc.psum_tensor('p%d' % s.kb.nm, list(shape), dt))
        return Buf(t)


def bcast_rows(ap2d_row, n=128):
    a = ap2d_row
    return bass.AP(tensor=a.tensor, offset=a.offset, ap=[[0, n]] + [list(x) for x in list(a.ap)[1:]])


def build(cfg):
    T, TC, L, NK = cfg['T'], cfg['TC'], cfg['L'], cfg['NK']
    NE = NK * NK
    TT = T + TC
    NTC, NTL = TC // 128, T // 128
    NTT = NTC + NTL
    IB = 16 if NK >= 16 else NK
    NIB = NK // IB
    EB = IB * NK
    dbg = cfg.get('dbg', ())
    _i, _t, _pml, _pmc = host_consts(T, TC)
    PML_NZ = [[bool(_pml[g, d].any()) for d in range(9)] for g in range(4)]
    PMC_NZ = [[bool(_pmc[g, d].any()) for d in range(3)] for g in range(4)]
    nc = bass.Bass("TRN2", target_bir_lowering=False)

    def din(name, shape, dt=F32):
        return nc.dram_tensor(name, list(shape), dt, kind="ExternalInput").ap()

    def dsc(name, shape, dt=F32):
        return nc.dram_tensor(name, list(shape), dt, kind="Internal").ap()

    xin = din('xin', [TT, D])
    condT = din('condT', [128, KC, 2])
    ada_w = din('ada_w', [L, D, 6 * D]); ada_b = din('ada_b', [L, 6 * D])
    norm_mix = din('norm_mix', [L, D]); norm_ffn = din('norm_ffn', [L, D]); norm_final = din('norm_final', [1, D])
    w_in = din('w_in', [L, D, D_IN]); b_gate = din('b_gate', [L, 16])
    sgu_norm = din('sgu_norm', [L, 512]); sgu_wT = din('sgu_wT', [L, 4, 128, 128]); sgu_b = din('sgu_b', [L, 4, 128])
    qk_convT = din('qk_convT', [L, 2048, 3]); mlstm_norm = din('mlstm_norm', [L, 1024])
    pool_w = din('pool_w', [L, 4, 128, 128]); pool_scaleT = din('pool_scaleT', [L, 128, 4])
    w_out = din('w_out', [L, D, D]); peer_wq = din('peer_wq', [L, D, D])
    peer_keysT = din('peer_keysT', [L, 2, 128, NK])
    peer_uT = din('peer_uT', [L, D, NE]); peer_v = din('peer_v', [L, NE, D])
    identd = din('ident', [128, 128])
    trid = din('tri', [2, 128, 128])
    pml = din('pml', [4, 9, 128, 128])
    pmc = din('pmc', [4, 3, 128, 128])
    yout = nc.dram_tensor('yout', [T, D], F32, kind="ExternalOutput").ap()
    dbg_out = {}
    for nm, shp in dbg:
        dbg_out[nm] = nc.dram_tensor('dbg_' + nm, list(shp), F32, kind="ExternalOutput").ap()

    XL = dsc('XL', [TT, D])
    MOD = dsc('MOD', [2, 6 * D])
    UT = dsc('UT', [512, TT]); QR = dsc('QR', [1024, TT]); KR = dsc('KR', [1024, TT])
    VS = dsc('VS', [TT, 512]); PS = dsc('PS', [TT, 512]); OS = dsc('OS', [TT, 1024]); VM = dsc('VM', [TT, 1024])
    GS = dsc('GS', [TT, 16])
    YT = dsc('YT', [D, TT], BF16)
    HFT = dsc('HFT', [D, TT], BF16)
    SS = dsc('SS', [TT, 8 * 2 * NK]); BI2 = dsc('BI2', [TT, 8])
    UTB = dsc('UTB', [D, NE], BF16); VTB = dsc('VTB', [NE, D], BF16)

    es = ExitStack()
    with es:
        kb = KB(nc, es)
        V, A_, P_, PE = 'dve', 'act', 'pool', 'pe'

        def kcv(ap2d):
            return ap2d.rearrange("(kc p) c -> p kc c", p=128)

        cst = Stage(kb); cst.__enter__()
        ident = cst.sb([128, 128], F32, True); kb.load(ident, ident[:, :], identd[:, :])
        identb = cst.sb([128, 128], BF16)
        kb.op(V, lambda: nc.vector.tensor_copy(out=identb[:, :], in_=ident[:, :]), [ident], [identb])
        tri = cst.sb([128, 2, 128], F32, True)
        for i in range(2): kb.load(tri, tri[:, i, :], trid[i])
        trib = cst.sb([128, 2, 128], BF16)
        kb.op(V, lambda: nc.vector.tensor_copy(out=trib[:, :, :], in_=tri[:, :, :]), [tri], [trib])
        ones = cst.sb([128, 128], F32)
        kb.op(V, lambda: nc.vector.memset(ones[:, :], 1.0), [], [ones])
        sct = cst.sb([128, KC, 2], F32, True); kb.load(sct, sct[:, :, :], condT[:, :, :])
        kb.op(A_, lambda: nc.scalar.activation(out=sct[:, :, :], in_=sct[:, :, :], func=AF.Silu), [sct], [sct])
        PSB = [cst.ps() for _ in range(8)]
        kb.dbase = kb.dnext
        with Stage(kb) as st0:
            cp = [st0.sb([128, D], F32, True) for _ in range(2)]
            for tt in range(NTT):
                b = cp[tt % 2]
                kb.load(b, b[:, :], xin[tt * 128:(tt + 1) * 128, :])
                kb.store(b, XL[tt * 128:(tt + 1) * 128, :], b[:, :])

        def rms_rstd(st, xt, width, outcol, junk):
            kb.op(V, lambda: nc.vector.scalar_tensor_tensor(out=junk[0], in0=xt[0], scalar=1.0, in1=xt[0],
                                                            op0=ALU.mult, op1=ALU.mult, accum_out=outcol[0]),
                  [xt[1]], [junk[1], outcol[1]])
            kb.op(V, lambda: nc.vector.tensor_scalar(out=outcol[0], in0=outcol[0], scalar1=1.0 / width, scalar2=EPS,
                                                     op0=ALU.mult, op1=ALU.add), [outcol[1]], [outcol[1]])
            kb.op(A_, lambda: nc.scalar.activation(out=outcol[0], in_=outcol[0], func=AF.Sqrt), [outcol[1]], [outcol[1]])
            kb.op(V, lambda: nc.vector.reciprocal(out=outcol[0], in_=outcol[0]), [outcol[1]], [outcol[1]])

        def conv_weights(st, src2d, nk, ncols, dstdram):
            CW = min(ncols, 2048)
            f = [st.sb([128, CW], F32, True) for _ in range(4)]
            g = [st.sb([128, CW], BF16, True) for _ in range(4)]
            i = 0
            for k in range(nk):
                for c0 in range(0, ncols, CW):
                    a, b = f[i % 4], g[i % 4]
                    kb.load(a, a[:, :], src2d[k * 128:(k + 1) * 128, c0:c0 + CW])
                    eng = (V, P_)[i % 2]
                    E = kb.E[eng]
                    kb.op(eng, lambda: E.tensor_copy(out=b[:, :], in_=a[:, :]), [a], [b])
                    kb.store(b, dstdram[k * 128:(k + 1) * 128, c0:c0 + CW], b[:, :], q='act')
                    i += 1

        for l in range(L):
            with Stage(kb) as st:
                wb = [st.sb([128, KC, 512], F32, True) for _ in range(2)]
                adab = st.sb([2, 6 * D], F32, True)
                for r in range(2): kb.load(adab, adab[r:r + 1, :], ada_b[l:l + 1, :])
                nrm = st.sb([2, 2, D], F32, True)
                for r in range(2):
                    kb.load(nrm, nrm[r:r + 1, 0, :], norm_mix[l:l + 1, :])
                    kb.load(nrm, nrm[r:r + 1, 1, :], norm_ffn[l:l + 1, :])
                mo = [st.sb([2, 512], F32, True) for _ in range(2)]
                for cb in range(24):
                    w = wb[cb % 2]
                    src = kcv(ada_w[l])
                    for q4 in range(4):
                        kb.load(w, w[:, q4 * 4:(q4 + 1) * 4, :], src[:, q4 * 4:(q4 + 1) * 4, cb * 512:(cb + 1) * 512])
                    pb = PSB[cb % 2]
                    for kc in range(KC):
                        kb.op(PE, lambda: nc.tensor.matmul(pb[0:2, :], lhsT=sct[:, kc, :], rhs=w[:, kc, :],
                                                           start=(kc == 0), stop=(kc == KC - 1)), [sct, w], [pb])
                    m = mo[cb % 2]
                    kb.op(V, lambda: nc.vector.tensor_tensor(out=m[:, :], in0=pb[0:2, :], in1=adab[:, cb * 512:(cb + 1) * 512],
                                                             op=ALU.add), [pb, adab], [m])
                    part = cb // 4
                    if part in (1, 4):
                        j = 0 if part == 1 else 1
                        c0 = (cb % 4) * 512
                        kb.op(V, lambda: nc.vector.scalar_tensor_tensor(out=m[:, :], in0=m[:, :], scalar=1.0,
                                                                        in1=nrm[:, j, c0:c0 + 512], op0=ALU.add, op1=ALU.mult),
                              [m, nrm], [m])
                    kb.store(m, MOD[:, cb * 512:(cb + 1) * 512], m[:, :])

            def modtile(st, buf, cond, part):
                kb.load(buf, buf[:, :], bcast_rows(MOD[cond:cond + 1, part * D:(part + 1) * D]))

            with Stage(kb) as st:
                A1 = st.sb([128, D], F32, True); B1 = st.sb([128, D], F32, True)
                bg = st.sb([128, 16], F32, True)
                kb.load(bg, bg[:, :], bcast_rows(b_gate[l:l + 1, :]))
                xt = [st.sb([128, D], F32, True) for _ in range(2)]
                hh = [st.sb([128, D], F32) for _ in range(2)]
                junk = st.sb([128, D], F32)
                rs = [st.sb([128, 1], F32) for _ in range(2)]
                hT = st.sb([128, KC, 1024], BF16)
                wf = [st.sb([128, KC, 512], F32, True) for _ in range(1)]
                wbf = [st.sb([128, KC, 512], BF16) for _ in range(2)]
                ev = [st.sb([128, 512], F32, True) for _ in range(3)]
                evi = [0]
                wi = [0]
                pbi = [0]
                w2 = kcv(w_in[l])
                sts = []
                t0 = 0
                while t0 < NTC: n = min(8, NTC - t0); sts.append((1, t0, n)); t0 += n
                while t0 < NTT: n = min(8, NTT - t0); sts.append((0, t0, n)); t0 += n
                curc = None
                for (cond, tb, n) in sts:
                    if cond != curc:
                        modtile(st, A1, cond, 1); modtile(st, B1, cond, 0); curc = cond
                    NS = n * 128
                    for ti in range(n):
                        tt = tb + ti
                        x = xt[tt % 2]; h = hh[tt % 2]; r = rs[tt % 2]
                        kb.load(x, x[:, :], XL[tt * 128:(tt + 1) * 128, :])
                        rms_rstd(st, (x[:, :], x), D, (r[:, :], r), (junk[:, :], junk))
                        kb.op(V, lambda: nc.vector.scalar_tensor_tensor(out=h[:, :], in0=x[:, :], scalar=r[:, 0:1], in1=A1[:, :],
                                                                        op0=ALU.mult, op1=ALU.mult), [x, r, A1], [h])
                        kb.op(P_, lambda: nc.gpsimd.tensor_tensor(out=h[:, :], in0=h[:, :], in1=B1[:, :], op=ALU.add), [h, B1], [h])
                        for k4 in range(4):
                            pb = PSB[k4 % 2]
                            for j in range(4):
                                kc = k4 * 4 + j
                                kb.op(PE, lambda: nc.tensor.transpose(pb[:, j * 128:(j + 1) * 128], h[:, kc * 128:(kc + 1) * 128], ident[:, :]),
                                      [h, ident], [pb])
                            kb.op(A_, lambda: nc.scalar.copy(out=hT[:, k4 * 4:(k4 + 1) * 4, ti * 128:(ti + 1) * 128],
                                                             in_=pb[:, :].rearrange("p (a b) -> p a b", a=4)), [pb], [hT])

                    def wload(c0, ncols):
                        i = wi[0]; wi[0] += 1
                        a, b = wf[0], wbf[i % 2]
                        for q4 in range(4):
                            kb.load(a, a[:, q4 * 4:(q4 + 1) * 4, 0:ncols], w2[:, q4 * 4:(q4 + 1) * 4, c0:c0 + ncols])
                        eng = (V, P_)[i % 2]; E = kb.E[eng]
                        kb.op(eng, lambda: E.tensor_copy(out=b[:, :, 0:ncols], in_=a[:, :, 0:ncols]), [a], [b])
                        return b

                    def evbuf():
                        e = ev[evi[0] % 3]; evi[0] += 1; return e

                    for (off, nb, dst, fn) in ((OFF_U, 4, UT, AF.Gelu_apprx_tanh), (OFF_Q, 8, QR, None), (OFF_K, 8, KR, None)):
                        for g4 in range(nb // 4):
                            wbb = wload(off + g4 * 512, 512)
                            for j in range(4):
                              for n0 in range(0, NS, 512):
                                nw = min(512, NS - n0)
                                blk = g4 * 4 + j
                                pbi[0] += 1
                                pb = PSB[2 + (pbi[0] % 2)]
                                for kc in range(KC):
                                    kb.op(PE, lambda: nc.tensor.matmul(pb[:, 0:nw], lhsT=wbb[:, kc, j * 128:(j + 1) * 128], rhs=hT[:, kc, n0:n0 + nw],
                                                                       start=(kc == 0), stop=(kc == KC - 1)), [wbb, hT], [pb])
                                e = evbuf()
                                if fn is None:
                                    kb.op(A_, lambda: nc.scalar.copy(out=e[:, 0:nw], in_=pb[:, 0:nw]), [pb], [e])
                                else:
                                    kb.op(A_, lambda: nc.scalar.activation(out=e[:, 0:nw], in_=pb[:, 0:nw], func=fn), [pb], [e])
                                kb.store(e, dst[blk * 128:(blk + 1) * 128, tb * 128 + n0:tb * 128 + n0 + nw], e[:, 0:nw], q='act')
                    for (off, ncols, dst, dc0, fn) in ((OFF_V, 512, VS, 0, AF.Gelu_apprx_tanh), (OFF_P, 512, PS, 0, None),
                                                       (OFF_O, 512, OS, 0, AF.Sigmoid), (OFF_O + 512, 512, OS, 512, AF.Sigmoid),
                                                       (OFF_VM, 512, VM, 0, None), (OFF_VM + 512, 512, VM, 512, None),
                                                       (OFF_G, 16, GS, 0, 'gate')):
                        wbb = wload(off, ncols)
                        for ti in range(n):
                            tt = tb + ti
                            pb = PSB[4 + (ti % 2)]
                            for kc in range(KC):
                                kb.op(PE, lambda: nc.tensor.matmul(pb[:, 0:ncols], lhsT=hT[:, kc, ti * 128:(ti + 1) * 128], rhs=wbb[:, kc, 0:ncols],
                                                                   start=(kc == 0), stop=(kc == KC - 1)), [wbb, hT], [pb])
                            e = evbuf()
                            if fn is None:
                                kb.op(A_, lambda: nc.scalar.copy(out=e[:, 0:ncols], in_=pb[:, 0:ncols]), [pb], [e])
                            elif fn == 'gate':
                                kb.op(V, lambda: nc.vector.tensor_tensor(out=e[:, 0:ncols], in0=pb[:, 0:ncols], in1=bg[:, :], op=ALU.add), [pb, bg], [e])
                            else:
                                kb.op(A_, lambda: nc.scalar.activation(out=e[:, 0:ncols], in_=pb[:, 0:ncols], func=fn), [pb], [e])
                            kb.store(e, dst[tt * 128:(tt + 1) * 128, dc0:dc0 + ncols], e[:, 0:ncols], q='act')

            if 'UT' in dbg_out and l == 0:
                with Stage(kb) as st:
                    for (nm, src) in (('UT', UT), ('QR', QR), ('VS', VS), ('GS', GS), ('OS', OS)):
                        if nm in dbg_out:
                            R, C = src.shape
                            for r0 in range(0, R, 128):
                                rr = min(128, R - r0)
                                b = st.sb([128, C], F32, True)
                                kb.load(b, b[0:rr, :], src[r0:r0 + rr, :]); kb.store(b, dbg_out[nm][r0:r0 + rr, :], b[0:rr, :])

            with Stage(kb) as st:
                sgn = st.sb([128, 512], F32, True); kb.load(sgn, sgn[:, :], bcast_rows(sgu_norm[l:l + 1, :]))
                wsf = st.sb([128, 4, 128], F32, True)
                for h in range(4): kb.load(wsf, wsf[:, h, :], sgu_wT[l, h])
                wsb = st.sb([128, 4, 128], BF16)
                kb.op(V, lambda: nc.vector.tensor_copy(out=wsb[:, :, :], in_=wsf[:, :, :]), [wsf], [wsb])
                sbr = st.sb([1, 4, 128], F32, True); kb.load(sbr, sbr[0:1, :, :], sgu_b[l:l + 1, :, :])
                sbb = st.sb([1, 4, 128], BF16)
                kb.op(V, lambda: nc.vector.tensor_copy(out=sbb[:, :, :], in_=sbr[:, :, :]), [sbr], [sbb])
                onesb = st.sb([1, 128], BF16)
                kb.op(V, lambda: nc.vector.memset(onesb[:, :], 1.0), [], [onesb])
                pwf = st.sb([128, 4, 128], F32, True)
                for g in range(4): kb.load(pwf, pwf[:, g, :], pool_w[l, g])
                pwb = st.sb([128, 4, 128], BF16)
                kb.op(V, lambda: nc.vector.tensor_copy(out=pwb[:, :, :], in_=pwf[:, :, :]), [pwf], [pwb])
                psc = st.sb([128, 4], F32, True); kb.load(psc, psc[:, :], pool_scaleT[l])
                pmf = st.sb([128, 9, 128], F32, True)
                pmlb = st.sb([128, 4, 9, 128], BF16); pmcb = st.sb([128, 4, 3, 128], BF16)
                for g in range(4):
                    for dk in range(9): kb.load(pmf, pmf[:, dk, :], pml[g, dk])
                    kb.op(V, lambda: nc.vector.tensor_copy(out=pmlb[:, g, :, :], in_=pmf[:, :, :]), [pmf], [pmlb])
                for g in range(4):
                    for dk in range(3): kb.load(pmf, pmf[:, dk, :], pmc[g, dk])
                    kb.op(V, lambda: nc.vector.tensor_copy(out=pmcb[:, g, :, :], in_=pmf[:, 0:3, :]), [pmf], [pmcb])
                XPf = st.sb([128, NTT, 512], F32, True)
                XP = st.sb([128, NTT, 4, 129], BF16)
                kb.op(P_, lambda: nc.gpsimd.memset(XP[:, :, :, :], 1.0), [], [XP])
                for tt in range(NTT):
                    kb.load(XPf, XPf[:, tt, :], PS[tt * 128:(tt + 1) * 128, :])
                kb.op(V, lambda: nc.vector.tensor_copy(out=XP[:, :, :, 0:128], in_=XPf[:, :, :].rearrange("p t (g c) -> p t g c", g=4)),
                      [XPf], [XP])
                vt = [st.sb([128, 512], F32, True) for _ in range(2)]
                vn = [st.sb([128, 512], BF16) for _ in range(2)]
                ut = [st.sb([128, 4, 128], F32, True) for _ in range(2)]
                ya = [st.sb([128, 4, 128], BF16, True) for _ in range(2)]
                yc = [st.sb([128, 4, 128], BF16, True) for _ in range(2)]
                junk = st.sb([128, 512], F32); rs = [st.sb([128, 1], F32) for _ in range(2)]
                mean = [st.sb([128, 512], F32) for _ in range(2)]
                rc = [st.sb([128, 4], F32) for _ in range(2)]
                dT = [st.sb([128, 4, 128], BF16) for _ in range(2)]
                for tt in range(NTT):
                    i2 = tt % 2
                    isctx = tt < NTC
                    v = vt[i2]; u = ut[i2]; r = rs[i2]; vb = vn[i2]
                    kb.load(v, v[:, :], VS[tt * 128:(tt + 1) * 128, :])
                    kb.load(u, u[:, :, :], UT[:, tt * 128:(tt + 1) * 128].rearrange("(h p) n -> p h n", p=128))
                    rms_rstd(st, (v[:, :], v), 512, (r[:, :], r), (junk[:, :], junk))
                    kb.op(V, lambda: nc.vector.scalar_tensor_tensor(out=vb[:, :], in0=v[:, :], scalar=r[:, 0:1], in1=sgn[:, :],
                                                                    op0=ALU.mult, op1=ALU.mult), [v, r, sgn], [vb])
                    pb = PSB[i2]
                    for h in range(4):
                        kb.op(PE, lambda: nc.tensor.matmul(pb[:, h * 128:(h + 1) * 128], lhsT=vb[:, h * 128:(h + 1) * 128], rhs=wsb[:, h, :],
                                                           start=True, stop=False), [vb, wsb], [pb])
                        kb.op(PE, lambda: nc.tensor.matmul(pb[:, h * 128:(h + 1) * 128], lhsT=onesb[0:1, :], rhs=sbb[0:1, h, :],
                                                           start=False, stop=True), [onesb, sbb], [pb])
                    y = ya[i2]
                    kb.op(V, lambda: nc.vector.tensor_tensor(out=y[:, :, :], in0=pb[:, :].rearrange("p (h n) -> p h n", h=4), in1=u[:, :, :],
                                                             op=ALU.mult), [pb, u], [y])
                    kb.store(y, YT[0:512, tt * 128:(tt + 1) * 128].rearrange("(h p) n -> p h n", p=128), y[:, :, :])
                    if isctx:
                        k, nt, tbase, pmb, dks, dko = tt, NTC, 0, pmcb, (-1, 0, 1), 1
                    else:
                        k, nt, tbase, pmb, dks, dko = tt - NTC, NTL, NTC, pmlb, tuple(range(-4, 5)), 4
                    mn = mean[i2]; rcc = rc[i2]
                    pb2 = PSB[2 + i2]; pb3 = PSB[4 + i2]
                    for g in range(4):
                        w = WINS[g]
                        nzm = PMC_NZ if isctx else PML_NZ
                        use = [dk for dk in dks if 0 <= k + dk < nt and nzm[g][dk + dko]]
                        pbg = pb2 if g < 2 else pb3
                        c0 = (g % 2) * 129
                        for j, dk in enumerate(use):
                            kb.op(PE, lambda: nc.tensor.matmul(pbg[:, c0:c0 + 129], lhsT=pmb[:, g, dk + dko, :], rhs=XP[:, tbase + k + dk, g, :],
                                                               start=(j == 0), stop=(j == len(use) - 1)), [pmb, XP], [pbg])
                    for g in range(4):
                        pbg = pb2 if g < 2 else pb3
                        c0 = (g % 2) * 129
                        kb.op(V, lambda: nc.vector.reciprocal(out=rcc[:, g:g + 1], in_=pbg[:, c0 + 128:c0 + 129]), [pbg], [rcc])
                        kb.op(V, lambda: nc.vector.scalar_tensor_tensor(out=mn[:, g * 128:(g + 1) * 128], in0=pbg[:, c0:c0 + 128], scalar=rcc[:, g:g + 1],
                                                                        in1=XPf[:, tt, g * 128:(g + 1) * 128], op0=ALU.mult, op1=ALU.subtract),
                              [pbg, rcc, XPf], [mn])
                    pb4 = PSB[6 + i2]
                    for g in range(4):
                        kb.op(PE, lambda: nc.tensor.transpose(pb4[:, g * 128:(g + 1) * 128], mn[:, g * 128:(g + 1) * 128], ident[:, :]), [mn, ident], [pb4])
                    d = dT[i2]
                    kb.op(A_, lambda: nc.scalar.copy(out=d[:, :, :], in_=pb4[:, :].rearrange("p (g n) -> p g n", g=4)), [pb4], [d])
                    for g in range(4):
                        kb.op(PE, lambda: nc.tensor.matmul(pb4[:, g * 128:(g + 1) * 128], lhsT=pwb[:, g, :], rhs=d[:, g, :], start=True, stop=True),
                              [pwb, d], [pb4])
                    y2 = yc[i2]
                    for g in range(4):
                        kb.op(V, lambda: nc.vector.tensor_scalar(out=y2[:, g, :], in0=pb4[:, g * 128:(g + 1) * 128], scalar1=psc[:, g:g + 1], scalar2=None,
                                                                 op0=ALU.mult), [pb4, psc], [y2])
                    kb.store(y2, YT[1536:2048, tt * 128:(tt + 1) * 128].rearrange("(g p) n -> p g n", p=128), y2[:, :, :])

            with Stage(kb) as st:
                EA = st.sb([128, NTT, 8], F32); EBc = st.sb([128, NTT, 8], F32); EBL = st.sb([128, NTT, 8], F32)
                gt = [st.sb([128, 16], F32, True) for _ in range(2)]
                lf = [st.sb([128, 8], F32) for _ in range(2)]
                t1 = [st.sb([128, 8], F32) for _ in range(2)]
                gi = [st.sb([128, 8], F32) for _ in range(2)]
                for tt in range(NTT):
                    i2 = tt % 2
                    g = gt[i2]; f = lf[i2]; a = t1[i2]; gii = gi[i2]
                    kb.load(g, g[:, :], GS[tt * 128:(tt + 1) * 128, :])
                    gv = g[:, :].rearrange("p (d g h) -> p d g h", d=2, g=2)
                    fv = f[:, :].rearrange("p (d h) -> p d h", d=2)
                    av = a[:, :].rearrange("p (d h) -> p d h", d=2)
                    kb.op(A_, lambda: nc.scalar.activation(out=av, in_=gv[:, :, 1, :], func=AF.Abs), [g], [a])
                    kb.op(A_, lambda: nc.scalar.activation(out=a[:, :], in_=a[:, :], func=AF.Exp, scale=-1.0), [a], [a])
                    kb.op(A_, lambda: nc.scalar.activation(out=a[:, :], in_=a[:, :], func=AF.Ln, bias=1.0), [a], [a])
                    kb.op(V, lambda: nc.vector.tensor_scalar(out=fv, in0=gv[:, :, 1, :], scalar1=0.0, scalar2=None, op0=ALU.min), [g], [f])
                    kb.op(V, lambda: nc.vector.tensor_tensor(out=f[:, :], in0=f[:, :], in1=a[:, :], op=ALU.subtract), [f, a], [f])
                    kb.op(V, lambda: nc.vector.tensor_copy(out=gii[:, :].rearrange("p (d h) -> p d h", d=2), in_=gv[:, :, 0, :]), [g], [gii])
                    pb = PSB[i2]
                    for dr in range(2):
                        kb.op(PE, lambda: nc.tensor.matmul(pb[:, dr * 4:dr * 4 + 4], lhsT=tri[:, dr, :], rhs=f[:, dr * 4:dr * 4 + 4], start=True, stop=True),
                              [tri, f], [pb])
                    kb.op(PE, lambda: nc.tensor.matmul(pb[:, 8:16], lhsT=ones[:, :], rhs=f[:, :], start=True, stop=True), [ones, f], [pb])
                    kb.op(A_, lambda: nc.scalar.activation(out=EBc[:, tt, :], in_=pb[:, 0:8], func=AF.Exp), [pb], [EBc])
                    kb.op(A_, lambda: nc.scalar.activation(out=EBL[:, tt, :], in_=pb[:, 8:16], func=AF.Exp), [pb], [EBL])
                    kb.op(V, lambda: nc.vector.tensor_tensor(out=gii[:, :], in0=gii[:, :], in1=pb[:, 0:8], op=ALU.subtract), [gii, pb], [gii])
                    kb.op(A_, lambda: nc.scalar.activation(out=EA[:, tt, :], in_=gii[:, :], func=AF.Exp), [gii], [EA])
                gnb = st.sb([128, 1024], F32, True); kb.load(gnb, gnb[:, :], bcast_rows(mlstm_norm[l:l + 1, :]))
                cw = st.sb([128, 16, 3], F32, True)
                kb.load(cw, cw[:, :, :], qk_convT[l].rearrange("(c p) k -> p c k", p=128))
                raw = [st.sb([128, TT + 2], F32, True) for _ in range(1)]
                cv = [st.sb([128, TT], F32) for _ in range(1)]
                qT = st.sb([128, 2, TT], BF16); kT = st.sb([128, 2, TT], BF16)
                kS = st.sb([128, NTT, 256], BF16)
                vf = [st.sb([128, 256], F32, True) for _ in range(2)]
                vw = [st.sb([128, 257], BF16) for _ in range(2)]
                stm = [st.sb([128, 128], BF16) for _ in range(2)]
                C32 = st.sb([128, 2, 257], F32); Cb = st.sb([128, 2, 257], BF16)
                hs = [st.sb([128, 257], F32) for _ in range(2)]
                dn = [st.sb([128, 1], F32) for _ in range(2)]
                HF = st.sb([128, NTT, 256], F32)
                ot = [st.sb([128, 256], F32, True) for _ in range(2)]
                yb = [st.sb([128, 256], F32) for _ in range(2)]
                ybT = [st.sb([128, 2, 128], BF16, True) for _ in range(2)]
                junk = st.sb([128, 256], F32); rs = [st.sb([128, 1], F32) for _ in range(2)]
                for hd in range(4):
                    for which, (src, dstT, scl) in enumerate(((QR, qT, 1.0), (KR, kT, 1.0 / 16.0))):
                        for c in range(2):
                            ch = hd * 2 + c
                            rw = raw[0]; co = cv[0]
                            wcol = which * 8 + ch
                            kb.op(P_, lambda: nc.gpsimd.memset(rw[:, :], 0.0), [], [rw])
                            kb.load(rw, rw[:, 1:TT + 1], src[ch * 128:(ch + 1) * 128, :])
                            for (s0, sl) in ((0, TC), (TC, T)):
                                kb.op(V, lambda: nc.vector.tensor_scalar(out=co[:, s0:s0 + sl], in0=rw[:, 1 + s0:1 + s0 + sl], scalar1=cw[:, wcol, 1:2],
                                                                         scalar2=None, op0=ALU.mult), [rw, cw], [co])
                                kb.op(V, lambda: nc.vector.scalar_tensor_tensor(out=co[:, s0 + 1:s0 + sl], in0=rw[:, 1 + s0:s0 + sl], scalar=cw[:, wcol, 0:1],
                                                                                in1=co[:, s0 + 1:s0 + sl], op0=ALU.mult, op1=ALU.add), [rw, cw, co], [co])
                                kb.op(V, lambda: nc.vector.scalar_tensor_tensor(out=co[:, s0:s0 + sl - 1], in0=rw[:, 2 + s0:1 + s0 + sl], scalar=cw[:, wcol, 2:3],
                                                                                in1=co[:, s0:s0 + sl - 1], op0=ALU.mult, op1=ALU.add), [rw, cw, co], [co])
                            kb.op(A_, lambda: nc.scalar.activation(out=co[:, :], in_=co[:, :], func=AF.Silu), [co], [co])
                            kb.op(V, lambda: nc.vector.tensor_scalar(out=dstT[:, c, :], in0=co[:, :], scalar1=scl, scalar2=None, op0=ALU.mult), [co], [dstT])
                            if which == 1:
                                for t4 in range(0, NTT, 4):
                                    nn = min(4, NTT - t4)
                                    pb = PSB[(t4 // 4) % 2]
                                    for j in range(nn):
                                        kb.op(PE, lambda: nc.tensor.transpose(pb[:, j * 128:(j + 1) * 128], co[:, (t4 + j) * 128:(t4 + j + 1) * 128], ident[:, :]),
                                              [co, ident], [pb])
                                    kb.op(V, lambda: nc.vector.tensor_scalar(out=kS[:, t4:t4 + nn, c * 128:(c + 1) * 128],
                                                                             in0=pb[:, 0:nn * 128].rearrange("p (a b) -> p a b", a=nn),
                                                                             scalar1=scl, scalar2=None, op0=ALU.mult), [pb], [kS])
                    for dr in range(2):
                        col = dr * 4 + hd
                        kb.op(V, lambda: nc.vector.memset(C32[:, :, :], 0.0), [], [C32])
                        kb.op(V, lambda: nc.vector.memset(Cb[:, :, :], 0.0), [], [Cb])
                        order = list(range(NTT)) if dr == 0 else (list(range(NTC - 1, -1, -1)) + list(range(NTT - 1, NTC - 1, -1)))
                        for ci, tt in enumerate(order):
                            i2 = ci % 2
                            v = vf[i2]; vv = vw[i2]; sm = stm[i2]; h_ = hs[i2]; d_ = dn[i2]
                            ts = slice(tt * 128, (tt + 1) * 128)
                            kb.load(v, v[:, :], VM[ts, hd * 256:(hd + 1) * 256])
                            kb.op(P_, lambda: nc.gpsimd.tensor_scalar(out=vv[:, 0:256], in0=v[:, :], scalar1=EA[:, tt, col:col + 1], scalar2=None, op0=ALU.mult),
                                  [v, EA], [vv])
                            kb.op(P_, lambda: nc.gpsimd.tensor_copy(out=vv[:, 256:257], in_=EA[:, tt, col:col + 1]), [EA], [vv])
                            pS = PSB[2 + i2]
                            for c in range(2):
                                kb.op(PE, lambda: nc.tensor.matmul(pS[:, 0:128], lhsT=kT[:, c, ts], rhs=qT[:, c, ts], start=(c == 0), stop=(c == 1)),
                                      [kT, qT], [pS])
                            kb.op(V, lambda: nc.vector.tensor_tensor(out=sm[:, :], in0=pS[:, 0:128], in1=tri[:, dr, :], op=ALU.mult), [pS, tri], [sm])
                            pH = PSB[4 + i2]
                            kb.op(PE, lambda: nc.tensor.matmul(pH[:, 0:257], lhsT=sm[:, :], rhs=vv[:, :], start=True, stop=False), [sm, vv], [pH])
                            for c in range(2):
                                kb.op(PE, lambda: nc.tensor.matmul(pH[:, 0:257], lhsT=qT[:, c, ts], rhs=Cb[:, c, :], start=False, stop=(c == 1)),
                                      [qT, Cb], [pH])
                            kb.op(A_, lambda: nc.scalar.activation(out=h_[:, :], in_=pH[:, 0:257], func=AF.Copy, scale=EBc[:, tt, col:col + 1]), [pH, EBc], [h_])
                            for c in range(2):
                                pC = PSB[6 + c]
                                kb.op(PE, lambda: nc.tensor.matmul(pC[:, 0:257], lhsT=kS[:, tt, c * 128:(c + 1) * 128], rhs=vv[:, :], start=True, stop=True),
                                      [kS, vv], [pC])
                                kb.op(V, lambda: nc.vector.tensor_tensor(out=C32[:, c, :], in0=pC[:, 0:257], in1=C32[:, c, :], op=ALU.add), [pC, C32], [C32])
                            kb.op(V, lambda: nc.vector.tensor_scalar(out=C32[:, :, :], in0=C32[:, :, :], scalar1=EBL[:, tt, col:col + 1], scalar2=None, op0=ALU.mult),
                                  [C32, EBL], [C32])
                            kb.op(P_, lambda: nc.gpsimd.tensor_copy(out=Cb[:, :, :], in_=C32[:, :, :]), [C32], [Cb])
                            kb.op(A_, lambda: nc.scalar.activation(out=d_[:, :], in_=h_[:, 256:257], func=AF.Abs), [h_], [d_])
                            kb.op(V, lambda: nc.vector.tensor_scalar(out=d_[:, :], in0=d_[:, :], scalar1=1.0, scalar2=None, op0=ALU.max), [d_], [d_])
                            kb.op(V, lambda: nc.vector.reciprocal(out=d_[:, :], in_=d_[:, :]), [d_], [d_])
                            if dr == 0:
                                kb.op(V, lambda: nc.vector.tensor_scalar(out=HF[:, tt, :], in0=h_[:, 0:256], scalar1=d_[:, 0:1], scalar2=None, op0=ALU.mult),
                                      [h_, d_], [HF])
                            else:
                                y_ = yb[i2]; o_ = ot[i2]; r = rs[i2]; yt_ = ybT[i2]
                                kb.op(V, lambda: nc.vector.scalar_tensor_tensor(out=y_[:, :], in0=h_[:, 0:256], scalar=d_[:, 0:1], in1=HF[:, tt, :],
                                                                                op0=ALU.mult, op1=ALU.add), [h_, d_, HF], [y_])
                                kb.load(o_, o_[:, :], OS[ts, hd * 256:(hd + 1) * 256])
                                rms_rstd(st, (y_[:, :], y_), 256, (r[:, :], r), (junk[:, :], junk))
                                kb.op(V, lambda: nc.vector.scalar_tensor_tensor(out=y_[:, :], in0=y_[:, :], scalar=r[:, 0:1], in1=gnb[:, hd * 256:(hd + 1) * 256],
                                                                                op0=ALU.mult, op1=ALU.mult), [y_, r, gnb], [y_])
                                kb.op(P_, lambda: nc.gpsimd.tensor_tensor(out=y_[:, :], in0=y_[:, :], in1=o_[:, :], op=ALU.mult), [y_, o_], [y_])
                                pT = PSB[i2]
                                for c in range(2):
                                    kb.op(PE, lambda: nc.tensor.transpose(pT[:, c * 128:(c + 1) * 128], y_[:, c * 128:(c + 1) * 128], ident[:, :]), [y_, ident], [pT])
                                kb.op(A_, lambda: nc.scalar.copy(out=yt_[:, :, :], in_=pT[:, 0:256].rearrange("p (c n) -> p c n", c=2)), [pT], [yt_])
                                kb.store(yt_, YT[512 + hd * 256:512 + (hd + 1) * 256, ts].rearrange("(c p) n -> p c n", p=128), yt_[:, :, :])

            if 'YT' in dbg_out and l == 0:
                with Stage(kb) as st:
                    for r0 in range(0, D, 128):
                        b = st.sb([128, TT], BF16, True); b2 = st.sb([128, TT], F32, True)
                        kb.load(b, b[:, :], YT[r0:r0 + 128, :])
                        kb.op(V, lambda: nc.vector.tensor_copy(out=b2[:, :], in_=b[:, :]), [b], [b2])
                        kb.store(b2, dbg_out['YT'][r0:r0 + 128, :], b2[:, :])

            with Stage(kb) as st:
                conv_weights(st, peer_uT[l], KC, NE, UTB)
            with Stage(kb) as st:
                conv_weights(st, peer_v[l], NE // 128, D, VTB)

            with Stage(kb) as st:
                wo = st.sb([128, KC, D], BF16)
                wf = [st.sb([128, D], F32, True) for _ in range(2)]
                for kc in range(KC):
                    a = wf[kc % 2]
                    kb.load(a, a[:, :], w_out[l, kc * 128:(kc + 1) * 128, :])
                    eng = (V, P_)[kc % 2]; E = kb.E[eng]
                    kb.op(eng, lambda: E.tensor_copy(out=wo[:, kc, :], in_=a[:, :]), [a], [wo])
                G1 = st.sb([128, D], F32, True); A2 = st.sb([128, D], F32, True); B2 = st.sb([128, D], F32, True)
                yt = [st.sb([128, KC, 128], BF16, True) for _ in range(2)]
                xt = [st.sb([128, D], F32, True) for _ in range(2)]
                hh = [st.sb([128, D], F32) for _ in range(2)]
                hT = [st.sb([128, KC, 128], BF16, True) for _ in range(2)]
                junk = st.sb([128, D], F32); rs = [st.sb([128, 1], F32) for _ in range(2)]
                curc = None
                for tt in range(NTT):
                    cond = 1 if tt < NTC else 0
                    if cond != curc:
                        modtile(st, G1, cond, 2); modtile(st, A2, cond, 4); modtile(st, B2, cond, 3); curc = cond
                    i2 = tt % 2
                    y = yt[i2]; x = xt[i2]; h = hh[i2]; r = rs[i2]; ht = hT[i2]
                    ts = slice(tt * 128, (tt + 1) * 128)
                    kb.load(y, y[:, :, :], YT[:, ts].rearrange("(kc p) n -> p kc n", p=128))
                    kb.load(x, x[:, :], XL[ts, :])
                    for nb in range(4):
                        pb = PSB[nb]
                        for kc in range(KC):
                            kb.op(PE, lambda: nc.tensor.matmul(pb[:, :], lhsT=y[:, kc, :], rhs=wo[:, kc, nb * 512:(nb + 1) * 512],
                                                               start=(kc == 0), stop=(kc == KC - 1)), [y, wo], [pb])
                        kb.op(V, lambda: nc.vector.tensor_tensor(out=h[:, nb * 512:(nb + 1) * 512], in0=pb[:, :], in1=G1[:, nb * 512:(nb + 1) * 512], op=ALU.mult),
                              [pb, G1], [h])
                    kb.op(P_, lambda: nc.gpsimd.tensor_tensor(out=x[:, :], in0=x[:, :], in1=h[:, :], op=ALU.add), [x, h], [x])
                    kb.store(x, XL[ts, :], x[:, :], q='act')
                    rms_rstd(st, (x[:, :], x), D, (r[:, :], r), (junk[:, :], junk))
                    kb.op(V, lambda: nc.vector.scalar_tensor_tensor(out=h[:, :], in0=x[:, :], scalar=r[:, 0:1], in1=A2[:, :], op0=ALU.mult, op1=ALU.mult),
                          [x, r, A2], [h])
                    kb.op(P_, lambda: nc.gpsimd.tensor_tensor(out=h[:, :], in0=h[:, :], in1=B2[:, :], op=ALU.add), [h, B2], [h])
                    for k4 in range(4):
                        pb = PSB[4 + k4 % 2]
                        for j in range(4):
                            kc = k4 * 4 + j
                            kb.op(PE, lambda: nc.tensor.transpose(pb[:, j * 128:(j + 1) * 128], h[:, kc * 128:(kc + 1) * 128], ident[:, :]), [h, ident], [pb])
                        kb.op(A_, lambda: nc.scalar.copy(out=ht[:, k4 * 4:(k4 + 1) * 4, :], in_=pb[:, :].rearrange("p (a b) -> p a b", a=4)), [pb], [ht])
                    kb.store(ht, HFT[:, ts].rearrange("(kc p) n -> p kc n", p=128), ht[:, :, :], q='act')

            with Stage(kb) as st:
                wq = st.sb([128, KC, D], BF16)
                wf = [st.sb([128, D], F32, True) for _ in range(2)]
                for kc in range(KC):
                    a = wf[kc % 2]
                    kb.load(a, a[:, :], peer_wq[l, kc * 128:(kc + 1) * 128, :])
                    eng = (V, P_)[kc % 2]; E = kb.E[eng]
                    kb.op(eng, lambda: E.tensor_copy(out=wq[:, kc, :], in_=a[:, :]), [a], [wq])
                kf = st.sb([128, 2, NK], F32, True)
                for p in range(2): kb.load(kf, kf[:, p, :], peer_keysT[l, p])
                kbb = st.sb([128, 2, NK], BF16)
                kb.op(V, lambda: nc.vector.tensor_copy(out=kbb[:, :, :], in_=kf[:, :, :]), [kf], [kbb])
                hfT = [st.sb([128, KC, 128], BF16, True) for _ in range(2)]
                qTt = [st.sb([128, 16, 128], BF16) for _ in range(2)]
                S = [st.sb([128, 8, 2, NK], F32, True) for _ in range(2)]
                S2 = [st.sb([128, 8, 2, NK], F32) for _ in range(2)]
                top = [st.sb([128, 8, 2, 16], F32) for _ in range(2)]
                cand = [st.sb([128, 8, 16, 16], F32) for _ in range(2)]
                cand2 = [st.sb([128, 8, 256], F32) for _ in range(2)]
                ctop = [st.sb([128, 8, 16], F32) for _ in range(2)]
                ex = [st.sb([128, 8, 16], F32) for _ in range(2)]
                zz = [st.sb([128, 8], F32) for _ in range(2)]
                b2 = [st.sb([128, 8], F32, True) for _ in range(2)]
                for tt in range(NTT):
                    i2 = tt % 2
                    ts = slice(tt * 128, (tt + 1) * 128)
                    hf = hfT[i2]; q = qTt[i2]; s_ = S[i2]; s2_ = S2[i2]; tp = top[i2]; cd = cand[i2]; cd2 = cand2[i2]
                    ct = ctop[i2]; e_ = ex[i2]; z_ = zz[i2]; bb = b2[i2]
                    kb.load(hf, hf[:, :, :], HFT[:, ts].rearrange("(kc p) n -> p kc n", p=128))
                    for c4 in range(4):
                        pb = PSB[c4 % 2]
                        for j in range(4):
                            oc = c4 * 4 + j
                            for kc in range(KC):
                                kb.op(PE, lambda: nc.tensor.matmul(pb[:, j * 128:(j + 1) * 128], lhsT=wq[:, kc, oc * 128:(oc + 1) * 128], rhs=hf[:, kc, :],
                                                                   start=(kc == 0), stop=(kc == KC - 1)), [wq, hf], [pb])
                        kb.op(A_, lambda: nc.scalar.copy(out=q[:, c4 * 4:(c4 + 1) * 4, :], in_=pb[:, :].rearrange("p (a b) -> p a b", a=4)), [pb], [q])
                    for hp in range(16):
                        pb = PSB[2 + (hp * NK) // 512 % 2] if NK == 128 else PSB[2]
                        c0 = (hp * NK) % 512
                        kb.op(PE, lambda: nc.tensor.matmul(pb[:, c0:c0 + NK], lhsT=q[:, hp, :], rhs=kbb[:, hp % 2, :], start=True, stop=True), [q, kbb], [pb])
                        if (c0 + NK == 512) or hp == 15:
                            n_in = (c0 + NK) // NK
                            hp0 = hp + 1 - n_in
                            sv = s_[:, :, :, :].rearrange("p h t k -> p (h t) k")
                            kb.op(A_, lambda: nc.scalar.copy(out=sv[:, hp0:hp + 1, :], in_=pb[:, 0:n_in * NK].rearrange("p (a k) -> p a k", k=NK)), [pb], [s_])
                    for h in range(8):
                        for p in range(2):
                            kb.op(V, lambda: nc.vector.max(out=tp[:, h, p, 0:8], in_=s_[:, h, p, :]), [s_], [tp])
                            kb.op(V, lambda: nc.vector.match_replace(out=s2_[:, h, p, :], in_to_replace=tp[:, h, p, 0:8], in_values=s_[:, h, p, :], imm_value=NEG),
                                  [tp, s_], [s2_])
                            kb.op(V, lambda: nc.vector.max(out=tp[:, h, p, 8:16], in_=s2_[:, h, p, :]), [s2_], [tp])
                    for h in range(8):
                        kb.op(P_, lambda: nc.gpsimd.tensor_tensor(out=cd[:, h, :, :], in0=tp[:, h, 0, :].unsqueeze(2).to_broadcast([128, 16, 16]),
                                                                  in1=tp[:, h, 1, :].unsqueeze(1).to_broadcast([128, 16, 16]), op=ALU.add), [tp], [cd])
                    for h in range(8):
                        cf = cd[:, h, :, :].rearrange("p a b -> p (a b)")
                        kb.op(V, lambda: nc.vector.max(out=ct[:, h, 0:8], in_=cf), [cd], [ct])
                        kb.op(V, lambda: nc.vector.match_replace(out=cd2[:, h, :], in_to_replace=ct[:, h, 0:8], in_values=cf, imm_value=NEG), [ct, cd], [cd2])
                        kb.op(V, lambda: nc.vector.max(out=ct[:, h, 8:16], in_=cd2[:, h, :]), [cd2], [ct])
                    kb.op(V, lambda: nc.vector.tensor_tensor(out=e_[:, :, :], in0=ct[:, :, :], in1=ct[:, :, 0:1].to_broadcast([128, 8, 16]), op=ALU.subtract),
                          [ct], [e_])
                    kb.op(A_, lambda: nc.scalar.activation(out=e_[:, :, :], in_=e_[:, :, :], func=AF.Exp), [e_], [e_])
                    kb.op(V, lambda: nc.vector.tensor_reduce(out=z_[:, :], in_=e_[:, :, :], axis=AX.X, op=ALU.add), [e_], [z_])
                    kb.op(A_, lambda: nc.scalar.activation(out=z_[:, :], in_=z_[:, :], func=AF.Ln), [z_], [z_])
                    kb.op(V, lambda: nc.vector.tensor_tensor(out=bb[:, :], in0=ct[:, :, 15], in1=ct[:, :, 0], op=ALU.subtract), [ct], [bb])
                    kb.op(V, lambda: nc.vector.tensor_tensor(out=bb[:, :], in0=bb[:, :], in1=z_[:, :], op=ALU.subtract), [bb, z_], [bb])
                    kb.op(V, lambda: nc.vector.tensor_tensor(out=s_[:, :, 0, :], in0=s_[:, :, 0, :], in1=ct[:, :, 15:16].to_broadcast([128, 8, NK]), op=ALU.subtract),
                          [s_, ct], [s_])
                    kb.store(s_, SS[ts, :], s_[:, :, :, :].rearrange("p h t k -> p (h t k)"))
                    kb.store(bb, BI2[ts, :], bb[:, :])

            with Stage(kb) as st:
                G = 4
                BI = 4
                EBK = BI * NK
                NBLK = NK // BI
                G2 = st.sb([128, D], F32, True)
                ub = [st.sb([128, KC, EBK], BF16, True) for _ in range(2)]
                vb = [st.sb([128, BI, D], BF16, True) for _ in range(2)]
                acc = st.sb([128, G, D], F32)
                hfg = st.sb([128, G, KC, 128], BF16, True)
                Sg = st.sb([128, G, 8, 2, NK], F32, True)
                b2g = st.sb([128, G, 8], F32, True)
                RD = 4
                zt = [st.sb([128, BI, NK], F32) for _ in range(RD)]
                et = [st.sb([128, BI, NK], F32) for _ in range(RD)]
                Gb = [st.sb([128, EBK], BF16) for _ in range(RD)]
                gA = [st.sb([128, EBK], BF16) for _ in range(G)]
                rawA = [st.sb([128, EBK], BF16) for _ in range(G)]
                Wg = [st.sb([128, EBK], F32) for _ in range(2)]
                WgT = [st.sb([128, BI, 128], BF16) for _ in range(2)]
                xt = st.sb([128, D], F32, True)
                pA = PSB[0]; pW = [PSB[1], PSB[2]]; pT = PSB[3]; PO = PSB[4:8]
                tiles = [tt for tt in range(NTT) if not (l == L - 1 and tt < NTC)]
                groups = []
                cur = []
                for tt in tiles:
                    if cur and ((tt < NTC) != (cur[0] < NTC) or len(cur) == G):
                        groups.append(cur); cur = []
                    cur.append(tt)
                if cur: groups.append(cur)
                curc = None
                cnt = [0]
                uTk = kcv(UTB)
                for grp in groups:
                    cond = 1 if grp[0] < NTC else 0
                    if cond != curc:
                        modtile(st, G2, cond, 5); curc = cond
                    for gi_, tt in enumerate(grp):
                        ts = slice(tt * 128, (tt + 1) * 128)
                        kb.load(hfg, hfg[:, gi_, :, :], HFT[:, ts].rearrange("(kc p) n -> p kc n", p=128))
                        kb.load(Sg, Sg[:, gi_, :, :, :].rearrange("p h t k -> p (h t k)"), SS[ts, :])
                        kb.load(b2g, b2g[:, gi_, :], BI2[ts, :])
                    def wload_u(blk):
                        u = ub[blk % 2]
                        e0 = blk * EBK
                        for q4 in range(4):
                            kb.load(u, u[:, q4 * 4:(q4 + 1) * 4, :], uTk[:, q4 * 4:(q4 + 1) * 4, e0:e0 + EBK])

                    def wload_v(blk):
                        v = vb[blk % 2]
                        e0 = blk * EBK
                        for c in range(BI):
                            kb.load(v, v[0:NK, c, :], VTB[e0 + c * NK:e0 + (c + 1) * NK, :])

                    def a_mm(blk, gi_):
                        u = ub[blk % 2]
                        for kc in range(KC):
                            kb.op(PE, lambda: nc.tensor.matmul(pA[:, 0:EBK], lhsT=hfg[:, gi_, kc, :], rhs=u[:, kc, :], start=(kc == 0), stop=(kc == KC - 1)),
                                  [hfg, u], [pA])

                    def tail_T(gi_):
                        wg = Wg[gi_ % 2]
                        for c in range(BI):
                            kb.op(PE, lambda: nc.tensor.transpose(pT[0:NK, c * 128:(c + 1) * 128], wg[:, c * NK:(c + 1) * NK], ident[:, :]), [wg, ident], [pT])

                    def tail_cast(pv):
                        wgt = WgT[pv[0] % 2]
                        kb.op(A_, lambda: nc.scalar.copy(out=wgt[0:NK, :, :], in_=pT[0:NK, 0:BI * 128].rearrange("p (a b) -> p a b", a=BI)), [pT], [wgt])

                    def tail_mm(pv, nb):
                        gi_, pblk = pv
                        wgt = WgT[gi_ % 2]; v = vb[pblk % 2]
                        for c in range(BI):
                            kb.op(PE, lambda: nc.tensor.matmul(PO[nb][:, :], lhsT=wgt[0:NK, c, :], rhs=v[0:NK, c, nb * 512:(nb + 1) * 512],
                                                               start=(c == 0), stop=(c == BI - 1)), [wgt, v], [PO[nb]])

                    def tail_add(pv, nb):
                        gi_, pblk = pv
                        if pblk == 0:
                            kb.op(V, lambda: nc.vector.tensor_copy(out=acc[:, gi_, nb * 512:(nb + 1) * 512], in_=PO[nb][:, :]), [PO[nb]], [acc])
                        else:
                            kb.op(V, lambda: nc.vector.tensor_tensor(out=acc[:, gi_, nb * 512:(nb + 1) * 512], in0=PO[nb][:, :],
                                                                     in1=acc[:, gi_, nb * 512:(nb + 1) * 512], op=ALU.add), [PO[nb], acc], [acc])

                    def heads(blk, gi_, pv):
                        k2 = gi_ % 2
                        pw = pW[k2]; wg = Wg[k2]; ga = gA[gi_]
                        nxt = blk + 1 < NBLK
                        for h in range(8):
                            r4 = cnt[0] % RD; cnt[0] += 1
                            z = zt[r4]; e = et[r4]; gb = Gb[r4]
                            kb.op(P_, lambda: nc.gpsimd.tensor_tensor(out=z[:, :, :],
                                                                      in0=Sg[:, gi_, h, 0, blk * BI:(blk + 1) * BI].unsqueeze(2).to_broadcast([128, BI, NK]),
                                                                      in1=Sg[:, gi_, h, 1, :].unsqueeze(1).to_broadcast([128, BI, NK]), op=ALU.add), [Sg], [z])
                            kb.op(A_, lambda: nc.scalar.activation(out=e[:, :, :], in_=z[:, :, :], func=AF.Exp, bias=b2g[:, gi_, h:h + 1]), [z, b2g], [e])
                            kb.op(V, lambda: nc.vector.scalar_tensor_tensor(out=gb[:, :], in0=z[:, :, :].rearrange("p a b -> p (a b)"), scalar=0.0,
                                                                            in1=e[:, :, :].rearrange("p a b -> p (a b)"), op0=ALU.is_ge, op1=ALU.mult), [z, e], [gb])
                            kb.op(PE, lambda: nc.tensor.matmul(pw[:, 0:EBK], lhsT=identb[:, :], rhs=gb[:, :], start=(h == 0), stop=(h == 7)), [identb, gb], [pw])
                            if nxt and h == 0: a_mm(blk + 1, gi_)
                            if pv is not None and h == 1: tail_cast(pv)
                            if nxt and h == 2:
                                kb.op(A_, lambda: nc.scalar.copy(out=rawA[gi_][:, :], in_=pA[:, 0:EBK]), [pA], [rawA[gi_]])
                            if pv is not None:
                                if 2 <= h <= 5: tail_mm(pv, h - 2)
                                if 3 <= h <= 6: tail_add(pv, h - 3)
                        kb.op(V, lambda: nc.vector.tensor_tensor(out=wg[:, :], in0=pw[:, 0:EBK], in1=ga[:, :], op=ALU.mult), [pw, ga], [wg])
                        tail_T(gi_)

                    wload_u(0); wload_v(0)
                    for gi_, tt in enumerate(grp):
                        a_mm(0, gi_)
                        kb.op(A_, lambda: nc.scalar.activation(out=gA[gi_][:, :], in_=pA[:, 0:EBK], func=AF.Gelu_apprx_tanh), [pA], [gA[gi_]])
                    pv = None
                    for blk in range(NBLK):
                        if blk + 1 < NBLK: wload_u(blk + 1)
                        for gi_, tt in enumerate(grp):
                            heads(blk, gi_, pv)
                            pv = (gi_, blk)
                            if gi_ == 0 and blk + 1 < NBLK: wload_v(blk + 1)
                        if blk + 1 < NBLK:
                            for gi_, tt in enumerate(grp):
                                kb.op(A_, lambda: nc.scalar.activation(out=gA[gi_][:, :], in_=rawA[gi_][:, :], func=AF.Gelu_apprx_tanh), [rawA[gi_]], [gA[gi_]])
                    tail_cast(pv)
                    for nb in range(4): tail_mm(pv, nb)
                    for nb in range(4): tail_add(pv, nb)
                    for gi_, tt in enumerate(grp):
                        ts = slice(tt * 128, (tt + 1) * 128)
                        kb.load(xt, xt[:, :], XL[ts, :])
                        kb.op(P_, lambda: nc.gpsimd.tensor_tensor(out=acc[:, gi_, :], in0=acc[:, gi_, :], in1=G2[:, :], op=ALU.mult), [acc, G2], [acc])
                        kb.op(V, lambda: nc.vector.tensor_tensor(out=xt[:, :], in0=xt[:, :], in1=acc[:, gi_, :], op=ALU.add), [xt, acc], [xt])
                        kb.store(xt, XL[ts, :], xt[:, :])

        with Stage(kb) as st:
            nf = st.sb([128, D], F32, True); kb.load(nf, nf[:, :], bcast_rows(norm_final[0:1, :]))
            xt = [st.sb([128, D], F32, True) for _ in range(2)]
            ot = [st.sb([128, D], F32, True) for _ in range(2)]
            junk = st.sb([128, D], F32); rs = [st.sb([128, 1], F32) for _ in range(2)]
            for k in range(NTL):
                tt = NTC + k; i2 = k % 2
                x = xt[i2]; o = ot[i2]; r = rs[i2]
                kb.load(x, x[:, :], XL[tt * 128:(tt + 1) * 128, :])
                rms_rstd(st, (x[:, :], x), D, (r[:, :], r), (junk[:, :], junk))
                kb.op(V, lambda: nc.vector.scalar_tensor_tensor(out=o[:, :], in0=x[:, :], scalar=r[:, 0:1], in1=nf[:, :], op0=ALU.mult, op1=ALU.mult),
                      [x, r, nf], [o])
                kb.store(o, yout[k * 128:(k + 1) * 128, :], o[:, :])
        kb.barrier()
        cst.__exit__(None, None, None)
    return nc


def host_consts(T, TC, grid_w=64):
    ident = np.eye(128, dtype=np.float32)
    s = np.arange(128)
    tri = np.stack([(s[:, None] <= s[None, :]), (s[:, None] >= s[None, :])]).astype(np.float32)
    pml = np.zeros((4, 9, 128, 128), np.float32)
    pmc = np.zeros((4, 3, 128, 128), np.float32)
    rl = s // grid_w; c = s % grid_w
    for g, w in enumerate(WINS):
        for dk in range(-4, 5):
            rin = 2 * dk + rl[:, None]
            rout = rl[None, :]
            okr = (rin >= rout - w // 2) & (rin < rout - w // 2 + w)
            okc = (c[:, None] >= c[None, :] - w // 2) & (c[:, None] < c[None, :] - w // 2 + w)
            pml[g, dk + 4] = (okr & okc)
        for dk in range(-1, 2):
            cin = 128 * dk + s[:, None]; cout = s[None, :]
            pmc[g, dk + 1] = (cin >= cout - w // 2) & (cin < cout - w // 2 + w)
    return ident, tri, pml, pmc


def make_in_maps(cfg, inp, nb):
    T, TC, L, NK = cfg['T'], cfg['TC'], cfg['L'], cfg['NK']
    f = lambda a: np.ascontiguousarray(np.asarray(a, dtype=np.float32))
    ident, tri, pml, pmc = host_consts(T, TC)
    shared = dict(
        ada_w=f(inp['ada_w']), ada_b=f(inp['ada_b']), norm_mix=f(inp['norm_mix']), norm_ffn=f(inp['norm_ffn']),
        norm_final=f(inp['norm_final']).reshape(1, D), w_in=f(inp['w_in']), b_gate=f(inp['b_gate']),
        sgu_norm=f(inp['sgu_norm']), sgu_wT=f(np.swapaxes(np.asarray(inp['sgu_w']), -1, -2)), sgu_b=f(inp['sgu_b']),
        qk_convT=f(np.swapaxes(np.asarray(inp['qk_conv']), -1, -2)), mlstm_norm=f(inp['mlstm_norm']),
        pool_w=f(inp['pool_w']), pool_scaleT=f(np.swapaxes(np.asarray(inp['pool_scale']).reshape(L, 4, 128), -1, -2)),
        w_out=f(inp['w_out']), peer_wq=f(inp['peer_wq']),
        peer_keysT=f(np.swapaxes(np.asarray(inp['peer_keys']), -1, -2)),
        peer_uT=f(np.swapaxes(np.asarray(inp['peer_u']), -1, -2)), peer_v=f(inp['peer_v']),
        ident=ident, tri=tri, pml=pml, pmc=pmc)
    maps = []
    for b in range(nb):
        m = dict(shared)
        m['xin'] = f(np.concatenate([np.asarray(inp['ctx'])[b], np.asarray(inp['x'])[b]], axis=0))
        cond = np.stack([np.asarray(inp['c'])[b], np.asarray(inp['c_ctx'])], axis=1)
        m['condT'] = f(cond.reshape(KC, 128, 2).transpose(1, 0, 2))
        maps.append(m)
    return maps


def kernel(**inputs):
    cfg = dict(T=4096, TC=256, L=4, NK=128)
    nb = 4
    nc = build(cfg)
    maps = make_in_maps(cfg, inputs, nb)
    res = run_bass_kernel_spmd(nc, maps, core_ids=list(range(nb)))
    out = np.stack([np.asarray(res.results[b]['yout'], dtype=np.float32) for b in range(nb)], axis=0)
    return out
```
